# Optimizing a Trainium2 kernel written in Bass

```python
import math
import jax, jax.numpy as jnp
from jax import lax
import numpy as np

D_MODEL = 1024
BATCH = 4
SEQ = 8192
DEPTH = 2
DEC_BATCH = 8
DEC_SEQ = 4096
PAST_LEN = 128

N_AB = (DEPTH + 1) // 2
N_C = DEPTH // 2

CONV_WIDTH = D_MODEL // 2
CONV_KERNEL = 31
HYENA_WIDTH = D_MODEL // 2
HYENA_ORDER = 2
HYENA_IN = (HYENA_ORDER + 1) * HYENA_WIDTH
SHORT_KERNEL = 3
HYENA_EMB_DIM = 33
HYENA_FILTER_HIDDEN = 64
HYENA_DECAY_TARGET = 1e-2
HYENA_FAST_DECAY_PCT = 0.3
HYENA_SLOW_DECAY_PCT = 1.5
N_FILTER_CH = 2 * HYENA_ORDER * HYENA_WIDTH
AB_IN_WIDTH = 2 * CONV_WIDTH + HYENA_IN
N_HEADS = 16
N_KV_HEADS = 4
HEAD_DIM = 64
GROUP = N_HEADS // N_KV_HEADS
ROT_DIM = HEAD_DIM // 4
ROPE_THETA = 500000.0
WINDOW = 128
BLOCK = 128
QKV_WIDTH = (N_HEADS + 2 * N_KV_HEADS) * HEAD_DIM
D_FF = 4 * D_MODEL
NORM_EPS = 1e-5
LN_EPS = 1e-5
FILTER_EPS = 1e-6

kernel_name = 'hybrid_conformer_hyena_swa_encoder'


def rms_norm(x, g):
    xf = x.astype(jnp.float32)
    y = xf * lax.rsqrt(jnp.mean(xf * xf, axis=-1, keepdims=True) + NORM_EPS)
    return (y * g.astype(jnp.float32)).astype(x.dtype)


def layer_norm(x, g, b):
    xf = x.astype(jnp.float32)
    mu = jnp.mean(xf, axis=-1, keepdims=True)
    var = jnp.mean(jnp.square(xf - mu), axis=-1, keepdims=True)
    y = (xf - mu) * lax.rsqrt(var + LN_EPS)
    return (y * g.astype(jnp.float32) + b.astype(jnp.float32)).astype(x.dtype)


def depthwise_conv(x, w, b):
    k, c = w.shape
    pad = k // 2
    y = lax.conv_general_dilated(
        x, w[:, None, :].astype(x.dtype), window_strides=(1,), padding=[(pad, pad)],
        dimension_numbers=('NWC', 'WIO', 'NWC'), feature_group_count=c)
    return y + b.astype(x.dtype)


def conformer_conv(u, w_dw, b_dw, ln_g, ln_b):
    a, gate = jnp.split(u, 2, axis=-1)
    h = a * jax.nn.sigmoid(gate)
    h = depthwise_conv(h, w_dw, b_dw)
    h = layer_norm(h, ln_g, ln_b)
    return jax.nn.silu(h)


def hyena_filters(L, w1, b1, w2, b2, w3, b3, w4, freq, decay):
    f32 = jnp.float32
    t = jnp.linspace(0.0, 1.0, L, dtype=f32)[:, None]
    bands = (HYENA_EMB_DIM - 1) // 2
    w = 2.0 * math.pi * jnp.arange(L, dtype=f32) / L
    f = jnp.linspace(1e-4, bands - 1, bands, dtype=f32)
    fw = w[:, None] * f[None, :]
    z = jnp.concatenate([t, jnp.cos(fw), -jnp.sin(fw)], axis=-1)
    fr = freq.astype(f32)
    h = jnp.sin(fr[0] * (z @ w1.astype(f32) + b1.astype(f32)))
    h = jnp.sin(fr[1] * (h @ w2.astype(f32) + b2.astype(f32)))
    h = jnp.sin(fr[2] * (h @ w3.astype(f32) + b3.astype(f32)))
    h = h @ w4.astype(f32)
    h = h * jnp.exp(-t * jnp.abs(decay.astype(f32)))
    h = h.reshape(L, HYENA_ORDER, 2, HYENA_WIDTH)
    fwd, bwd = h[:, :, 0], h[:, :, 1]
    zero = jnp.zeros((1, HYENA_ORDER, HYENA_WIDTH), f32)
    k = jnp.concatenate([fwd, zero, bwd[:0:-1]], axis=0)
    k = k * lax.rsqrt(jnp.sum(k * k, axis=0, keepdims=True) + FILTER_EPS)
    return k


def hyena(u, short_w, short_b, w1, b1, w2, b2, w3, b3, w4, freq, decay, skip):
    L = u.shape[1]
    u = depthwise_conv(u, short_w, short_b)
    x1, x2, v = jnp.split(u, 3, axis=-1)
    k = hyena_filters(L, w1, b1, w2, b2, w3, b3, w4, freq, decay)
    k_f = jnp.fft.rfft(k, axis=0)
    sk = skip.astype(jnp.float32)
    z = v.astype(jnp.float32)
    for n, gate in enumerate((x1, x2)):
        zf = jnp.fft.rfft(z, n=2 * L, axis=1)
        y = jnp.fft.irfft(zf * k_f[None, :, n, :], n=2 * L, axis=1)[:, :L]
        z = gate.astype(jnp.float32) * (y + z * sk[n])
    return z.astype(u.dtype)


def ab_mixer(h, w_in, cv_dw_w, cv_dw_b, cv_ln_g, cv_ln_b, hy_short_w, hy_short_b,
             hy_w1, hy_b1, hy_w2, hy_b2, hy_w3, hy_b3, hy_w4, hy_freq, hy_decay, hy_skip, w_out):
    u = h @ w_in
    y_a = conformer_conv(u[..., :2 * CONV_WIDTH], cv_dw_w, cv_dw_b, cv_ln_g, cv_ln_b)
    y_b = hyena(u[..., 2 * CONV_WIDTH:], hy_short_w, hy_short_b, hy_w1, hy_b1, hy_w2, hy_b2,
                hy_w3, hy_b3, hy_w4, hy_freq, hy_decay, hy_skip)
    return jnp.concatenate([y_a, y_b], axis=-1) @ w_out


def rope_partial(x):
    L = x.shape[1]
    inv = ROPE_THETA ** (-(jnp.arange(0, ROT_DIM, 2, dtype=jnp.float32) / ROT_DIM))
    ang = jnp.arange(L, dtype=jnp.float32)[:, None] * inv[None, :]
    cos = jnp.cos(ang)[None, :, None, :]
    sin = jnp.sin(ang)[None, :, None, :]
    xr = x[..., :ROT_DIM].astype(jnp.float32)
    a, b = xr[..., :ROT_DIM // 2], xr[..., ROT_DIM // 2:]
    rot = jnp.concatenate([a * cos - b * sin, b * cos + a * sin], axis=-1)
    return jnp.concatenate([rot.astype(x.dtype), x[..., ROT_DIM:]], axis=-1)


def window_attention(h, w_qkv, sink, w_o):
    B, L, _ = h.shape
    nb = L // BLOCK
    qkv = h @ w_qkv
    q = qkv[..., :N_HEADS * HEAD_DIM].reshape(B, L, N_HEADS, HEAD_DIM)
    k = qkv[..., N_HEADS * HEAD_DIM:(N_HEADS + N_KV_HEADS) * HEAD_DIM].reshape(B, L, N_KV_HEADS, HEAD_DIM)
    v = qkv[..., (N_HEADS + N_KV_HEADS) * HEAD_DIM:].reshape(B, L, N_KV_HEADS, HEAD_DIM)
    q = rope_partial(q)
    k = rope_partial(k)
    qb = q.reshape(B, nb, BLOCK, N_KV_HEADS, GROUP, HEAD_DIM)
    padw = ((0, 0), (BLOCK, BLOCK), (0, 0), (0, 0))
    kp = jnp.pad(k, padw).reshape(B, nb + 2, BLOCK, N_KV_HEADS, HEAD_DIM)
    vp = jnp.pad(v, padw).reshape(B, nb + 2, BLOCK, N_KV_HEADS, HEAD_DIM)
    kb = jnp.concatenate([kp[:, :-2], kp[:, 1:-1], kp[:, 2:]], axis=2)
    vb = jnp.concatenate([vp[:, :-2], vp[:, 1:-1], vp[:, 2:]], axis=2)
    s = jnp.einsum('bnqkgd,bnskd->bnkgqs', qb, kb,
                   preferred_element_type=jnp.float32) * (HEAD_DIM ** -0.5)
    qpos = jnp.arange(nb)[:, None] * BLOCK + jnp.arange(BLOCK)[None, :]
    kpos = jnp.arange(nb)[:, None] * BLOCK - BLOCK + jnp.arange(3 * BLOCK)[None, :]
    rel = kpos[:, None, :] - qpos[:, :, None]
    valid = (jnp.abs(rel) <= WINDOW) & (kpos >= 0)[:, None, :] & (kpos < L)[:, None, :]
    s = jnp.where(valid[None, :, None, None], s, -jnp.inf)
    sink_b = sink.astype(jnp.float32).reshape(N_KV_HEADS, GROUP)[None, None, :, :, None]
    m = jnp.maximum(jnp.max(s, axis=-1), sink_b)
    p = jnp.exp(s - m[..., None])
    denom = jnp.sum(p, axis=-1) + jnp.exp(sink_b - m)
    p = (p / denom[..., None]).astype(vb.dtype)
    o = jnp.einsum('bnkgqs,bnskd->bnqkgd', p, vb)
    return o.reshape(B, L, N_HEADS * HEAD_DIM) @ w_o


def sq_relu_mlp(h, w_up, w_down):
    return jnp.square(jax.nn.relu(h @ w_up)) @ w_down


def trunk(x, norm_mix, norm_mlp, norm_final, ab_w_in, ab_w_out, cv_dw_w, cv_dw_b, cv_ln_g, cv_ln_b,
          hy_short_w, hy_short_b, hy_w1, hy_b1, hy_w2, hy_b2, hy_w3, hy_b3, hy_w4, hy_freq,
          hy_decay, hy_skip, at_w_qkv, at_sink, at_w_o, mlp_w_up, mlp_w_down):
    for i in range(DEPTH):
        j = i // 2
        hn = rms_norm(x, norm_mix[i])
        if i % 2 == 0:
            x = x + ab_mixer(hn, ab_w_in[j], cv_dw_w[j], cv_dw_b[j], cv_ln_g[j], cv_ln_b[j],
                             hy_short_w[j], hy_short_b[j], hy_w1[j], hy_b1[j], hy_w2[j], hy_b2[j],
                             hy_w3[j], hy_b3[j], hy_w4[j], hy_freq[j], hy_decay[j], hy_skip[j],
                             ab_w_out[j])
        else:
            x = x + window_attention(hn, at_w_qkv[j], at_sink[j], at_w_o[j])
        hn = rms_norm(x, norm_mlp[i])
        x = x + sq_relu_mlp(hn, mlp_w_up[i], mlp_w_down[i])
    return rms_norm(x, norm_final)


def setup_inputs(seed: int = 0) -> dict:
    key = jax.random.key(seed)
    ks = jax.random.split(key, 32)
    f32 = jnp.float32

    def nrm(k, shape, scale):
        return jax.random.normal(k, shape, f32) * scale

    max_decay = math.log(HYENA_DECAY_TARGET) / HYENA_FAST_DECAY_PCT
    min_decay = math.log(HYENA_DECAY_TARGET) / HYENA_SLOW_DECAY_PCT
    base = jnp.tile(jnp.linspace(min_decay, max_decay, HYENA_WIDTH, dtype=f32), 2 * HYENA_ORDER)
    FH = HYENA_FILTER_HIDDEN
    return {
        'x_prompt': nrm(ks[0], (BATCH, SEQ, D_MODEL), 1.0),
        'x_sample': nrm(ks[1], (DEC_BATCH, DEC_SEQ, D_MODEL), 1.0),
        'norm_mix': 1.0 + nrm(ks[2], (DEPTH, D_MODEL), 0.02),
        'norm_mlp': 1.0 + nrm(ks[3], (DEPTH, D_MODEL), 0.02),
        'norm_final': 1.0 + nrm(ks[4], (D_MODEL,), 0.02),
        'ab_w_in': nrm(ks[5], (N_AB, D_MODEL, AB_IN_WIDTH), D_MODEL ** -0.5),
        'ab_w_out': nrm(ks[6], (N_AB, D_MODEL, D_MODEL), D_MODEL ** -0.5),
        'cv_dw_w': nrm(ks[7], (N_AB, CONV_KERNEL, CONV_WIDTH), CONV_KERNEL ** -0.5),
        'cv_dw_b': nrm(ks[8], (N_AB, CONV_WIDTH), 0.02),
        'cv_ln_g': 1.0 + nrm(ks[9], (N_AB, CONV_WIDTH), 0.02),
        'cv_ln_b': nrm(ks[10], (N_AB, CONV_WIDTH), 0.02),
        'hy_short_w': nrm(ks[11], (N_AB, SHORT_KERNEL, HYENA_IN), SHORT_KERNEL ** -0.5),
        'hy_short_b': nrm(ks[12], (N_AB, HYENA_IN), 0.02),
        'hy_w1': nrm(ks[13], (N_AB, HYENA_EMB_DIM, FH), HYENA_EMB_DIM ** -0.5),
        'hy_b1': nrm(ks[14], (N_AB, FH), 0.02),
        'hy_w2': nrm(ks[15], (N_AB, FH, FH), FH ** -0.5),
        'hy_b2': nrm(ks[16], (N_AB, FH), 0.02),
        'hy_w3': nrm(ks[17], (N_AB, FH, FH), FH ** -0.5),
        'hy_b3': nrm(ks[18], (N_AB, FH), 0.02),
        'hy_w4': nrm(ks[19], (N_AB, FH, N_FILTER_CH), FH ** -0.5),
        'hy_freq': 1.0 + nrm(ks[20], (N_AB, 3, FH), 0.1),
        'hy_decay': base[None, :] * (1.0 + nrm(ks[21], (N_AB, N_FILTER_CH), 0.05)),
        'hy_skip': nrm(ks[22], (N_AB, HYENA_ORDER, HYENA_WIDTH), 0.5),
        'at_w_qkv': nrm(ks[23], (N_C, D_MODEL, QKV_WIDTH), D_MODEL ** -0.5),
        'at_sink': nrm(ks[24], (N_C, N_HEADS), 0.5),
        'at_w_o': nrm(ks[25], (N_C, N_HEADS * HEAD_DIM, D_MODEL), (N_HEADS * HEAD_DIM) ** -0.5),
        'mlp_w_up': nrm(ks[26], (DEPTH, D_MODEL, D_FF), D_MODEL ** -0.5),
        'mlp_w_down': nrm(ks[27], (DEPTH, D_FF, D_MODEL), D_FF ** -0.5),
    }


def reference(x_prompt, x_sample, norm_mix, norm_mlp, norm_final, ab_w_in, ab_w_out, cv_dw_w, cv_dw_b,
              cv_ln_g, cv_ln_b, hy_short_w, hy_short_b, hy_w1, hy_b1, hy_w2, hy_b2, hy_w3, hy_b3, hy_w4,
              hy_freq, hy_decay, hy_skip, at_w_qkv, at_sink, at_w_o, mlp_w_up, mlp_w_down):
    weights = (norm_mix, norm_mlp, norm_final, ab_w_in, ab_w_out, cv_dw_w, cv_dw_b, cv_ln_g, cv_ln_b,
               hy_short_w, hy_short_b, hy_w1, hy_b1, hy_w2, hy_b2, hy_w3, hy_b3, hy_w4, hy_freq,
               hy_decay, hy_skip, at_w_qkv, at_sink, at_w_o, mlp_w_up, mlp_w_down)
    y_prompt = trunk(x_prompt, *weights)
    y_sample = trunk(x_sample, *weights)
    return (y_prompt, y_sample)
```

```python
import math
from contextlib import ExitStack

import numpy as np
import concourse.bass as bass
import concourse.mybir as mybir
from concourse.bass_utils import run_bass_kernel_spmd

F32 = mybir.dt.float32
BF16 = mybir.dt.bfloat16
AF = mybir.ActivationFunctionType
ALU = mybir.AluOpType
AX = mybir.AxisListType

NCORES = 8
T = 8192
D = 1024
TT = 512
NT = T // TT
ENGS = ("pe", "act", "dve", "pool", "sp")


class Op:
    __slots__ = ("eng", "fn", "dma", "deps", "sig", "count", "sem", "semval")

    def __init__(self, eng, fn, dma):
        self.eng = eng
        self.fn = fn
        self.dma = dma
        self.deps = set()
        self.sig = False
        self.count = 0
        self.sem = None
        self.semval = 0


class Sched:
    NDMA_SEM = 12

    def __init__(self, nc, es):
        self.nc = nc
        self.streams = {k: [] for k in ENGS}
        self.w = {}
        self.r = {}
        self.dma_rr = {k: 0 for k in ENGS}
        self.dma_last = {}
        self.dma_cnt = {}
        self.esem = {k: es.enter_context(nc.semaphore("e_" + k)) for k in ENGS}
        self.dsem = {}
        for k in ("sp", "act", "pool"):
            for i in range(self.NDMA_SEM):
                self.dsem[(k, i)] = es.enter_context(nc.semaphore("d_%s%d" % (k, i)))
        self.ecount = {k: 0 for k in ENGS}
        self.seen = {k: {} for k in ENGS}
        self.lastc = {}

    def op(self, eng, fn, reads=(), writes=(), dma=False, deps=()):
        o = Op(eng, fn, dma)
        for d in deps:
            if d is not None:
                o.deps.add(d)
        for r in reads:
            lw = self.w.get(r)
            if lw is not None:
                o.deps.add(lw)
        for w_ in writes:
            lw = self.w.get(w_)
            if lw is not None:
                o.deps.add(lw)
            for rd in self.r.get(w_, ()):
                o.deps.add(rd)
        for r in reads:
            self.r.setdefault(r, []).append(o)
        for w_ in writes:
            self.w[w_] = o
            self.r[w_] = []
        if dma:
            i = self.dma_rr[eng]
            self.dma_rr[eng] = (i + 1) % self.NDMA_SEM
            key = (eng, i)
            prev = self.dma_last.get(key)
            if prev is not None:
                o.deps.add(prev)
            self.dma_last[key] = o
            self.dma_cnt[key] = self.dma_cnt.get(key, 0) + 1
            o.sem = key
            o.semval = 16 * self.dma_cnt[key]
        else:
            self.lastc[eng] = o
        o.deps.discard(o)
        self.streams[eng].append(o)
        return o

    def I(self, eng, name, *args, reads=(), writes=(), deps=(), **kw):
        return self.op(eng, lambda e: getattr(e, name)(*args, **kw), reads, writes, deps=deps)

    def dma(self, eng, out, in_, reads=(), writes=(), deps=(), **kw):
        return self.op(eng, lambda e: e.dma_start(out=out, in_=in_, **kw), reads, writes, dma=True, deps=deps)

    def barrier(self):
        dmas = list(self.dma_last.values())
        lastc = dict(self.lastc)
        for k in ENGS:
            o = Op(k, None, False)
            for kk, lo in lastc.items():
                if kk != k:
                    o.deps.add(lo)
            for d in dmas:
                o.deps.add(d)
            self.streams[k].append(o)
        self.w = {}
        self.r = {}

    def flush(self):
        nc = self.nc
        for k in ENGS:
            for o in self.streams[k]:
                for d in o.deps:
                    if not d.dma and (d.eng != o.eng or o.dma or o.eng != "pe"):
                        d.sig = True
        for k in ENGS:
            for o in self.streams[k]:
                if o.sig and o.count == 0:
                    self.ecount[k] += 1
                    o.count = self.ecount[k]
        streams = self.streams
        self.streams = {k: [] for k in ENGS}
        with nc.Block() as block:
            def run(k, e):
                seen = self.seen[k]
                for o in streams[k]:
                    need = {}
                    for d in o.deps:
                        if d.dma:
                            s, v = self.dsem[d.sem], d.semval
                        elif d.eng == k and not o.dma and k == "pe":
                            continue
                        else:
                            assert d.count > 0
                            s, v = self.esem[d.eng], d.count
                        if need.get(id(s), (None, 0))[1] < v:
                            need[id(s)] = (s, v)
                    for s, v in need.values():
                        if seen.get(id(s), 0) < v:
                            e.wait_ge(s, v)
                            seen[id(s)] = v
                    if o.fn is None:
                        continue
                    ins = o.fn(e)
                    if o.dma:
                        ins.then_inc(self.dsem[o.sem], 16)
                    elif o.sig:
                        ins.then_inc(self.esem[k], 1)

            @block.tensor
            def _(e):
                run("pe", e)

            @block.scalar
            def _(e):
                run("act", e)

            @block.vector
            def _(e):
                run("dve", e)

            @block.gpsimd
            def _(e):
                run("pool", e)

            @block.sync
            def _(e):
                run("sp", e)


class Ctx:
    pass


def sb(es, nc, name, shape, dt):
    return es.enter_context(nc.sbuf_tensor(name, list(shape), dt))


def load_weight_bf16(C, es, w_ap, K, N, name, eng_cast=("pool", "act")):
    nc, S = C.nc, C.S
    KC = K // 128
    wb = sb(es, nc, name, [128, KC, N], BF16)
    CW = min(N, 2048)
    with ExitStack() as es2:
        st = [sb(es2, nc, name + "_st%d" % i, [128, CW], F32) for i in range(2)]
        j = 0
        for kc in range(KC):
            for c0 in range(0, N, CW):
                cw = min(CW, N - c0)
                s = st[j % 2]
                S.dma("sp", s[:, 0:cw], w_ap[kc * 128:(kc + 1) * 128, c0:c0 + cw],
                      writes=[(name + "st", j % 2)])
                eng = eng_cast[j % len(eng_cast)]
                if eng == "act":
                    S.I("act", "activation", out=wb[:, kc, c0:c0 + cw], in_=s[:, 0:cw], func=AF.Copy,
                        reads=[(name + "st", j % 2)], writes=[(name, kc)])
                else:
                    S.I(eng, "tensor_copy", wb[:, kc, c0:c0 + cw], s[:, 0:cw],
                        reads=[(name + "st", j % 2)], writes=[(name, kc)])
                j += 1
        S.barrier()
        S.flush()
    return wb


def alloc_norm_tiles(C, es, tag, TK, nx=2, nT=2):
    nc = C.nc
    NS = TK // 128
    N = Ctx()
    N.TK, N.NS, N.tag = TK, NS, tag
    N.xt = [sb(es, nc, tag + "xt%d" % i, [128, NS, D], F32) for i in range(nx)]
    N.hn = sb(es, nc, tag + "hn", [128, NS, D], BF16)
    N.hnT = [sb(es, nc, tag + "hnT%d" % i, [128, 8, TK], BF16) for i in range(nT)]
    N.ss = [sb(es, nc, tag + "ss%d" % i, [128, NS], F32) for i in range(nx)]
    N.rstd = [sb(es, nc, tag + "rstd%d" % i, [128, NS], F32) for i in range(nx)]
    N.junk = sb(es, nc, tag + "junk", [128, D], BF16)
    return N


def norm_load(C, N, x_src, ti):
    S = C.S
    slot = ti % len(N.xt)
    S.dma("sp", N.xt[slot][:, :, :],
          x_src[ti * N.TK:(ti + 1) * N.TK, :].rearrange("(s p) d -> p s d", p=128),
          writes=[(N.tag + "xt", slot)])


def norm_tile(C, N, gcol, ti):
    S = C.S
    tag = N.tag
    slot = ti % len(N.xt)
    tslot = ti % len(N.hnT)
    xs = N.xt[slot]
    ss, rstd = N.ss[slot], N.rstd[slot]
    S.I("pool", "memset", ss[:, :], 0.0, writes=[(tag + "ss", slot)])
    for s in range(N.NS):
        S.I("act", "activation", out=N.junk[:, :], in_=xs[:, s, :], func=AF.Square,
            accum_out=ss[:, s:s + 1],
            reads=[(tag + "xt", slot)], writes=[(tag + "ss", slot), (tag + "junk", 0)])
    S.I("act", "activation", out=rstd[:, :], in_=ss[:, :], func=AF.Ln, scale=1.0 / D, bias=C.eps5[:, 0:1],
        reads=[(tag + "ss", slot)], writes=[(tag + "rstd", slot)])
    S.I("act", "activation", out=rstd[:, :], in_=rstd[:, :], func=AF.Exp, scale=-0.5,
        reads=[(tag + "rstd", slot)], writes=[(tag + "rstd", slot)])
    for s in range(N.NS):
        S.I("dve", "scalar_tensor_tensor", out=N.hn[:, s, :], in0=xs[:, s, :], scalar=rstd[:, s:s + 1],
            in1=gcol[:, :], op0=ALU.mult, op1=ALU.mult,
            reads=[(tag + "xt", slot), (tag + "rstd", slot)], writes=[(tag + "hn", s)])
    hT = N.hnT[tslot]
    for kc in range(8):
        b = C.psT[C.psT_i % len(C.psT)]
        C.psT_i += 1
        for s in range(N.NS):
            S.I("pe", "matmul", C.ps[b][:, s * 128:(s + 1) * 128], N.hn[:, s, kc * 128:(kc + 1) * 128],
                C.ident[:, :], start=True, stop=True,
                reads=[(tag + "hn", s)], writes=[("ps", b)])
        if kc % 2 == 0:
            S.I("act", "activation", out=hT[:, kc, :], in_=C.ps[b][:, 0:N.TK], func=AF.Copy,
                reads=[("ps", b)], writes=[(tag + "hnT", tslot, kc)])
        else:
            S.I("dve", "tensor_copy", hT[:, kc, :], C.ps[b][:, 0:N.TK],
                reads=[("ps", b)], writes=[(tag + "hnT", tslot, kc)])
    return xs, hT, slot, tslot


def load_bcast_row(C, es, row_ap, n, name):
    t = sb(es, C.nc, name, [128, n], F32)
    C.S.dma("sp", t[:, :], row_ap.partition_broadcast(128), writes=[(name, 0)])
    return t


def phase_inproj(C, x_src, g_row, w_ap, NOUT, u_dst, tag):
    nc, S = C.nc, C.S
    with ExitStack() as es:
        gcol = load_bcast_row(C, es, g_row, D, tag + "g")
        wb = load_weight_bf16(C, es, w_ap, D, NOUT, tag + "w")
        N = alloc_norm_tiles(C, es, tag, TT)
        ost = [sb(es, nc, tag + "ost%d" % i, [128, TT], F32) for i in range(4)]
        oi = 0
        norm_load(C, N, x_src, 0)
        nxt = norm_tile(C, N, gcol, 0)
        for ti in range(NT):
            if ti + 1 < NT:
                norm_load(C, N, x_src, ti + 1)
            xs, hT, slot, tslot = nxt
            for oc in range(NOUT // 128):
                if oc == (NOUT // 128) // 2 and ti + 1 < NT:
                    nxt = norm_tile(C, N, gcol, ti + 1)
                b = C.psM[C.psM_i % len(C.psM)]
                C.psM_i += 1
                for kc in range(8):
                    S.I("pe", "matmul", C.ps[b][:, :], wb[:, kc, oc * 128:(oc + 1) * 128], hT[:, kc, :],
                        start=(kc == 0), stop=(kc == 7),
                        reads=[(tag + "hnT", tslot, kc)], writes=[("ps", b)])
                o = oi % 4
                oi += 1
                if oc % 2 == 0:
                    S.I("act", "activation", out=ost[o][:, :], in_=C.ps[b][:, :], func=AF.Copy,
                        reads=[("ps", b)], writes=[(tag + "ost", o)])
                else:
                    S.I("dve", "tensor_copy", ost[o][:, :], C.ps[b][:, :],
                        reads=[("ps", b)], writes=[(tag + "ost", o)])
                S.dma("sp", u_dst[oc * 128:(oc + 1) * 128, ti * TT:(ti + 1) * TT], ost[o][:, :],
                      reads=[(tag + "ost", o)])
        S.barrier()
        S.flush()


def colvec(C, es, ap, R, NCOL, name):
    nc, S = C.nc, C.S
    NCH = NCOL // 128
    out = sb(es, nc, name, [128, NCH, R], F32)
    with ExitStack() as es2:
        rows = sb(es2, nc, name + "_rows", [R, NCOL], F32)
        S.dma("sp", rows[:, :], ap, writes=[(name + "rows", 0)])
        for c in range(NCH):
            b = C.psT[C.psT_i % len(C.psT)]
            C.psT_i += 1
            S.I("pe", "matmul", C.ps[b][:, 0:R], rows[0:R, c * 128:(c + 1) * 128], C.identf[0:R, 0:R],
                start=True, stop=True, reads=[(name + "rows", 0)], writes=[("ps", b)])
            S.I("dve", "tensor_copy", out[:, c, :], C.ps[b][:, 0:R], reads=[("ps", b)], writes=[(name, c)])
        S.barrier()
        S.flush()
    return out


def phase_conformer(C, u, ya_dst):
    nc, S = C.nc, C.S
    W = C.W
    tag = "B"
    HW = 15
    with ExitStack() as es:
        wcol = colvec(C, es, W["cv_dw_w"], 31, 512, "Bw")
        bcol = colvec(C, es, W["cv_dw_b"], 1, 512, "Bb")
        gcol = colvec(C, es, W["cv_ln_g"], 1, 512, "Bg")
        becol = colvec(C, es, W["cv_ln_b"], 1, 512, "Bbe")
        at = [sb(es, nc, "Bat%d" % i, [128, TT + 2 * HW], F32) for i in range(4)]
        gt = [sb(es, nc, "Bgt%d" % i, [128, TT + 2 * HW], F32) for i in range(4)]
        ht = [sb(es, nc, "Bht%d" % i, [128, TT + 2 * HW], F32) for i in range(4)]
        cv = [sb(es, nc, "Bcv%d" % i, [128, TT], F32) for i in range(4)]
        sq = [sb(es, nc, "Bsq%d" % i, [128, TT], F32) for i in range(2)]
        mean = sb(es, nc, "Bmean", [128, TT], F32)
        msq = sb(es, nc, "Bmsq", [128, TT], F32)
        rstd = sb(es, nc, "Brstd", [128, TT], F32)
        t1 = [sb(es, nc, "Bt1%d" % i, [128, TT], F32) for i in range(2)]
        yo = [sb(es, nc, "Byo%d" % i, [128, TT], BF16) for i in range(2)]
        for i in range(4):
            S.I("pool", "memset", at[i][:, :], 0.0, writes=[("Bat", i)])
            S.I("pool", "memset", gt[i][:, :], 0.0, writes=[("Bgt", i)])
        for ti in range(NT):
            t0 = ti * TT
            lo = max(t0 - HW, 0)
            hi = min(t0 + TT + HW, T)
            c0 = lo - (t0 - HW)
            c1 = c0 + (hi - lo)
            for c in range(4):
                if ti == NT - 1:
                    S.I("pool", "memset", at[c][:, c1:], 0.0, writes=[("Bat", c)])
                    S.I("pool", "memset", gt[c][:, c1:], 0.0, writes=[("Bgt", c)])
                S.dma("sp", at[c][:, c0:c1], u[c * 128:(c + 1) * 128, lo:hi], writes=[("Bat", c)])
                S.dma("sp", gt[c][:, c0:c1], u[512 + c * 128:512 + (c + 1) * 128, lo:hi], writes=[("Bgt", c)])
            for c in range(4):
                S.I("act", "activation", out=gt[c][:, :], in_=gt[c][:, :], func=AF.Sigmoid,
                    reads=[("Bgt", c)], writes=[("Bgt", c)])
                eng = "dve" if c < 2 else "pool"
                S.I(eng, "tensor_tensor", ht[c][:, :], at[c][:, :], gt[c][:, :], op=ALU.mult,
                    reads=[("Bat", c), ("Bgt", c)], writes=[("Bht", c)])
                if ti == NT // 2 - 1:
                    S.I(eng, "tensor_scalar", ht[c][:, TT + HW:], ht[c][:, TT + HW:], C.flagcol[:, 0:1], None,
                        op0=ALU.mult, reads=[("Bht", c)], writes=[("Bht", c)])
                if ti == NT // 2:
                    S.I(eng, "tensor_scalar", ht[c][:, 0:HW], ht[c][:, 0:HW], C.flagcol[:, 0:1], None,
                        op0=ALU.mult, reads=[("Bht", c)], writes=[("Bht", c)])
            for j in range(31):
                for c in range(4):
                    eng = "dve"
                    if j == 0:
                        S.I(eng, "tensor_scalar", cv[c][:, :], ht[c][:, 0:TT], wcol[:, c, 0:1], bcol[:, c, 0:1],
                            op0=ALU.mult, op1=ALU.add, reads=[("Bht", c)], writes=[("Bcv", c)])
                    else:
                        S.I(eng, "scalar_tensor_tensor", out=cv[c][:, :], in0=ht[c][:, j:j + TT],
                            scalar=wcol[:, c, j:j + 1], in1=cv[c][:, :], op0=ALU.mult, op1=ALU.add,
                            reads=[("Bht", c)], writes=[("Bcv", c)])
            b1 = C.psM[C.psM_i % len(C.psM)]
            C.psM_i += 1
            b2 = C.psM[C.psM_i % len(C.psM)]
            C.psM_i += 1
            for c in range(4):
                S.I("pe", "matmul", C.ps[b1][:, :], C.onesf[:, :], cv[c][:, :], start=(c == 0), stop=(c == 3),
                    reads=[("Bcv", c)], writes=[("ps", b1)])
            for c in range(4):
                S.I("act", "activation", out=sq[c % 2][:, :], in_=cv[c][:, :], func=AF.Square,
                    reads=[("Bcv", c)], writes=[("Bsq", c % 2)])
                S.I("pe", "matmul", C.ps[b2][:, :], C.onesf[:, :], sq[c % 2][:, :], start=(c == 0), stop=(c == 3),
                    reads=[("Bsq", c % 2)], writes=[("ps", b2)])
            S.I("act", "activation", out=mean[:, :], in_=C.ps[b1][:, :], func=AF.Copy, scale=1.0 / 512,
                reads=[("ps", b1)], writes=[("Bmean", 0)])
            S.I("dve", "tensor_tensor", msq[:, :], mean[:, :], mean[:, :], op=ALU.mult,
                reads=[("Bmean", 0)], writes=[("Bmsq", 0)])
            S.I("dve", "scalar_tensor_tensor", out=rstd[:, :], in0=C.ps[b2][:, :], scalar=1.0 / 512, in1=msq[:, :],
                op0=ALU.mult, op1=ALU.subtract, reads=[("ps", b2), ("Bmsq", 0)], writes=[("Brstd", 0)])
            S.I("act", "activation", out=rstd[:, :], in_=rstd[:, :], func=AF.Ln, bias=C.eps5[:, 0:1],
                reads=[("Brstd", 0)], writes=[("Brstd", 0)])
            S.I("act", "activation", out=rstd[:, :], in_=rstd[:, :], func=AF.Exp, scale=-0.5,
                reads=[("Brstd", 0)], writes=[("Brstd", 0)])
            for c in range(4):
                k = c % 2
                S.I("dve", "tensor_tensor", t1[k][:, :], cv[c][:, :], mean[:, :], op=ALU.subtract,
                    reads=[("Bcv", c), ("Bmean", 0)], writes=[("Bt1", k)])
                S.I("pool", "tensor_tensor", t1[k][:, :], t1[k][:, :], rstd[:, :], op=ALU.mult,
                    reads=[("Bt1", k), ("Brstd", 0)], writes=[("Bt1", k)])
                S.I("act", "activation", out=yo[k][:, :], in_=t1[k][:, :], func=AF.Silu,
                    scale=gcol[:, c, 0:1], bias=becol[:, c, 0:1],
                    reads=[("Bt1", k)], writes=[("Byo", k)])
                S.dma("sp", ya_dst[c * 128:(c + 1) * 128, t0:t0 + TT], yo[k][:, :], reads=[("Byo", k)])
        S.barrier()
        S.flush()


HC128_COLS = {}
HC65_COLS = {}


def hyena_consts():
    p = np.arange(128)
    n1 = (p % 64)[:, None].astype(np.float64)
    k1 = np.arange(65)[None, :].astype(np.float64)
    th = 2 * np.pi * n1 * k1 / 128.0
    F1cat = np.concatenate([np.cos(th), -np.sin(th)], 1)
    n2 = (p % 64)[:, None].astype(np.float64)
    ph = 2 * np.pi * n2 * k1 / 8192.0
    Tc2 = np.concatenate([np.cos(ph), np.cos(ph)], 1)
    Ts2 = np.concatenate([np.sin(ph), np.sin(ph)], 1)
    q64 = np.arange(64)
    psi = 2 * np.pi * (p % 64)[:, None] * q64[None, :] / 64.0
    Mc = np.cos(psi)
    Ms = np.sin(psi)
    rhsA = np.concatenate([Mc, Ms], 1)
    rhsB = np.concatenate([-Ms, Mc], 1)
    sg = ((-1.0) ** np.arange(65))[None, :].repeat(128, 0)
    sgn2 = np.concatenate([sg, sg], 1)
    parts = [("F1cat", F1cat), ("Tc2", Tc2), ("Ts2", Ts2), ("Mc", Mc), ("Ms", Ms), ("nMs", -Ms),
             ("rhsA", rhsA), ("rhsB", rhsB), ("sgn2", sgn2)]
    off = 0
    for nm, a in parts:
        HC128_COLS[nm] = (off, a.shape[1])
        off += a.shape[1]
    hc128 = np.concatenate([a for _, a in parts], 1).astype(np.float32)
    kk = np.arange(65)[:, None].astype(np.float64)
    col = np.arange(128)[None, :]
    phi = 2 * np.pi * kk * (col % 64) / 8192.0
    Tci = np.cos(phi)
    Tsi = np.sin(phi)
    wk = np.full((65, 1), 2.0)
    wk[0, 0] = 1.0
    wk[64, 0] = 1.0
    n1c = np.arange(64)[None, :].astype(np.float64)
    thi = 2 * np.pi * kk * n1c / 128.0
    gc = wk * np.cos(thi) / 8192.0
    gs = -wk * np.sin(thi) / 8192.0
    zz = np.zeros((65, 64))
    parts = [("Tci", Tci), ("Tsi", Tsi), ("Gc_a", np.concatenate([gc, zz], 1)), ("Gs_a", np.concatenate([gs, zz], 1)),
             ("Gc_b", np.concatenate([zz, gc], 1)), ("Gs_b", np.concatenate([zz, gs], 1))]
    off = 0
    for nm, a in parts:
        HC65_COLS[nm] = (off, a.shape[1])
        off += a.shape[1]
    hc65 = np.concatenate([a for _, a in parts], 1).astype(np.float32)
    return hc128, hc65


def hyena_pos_tables(is_prompt):
    L = 8192 if is_prompt else 4096
    q = np.arange(8192)
    n2 = q // 128
    hp = q % 128
    half = hp // 64
    n1 = hp % 64
    pl = n1 * 64 + n2
    pos = pl + (4096 * half if is_prompt else 0)
    t = pos.astype(np.float64) / (L - 1)
    bands = 16
    f = np.linspace(1e-4, bands - 1, bands)
    w = 2 * np.pi * pos.astype(np.float64) / L
    fw = w[None, :] * f[:, None]
    zT = np.concatenate([t[None, :], np.cos(fw), -np.sin(fw)], 0).astype(np.float32)
    tn = -t.reshape(64, 128).T.copy()
    if not is_prompt:
        tn[64:, :] = -1e4
    tnb = tn.copy()
    tnb[0, 0] = -1e4
    return zT, np.concatenate([tn, tnb], 1).astype(np.float32)


def hc(C, name, rows=128):
    off, n = (HC128_COLS if rows == 128 else HC65_COLS)[name]
    t = C.hc128 if rows == 128 else C.hc65
    return t[0:rows, off:off + n]


def phase_shortconv(C, u, uh):
    nc, S = C.nc, C.S
    W = C.W
    with ExitStack() as es:
        wcol = colvec(C, es, W["hy_short_w"], 3, 1536, "Sw")
        bcol = colvec(C, es, W["hy_short_b"], 1, 1536, "Sb")
        it = [sb(es, nc, "Sit%d" % i, [128, TT + 2], F32) for i in range(3)]
        cv = [sb(es, nc, "Scv%d" % i, [128, TT], F32) for i in range(2)]
        ot = [sb(es, nc, "Sot%d" % i, [128, 4, 128], F32) for i in range(2)]
        k = 0
        for c in range(12):
            for ti in range(NT):
                t0 = ti * TT
                lo = max(t0 - 1, 0)
                hi = min(t0 + TT + 1, T)
                c0 = lo - (t0 - 1)
                c1 = c0 + (hi - lo)
                i3 = k % 3
                i2 = k % 2
                k += 1
                if ti == 0:
                    S.I("pool", "memset", it[i3][:, 0:1], 0.0, writes=[("Sit", i3)])
                if ti == NT - 1:
                    S.I("pool", "memset", it[i3][:, TT + 1:TT + 2], 0.0, writes=[("Sit", i3)])
                S.dma("sp", it[i3][:, c0:c1], u[1024 + c * 128:1024 + (c + 1) * 128, lo:hi], writes=[("Sit", i3)])
                if ti == NT // 2 - 1:
                    S.I("pool", "tensor_scalar", it[i3][:, TT + 1:TT + 2], it[i3][:, TT + 1:TT + 2],
                        C.flagcol[:, 0:1], None, op0=ALU.mult, reads=[("Sit", i3)], writes=[("Sit", i3)])
                if ti == NT // 2:
                    S.I("pool", "tensor_scalar", it[i3][:, 0:1], it[i3][:, 0:1],
                        C.flagcol[:, 0:1], None, op0=ALU.mult, reads=[("Sit", i3)], writes=[("Sit", i3)])
                S.I("dve", "tensor_scalar", cv[i2][:, :], it[i3][:, 0:TT], wcol[:, c, 0:1], bcol[:, c, 0:1],
                    op0=ALU.mult, op1=ALU.add, reads=[("Sit", i3)], writes=[("Scv", i2)])
                for j in (1, 2):
                    S.I("dve", "scalar_tensor_tensor", out=cv[i2][:, :], in0=it[i3][:, j:j + TT],
                        scalar=wcol[:, c, j:j + 1], in1=cv[i2][:, :], op0=ALU.mult, op1=ALU.add,
                        reads=[("Sit", i3)], writes=[("Scv", i2)])
                b = C.psT[C.psT_i % len(C.psT)]
                C.psT_i += 1
                for s in range(4):
                    S.I("pe", "matmul", C.ps[b][:, s * 128:(s + 1) * 128], cv[i2][:, s * 128:(s + 1) * 128],
                        C.identf[:, :], start=True, stop=True, reads=[("Scv", i2)], writes=[("ps", b)])
                S.I("act", "activation", out=ot[i2][:, :, :], in_=C.ps[b][:, :].rearrange("p (s c) -> p s c", s=4),
                    func=AF.Copy, reads=[("ps", b)], writes=[("Sot", i2)])
                S.dma("sp", uh[t0:t0 + TT, c * 128:(c + 1) * 128].rearrange("(s p) c -> p s c", p=128),
                      ot[i2][:, :, :], reads=[("Sot", i2)])
        S.barrier()
        S.flush()


def alloc_fft_tiles(C, es, pfx):
    nc = C.nc
    Fq = Ctx()
    Fq.A2 = sb(es, nc, pfx + "A2", [64, 16, 130], F32)
    Fq.B = sb(es, nc, pfx + "B", [64, 2, 16, 65], F32)
    Fq.tmpA = sb(es, nc, pfx + "tmpA", [128, 2080], F32)
    Fq.tmpB = sb(es, nc, pfx + "tmpB", [128, 2080], F32)
    Fq.pfx = pfx
    return Fq


def fft_fwd_group(C, Fq, src, src_tok, ch0, X, X_tok):
    S = C.S
    pfx = Fq.pfx
    F1 = hc(C, "F1cat")
    slot = 0
    while slot < 16:
        nb = 2
        b = C.psM[C.psM_i % len(C.psM)]
        C.psM_i += 1
        for s in range(nb):
            i = slot + s
            h, chl = i // 8, i % 8
            S.I("pe", "matmul", C.ps[b][0:64, s * 256:s * 256 + 130],
                src[h * 64:(h + 1) * 64, :, ch0 + chl], F1[h * 64:(h + 1) * 64, :],
                start=True, stop=True, reads=[src_tok], writes=[("ps", b)])
        S.I("act", "activation", out=Fq.A2[:, slot:slot + nb, :],
            in_=C.ps[b][0:64, :].rearrange("p (s c) -> p s c", s=2)[:, :, 0:130], func=AF.Copy,
            reads=[("ps", b)], writes=[(pfx + "A2", 0)])
        slot += nb
    if C.dbg.get("ffstop") == 1:
        return
    Tc = hc(C, "Tc2")[0:64, :].unsqueeze(1).broadcast_to([64, 16, 130])
    Ts = hc(C, "Ts2")[0:64, :].unsqueeze(1).broadcast_to([64, 16, 130])
    P1 = Fq.tmpA[0:64, 0:2080].rearrange("p (a c) -> p a c", a=16)
    P2 = Fq.tmpB[0:64, 0:2080].rearrange("p (a c) -> p a c", a=16)
    S.I("dve", "tensor_tensor", P1, Fq.A2[:, :, :], Tc, op=ALU.mult,
        reads=[(pfx + "A2", 0)], writes=[(pfx + "tmpA", 0)])
    S.I("pool", "tensor_tensor", P2, Fq.A2[:, :, :], Ts, op=ALU.mult,
        reads=[(pfx + "A2", 0)], writes=[(pfx + "tmpB", 0)])
    S.I("dve", "tensor_tensor", Fq.B[:, 0, :, :], P1[:, :, 0:65], P2[:, :, 65:130], op=ALU.add,
        reads=[(pfx + "tmpA", 0), (pfx + "tmpB", 0)], writes=[(pfx + "B", 0)])
    S.I("pool", "tensor_tensor", Fq.B[:, 1, :, :], P1[:, :, 65:130], P2[:, :, 0:65], op=ALU.subtract,
        reads=[(pfx + "tmpA", 0), (pfx + "tmpB", 0)], writes=[(pfx + "B", 1)])
    if C.dbg.get("ffstop") == 2:
        return
    Mc, Ms, nMs = hc(C, "Mc")[0:64, :], hc(C, "Ms")[0:64, :], hc(C, "nMs")[0:64, :]
    for q in range(4):
        Br = Fq.B[:, 0, 4 * q:4 * q + 4, :]
        Bi = Fq.B[:, 1, 4 * q:4 * q + 4, :]
        b1 = C.psM[C.psM_i % len(C.psM)]
        C.psM_i += 1
        b2 = C.psM[C.psM_i % len(C.psM)]
        C.psM_i += 1
        S.I("pe", "matmul", C.ps[b1][0:64, 0:260], Mc, Br, start=True, stop=False,
            reads=[(pfx + "B", 0)], writes=[("ps", b1)])
        S.I("pe", "matmul", C.ps[b1][0:64, 0:260], Ms, Bi, start=False, stop=True,
            reads=[(pfx + "B", 1)], writes=[("ps", b1)])
        S.I("pe", "matmul", C.ps[b2][0:64, 0:260], Mc, Bi, start=True, stop=False,
            reads=[(pfx + "B", 1)], writes=[("ps", b2)])
        S.I("pe", "matmul", C.ps[b2][0:64, 0:260], nMs, Br, start=False, stop=True,
            reads=[(pfx + "B", 0)], writes=[("ps", b2)])
        h, c4 = q // 2, 4 * (q % 2)
        S.I("act", "activation", out=X[:, 0, h, c4:c4 + 4, :],
            in_=C.ps[b1][0:64, 0:260].rearrange("p (a c) -> p a c", a=4), func=AF.Copy,
            reads=[("ps", b1)], writes=[X_tok])
        S.I("act", "activation", out=X[:, 1, h, c4:c4 + 4, :],
            in_=C.ps[b2][0:64, 0:260].rearrange("p (a c) -> p a c", a=4), func=AF.Copy,
            reads=[("ps", b2)], writes=[X_tok])


def phase_filters(C, Gd, zT_in, tneg_in):
    nc, S = C.nc, C.S
    W = C.W
    with ExitStack() as es:
        fcol = sb(es, nc, "Ffcol", [64, 3], F32)
        bcol = sb(es, nc, "Fbcol", [64, 3], F32)
        f8 = sb(es, nc, "Ff8", [64, 3], F32)
        b8 = sb(es, nc, "Fb8", [64, 3], F32)
        f4 = sb(es, nc, "Ff4", [64, 3], F32)
        b4 = sb(es, nc, "Fb4", [64, 3], F32)
        with ExitStack() as es2:
            rows = sb(es2, nc, "Frows", [6, 64], F32)
            S.dma("sp", rows[0:3, :], W["hy_freq"], writes=[("Frows", 0)])
            for i, nm in enumerate(("hy_b1", "hy_b2", "hy_b3")):
                S.dma("sp", rows[3 + i:4 + i, :], W[nm], writes=[("Frows", 0)])
            b = C.psT[C.psT_i % len(C.psT)]
            C.psT_i += 1
            S.I("pe", "matmul", C.ps[b][0:64, 0:6], rows[0:6, 0:64], C.identf[0:6, 0:6], start=True, stop=True,
                reads=[("Frows", 0)], writes=[("ps", b)])
            S.I("dve", "tensor_copy", fcol[:, :], C.ps[b][0:64, 0:3], reads=[("ps", b)], writes=[("Ffcol", 0)])
            S.I("dve", "tensor_copy", bcol[:, :], C.ps[b][0:64, 3:6], reads=[("ps", b)], writes=[("Fbcol", 0)])
            S.I("dve", "tensor_tensor", bcol[:, :], bcol[:, :], fcol[:, :], op=ALU.mult,
                reads=[("Ffcol", 0), ("Fbcol", 0)], writes=[("Fbcol", 0)])
            S.I("dve", "tensor_scalar", f8[:, :], fcol[:, :], 0.125, None, op0=ALU.mult,
                reads=[("Ffcol", 0)], writes=[("Ff8", 0)])
            S.I("dve", "tensor_scalar", b8[:, :], bcol[:, :], 0.125, None, op0=ALU.mult,
                reads=[("Fbcol", 0)], writes=[("Fb8", 0)])
            S.I("dve", "tensor_scalar", f4[:, :], fcol[:, :], 0.25, None, op0=ALU.mult,
                reads=[("Ffcol", 0)], writes=[("Ff4", 0)])
            S.I("dve", "tensor_scalar", b4[:, :], bcol[:, :], 0.25, None, op0=ALU.mult,
                reads=[("Fbcol", 0)], writes=[("Fb4", 0)])
            S.barrier()
            S.flush()
        w1 = sb(es, nc, "Fw1", [33, 64], F32)
        w2 = sb(es, nc, "Fw2", [64, 64], F32)
        w3 = sb(es, nc, "Fw3", [64, 64], F32)
        w4 = sb(es, nc, "Fw4", [64, 2048], F32)
        S.dma("sp", w1[:, :], W["hy_w1"])
        S.dma("sp", w2[:, :], W["hy_w2"])
        S.dma("sp", w3[:, :], W["hy_w3"])
        S.dma("sp", w4[:, :], W["hy_w4"])
        absd = sb(es, nc, "Fabsd", [128, 2048], F32)
        S.dma("sp", absd[:, :], W["hy_decay"].partition_broadcast(128), writes=[("Fabsd", 0)])
        S.I("act", "activation", out=absd[:, :], in_=absd[:, :], func=AF.Abs,
            reads=[("Fabsd", 0)], writes=[("Fabsd", 0)])
        tneg = sb(es, nc, "Ftneg", [128, 128], F32)
        S.dma("sp", tneg[:, :], tneg_in)
        negpi = sb(es, nc, "Fnegpi", [128, 1], F32)
        S.I("pool", "memset", negpi[:, :], -math.pi)
        eps6 = sb(es, nc, "Feps6", [128, 1], F32)
        S.I("pool", "memset", eps6[:, :], 1e-6)
        h3T = sb(es, nc, "Fh3T", [64, 8192], F32)
        S.barrier()
        S.flush()
        if C.dbg.get("fstop") == 1:
            return
        with ExitStack() as es2:
            zt = [sb(es2, nc, "Fzt%d" % i, [33, 512], F32) for i in range(2)]
            ha = [sb(es2, nc, "Fha%d" % i, [64, 512], F32) for i in range(2)]
            hb = [sb(es2, nc, "Fhb%d" % i, [64, 512], F32) for i in range(2)]
            hc_ = [sb(es2, nc, "Fhc%d" % i, [64, 512], F32) for i in range(2)]
            for blk in range(16):
                i2 = blk % 2
                S.dma("sp", zt[i2][:, :], zT_in[:, blk * 512:(blk + 1) * 512], writes=[("Fzt", i2)])
                cur = zt[i2][0:33, :]
                cur_tok = ("Fzt", i2)
                for l, wl in enumerate((w1, w2, w3)):
                    K = 33 if l == 0 else 64
                    b = C.psM[C.psM_i % len(C.psM)]
                    C.psM_i += 1
                    S.I("pe", "matmul", C.ps[b][0:64, :], wl[0:K, :], cur, start=True, stop=True,
                        reads=[cur_tok], writes=[("ps", b)])
                    sa, sb_ = ha[i2], hc_[i2]
                    S.I("act", "activation", out=sa[:, :], in_=C.ps[b][0:64, :], func=AF.Sin,
                        scale=f8[:, l:l + 1], bias=b8[:, l:l + 1], reads=[("ps", b)], writes=[("Fha", i2)])
                    S.I("act", "activation", out=sb_[:, :], in_=C.ps[b][0:64, :], func=AF.Sin,
                        scale=f4[:, l:l + 1], bias=b4[:, l:l + 1], reads=[("ps", b)], writes=[("Fhc", i2)])
                    S.I("dve", "tensor_tensor", sa[:, :], sa[:, :], sa[:, :], op=ALU.mult,
                        reads=[("Fha", i2)], writes=[("Fha", i2)])
                    S.I("dve", "tensor_scalar", sa[:, :], sa[:, :], -2.0, 1.0, op0=ALU.mult, op1=ALU.add,
                        reads=[("Fha", i2)], writes=[("Fha", i2)])
                    S.I("dve", "scalar_tensor_tensor", out=sa[:, :], in0=sb_[:, :], scalar=2.0, in1=sa[:, :],
                        op0=ALU.mult, op1=ALU.mult, reads=[("Fha", i2), ("Fhc", i2)], writes=[("Fha", i2)])
                    S.I("dve", "tensor_tensor", sb_[:, :], sb_[:, :], sb_[:, :], op=ALU.mult,
                        reads=[("Fhc", i2)], writes=[("Fhc", i2)])
                    S.I("dve", "tensor_scalar", sb_[:, :], sb_[:, :], -2.0, 1.0, op0=ALU.mult, op1=ALU.add,
                        reads=[("Fhc", i2)], writes=[("Fhc", i2)])
                    dst = h3T[:, blk * 512:(blk + 1) * 512] if l == 2 else hb[i2][:, :]
                    dtok = ("Fh3T", blk) if l == 2 else ("Fhb", i2)
                    S.I("dve", "scalar_tensor_tensor", out=dst, in0=sa[:, :], scalar=2.0, in1=sb_[:, :],
                        op0=ALU.mult, op1=ALU.mult, reads=[("Fha", i2), ("Fhc", i2)], writes=[dtok])
                    cur = hb[i2][:, :]
                    cur_tok = ("Fhb", i2)
            S.barrier()
            S.flush()
        if C.dbg.get("fstop") == 2:
            return
        fw = sb(es, nc, "Ffw", [128, 64, 128], F32)
        bw = sb(es, nc, "Fbw", [128, 64, 128], F32)
        win = [sb(es, nc, "Fwin%d" % i, [128, 4, 128], F32) for i in range(2)]
        sq = [sb(es, nc, "Fsq%d" % i, [128, 4, 128], F32) for i in range(2)]
        scl = sb(es, nc, "Fscl", [128, 128], F32)
        Fq = alloc_fft_tiles(C, es, "Fq")
        XF = sb(es, nc, "FXF", [64, 2, 2, 8, 65], F32)
        XB = sb(es, nc, "FXB", [64, 2, 2, 8, 65], F32)
        Go = [[sb(es, nc, "FGo%d_%d" % (w_, 0), [64, 2, 8, 65], F32)] * 2 for w_ in range(3)]
        t65 = [Fq.tmpA[0:64, 0:1040].rearrange("p (r a c) -> p r a c", r=2, a=8),
               Fq.tmpB[0:64, 0:1040].rearrange("p (r a c) -> p r a c", r=2, a=8)]
        nflag = sb(es, nc, "Fnflag", [128, 1], F32)
        S.I("dve", "tensor_scalar", nflag[:, :], C.flagcol[:, 0:1], -1.0, None, op0=ALU.mult)
        wi = 0
        gi = 0
        for o in range(2):
            for cc in range(4):
                bss = C.psT[0]
                nmm = 0
                for d, dst in ((0, fw), (1, bw)):
                    colbase = (o * 2 + d) * 512 + cc * 128
                    for nb in range(16):
                        b = C.psM[C.psM_i % len(C.psM)]
                        C.psM_i += 1
                        wv = win[wi % 2]
                        wt = ("Fwin", wi % 2)
                        sv = sq[wi % 2]
                        st_ = ("Fsq", wi % 2)
                        wi += 1
                        for s in range(4):
                            n2 = nb * 4 + s
                            S.I("pe", "matmul", C.ps[b][:, s * 128:(s + 1) * 128], h3T[0:64, n2 * 128:(n2 + 1) * 128],
                                w4[0:64, colbase:colbase + 128], start=True, stop=True,
                                reads=[("Fh3T", n2 // 4)], writes=[("ps", b)])
                            S.I("act", "activation", out=wv[:, s, :], in_=absd[:, colbase:colbase + 128], func=AF.Exp,
                                scale=tneg[:, d * 64 + n2:d * 64 + n2 + 1], reads=[("Fabsd", 0)], writes=[wt])
                        S.I("dve", "tensor_tensor", dst[:, nb * 4:(nb + 1) * 4, :],
                            C.ps[b][:, :].rearrange("p (s c) -> p s c", s=4), wv[:, :, :], op=ALU.mult,
                            reads=[("ps", b), wt], writes=[("Ffilt", d, nb)])
                        S.I("pool", "tensor_tensor", sv[:, :, :], dst[:, nb * 4:(nb + 1) * 4, :],
                            dst[:, nb * 4:(nb + 1) * 4, :], op=ALU.mult, reads=[("Ffilt", d, nb)], writes=[st_])
                        for s in range(4):
                            S.I("pe", "matmul", C.ps[bss][:, 0:128], C.onesf[:, :], sv[:, s, :],
                                start=(nmm == 0), stop=(nmm == 127), reads=[st_], writes=[("ps", bss)])
                            nmm += 1
                S.I("act", "activation", out=scl[:, :], in_=C.ps[bss][:, 0:128], func=AF.Ln, bias=eps6[:, 0:1],
                    reads=[("ps", bss)], writes=[("Fscl", 0)])
                S.I("act", "activation", out=scl[:, :], in_=scl[:, :], func=AF.Exp, scale=-0.5,
                    reads=[("Fscl", 0)], writes=[("Fscl", 0)])
                sclb = scl[:, :].unsqueeze(1).broadcast_to([128, 64, 128])
                allf = [("Ffilt", 0, nb) for nb in range(16)]
                allb = [("Ffilt", 1, nb) for nb in range(16)]
                S.I("dve", "tensor_tensor", fw[:, :, :], fw[:, :, :], sclb, op=ALU.mult,
                    reads=[("Fscl", 0)] + allf, writes=allf + [("Ffw", 0)])
                S.I("pool", "tensor_tensor", bw[:, :, :], bw[:, :, :], sclb, op=ALU.mult,
                    reads=[("Fscl", 0)] + allb, writes=allb + [("Fbw", 0)])
                if C.dbg.get("fstop") == 3:
                    S.barrier()
                    S.flush()
                    return
                sg = hc(C, "sgn2")[0:64, :].rearrange("p (r c) -> p r c", r=2).unsqueeze(2).broadcast_to([64, 2, 8, 65])
                for g in range(16):
                    ch0 = g * 8
                    fft_fwd_group(C, Fq, fw, ("Ffw", 0), ch0, XF, ("FXF", 0))
                    fft_fwd_group(C, Fq, bw, ("Fbw", 0), ch0, XB, ("FXB", 0))
                    if C.dbg.get("fstop") == 4:
                        S.barrier()
                        S.flush()
                        return
                    k2 = 0
                    Gaa, Gab, Gba = Go[0][k2], Go[1][k2], Go[2][k2]
                    Fl, Fh = XF[:, :, 0, :, :], XF[:, :, 1, :, :]
                    Bl, Bh = XB[:, :, 0, :, :], XB[:, :, 1, :, :]
                    S.I("dve", "tensor_tensor", Gaa[:, 0, :, :], Fl[:, 0, :, :], Bl[:, 0, :, :], op=ALU.add,
                        reads=[("FXF", 0), ("FXB", 0)], writes=[("FGo", 0, k2)])
                    S.I("dve", "tensor_tensor", Gaa[:, 1, :, :], Fl[:, 1, :, :], Bl[:, 1, :, :], op=ALU.subtract,
                        reads=[("FXF", 0), ("FXB", 0)], writes=[("FGo", 0, k2)])
                    S.I("pool", "tensor_tensor", t65[0], Fl, sg, op=ALU.mult,
                        reads=[("FXF", 0)], writes=[("FqtmpA", 0)])
                    S.I("pool", "tensor_tensor", t65[0], t65[0], Fh, op=ALU.add,
                        reads=[("FXF", 0), ("FqtmpA", 0)], writes=[("FqtmpA", 0)])
                    S.I("pool", "tensor_scalar", Gba[:, :, :, :], t65[0], C.flagcol[0:64, 0:1], None,
                        op0=ALU.mult, reads=[("FqtmpA", 0)], writes=[("FGo", 2, k2)])
                    S.I("dve", "tensor_tensor", t65[1], Bl, sg, op=ALU.mult,
                        reads=[("FXB", 0)], writes=[("FqtmpB", 0)])
                    S.I("dve", "tensor_tensor", t65[1], t65[1], Bh, op=ALU.add,
                        reads=[("FXB", 0), ("FqtmpB", 0)], writes=[("FqtmpB", 0)])
                    S.I("dve", "tensor_scalar", Gab[:, 0, :, :], t65[1][:, 0, :, :], C.flagcol[0:64, 0:1], None,
                        op0=ALU.mult, reads=[("FqtmpB", 0)], writes=[("FGo", 1, k2)])
                    S.I("dve", "tensor_scalar", Gab[:, 1, :, :], t65[1][:, 1, :, :], nflag[0:64, 0:1], None,
                        op0=ALU.mult, reads=[("FqtmpB", 0)], writes=[("FGo", 1, k2)])
                    for w_ in range(3):
                        S.dma("sp", Gd[o, cc, w_, :, :, g * 8:(g + 1) * 8, :], Go[w_][k2][:, :, :, :],
                              reads=[("FGo", w_, k2)])
        S.barrier()
        S.flush()


def phase_hyena(C, uh, Gd, yb):
    nc, S = C.nc, C.S
    W = C.W
    with ExitStack() as es:
        skb = sb(es, nc, "Hskb", [128, 2, 512], F32)
        for o in range(2):
            S.dma("sp", skb[:, o, :], W["hy_skip"][o:o + 1, :].partition_broadcast(128), writes=[("Hskb", 0)])
        z = sb(es, nc, "Hz", [128, 64, 128], F32)
        xg = sb(es, nc, "Hxg", [128, 64, 128], F32)
        Fq = alloc_fft_tiles(C, es, "Hq")
        X = sb(es, nc, "HX", [64, 2, 2, 8, 65], F32)
        Y = sb(es, nc, "HY", [64, 2, 2, 8, 65], F32)
        Gt = [[sb(es, nc, "HG%d_%d" % (w_, i), [64, 2, 8, 65], F32) for i in range(2)] for w_ in range(3)]
        PQ = [sb(es, nc, "HPQ%d" % i, [64, 2, 8, 65], F32) for i in range(4)]
        C2 = sb(es, nc, "HC2", [65, 16, 128], F32)
        Dt = sb(es, nc, "HDt", [65, 2, 16, 64], F32)
        gt = [sb(es, nc, "Hgt%d" % i, [128, 64, 8], F32) for i in range(2)]
        gi = 0
        ztoks = [("Hz", g) for g in range(16)]
        for cc in range(4):
            for q4 in range(16):
                S.dma("sp", z[:, q4 * 4:(q4 + 1) * 4, :],
                      uh[:, 1024 + cc * 128:1024 + (cc + 1) * 128].rearrange("(p n) c -> p n c", n=64)[:, q4 * 4:(q4 + 1) * 4, :],
                      writes=ztoks)
            for o in range(2):
                for q4 in range(16):
                    S.dma("sp", xg[:, q4 * 4:(q4 + 1) * 4, :],
                          uh[:, o * 512 + cc * 128:o * 512 + (cc + 1) * 128].rearrange("(p n) c -> p n c", n=64)[:, q4 * 4:(q4 + 1) * 4, :],
                          writes=[("Hxg", 0)])
                for g in range(16):
                    ch0 = g * 8
                    k2 = gi % 2
                    gi += 1
                    for w_ in range(3):
                        S.dma("sp", Gt[w_][k2][:, :, :, :], Gd[o, cc, w_, :, :, g * 8:(g + 1) * 8, :],
                              writes=[("HG", w_, k2)])
                    fft_fwd_group(C, Fq, z, ("Hz", g), ch0, X, ("HX", 0))
                    Gaa, Gab, Gba = Gt[0][k2], Gt[1][k2], Gt[2][k2]
                    Xa, Xb = X[:, :, 0, :, :], X[:, :, 1, :, :]

                    def bc(Gx, r):
                        return Gx[:, r:r + 1, :, :].broadcast_to([64, 2, 8, 65])
                    for half, (GA, GB) in enumerate(((Gaa, Gab), (Gba, Gaa))):
                        ta = ("HG", (0, 1, 2)[[Gaa, Gab, Gba].index(GA)], k2)
                        tb = ("HG", (0, 1, 2)[[Gaa, Gab, Gba].index(GB)], k2)
                        S.I("dve", "tensor_tensor", PQ[0][:, :, :, :], Xa, bc(GA, 0), op=ALU.mult,
                            reads=[("HX", 0), ta], writes=[("HPQ", 0)])
                        S.I("dve", "tensor_tensor", PQ[1][:, :, :, :], Xa, bc(GA, 1), op=ALU.mult,
                            reads=[("HX", 0), ta], writes=[("HPQ", 1)])
                        S.I("pool", "tensor_tensor", PQ[2][:, :, :, :], Xb, bc(GB, 0), op=ALU.mult,
                            reads=[("HX", 0), tb], writes=[("HPQ", 2)])
                        S.I("pool", "tensor_tensor", PQ[3][:, :, :, :], Xb, bc(GB, 1), op=ALU.mult,
                            reads=[("HX", 0), tb], writes=[("HPQ", 3)])
                        S.I("pool", "tensor_tensor", PQ[0][:, :, :, :], PQ[0][:, :, :, :], PQ[2][:, :, :, :], op=ALU.add,
                            reads=[("HPQ", 0), ("HPQ", 2)], writes=[("HPQ", 0)])
                        S.I("pool", "tensor_tensor", PQ[1][:, :, :, :], PQ[1][:, :, :, :], PQ[3][:, :, :, :], op=ALU.add,
                            reads=[("HPQ", 1), ("HPQ", 3)], writes=[("HPQ", 1)])
                        S.I("dve", "tensor_tensor", Y[:, 0, half, :, :], PQ[0][:, 0, :, :], PQ[1][:, 1, :, :],
                            op=ALU.subtract, reads=[("HPQ", 0), ("HPQ", 1)], writes=[("HY", half)])
                        S.I("dve", "tensor_tensor", Y[:, 1, half, :, :], PQ[0][:, 1, :, :], PQ[1][:, 0, :, :],
                            op=ALU.add, reads=[("HPQ", 0), ("HPQ", 1)], writes=[("HY", half)])
                    rhsA, rhsB = hc(C, "rhsA")[0:64, :], hc(C, "rhsB")[0:64, :]
                    for s0 in range(0, 16, 4):
                        b = C.psM[C.psM_i % len(C.psM)]
                        C.psM_i += 1
                        for s in range(4):
                            i = s0 + s
                            h, chl = i // 8, i % 8
                            S.I("pe", "matmul", C.ps[b][0:65, s * 128:(s + 1) * 128], Y[:, 0, h, chl, :], rhsA,
                                start=True, stop=False, reads=[("HY", h)], writes=[("ps", b)])
                            S.I("pe", "matmul", C.ps[b][0:65, s * 128:(s + 1) * 128], Y[:, 1, h, chl, :], rhsB,
                                start=False, stop=True, reads=[("HY", h)], writes=[("ps", b)])
                        S.I("act", "activation", out=C2[0:65, s0:s0 + 4, :],
                            in_=C.ps[b][0:65, :].rearrange("p (s c) -> p s c", s=4), func=AF.Copy,
                            reads=[("ps", b)], writes=[("HC2", 0)])
                    Tci = hc(C, "Tci", 65).unsqueeze(1).broadcast_to([65, 16, 128])
                    Tsi = hc(C, "Tsi", 65).unsqueeze(1).broadcast_to([65, 16, 128])
                    P1 = Fq.tmpA[0:65, 0:2048].rearrange("p (a c) -> p a c", a=16)
                    P2 = Fq.tmpB[0:65, 0:2048].rearrange("p (a c) -> p a c", a=16)
                    S.I("dve", "tensor_tensor", P1, C2[0:65, :, :], Tci, op=ALU.mult,
                        reads=[("HC2", 0)], writes=[("HqtmpA", 0)])
                    S.I("pool", "tensor_tensor", P2, C2[0:65, :, :], Tsi, op=ALU.mult,
                        reads=[("HC2", 0)], writes=[("HqtmpB", 0)])
                    S.I("dve", "tensor_tensor", Dt[0:65, 0, :, :], P1[:, :, 0:64], P2[:, :, 64:128], op=ALU.subtract,
                        reads=[("HqtmpA", 0), ("HqtmpB", 0)], writes=[("HDt", 0)])
                    S.I("pool", "tensor_tensor", Dt[0:65, 1, :, :], P2[:, :, 0:64], P1[:, :, 64:128], op=ALU.add,
                        reads=[("HqtmpA", 0), ("HqtmpB", 0)], writes=[("HDt", 1)])
                    b = C.psM[C.psM_i % len(C.psM)]
                    C.psM_i += 1
                    fin = [("Gc_a", 0, 0), ("Gs_a", 1, 0), ("Gc_b", 0, 1), ("Gs_b", 1, 1)]
                    for i, (gn, ri, h) in enumerate(fin):
                        S.I("pe", "matmul", C.ps[b][:, :], hc(C, gn, 65), Dt[0:65, ri, 8 * h:8 * h + 8, :],
                            start=(i == 0), stop=(i == 3), reads=[("HDt", ri)], writes=[("ps", b)])
                    yv = C.ps[b][:, :].rearrange("p (c n) -> p n c", c=8)
                    zv = z[:, :, ch0:ch0 + 8]
                    xv = xg[:, :, ch0:ch0 + 8]
                    skv = skb[:, o, cc * 128 + ch0:cc * 128 + ch0 + 8].unsqueeze(1).broadcast_to([128, 64, 8])
                    gk = gt[g % 2]
                    S.I("pool", "tensor_tensor", gk[:, :, :], zv, skv, op=ALU.mult,
                        reads=[("Hz", g), ("Hskb", 0)], writes=[("Hgt", g % 2)])
                    S.I("dve", "tensor_tensor", gk[:, :, :], yv, gk[:, :, :], op=ALU.add,
                        reads=[("ps", b), ("Hgt", g % 2)], writes=[("Hgt", g % 2)])
                    S.I("pool", "tensor_tensor", zv, gk[:, :, :], xv, op=ALU.mult,
                        reads=[("Hgt", g % 2), ("Hxg", 0)], writes=[("Hz", g)])
            for q4 in range(16):
                S.dma("sp", yb[:, cc * 128:(cc + 1) * 128].rearrange("(p n) c -> p n c", n=64)[:, q4 * 4:(q4 + 1) * 4, :],
                      z[:, q4 * 4:(q4 + 1) * 4, :], reads=ztoks)
        S.barrier()
        S.flush()


def phase_outproj(C, x_src, ya, yb, w_ap, x_dst):
    nc, S = C.nc, C.S
    tag = "O"
    with ExitStack() as es:
        wb = load_weight_bf16(C, es, w_ap, D, D, "Ow")
        xt = [sb(es, nc, "Oxt%d" % i, [128, 4, D], F32) for i in range(2)]
        yaT = [sb(es, nc, "OyaT%d" % i, [128, 4, TT], BF16) for i in range(2)]
        ybt = [sb(es, nc, "Oybt%d" % i, [128, 4, 512], F32) for i in range(2)]
        ybb = sb(es, nc, "Oybb", [128, 4, 512], BF16)
        ybT = sb(es, nc, "OybT", [128, 4, TT], BF16)
        for ti in range(NT):
            t0 = ti * TT
            sl = ti % 2
            S.dma("sp", xt[sl][:, :, :], x_src[t0:t0 + TT, :].rearrange("(s p) d -> p s d", p=128),
                  writes=[("Oxt", sl)])
            S.dma("sp", yaT[sl][:, :, :], ya[:, t0:t0 + TT].rearrange("(k p) t -> p k t", p=128),
                  writes=[("OyaT", sl)])
            S.dma("sp", ybt[sl][:, :, :], yb[t0:t0 + TT, :].rearrange("(s p) c -> p s c", p=128),
                  writes=[("Oybt", sl)])
            S.I("pool", "tensor_copy", ybb[:, :, :], ybt[sl][:, :, :], reads=[("Oybt", sl)], writes=[("Oybb", 0)])
            for kc in range(4):
                b = C.psT[C.psT_i % len(C.psT)]
                C.psT_i += 1
                for s in range(4):
                    S.I("pe", "matmul", C.ps[b][:, s * 128:(s + 1) * 128], ybb[:, s, kc * 128:(kc + 1) * 128],
                        C.ident[:, :], start=True, stop=True, reads=[("Oybb", 0)], writes=[("ps", b)])
                S.I("act", "activation", out=ybT[:, kc, :], in_=C.ps[b][:, :], func=AF.Copy,
                    reads=[("ps", b)], writes=[("OybT", kc)])
            for s in range(4):
                for nh in range(2):
                    b = C.psM[C.psM_i % len(C.psM)]
                    C.psM_i += 1
                    for kc in range(8):
                        if kc < 4:
                            lt, tok = yaT[sl][:, kc, s * 128:(s + 1) * 128], ("OyaT", sl)
                        else:
                            lt, tok = ybT[:, kc - 4, s * 128:(s + 1) * 128], ("OybT", kc - 4)
                        S.I("pe", "matmul", C.ps[b][:, :], lt, wb[:, kc, nh * 512:(nh + 1) * 512],
                            start=(kc == 0), stop=(kc == 7), reads=[tok], writes=[("ps", b)])
                    S.I("dve", "tensor_tensor", xt[sl][:, s, nh * 512:(nh + 1) * 512], C.ps[b][:, :],
                        xt[sl][:, s, nh * 512:(nh + 1) * 512], op=ALU.add,
                        reads=[("ps", b)], writes=[("Oxt", sl)])
            S.dma("sp", x_dst[t0:t0 + TT, :].rearrange("(s p) d -> p s d", p=128), xt[sl][:, :, :],
                  reads=[("Oxt", sl)])
        S.barrier()
        S.flush()


def attn_consts(is_prompt):
    d = np.arange(128) % 64
    inv = 500000.0 ** (-(np.arange(0, 16, 2, dtype=np.float64) / 16.0))
    tau = np.arange(T)
    pos = tau if is_prompt else tau % 4096
    cos = np.ones((128, T), np.float64)
    sin = np.zeros((128, T), np.float64)
    for p in range(128):
        dd = d[p]
        if dd < 16:
            ang = pos * inv[dd % 8]
            cos[p] = np.cos(ang)
            sin[p] = np.sin(ang)
    P = np.zeros((128, 128), np.float32)
    for m in range(128):
        dd = m % 64
        if dd < 8:
            P[m + 8, m] = -1.0
        elif dd < 16:
            P[m - 8, m] = 1.0
    q = np.arange(128)[:, None]
    s = np.arange(128)[None, :]
    NEG = -30000.0
    prev = np.where(s >= q, 0.0, NEG)
    cur = np.zeros((128, 128))
    nxt = np.where(s <= q, 0.0, NEG)
    band = np.concatenate([prev, cur, nxt], 1).astype(np.float32)
    full = np.full((128, 128), NEG)
    if is_prompt:
        m31, m32 = band, band
    else:
        m31 = np.concatenate([prev, cur, full], 1).astype(np.float32)
        m32 = np.concatenate([full, cur, nxt], 1).astype(np.float32)
    masks = np.stack([band, m31, m32], 0).astype(np.float32)
    return cos.astype(np.float32), sin.astype(np.float32), P, masks


def phase_qkv(C, x_src, g_row, w_ap, qd, kd, vd, cos_in, sin_in, prot_in):
    nc, S = C.nc, C.S
    tag = "Q"
    with ExitStack() as es:
        gcol = load_bcast_row(C, es, g_row, D, "Qg")
        wb = load_weight_bf16(C, es, w_ap, D, 1536, "Qw")
        prf = sb(es, nc, "Qprf", [128, 128], F32)
        prb = sb(es, nc, "Qprb", [128, 128], BF16)
        S.dma("sp", prf[:, :], prot_in, writes=[("Qprf", 0)])
        S.I("dve", "tensor_copy", prb[:, :], prf[:, :], reads=[("Qprf", 0)], writes=[("Qprb", 0)])
        N = alloc_norm_tiles(C, es, tag, TT)
        cs = [sb(es, nc, "Qcs%d" % i, [128, TT], F32) for i in range(2)]
        sn = [sb(es, nc, "Qsn%d" % i, [128, TT], F32) for i in range(2)]
        xb = [sb(es, nc, "Qxb%d" % i, [128, TT], BF16) for i in range(2)]
        t1 = [sb(es, nc, "Qt1%d" % i, [128, TT], F32) for i in range(2)]
        t2 = [sb(es, nc, "Qt2%d" % i, [128, TT], F32) for i in range(2)]
        qo = [sb(es, nc, "Qqo%d" % i, [128, TT], BF16) for i in range(2)]
        vo = [sb(es, nc, "Qvo%d" % i, [128, 256], F32) for i in range(2)]
        k = 0
        norm_load(C, N, x_src, 0)
        nxt = norm_tile(C, N, gcol, 0)
        for ti in range(NT):
            t0 = ti * TT
            if ti + 1 < NT:
                norm_load(C, N, x_src, ti + 1)
            c2 = ti % 2
            S.dma("sp", cs[c2][:, :], cos_in[:, t0:t0 + TT], writes=[("Qcs", c2)])
            S.dma("sp", sn[c2][:, :], sin_in[:, t0:t0 + TT], writes=[("Qsn", c2)])
            xs, hT, slot, tslot = nxt
            for oc in range(10):
                if oc == 6 and ti + 1 < NT:
                    nxt = norm_tile(C, N, gcol, ti + 1)
                b = C.psM[C.psM_i % len(C.psM)]
                C.psM_i += 1
                for kc in range(8):
                    S.I("pe", "matmul", C.ps[b][:, :], wb[:, kc, oc * 128:(oc + 1) * 128], hT[:, kc, :],
                        start=(kc == 0), stop=(kc == 7), reads=[("QhnT", tslot, kc)], writes=[("ps", b)])
                k2 = k % 2
                k += 1
                if C.dbg.get("qstop") == 1:
                    S.I("act", "activation", out=qo[k2][:, :], in_=C.ps[b][:, :], func=AF.Copy,
                        reads=[("ps", b)], writes=[("Qqo", k2)])
                    S.dma("sp", qd[oc * 128:(oc + 1) * 128, t0:t0 + TT], qo[k2][:, :], reads=[("Qqo", k2)])
                    continue
                S.I("dve", "tensor_copy", xb[k2][:, :], C.ps[b][:, :],
                    reads=[("ps", b)], writes=[("Qxb", k2)])
                b2 = C.psM[C.psM_i % len(C.psM)]
                C.psM_i += 1
                S.I("pe", "matmul", C.ps[b2][:, :], prb[:, :], xb[k2][:, :], start=True, stop=True,
                    reads=[("Qxb", k2), ("Qprb", 0)], writes=[("ps", b2)])
                S.I("dve", "tensor_tensor", t1[k2][:, :], C.ps[b][:, :], cs[c2][:, :], op=ALU.mult,
                    reads=[("ps", b), ("Qcs", c2)], writes=[("Qt1", k2)])
                S.I("dve", "tensor_tensor", t2[k2][:, :], C.ps[b2][:, :], sn[c2][:, :], op=ALU.mult,
                    reads=[("ps", b2), ("Qsn", c2)], writes=[("Qt2", k2)])
                S.I("pool", "tensor_tensor", qo[k2][:, :], t1[k2][:, :], t2[k2][:, :], op=ALU.add,
                    reads=[("Qt1", k2), ("Qt2", k2)], writes=[("Qqo", k2)])
                if oc < 8:
                    S.dma("sp", qd[oc * 128:(oc + 1) * 128, t0:t0 + TT], qo[k2][:, :], reads=[("Qqo", k2)])
                else:
                    S.dma("sp", kd[(oc - 8) * 128:(oc - 7) * 128, t0:t0 + TT], qo[k2][:, :], reads=[("Qqo", k2)])
            for s in range(4):
                if C.dbg.get("qstop") == 2:
                    break
                b = C.psM[C.psM_i % len(C.psM)]
                C.psM_i += 1
                for kc in range(8):
                    S.I("pe", "matmul", C.ps[b][:, 0:256], hT[:, kc, s * 128:(s + 1) * 128], wb[:, kc, 1280:1536],
                        start=(kc == 0), stop=(kc == 7), reads=[("QhnT", tslot, kc)], writes=[("ps", b)])
                S.I("act", "activation", out=vo[s % 2][:, :], in_=C.ps[b][:, 0:256], func=AF.Copy,
                    reads=[("ps", b)], writes=[("Qvo", s % 2)])
                S.dma("sp", vd[t0 + s * 128:t0 + (s + 1) * 128, :], vo[s % 2][:, :], reads=[("Qvo", s % 2)])
        S.barrier()
        S.flush()


def phase_attn(C, x_src, qd, kd, vd, wo_ap, x_dst, masks_in):
    nc, S = C.nc, C.S
    W = C.W
    NB = T // 128
    with ExitStack() as es:
        wo = load_weight_bf16(C, es, wo_ap, D, D, "Two")
        sinkb = load_bcast_row(C, es, W["at_sink"], 16, "Tsink")
        mk = sb(es, nc, "Tmk", [128, 3, 384], F32)
        S.dma("sp", mk[:, :, :], masks_in.rearrange("m p s -> p m s"), writes=[("Tmk", 0)])
        qT = [sb(es, nc, "TqT%d" % i, [128, 8, 128], BF16) for i in range(2)]
        kT = [sb(es, nc, "TkT%d" % i, [128, 4, 384], BF16) for i in range(2)]
        vt = [sb(es, nc, "Tvt%d" % i, [128, 3, 256], BF16) for i in range(2)]
        vf = [sb(es, nc, "Tvf%d" % i, [128, 3, 256], F32) for i in range(2)]
        xt = [sb(es, nc, "Txt%d" % i, [128, D], F32) for i in range(2)]
        sm = [sb(es, nc, "Tsm%d" % i, [128, 384], F32) for i in range(4)]
        pb = [sb(es, nc, "Tpb%d" % i, [128, 384], BF16) for i in range(4)]
        pT = [sb(es, nc, "TpT%d" % i, [128, 3, 128], BF16) for i in range(4)]
        st = [sb(es, nc, "Tst%d" % i, [128, 8], F32) for i in range(4)]
        sbanks = [4, 5, 0, 1]
        tbanks = [2, 3]
        tbi = 0
        rdn = sb(es, nc, "Trdn", [128, 16], F32)
        ob = sb(es, nc, "Tob", [128, 16, 64], BF16)
        oT = sb(es, nc, "ToT", [128, 8, 128], BF16)
        hh = 0
        for n in range(NB):
            sl = n % 2
            kb0 = max(n - 1, 0)
            kb1 = min(n + 1, NB - 1)
            nk = kb1 - kb0 + 1
            mo = (kb0 - (n - 1)) * 128
            mi = 1 if n == NB // 2 - 1 else (2 if n == NB // 2 else 0)
            S.dma("sp", qT[sl][:, :, :], qd[:, n * 128:(n + 1) * 128].rearrange("(k p) t -> p k t", p=128),
                  writes=[("TqT", sl)])
            for kv in range(4):
                for dup in range(2):
                    S.dma("sp", kT[sl][dup * 64:(dup + 1) * 64, kv, 0:nk * 128],
                          kd[kv * 64:(kv + 1) * 64, kb0 * 128:(kb1 + 1) * 128], writes=[("TkT", sl)])
            S.dma("sp", vf[sl][:, 0:nk, :], vd[kb0 * 128:(kb1 + 1) * 128, :].rearrange("(k p) c -> p k c", p=128),
                  writes=[("Tvf", sl)])
            S.I("pool", "tensor_copy", vt[sl][:, 0:nk, :], vf[sl][:, 0:nk, :], reads=[("Tvf", sl)], writes=[("Tvt", sl)])
            S.dma("sp", xt[sl][:, :], x_src[n * 128:(n + 1) * 128, :], writes=[("Txt", sl)])
            bo = [6, 7]
            hp_ = []
            for h in range(16):
                h2 = hh % 4
                hh += 1
                bt = tbanks[tbi % 2]
                tbi += 1
                hp_.append((h2, sbanks[hh % 4], bt))

            def stA(h):
                kv = h // 4
                qc, hp = h // 2, h % 2
                h2, b, bt = hp_[h]
                smh, sth = sm[h2], st[h2]
                S.I("pe", "matmul", C.ps[b][:, 0:nk * 128], qT[sl][hp * 64:(hp + 1) * 64, qc, :],
                    kT[sl][hp * 64:(hp + 1) * 64, kv, 0:nk * 128], start=True, stop=True,
                    reads=[("TqT", sl), ("TkT", sl)], writes=[("ps", b)])
                S.I("dve", "scalar_tensor_tensor", out=smh[:, 0:nk * 128], in0=C.ps[b][:, 0:nk * 128], scalar=0.125,
                    in1=mk[:, mi, mo:mo + nk * 128], op0=ALU.mult, op1=ALU.add,
                    reads=[("ps", b), ("Tmk", 0)], writes=[("Tsm", h2)])
                S.I("dve", "reduce_max", sth[:, 0:1], smh[:, 0:nk * 128], axis=AX.X,
                    reads=[("Tsm", h2)], writes=[("Tst", h2, 0)])
                S.I("dve", "tensor_tensor", sth[:, 1:2], sth[:, 0:1], sinkb[:, h:h + 1], op=ALU.max,
                    reads=[("Tst", h2, 0), ("Tsink", 0)], writes=[("Tst", h2, 1)])
                S.I("dve", "tensor_scalar", sth[:, 2:3], sth[:, 1:2], -1.0, None, op0=ALU.mult,
                    reads=[("Tst", h2, 1)], writes=[("Tst", h2, 2)])

            def stB(h):
                h2, b, bt = hp_[h]
                smh, pbh, sth = sm[h2], pb[h2], st[h2]
                S.I("act", "activation", out=pbh[:, 0:nk * 128], in_=smh[:, 0:nk * 128], func=AF.Exp,
                    bias=sth[:, 2:3], accum_out=sth[:, 3:4],
                    reads=[("Tsm", h2), ("Tst", h2, 2)], writes=[("Tpb", h2), ("Tst", h2, 3)])
                S.I("act", "activation", out=sth[:, 4:5], in_=sinkb[:, h:h + 1], func=AF.Exp, bias=sth[:, 2:3],
                    reads=[("Tst", h2, 2), ("Tsink", 0)], writes=[("Tst", h2, 4)])
                S.I("dve", "tensor_tensor", sth[:, 5:6], sth[:, 3:4], sth[:, 4:5], op=ALU.add,
                    reads=[("Tst", h2, 3), ("Tst", h2, 4)], writes=[("Tst", h2, 5)])
                S.I("dve", "reciprocal", rdn[:, h:h + 1], sth[:, 5:6], reads=[("Tst", h2, 5)], writes=[("Trdn", h)])
                for kb in range(nk):
                    S.I("pe", "matmul", C.ps[bt][:, kb * 128:(kb + 1) * 128], pbh[:, kb * 128:(kb + 1) * 128],
                        C.ident[:, :], start=True, stop=True, reads=[("Tpb", h2)], writes=[("ps", bt)])

            def stC(h):
                kv = h // 4
                h2, b, bt = hp_[h]
                pTh = pT[h2]
                S.I("act", "activation", out=pTh[:, 0:nk, :],
                    in_=C.ps[bt][:, 0:nk * 128].rearrange("p (k c) -> p k c", k=nk), func=AF.Copy,
                    reads=[("ps", bt)], writes=[("TpT", h2)])
                bb = bo[h // 8]
                for kb in range(nk):
                    S.I("pe", "matmul", C.ps[bb][:, (h % 8) * 64:(h % 8 + 1) * 64], pTh[:, kb, :],
                        vt[sl][:, kb, kv * 64:(kv + 1) * 64], start=(kb == 0), stop=(kb == nk - 1),
                        reads=[("TpT", h2), ("Tvt", sl)], writes=[("ps", bb)])

            for i in range(18):
                if i < 16:
                    stA(i)
                if 0 <= i - 1 < 16:
                    stB(i - 1)
                if 0 <= i - 2 < 16:
                    stC(i - 2)
            for j in range(2):
                S.I("dve", "tensor_tensor", ob[:, j * 8:(j + 1) * 8, :],
                    C.ps[bo[j]][:, :].rearrange("p (h d) -> p h d", h=8),
                    rdn[:, j * 8:(j + 1) * 8].unsqueeze(2).broadcast_to([128, 8, 64]), op=ALU.mult,
                    reads=[("ps", bo[j])] + [("Trdn", hq) for hq in range(j * 8, j * 8 + 8)], writes=[("Tob", j)])
            for half in range(2):
                bt = tbanks[tbi % 2]
                tbi += 1
                for jj in range(4):
                    kc = half * 4 + jj
                    S.I("pe", "matmul", C.ps[bt][:, jj * 128:(jj + 1) * 128],
                        ob[:, 2 * kc:2 * kc + 2, :].rearrange("p h d -> p (h d)"),
                        C.ident[:, :], start=True, stop=True, reads=[("Tob", kc // 4)], writes=[("ps", bt)])
                S.I("act", "activation", out=oT[:, half * 4:(half + 1) * 4, :],
                    in_=C.ps[bt][:, :].rearrange("p (k c) -> p k c", k=4), func=AF.Copy,
                    reads=[("ps", bt)], writes=[("ToT", half)])
            for nh in range(2):
                b = sbanks[(hh + 1 + nh) % 4]
                for kc in range(8):
                    S.I("pe", "matmul", C.ps[b][:, :], oT[:, kc, :], wo[:, kc, nh * 512:(nh + 1) * 512],
                        start=(kc == 0), stop=(kc == 7), reads=[("ToT", kc // 4)], writes=[("ps", b)])
                S.I("dve", "tensor_tensor", xt[sl][:, nh * 512:(nh + 1) * 512], C.ps[b][:, :],
                    xt[sl][:, nh * 512:(nh + 1) * 512], op=ALU.add, reads=[("ps", b)], writes=[("Txt", sl)])
            S.dma("sp", x_dst[n * 128:(n + 1) * 128, :], xt[sl][:, :], reads=[("Txt", sl)])
        S.barrier()
        S.flush()


def phase_mlp(C, x_src, g_row, wup_ap, wdn_ap, x_dst, tag, final_g_row=None):
    nc, S = C.nc, C.S
    DFF = 4096
    TK = 256
    with ExitStack() as es:
        gcol = load_bcast_row(C, es, g_row, D, tag + "g")
        gfin = load_bcast_row(C, es, final_g_row, D, tag + "gf") if final_g_row is not None else None
        wup = load_weight_bf16(C, es, wup_ap, D, DFF, tag + "wu")
        wdn = load_weight_bf16(C, es, wdn_ap, DFF, D, tag + "wd")
        N = alloc_norm_tiles(C, es, tag, TK, nx=2, nT=2)
        hT = sb(es, nc, tag + "hT", [128, 32, TK], BF16)
        rl = [sb(es, nc, tag + "rl%d" % i, [128, TK], F32) for i in range(2)]
        ss2 = sb(es, nc, tag + "ss2", [128, 2], F32)
        rs2 = sb(es, nc, tag + "rs2", [128, 2], F32)
        norm_load(C, N, x_src, 0)
        nxt = norm_tile(C, N, gcol, 0)
        for ti in range(T // TK):
            if ti + 1 < T // TK:
                norm_load(C, N, x_src, ti + 1)
            xs, hTn, slot, tslot = nxt
            for fc in range(32):
                b = C.psM[C.psM_i % len(C.psM)]
                C.psM_i += 1
                for kc in range(8):
                    S.I("pe", "matmul", C.ps[b][:, 0:TK], wup[:, kc, fc * 128:(fc + 1) * 128], hTn[:, kc, :],
                        start=(kc == 0), stop=(kc == 7),
                        reads=[(tag + "hnT", tslot, kc)], writes=[("ps", b)])
                j = fc % 2
                if j == 0:
                    S.I("act", "activation", out=rl[0][:, :], in_=C.ps[b][:, 0:TK], func=AF.Relu,
                        reads=[("ps", b)], writes=[(tag + "rl", 0)])
                else:
                    S.I("dve", "tensor_scalar", rl[1][:, :], C.ps[b][:, 0:TK], 0.0, None, op0=ALU.max,
                        reads=[("ps", b)], writes=[(tag + "rl", 1)])
                S.I("pool", "tensor_tensor", hT[:, fc, :], rl[j][:, :], rl[j][:, :], op=ALU.mult,
                    reads=[(tag + "rl", j)], writes=[(tag + "hT", fc)])
            if ti + 1 < T // TK:
                nxt = norm_tile(C, N, gcol, ti + 1)
            for s in range(N.NS):
                xr = (tag + "xt", slot)
                for nh in range(2):
                    b = C.psM[C.psM_i % len(C.psM)]
                    C.psM_i += 1
                    for fc in range(32):
                        S.I("pe", "matmul", C.ps[b][:, :], hT[:, fc, s * 128:(s + 1) * 128],
                            wdn[:, fc, nh * 512:(nh + 1) * 512], start=(fc == 0), stop=(fc == 31),
                            reads=[(tag + "hT", fc)], writes=[("ps", b)])
                    S.I("dve", "tensor_tensor", xs[:, s, nh * 512:(nh + 1) * 512], C.ps[b][:, :],
                        xs[:, s, nh * 512:(nh + 1) * 512], op=ALU.add,
                        reads=[("ps", b)], writes=[xr])
                r0 = ti * TK + s * 128
                if gfin is not None:
                    S.I("pool", "memset", ss2[:, 0:1], 0.0, writes=[(tag + "ss2", 0)])
                    S.I("act", "activation", out=N.junk[:, :], in_=xs[:, s, :], func=AF.Square,
                        accum_out=ss2[:, 0:1], reads=[xr], writes=[(tag + "ss2", 0), (tag + "junk", 0)])
                    S.I("act", "activation", out=rs2[:, 0:1], in_=ss2[:, 0:1], func=AF.Ln, scale=1.0 / D,
                        bias=C.eps5[:, 0:1], reads=[(tag + "ss2", 0)], writes=[(tag + "rs2", 0)])
                    S.I("act", "activation", out=rs2[:, 0:1], in_=rs2[:, 0:1], func=AF.Exp, scale=-0.5,
                        reads=[(tag + "rs2", 0)], writes=[(tag + "rs2", 0)])
                    S.I("dve", "scalar_tensor_tensor", out=xs[:, s, :], in0=xs[:, s, :], scalar=rs2[:, 0:1],
                        in1=gfin[:, :], op0=ALU.mult, op1=ALU.mult,
                        reads=[(tag + "rs2", 0)], writes=[xr])
                S.dma("sp", x_dst[r0:r0 + 128, :], xs[:, s, :], reads=[xr])
        S.barrier()
        S.flush()


WEIGHT_SPECS = [
    ("norm_mix", (2, 1024)), ("norm_mlp", (2, 1024)), ("norm_final", (1, 1024)),
    ("ab_w_in", (1024, 2560)), ("ab_w_out", (1024, 1024)),
    ("cv_dw_w", (31, 512)), ("cv_dw_b", (1, 512)), ("cv_ln_g", (1, 512)), ("cv_ln_b", (1, 512)),
    ("hy_short_w", (3, 1536)), ("hy_short_b", (1, 1536)),
    ("hy_w1", (33, 64)), ("hy_b1", (1, 64)), ("hy_w2", (64, 64)), ("hy_b2", (1, 64)),
    ("hy_w3", (64, 64)), ("hy_b3", (1, 64)), ("hy_w4", (64, 2048)),
    ("hy_freq", (3, 64)), ("hy_decay", (1, 2048)), ("hy_skip", (2, 512)),
    ("at_w_qkv", (1024, 1536)), ("at_sink", (1, 16)), ("at_w_o", (1024, 1024)),
    ("mlp_w_up", (2, 1024, 4096)), ("mlp_w_down", (2, 4096, 1024)),
]


def build_program(dbg=None):
    nc = bass.Bass("TRN2", target_bir_lowering=False)
    C = Ctx()
    C.nc = nc
    C.dbg = dbg or {}
    W = {}
    x_in = nc.dram_tensor("x", [T, D], F32, kind="ExternalInput").ap()
    for name, shp in WEIGHT_SPECS:
        W[name] = nc.dram_tensor(name, list(shp), F32, kind="ExternalInput").ap()
    ident_in = nc.dram_tensor("ident", [128, 128], F32, kind="ExternalInput").ap()
    flag_in = nc.dram_tensor("flag", [1, 8], F32, kind="ExternalInput").ap()
    hc128_np, hc65_np = hyena_consts()
    hc128_in = nc.dram_tensor("hc128", list(hc128_np.shape), F32, kind="ExternalInput").ap()
    hc65_in = nc.dram_tensor("hc65", list(hc65_np.shape), F32, kind="ExternalInput").ap()
    zT_in = nc.dram_tensor("zT", [33, 8192], F32, kind="ExternalInput").ap()
    tneg_in = nc.dram_tensor("tneg", [128, 128], F32, kind="ExternalInput").ap()
    cos_in = nc.dram_tensor("ropecos", [128, T], F32, kind="ExternalInput").ap()
    sin_in = nc.dram_tensor("ropesin", [128, T], F32, kind="ExternalInput").ap()
    prot_in = nc.dram_tensor("prot", [128, 128], F32, kind="ExternalInput").ap()
    masks_in = nc.dram_tensor("amasks", [3, 128, 384], F32, kind="ExternalInput").ap()
    y_out = nc.dram_tensor("y", [T, D], F32, kind="ExternalOutput").ap()
    C.W = W

    def scratch(name, shape, dt=F32):
        kind = "ExternalOutput" if name in C.dbg.get("out", ()) else "Internal"
        return nc.dram_tensor(name, list(shape), dt, kind=kind).ap()

    u = scratch("u", [2560, T])
    ya = scratch("ya", [512, T], BF16)
    uh = scratch("uh", [T, 1536])
    Gd = scratch("Gd", [2, 4, 3, 64, 2, 128, 65])
    yb = scratch("yb", [T, 512])
    x1 = scratch("x1", [T, D])
    x2 = scratch("x2", [T, D])
    x3 = scratch("x3", [T, D])
    qd = scratch("qd", [1024, T], BF16)
    kd = scratch("kd", [256, T], BF16)
    vd = scratch("vd", [T, 256])
    stop = C.dbg.get("stop")

    with ExitStack() as es:
        S = Sched(nc, es)
        C.S = S
        C.ps = [es.enter_context(nc.psum_tensor("ps%d" % i, [128, 512], F32)) for i in range(8)]
        C.psT = [0, 1, 2, 3]
        C.psM = [4, 5, 6, 7]
        C.psT_i = 0
        C.psM_i = 0
        C.ident = sb(es, nc, "ident_b", [128, 128], BF16)
        C.identf = sb(es, nc, "ident_f", [128, 128], F32)
        C.eps5 = sb(es, nc, "eps5", [128, 1], F32)
        S.I("pool", "memset", C.eps5[:, :], 1e-5)
        S.dma("sp", C.identf[:, :], ident_in[:, :], writes=[("identf", 0)])
        S.I("dve", "tensor_copy", C.ident[:, :], C.identf[:, :], reads=[("identf", 0)], writes=[("ident", 0)])
        C.onesf = sb(es, nc, "onesf", [128, 128], F32)
        S.I("pool", "memset", C.onesf[:, :], 1.0)
        C.flagcol = sb(es, nc, "flagcol", [128, 8], F32)
        S.dma("sp", C.flagcol[:, :], flag_in.partition_broadcast(128))
        C.hc128 = sb(es, nc, "hc128_t", list(hc128_np.shape), F32)
        C.hc65 = sb(es, nc, "hc65_t", list(hc65_np.shape), F32)
        S.dma("sp", C.hc128[:, :], hc128_in)
        S.dma("sp", C.hc65[:, :], hc65_in)
        S.barrier()
        S.flush()

        skip = C.dbg.get("skip", "")
        if stop != "M" and "A" not in skip:
            phase_inproj(C, x_in, W["norm_mix"][0:1, :], W["ab_w_in"], 2560, u, "A")
        if stop == "A":
            return nc
        if "B" not in skip:
            phase_conformer(C, u, ya)
        if stop == "B":
            return nc
        if "S" not in skip:
            phase_shortconv(C, u, uh)
        if stop == "S":
            return nc
        if "F" not in skip:
            phase_filters(C, Gd, zT_in, tneg_in)
        if stop == "F":
            return nc
        if "H" not in skip:
            phase_hyena(C, uh, Gd, yb)
        if stop == "H":
            return nc
        if "O" not in skip:
            phase_outproj(C, x_in, ya, yb, W["ab_w_out"], x1)
        if stop == "O":
            return nc
        if "M0" not in skip:
            phase_mlp(C, x1, W["norm_mlp"][0:1, :], W["mlp_w_up"][0], W["mlp_w_down"][0], x2, "M0")
        if stop == "M0":
            return nc
        xa = x_in if "X" in skip else x2
        phase_qkv(C, xa, W["norm_mix"][1:2, :], W["at_w_qkv"], qd, kd, vd, cos_in, sin_in, prot_in)
        if stop == "Q":
            return nc
        phase_attn(C, xa, qd, kd, vd, W["at_w_o"], x3, masks_in)
        if stop == "T":
            return nc
        phase_mlp(C, x3, W["norm_mlp"][1:2, :], W["mlp_w_up"][1], W["mlp_w_down"][1], y_out, "M1",
                  final_g_row=W["norm_final"][0:1, :])
        return nc
        phase_mlp(C, x_in, W["norm_mlp"][0:1, :], W["mlp_w_up"][0], W["mlp_w_down"][0], y_out, "M0",
                  final_g_row=W["norm_final"][0:1, :])
    return nc


_CACHE = {}


def make_in_maps(inputs):
    xp = np.asarray(inputs["x_prompt"], np.float32)
    xs = np.asarray(inputs["x_sample"], np.float32)
    base = {}
    for name, shp in WEIGHT_SPECS:
        base[name] = np.ascontiguousarray(np.asarray(inputs[name], np.float32)).reshape(shp)
    base["ident"] = np.eye(128, dtype=np.float32)
    base["hc128"], base["hc65"] = hyena_consts()
    ptab = {True: hyena_pos_tables(True), False: hyena_pos_tables(False)}
    atab = {True: attn_consts(True), False: attn_consts(False)}
    maps = []
    for c in range(NCORES):
        m = dict(base)
        m["flag"] = np.full((1, 8), 1.0 if c < 4 else 0.0, np.float32)
        m["zT"], m["tneg"] = ptab[c < 4]
        m["ropecos"], m["ropesin"], m["prot"], m["amasks"] = atab[c < 4]
        if c < 4:
            m["x"] = np.ascontiguousarray(xp[c])
        else:
            m["x"] = np.ascontiguousarray(xs[2 * (c - 4):2 * (c - 4) + 2].reshape(T, D))
        maps.append(m)
    return maps


def kernel(**inputs):
    if "nc" not in _CACHE:
        _CACHE["nc"] = build_program()
    nc = _CACHE["nc"]
    maps = make_in_maps(inputs)
    res = run_bass_kernel_spmd(nc, maps, core_ids=list(range(NCORES)))
    ys = [np.asarray(r["y"], np.float32) for r in res.results]
    y_prompt = np.stack(ys[0:4], axis=0)
    y_sample = np.concatenate([y.reshape(2, 4096, D) for y in ys[4:8]], axis=0)
    return (y_prompt, y_sample)
```

```python
import math
from contextlib import ExitStack

import numpy as np
import concourse.bass as bass
import concourse.mybir as mybir
from concourse.bass_utils import run_bass_kernel_spmd

F32 = mybir.dt.float32
BF16 = mybir.dt.bfloat16
AF = mybir.ActivationFunctionType
ALU = mybir.AluOpType
AX = mybir.AxisListType

NCORES = 8
T = 8192
D = 1024
TT = 512
NT = T // TT
ENGS = ("pe", "act", "dve", "pool", "sp")


class Op:
    __slots__ = ("eng", "fn", "dma", "deps", "sig", "count", "sem", "semval")

    def __init__(self, eng, fn, dma):
        self.eng = eng
        self.fn = fn
        self.dma = dma
        self.deps = set()
        self.sig = False
        self.count = 0
        self.sem = None
        self.semval = 0


class Sched:
    NDMA_SEM = 12

    def __init__(self, nc, es):
        self.nc = nc
        self.streams = {k: [] for k in ENGS}
        self.w = {}
        self.r = {}
        self.dma_rr = {k: 0 for k in ENGS}
        self.dma_last = {}
        self.dma_cnt = {}
        self.esem = {k: es.enter_context(nc.semaphore("e_" + k)) for k in ENGS}
        self.dsem = {}
        for k in ("sp", "act", "pool"):
            for i in range(self.NDMA_SEM):
                self.dsem[(k, i)] = es.enter_context(nc.semaphore("d_%s%d" % (k, i)))
        self.ecount = {k: 0 for k in ENGS}
        self.seen = {k: {} for k in ENGS}
        self.lastc = {}

    def op(self, eng, fn, reads=(), writes=(), dma=False, deps=()):
        o = Op(eng, fn, dma)
        for d in deps:
            if d is not None:
                o.deps.add(d)
        for r in reads:
            lw = self.w.get(r)
            if lw is not None:
                o.deps.add(lw)
        for w_ in writes:
            lw = self.w.get(w_)
            if lw is not None:
                o.deps.add(lw)
            for rd in self.r.get(w_, ()):
                o.deps.add(rd)
        for r in reads:
            self.r.setdefault(r, []).append(o)
        for w_ in writes:
            self.w[w_] = o
            self.r[w_] = []
        if dma:
            i = self.dma_rr[eng]
            self.dma_rr[eng] = (i + 1) % self.NDMA_SEM
            key = (eng, i)
            prev = self.dma_last.get(key)
            if prev is not None:
                o.deps.add(prev)
            self.dma_last[key] = o
            self.dma_cnt[key] = self.dma_cnt.get(key, 0) + 1
            o.sem = key
            o.semval = 16 * self.dma_cnt[key]
        else:
            self.lastc[eng] = o
        o.deps.discard(o)
        self.streams[eng].append(o)
        return o

    def I(self, eng, name, *args, reads=(), writes=(), deps=(), **kw):
        return self.op(eng, lambda e: getattr(e, name)(*args, **kw), reads, writes, deps=deps)

    def dma(self, eng, out, in_, reads=(), writes=(), deps=(), **kw):
        return self.op(eng, lambda e: e.dma_start(out=out, in_=in_, **kw), reads, writes, dma=True, deps=deps)

    def barrier(self):
        dmas = list(self.dma_last.values())
        lastc = dict(self.lastc)
        for k in ENGS:
            o = Op(k, None, False)
            for kk, lo in lastc.items():
                if kk != k:
                    o.deps.add(lo)
            for d in dmas:
                o.deps.add(d)
            self.streams[k].append(o)
        self.w = {}
        self.r = {}

    def flush(self):
        nc = self.nc
        for k in ENGS:
            for o in self.streams[k]:
                for d in o.deps:
                    if not d.dma and (d.eng != o.eng or o.dma or o.eng != "pe"):
                        d.sig = True
        for k in ENGS:
            for o in self.streams[k]:
                if o.sig and o.count == 0:
                    self.ecount[k] += 1
                    o.count = self.ecount[k]
        streams = self.streams
        self.streams = {k: [] for k in ENGS}
        with nc.Block() as block:
            def run(k, e):
                seen = self.seen[k]
                for o in streams[k]:
                    need = {}
                    for d in o.deps:
                        if d.dma:
                            s, v = self.dsem[d.sem], d.semval
                        elif d.eng == k and not o.dma and k == "pe":
                            continue
                        else:
                            assert d.count > 0
                            s, v = self.esem[d.eng], d.count
                        if need.get(id(s), (None, 0))[1] < v:
                            need[id(s)] = (s, v)
                    for s, v in need.values():
                        if seen.get(id(s), 0) < v:
                            e.wait_ge(s, v)
                            seen[id(s)] = v
                    if o.fn is None:
                        continue
                    ins = o.fn(e)
                    if o.dma:
                        ins.then_inc(self.dsem[o.sem], 16)
                    elif o.sig:
                        ins.then_inc(self.esem[k], 1)

            @block.tensor
            def _(e):
                run("pe", e)

            @block.scalar
            def _(e):
                run("act", e)

            @block.vector
            def _(e):
                run("dve", e)

            @block.gpsimd
            def _(e):
                run("pool", e)

            @block.sync
            def _(e):
                run("sp", e)


class Ctx:
    pass


def sb(es, nc, name, shape, dt):
    return es.enter_context(nc.sbuf_tensor(name, list(shape), dt))


def load_weight_bf16(C, es, w_ap, K, N, name, eng_cast=("pool", "act")):
    nc, S = C.nc, C.S
    KC = K // 128
    wb = sb(es, nc, name, [128, KC, N], BF16)
    CW = min(N, 2048)
    with ExitStack() as es2:
        st = [sb(es2, nc, name + "_st%d" % i, [128, CW], F32) for i in range(2)]
        j = 0
        for kc in range(KC):
            for c0 in range(0, N, CW):
                cw = min(CW, N - c0)
                s = st[j % 2]
                S.dma("sp", s[:, 0:cw], w_ap[kc * 128:(kc + 1) * 128, c0:c0 + cw],
                      writes=[(name + "st", j % 2)])
                eng = eng_cast[j % len(eng_cast)]
                if eng == "act":
                    S.I("act", "activation", out=wb[:, kc, c0:c0 + cw], in_=s[:, 0:cw], func=AF.Copy,
                        reads=[(name + "st", j % 2)], writes=[(name, kc)])
                else:
                    S.I(eng, "tensor_copy", wb[:, kc, c0:c0 + cw], s[:, 0:cw],
                        reads=[(name + "st", j % 2)], writes=[(name, kc)])
                j += 1
        S.barrier()
        S.flush()
    return wb


def alloc_norm_tiles(C, es, tag, TK, nx=2, nT=2):
    nc = C.nc
    NS = TK // 128
    N = Ctx()
    N.TK, N.NS, N.tag = TK, NS, tag
    N.xt = [sb(es, nc, tag + "xt%d" % i, [128, NS, D], F32) for i in range(nx)]
    N.hn = sb(es, nc, tag + "hn", [128, NS, D], BF16)
    N.hnT = [sb(es, nc, tag + "hnT%d" % i, [128, 8, TK], BF16) for i in range(nT)]
    N.ss = [sb(es, nc, tag + "ss%d" % i, [128, NS], F32) for i in range(nx)]
    N.rstd = [sb(es, nc, tag + "rstd%d" % i, [128, NS], F32) for i in range(nx)]
    N.junk = sb(es, nc, tag + "junk", [128, D], BF16)
    return N


def norm_load(C, N, x_src, ti):
    S = C.S
    slot = ti % len(N.xt)
    S.dma("sp", N.xt[slot][:, :, :],
          x_src[ti * N.TK:(ti + 1) * N.TK, :].rearrange("(s p) d -> p s d", p=128),
          writes=[(N.tag + "xt", slot)])


def norm_tile(C, N, gcol, ti):
    S = C.S
    tag = N.tag
    slot = ti % len(N.xt)
    tslot = ti % len(N.hnT)
    xs = N.xt[slot]
    ss, rstd = N.ss[slot], N.rstd[slot]
    S.I("pool", "memset", ss[:, :], 0.0, writes=[(tag + "ss", slot)])
    for s in range(N.NS):
        S.I("act", "activation", out=N.junk[:, :], in_=xs[:, s, :], func=AF.Square,
            accum_out=ss[:, s:s + 1],
            reads=[(tag + "xt", slot)], writes=[(tag + "ss", slot), (tag + "junk", 0)])
    S.I("act", "activation", out=rstd[:, :], in_=ss[:, :], func=AF.Ln, scale=1.0 / D, bias=C.eps5[:, 0:1],
        reads=[(tag + "ss", slot)], writes=[(tag + "rstd", slot)])
    S.I("act", "activation", out=rstd[:, :], in_=rstd[:, :], func=AF.Exp, scale=-0.5,
        reads=[(tag + "rstd", slot)], writes=[(tag + "rstd", slot)])
    for s in range(N.NS):
        S.I("dve", "scalar_tensor_tensor", out=N.hn[:, s, :], in0=xs[:, s, :], scalar=rstd[:, s:s + 1],
            in1=gcol[:, :], op0=ALU.mult, op1=ALU.mult,
            reads=[(tag + "xt", slot), (tag + "rstd", slot)], writes=[(tag + "hn", s)])
    hT = N.hnT[tslot]
    for kc in range(8):
        b = C.psT[C.psT_i % len(C.psT)]
        C.psT_i += 1
        for s in range(N.NS):
            S.I("pe", "matmul", C.ps[b][:, s * 128:(s + 1) * 128], N.hn[:, s, kc * 128:(kc + 1) * 128],
                C.ident[:, :], start=True, stop=True,
                reads=[(tag + "hn", s)], writes=[("ps", b)])
        if kc % 2 == 0:
            S.I("act", "activation", out=hT[:, kc, :], in_=C.ps[b][:, 0:N.TK], func=AF.Copy,
                reads=[("ps", b)], writes=[(tag + "hnT", tslot, kc)])
        else:
            S.I("dve", "tensor_copy", hT[:, kc, :], C.ps[b][:, 0:N.TK],
                reads=[("ps", b)], writes=[(tag + "hnT", tslot, kc)])
    return xs, hT, slot, tslot


def load_bcast_row(C, es, row_ap, n, name):
    t = sb(es, C.nc, name, [128, n], F32)
    C.S.dma("sp", t[:, :], row_ap.partition_broadcast(128), writes=[(name, 0)])
    return t


def phase_inproj(C, x_src, g_row, w_ap, NOUT, u_dst, tag):
    nc, S = C.nc, C.S
    with ExitStack() as es:
        gcol = load_bcast_row(C, es, g_row, D, tag + "g")
        wb = load_weight_bf16(C, es, w_ap, D, NOUT, tag + "w")
        N = alloc_norm_tiles(C, es, tag, TT)
        ost = [sb(es, nc, tag + "ost%d" % i, [128, TT], F32) for i in range(4)]
        oi = 0
        norm_load(C, N, x_src, 0)
        nxt = norm_tile(C, N, gcol, 0)
        for ti in range(NT):
            if ti + 1 < NT:
                norm_load(C, N, x_src, ti + 1)
            xs, hT, slot, tslot = nxt
            for oc in range(NOUT // 128):
                if oc == (NOUT // 128) // 2 and ti + 1 < NT:
                    nxt = norm_tile(C, N, gcol, ti + 1)
                b = C.psM[C.psM_i % len(C.psM)]
                C.psM_i += 1
                for kc in range(8):
                    S.I("pe", "matmul", C.ps[b][:, :], wb[:, kc, oc * 128:(oc + 1) * 128], hT[:, kc, :],
                        start=(kc == 0), stop=(kc == 7),
                        reads=[(tag + "hnT", tslot, kc)], writes=[("ps", b)])
                o = oi % 4
                oi += 1
                if oc % 2 == 0:
                    S.I("act", "activation", out=ost[o][:, :], in_=C.ps[b][:, :], func=AF.Copy,
                        reads=[("ps", b)], writes=[(tag + "ost", o)])
                else:
                    S.I("dve", "tensor_copy", ost[o][:, :], C.ps[b][:, :],
                        reads=[("ps", b)], writes=[(tag + "ost", o)])
                S.dma("sp", u_dst[oc * 128:(oc + 1) * 128, ti * TT:(ti + 1) * TT], ost[o][:, :],
                      reads=[(tag + "ost", o)])
        S.barrier()
        S.flush()


def colvec(C, es, ap, R, NCOL, name):
    nc, S = C.nc, C.S
    NCH = NCOL // 128
    out = sb(es, nc, name, [128, NCH, R], F32)
    with ExitStack() as es2:
        rows = sb(es2, nc, name + "_rows", [R, NCOL], F32)
        S.dma("sp", rows[:, :], ap, writes=[(name + "rows", 0)])
        for c in range(NCH):
            b = C.psT[C.psT_i % len(C.psT)]
            C.psT_i += 1
            S.I("pe", "matmul", C.ps[b][:, 0:R], rows[0:R, c * 128:(c + 1) * 128], C.identf[0:R, 0:R],
                start=True, stop=True, reads=[(name + "rows", 0)], writes=[("ps", b)])
            S.I("dve", "tensor_copy", out[:, c, :], C.ps[b][:, 0:R], reads=[("ps", b)], writes=[(name, c)])
        S.barrier()
        S.flush()
    return out


def phase_conformer(C, u, ya_dst):
    nc, S = C.nc, C.S
    W = C.W
    tag = "B"
    HW = 15
    with ExitStack() as es:
        wcol = colvec(C, es, W["cv_dw_w"], 31, 512, "Bw")
        bcol = colvec(C, es, W["cv_dw_b"], 1, 512, "Bb")
        gcol = colvec(C, es, W["cv_ln_g"], 1, 512, "Bg")
        becol = colvec(C, es, W["cv_ln_b"], 1, 512, "Bbe")
        at = [sb(es, nc, "Bat%d" % i, [128, TT + 2 * HW], F32) for i in range(4)]
        gt = [sb(es, nc, "Bgt%d" % i, [128, TT + 2 * HW], F32) for i in range(4)]
        ht = [sb(es, nc, "Bht%d" % i, [128, TT + 2 * HW], F32) for i in range(4)]
        cv = [sb(es, nc, "Bcv%d" % i, [128, TT], F32) for i in range(4)]
        sq = [sb(es, nc, "Bsq%d" % i, [128, TT], F32) for i in range(2)]
        mean = sb(es, nc, "Bmean", [128, TT], F32)
        msq = sb(es, nc, "Bmsq", [128, TT], F32)
        rstd = sb(es, nc, "Brstd", [128, TT], F32)
        t1 = [sb(es, nc, "Bt1%d" % i, [128, TT], F32) for i in range(2)]
        yo = [sb(es, nc, "Byo%d" % i, [128, TT], BF16) for i in range(2)]
        for i in range(4):
            S.I("pool", "memset", at[i][:, :], 0.0, writes=[("Bat", i)])
            S.I("pool", "memset", gt[i][:, :], 0.0, writes=[("Bgt", i)])
        for ti in range(NT):
            t0 = ti * TT
            lo = max(t0 - HW, 0)
            hi = min(t0 + TT + HW, T)
            c0 = lo - (t0 - HW)
            c1 = c0 + (hi - lo)
            for c in range(4):
                if ti == NT - 1:
                    S.I("pool", "memset", at[c][:, c1:], 0.0, writes=[("Bat", c)])
                    S.I("pool", "memset", gt[c][:, c1:], 0.0, writes=[("Bgt", c)])
                S.dma("sp", at[c][:, c0:c1], u[c * 128:(c + 1) * 128, lo:hi], writes=[("Bat", c)])
                S.dma("sp", gt[c][:, c0:c1], u[512 + c * 128:512 + (c + 1) * 128, lo:hi], writes=[("Bgt", c)])
            for c in range(4):
                S.I("act", "activation", out=gt[c][:, :], in_=gt[c][:, :], func=AF.Sigmoid,
                    reads=[("Bgt", c)], writes=[("Bgt", c)])
                eng = "dve" if c < 2 else "pool"
                S.I(eng, "tensor_tensor", ht[c][:, :], at[c][:, :], gt[c][:, :], op=ALU.mult,
                    reads=[("Bat", c), ("Bgt", c)], writes=[("Bht", c)])
                if ti == NT // 2 - 1:
                    S.I(eng, "tensor_scalar", ht[c][:, TT + HW:], ht[c][:, TT + HW:], C.flagcol[:, 0:1], None,
                        op0=ALU.mult, reads=[("Bht", c)], writes=[("Bht", c)])
                if ti == NT // 2:
                    S.I(eng, "tensor_scalar", ht[c][:, 0:HW], ht[c][:, 0:HW], C.flagcol[:, 0:1], None,
                        op0=ALU.mult, reads=[("Bht", c)], writes=[("Bht", c)])
            for j in range(31):
                for c in range(4):
                    eng = "dve"
                    if j == 0:
                        S.I(eng, "tensor_scalar", cv[c][:, :], ht[c][:, 0:TT], wcol[:, c, 0:1], bcol[:, c, 0:1],
                            op0=ALU.mult, op1=ALU.add, reads=[("Bht", c)], writes=[("Bcv", c)])
                    else:
                        S.I(eng, "scalar_tensor_tensor", out=cv[c][:, :], in0=ht[c][:, j:j + TT],
                            scalar=wcol[:, c, j:j + 1], in1=cv[c][:, :], op0=ALU.mult, op1=ALU.add,
                            reads=[("Bht", c)], writes=[("Bcv", c)])
            b1 = C.psM[C.psM_i % len(C.psM)]
            C.psM_i += 1
            b2 = C.psM[C.psM_i % len(C.psM)]
            C.psM_i += 1
            for c in range(4):
                S.I("pe", "matmul", C.ps[b1][:, :], C.onesf[:, :], cv[c][:, :], start=(c == 0), stop=(c == 3),
                    reads=[("Bcv", c)], writes=[("ps", b1)])
            for c in range(4):
                S.I("act", "activation", out=sq[c % 2][:, :], in_=cv[c][:, :], func=AF.Square,
                    reads=[("Bcv", c)], writes=[("Bsq", c % 2)])
                S.I("pe", "matmul", C.ps[b2][:, :], C.onesf[:, :], sq[c % 2][:, :], start=(c == 0), stop=(c == 3),
                    reads=[("Bsq", c % 2)], writes=[("ps", b2)])
            S.I("act", "activation", out=mean[:, :], in_=C.ps[b1][:, :], func=AF.Copy, scale=1.0 / 512,
                reads=[("ps", b1)], writes=[("Bmean", 0)])
            S.I("dve", "tensor_tensor", msq[:, :], mean[:, :], mean[:, :], op=ALU.mult,
                reads=[("Bmean", 0)], writes=[("Bmsq", 0)])
            S.I("dve", "scalar_tensor_tensor", out=rstd[:, :], in0=C.ps[b2][:, :], scalar=1.0 / 512, in1=msq[:, :],
                op0=ALU.mult, op1=ALU.subtract, reads=[("ps", b2), ("Bmsq", 0)], writes=[("Brstd", 0)])
            S.I("act", "activation", out=rstd[:, :], in_=rstd[:, :], func=AF.Ln, bias=C.eps5[:, 0:1],
                reads=[("Brstd", 0)], writes=[("Brstd", 0)])
            S.I("act", "activation", out=rstd[:, :], in_=rstd[:, :], func=AF.Exp, scale=-0.5,
                reads=[("Brstd", 0)], writes=[("Brstd", 0)])
            for c in range(4):
                k = c % 2
                S.I("dve", "tensor_tensor", t1[k][:, :], cv[c][:, :], mean[:, :], op=ALU.subtract,
                    reads=[("Bcv", c), ("Bmean", 0)], writes=[("Bt1", k)])
                S.I("pool", "tensor_tensor", t1[k][:, :], t1[k][:, :], rstd[:, :], op=ALU.mult,
                    reads=[("Bt1", k), ("Brstd", 0)], writes=[("Bt1", k)])
                S.I("act", "activation", out=yo[k][:, :], in_=t1[k][:, :], func=AF.Silu,
                    scale=gcol[:, c, 0:1], bias=becol[:, c, 0:1],
                    reads=[("Bt1", k)], writes=[("Byo", k)])
                S.dma("sp", ya_dst[c * 128:(c + 1) * 128, t0:t0 + TT], yo[k][:, :], reads=[("Byo", k)])
        S.barrier()
        S.flush()


HC128_COLS = {}
HC65_COLS = {}


def hyena_consts():
    p = np.arange(128)
    n1 = (p % 64)[:, None].astype(np.float64)
    k1 = np.arange(65)[None, :].astype(np.float64)
    th = 2 * np.pi * n1 * k1 / 128.0
    F1cat = np.concatenate([np.cos(th), -np.sin(th)], 1)
    n2 = (p % 64)[:, None].astype(np.float64)
    ph = 2 * np.pi * n2 * k1 / 8192.0
    Tc2 = np.concatenate([np.cos(ph), np.cos(ph)], 1)
    Ts2 = np.concatenate([np.sin(ph), np.sin(ph)], 1)
    q64 = np.arange(64)
    psi = 2 * np.pi * (p % 64)[:, None] * q64[None, :] / 64.0
    Mc = np.cos(psi)
    Ms = np.sin(psi)
    rhsA = np.concatenate([Mc, Ms], 1)
    rhsB = np.concatenate([-Ms, Mc], 1)
    sg = ((-1.0) ** np.arange(65))[None, :].repeat(128, 0)
    sgn2 = np.concatenate([sg, sg], 1)
    parts = [("F1cat", F1cat), ("Tc2", Tc2), ("Ts2", Ts2), ("Mc", Mc), ("Ms", Ms), ("nMs", -Ms),
             ("rhsA", rhsA), ("rhsB", rhsB), ("sgn2", sgn2)]
    off = 0
    for nm, a in parts:
        HC128_COLS[nm] = (off, a.shape[1])
        off += a.shape[1]
    hc128 = np.concatenate([a for _, a in parts], 1).astype(np.float32)
    kk = np.arange(65)[:, None].astype(np.float64)
    col = np.arange(128)[None, :]
    phi = 2 * np.pi * kk * (col % 64) / 8192.0
    Tci = np.cos(phi)
    Tsi = np.sin(phi)
    wk = np.full((65, 1), 2.0)
    wk[0, 0] = 1.0
    wk[64, 0] = 1.0
    n1c = np.arange(64)[None, :].astype(np.float64)
    thi = 2 * np.pi * kk * n1c / 128.0
    gc = wk * np.cos(thi) / 8192.0
    gs = -wk * np.sin(thi) / 8192.0
    zz = np.zeros((65, 64))
    parts = [("Tci", Tci), ("Tsi", Tsi), ("Gc_a", np.concatenate([gc, zz], 1)), ("Gs_a", np.concatenate([gs, zz], 1)),
             ("Gc_b", np.concatenate([zz, gc], 1)), ("Gs_b", np.concatenate([zz, gs], 1))]
    off = 0
    for nm, a in parts:
        HC65_COLS[nm] = (off, a.shape[1])
        off += a.shape[1]
    hc65 = np.concatenate([a for _, a in parts], 1).astype(np.float32)
    return hc128, hc65


def hyena_pos_tables(is_prompt):
    L = 8192 if is_prompt else 4096
    q = np.arange(8192)
    n2 = q // 128
    hp = q % 128
    half = hp // 64
    n1 = hp % 64
    pl = n1 * 64 + n2
    pos = pl + (4096 * half if is_prompt else 0)
    t = pos.astype(np.float64) / (L - 1)
    bands = 16
    f = np.linspace(1e-4, bands - 1, bands)
    w = 2 * np.pi * pos.astype(np.float64) / L
    fw = w[None, :] * f[:, None]
    zT = np.concatenate([t[None, :], np.cos(fw), -np.sin(fw)], 0).astype(np.float32)
    tn = -t.reshape(64, 128).T.copy()
    if not is_prompt:
        tn[64:, :] = -1e4
    tnb = tn.copy()
    tnb[0, 0] = -1e4
    return zT, np.concatenate([tn, tnb], 1).astype(np.float32)


def hc(C, name, rows=128):
    off, n = (HC128_COLS if rows == 128 else HC65_COLS)[name]
    t = C.hc128 if rows == 128 else C.hc65
    return t[0:rows, off:off + n]


def phase_shortconv(C, u, uh):
    nc, S = C.nc, C.S
    W = C.W
    with ExitStack() as es:
        wcol = colvec(C, es, W["hy_short_w"], 3, 1536, "Sw")
        bcol = colvec(C, es, W["hy_short_b"], 1, 1536, "Sb")
        it = [sb(es, nc, "Sit%d" % i, [128, TT + 2], F32) for i in range(3)]
        cv = [sb(es, nc, "Scv%d" % i, [128, TT], F32) for i in range(2)]
        ot = [sb(es, nc, "Sot%d" % i, [128, 4, 128], F32) for i in range(2)]
        k = 0
        for c in range(12):
            for ti in range(NT):
                t0 = ti * TT
                lo = max(t0 - 1, 0)
                hi = min(t0 + TT + 1, T)
                c0 = lo - (t0 - 1)
                c1 = c0 + (hi - lo)
                i3 = k % 3
                i2 = k % 2
                k += 1
                if ti == 0:
                    S.I("pool", "memset", it[i3][:, 0:1], 0.0, writes=[("Sit", i3)])
                if ti == NT - 1:
                    S.I("pool", "memset", it[i3][:, TT + 1:TT + 2], 0.0, writes=[("Sit", i3)])
                S.dma("sp", it[i3][:, c0:c1], u[1024 + c * 128:1024 + (c + 1) * 128, lo:hi], writes=[("Sit", i3)])
                if ti == NT // 2 - 1:
                    S.I("pool", "tensor_scalar", it[i3][:, TT + 1:TT + 2], it[i3][:, TT + 1:TT + 2],
                        C.flagcol[:, 0:1], None, op0=ALU.mult, reads=[("Sit", i3)], writes=[("Sit", i3)])
                if ti == NT // 2:
                    S.I("pool", "tensor_scalar", it[i3][:, 0:1], it[i3][:, 0:1],
                        C.flagcol[:, 0:1], None, op0=ALU.mult, reads=[("Sit", i3)], writes=[("Sit", i3)])
                S.I("dve", "tensor_scalar", cv[i2][:, :], it[i3][:, 0:TT], wcol[:, c, 0:1], bcol[:, c, 0:1],
                    op0=ALU.mult, op1=ALU.add, reads=[("Sit", i3)], writes=[("Scv", i2)])
                for j in (1, 2):
                    S.I("dve", "scalar_tensor_tensor", out=cv[i2][:, :], in0=it[i3][:, j:j + TT],
                        scalar=wcol[:, c, j:j + 1], in1=cv[i2][:, :], op0=ALU.mult, op1=ALU.add,
                        reads=[("Sit", i3)], writes=[("Scv", i2)])
                b = C.psT[C.psT_i % len(C.psT)]
                C.psT_i += 1
                for s in range(4):
                    S.I("pe", "matmul", C.ps[b][:, s * 128:(s + 1) * 128], cv[i2][:, s * 128:(s + 1) * 128],
                        C.identf[:, :], start=True, stop=True, reads=[("Scv", i2)], writes=[("ps", b)])
                S.I("act", "activation", out=ot[i2][:, :, :], in_=C.ps[b][:, :].rearrange("p (s c) -> p s c", s=4),
                    func=AF.Copy, reads=[("ps", b)], writes=[("Sot", i2)])
                S.dma("act", uh[t0:t0 + TT, c * 128:(c + 1) * 128].rearrange("(s p) c -> p s c", p=128),
                      ot[i2][:, :, :], reads=[("Sot", i2)])
        S.barrier()
        S.flush()


def alloc_fft_tiles(C, es, pfx):
    nc = C.nc
    Fq = Ctx()
    Fq.A2 = sb(es, nc, pfx + "A2", [64, 16, 130], F32)
    Fq.B = sb(es, nc, pfx + "B", [64, 2, 16, 65], F32)
    Fq.tmpA = sb(es, nc, pfx + "tmpA", [128, 2080], F32)
    Fq.tmpB = sb(es, nc, pfx + "tmpB", [128, 2080], F32)
    Fq.pfx = pfx
    return Fq


def fft_fwd_group(C, Fq, src, src_tok, ch0, X, X_tok):
    S = C.S
    pfx = Fq.pfx
    F1 = hc(C, "F1cat")
    slot = 0
    while slot < 16:
        nb = 2
        b = C.psM[C.psM_i % len(C.psM)]
        C.psM_i += 1
        for s in range(nb):
            i = slot + s
            h, chl = i // 8, i % 8
            S.I("pe", "matmul", C.ps[b][0:64, s * 256:s * 256 + 130],
                src[h * 64:(h + 1) * 64, :, ch0 + chl], F1[h * 64:(h + 1) * 64, :],
                start=True, stop=True, reads=[src_tok], writes=[("ps", b)])
        S.I("act", "activation", out=Fq.A2[:, slot:slot + nb, :],
            in_=C.ps[b][0:64, :].rearrange("p (s c) -> p s c", s=2)[:, :, 0:130], func=AF.Copy,
            reads=[("ps", b)], writes=[(pfx + "A2", 0)])
        slot += nb
    if C.dbg.get("ffstop") == 1:
        return
    Tc = hc(C, "Tc2")[0:64, :].unsqueeze(1).broadcast_to([64, 16, 130])
    Ts = hc(C, "Ts2")[0:64, :].unsqueeze(1).broadcast_to([64, 16, 130])
    P1 = Fq.tmpA[0:64, 0:2080].rearrange("p (a c) -> p a c", a=16)
    P2 = Fq.tmpB[0:64, 0:2080].rearrange("p (a c) -> p a c", a=16)
    S.I("dve", "tensor_tensor", P1, Fq.A2[:, :, :], Tc, op=ALU.mult,
        reads=[(pfx + "A2", 0)], writes=[(pfx + "tmpA", 0)])
    S.I("pool", "tensor_tensor", P2, Fq.A2[:, :, :], Ts, op=ALU.mult,
        reads=[(pfx + "A2", 0)], writes=[(pfx + "tmpB", 0)])
    S.I("dve", "tensor_tensor", Fq.B[:, 0, :, :], P1[:, :, 0:65], P2[:, :, 65:130], op=ALU.add,
        reads=[(pfx + "tmpA", 0), (pfx + "tmpB", 0)], writes=[(pfx + "B", 0)])
    S.I("pool", "tensor_tensor", Fq.B[:, 1, :, :], P1[:, :, 65:130], P2[:, :, 0:65], op=ALU.subtract,
        reads=[(pfx + "tmpA", 0), (pfx + "tmpB", 0)], writes=[(pfx + "B", 1)])
    if C.dbg.get("ffstop") == 2:
        return
    Mc, Ms, nMs = hc(C, "Mc")[0:64, :], hc(C, "Ms")[0:64, :], hc(C, "nMs")[0:64, :]
    for q in range(4):
        Br = Fq.B[:, 0, 4 * q:4 * q + 4, :]
        Bi = Fq.B[:, 1, 4 * q:4 * q + 4, :]
        b1 = C.psM[C.psM_i % len(C.psM)]
        C.psM_i += 1
        b2 = C.psM[C.psM_i % len(C.psM)]
        C.psM_i += 1
        S.I("pe", "matmul", C.ps[b1][0:64, 0:260], Mc, Br, start=True, stop=False,
            reads=[(pfx + "B", 0)], writes=[("ps", b1)])
        S.I("pe", "matmul", C.ps[b1][0:64, 0:260], Ms, Bi, start=False, stop=True,
            reads=[(pfx + "B", 1)], writes=[("ps", b1)])
        S.I("pe", "matmul", C.ps[b2][0:64, 0:260], Mc, Bi, start=True, stop=False,
            reads=[(pfx + "B", 1)], writes=[("ps", b2)])
        S.I("pe", "matmul", C.ps[b2][0:64, 0:260], nMs, Br, start=False, stop=True,
            reads=[(pfx + "B", 0)], writes=[("ps", b2)])
        h, c4 = q // 2, 4 * (q % 2)
        S.I("act", "activation", out=X[:, 0, h, c4:c4 + 4, :],
            in_=C.ps[b1][0:64, 0:260].rearrange("p (a c) -> p a c", a=4), func=AF.Copy,
            reads=[("ps", b1)], writes=[X_tok])
        S.I("act", "activation", out=X[:, 1, h, c4:c4 + 4, :],
            in_=C.ps[b2][0:64, 0:260].rearrange("p (a c) -> p a c", a=4), func=AF.Copy,
            reads=[("ps", b2)], writes=[X_tok])


def phase_filters(C, Gd, zT_in, tneg_in):
    nc, S = C.nc, C.S
    W = C.W
    with ExitStack() as es:
        fcol = sb(es, nc, "Ffcol", [64, 3], F32)
        bcol = sb(es, nc, "Fbcol", [64, 3], F32)
        f8 = sb(es, nc, "Ff8", [64, 3], F32)
        b8 = sb(es, nc, "Fb8", [64, 3], F32)
        f4 = sb(es, nc, "Ff4", [64, 3], F32)
        b4 = sb(es, nc, "Fb4", [64, 3], F32)
        with ExitStack() as es2:
            rows = sb(es2, nc, "Frows", [6, 64], F32)
            S.dma("sp", rows[0:3, :], W["hy_freq"], writes=[("Frows", 0)])
            for i, nm in enumerate(("hy_b1", "hy_b2", "hy_b3")):
                S.dma("sp", rows[3 + i:4 + i, :], W[nm], writes=[("Frows", 0)])
            b = C.psT[C.psT_i % len(C.psT)]
            C.psT_i += 1
            S.I("pe", "matmul", C.ps[b][0:64, 0:6], rows[0:6, 0:64], C.identf[0:6, 0:6], start=True, stop=True,
                reads=[("Frows", 0)], writes=[("ps", b)])
            S.I("dve", "tensor_copy", fcol[:, :], C.ps[b][0:64, 0:3], reads=[("ps", b)], writes=[("Ffcol", 0)])
            S.I("dve", "tensor_copy", bcol[:, :], C.ps[b][0:64, 3:6], reads=[("ps", b)], writes=[("Fbcol", 0)])
            S.I("dve", "tensor_tensor", bcol[:, :], bcol[:, :], fcol[:, :], op=ALU.mult,
                reads=[("Ffcol", 0), ("Fbcol", 0)], writes=[("Fbcol", 0)])
            S.I("dve", "tensor_scalar", f8[:, :], fcol[:, :], 0.125, None, op0=ALU.mult,
                reads=[("Ffcol", 0)], writes=[("Ff8", 0)])
            S.I("dve", "tensor_scalar", b8[:, :], bcol[:, :], 0.125, None, op0=ALU.mult,
                reads=[("Fbcol", 0)], writes=[("Fb8", 0)])
            S.I("dve", "tensor_scalar", f4[:, :], fcol[:, :], 0.25, None, op0=ALU.mult,
                reads=[("Ffcol", 0)], writes=[("Ff4", 0)])
            S.I("dve", "tensor_scalar", b4[:, :], bcol[:, :], 0.25, None, op0=ALU.mult,
                reads=[("Fbcol", 0)], writes=[("Fb4", 0)])
            S.barrier()
            S.flush()
        w1 = sb(es, nc, "Fw1", [33, 64], F32)
        w2 = sb(es, nc, "Fw2", [64, 64], F32)
        w3 = sb(es, nc, "Fw3", [64, 64], F32)
        w4 = sb(es, nc, "Fw4", [64, 2048], F32)
        S.dma("sp", w1[:, :], W["hy_w1"])
        S.dma("sp", w2[:, :], W["hy_w2"])
        S.dma("sp", w3[:, :], W["hy_w3"])
        S.dma("sp", w4[:, :], W["hy_w4"])
        absd = sb(es, nc, "Fabsd", [128, 2048], F32)
        S.dma("sp", absd[:, :], W["hy_decay"].partition_broadcast(128), writes=[("Fabsd", 0)])
        S.I("act", "activation", out=absd[:, :], in_=absd[:, :], func=AF.Abs,
            reads=[("Fabsd", 0)], writes=[("Fabsd", 0)])
        tneg = sb(es, nc, "Ftneg", [128, 128], F32)
        S.dma("sp", tneg[:, :], tneg_in)
        negpi = sb(es, nc, "Fnegpi", [128, 1], F32)
        S.I("pool", "memset", negpi[:, :], -math.pi)
        eps6 = sb(es, nc, "Feps6", [128, 1], F32)
        S.I("pool", "memset", eps6[:, :], 1e-6)
        h3T = sb(es, nc, "Fh3T", [64, 8192], F32)
        S.barrier()
        S.flush()
        if C.dbg.get("fstop") == 1:
            return
        with ExitStack() as es2:
            zt = [sb(es2, nc, "Fzt%d" % i, [33, 512], F32) for i in range(2)]
            ha = [sb(es2, nc, "Fha%d" % i, [64, 512], F32) for i in range(2)]
            hb = [sb(es2, nc, "Fhb%d" % i, [64, 512], F32) for i in range(2)]
            hc_ = [sb(es2, nc, "Fhc%d" % i, [64, 512], F32) for i in range(2)]
            for blk in range(16):
                i2 = blk % 2
                S.dma("sp", zt[i2][:, :], zT_in[:, blk * 512:(blk + 1) * 512], writes=[("Fzt", i2)])
                cur = zt[i2][0:33, :]
                cur_tok = ("Fzt", i2)
                for l, wl in enumerate((w1, w2, w3)):
                    K = 33 if l == 0 else 64
                    b = C.psM[C.psM_i % len(C.psM)]
                    C.psM_i += 1
                    S.I("pe", "matmul", C.ps[b][0:64, :], wl[0:K, :], cur, start=True, stop=True,
                        reads=[cur_tok], writes=[("ps", b)])
                    sa, sb_ = ha[i2], hc_[i2]
                    S.I("act", "activation", out=sa[:, :], in_=C.ps[b][0:64, :], func=AF.Sin,
                        scale=f8[:, l:l + 1], bias=b8[:, l:l + 1], reads=[("ps", b)], writes=[("Fha", i2)])
                    S.I("act", "activation", out=sb_[:, :], in_=C.ps[b][0:64, :], func=AF.Sin,
                        scale=f4[:, l:l + 1], bias=b4[:, l:l + 1], reads=[("ps", b)], writes=[("Fhc", i2)])
                    S.I("dve", "tensor_tensor", sa[:, :], sa[:, :], sa[:, :], op=ALU.mult,
                        reads=[("Fha", i2)], writes=[("Fha", i2)])
                    S.I("dve", "tensor_scalar", sa[:, :], sa[:, :], -2.0, 1.0, op0=ALU.mult, op1=ALU.add,
                        reads=[("Fha", i2)], writes=[("Fha", i2)])
                    S.I("dve", "scalar_tensor_tensor", out=sa[:, :], in0=sb_[:, :], scalar=2.0, in1=sa[:, :],
                        op0=ALU.mult, op1=ALU.mult, reads=[("Fha", i2), ("Fhc", i2)], writes=[("Fha", i2)])
                    S.I("dve", "tensor_tensor", sb_[:, :], sb_[:, :], sb_[:, :], op=ALU.mult,
                        reads=[("Fhc", i2)], writes=[("Fhc", i2)])
                    S.I("dve", "tensor_scalar", sb_[:, :], sb_[:, :], -2.0, 1.0, op0=ALU.mult, op1=ALU.add,
                        reads=[("Fhc", i2)], writes=[("Fhc", i2)])
                    dst = h3T[:, blk * 512:(blk + 1) * 512] if l == 2 else hb[i2][:, :]
                    dtok = ("Fh3T", blk) if l == 2 else ("Fhb", i2)
                    S.I("dve", "scalar_tensor_tensor", out=dst, in0=sa[:, :], scalar=2.0, in1=sb_[:, :],
                        op0=ALU.mult, op1=ALU.mult, reads=[("Fha", i2), ("Fhc", i2)], writes=[dtok])
                    cur = hb[i2][:, :]
                    cur_tok = ("Fhb", i2)
            S.barrier()
            S.flush()
        if C.dbg.get("fstop") == 2:
            return
        fw = sb(es, nc, "Ffw", [128, 64, 128], F32)
        bw = sb(es, nc, "Fbw", [128, 64, 128], F32)
        win = [sb(es, nc, "Fwin%d" % i, [128, 4, 128], F32) for i in range(2)]
        sq = [sb(es, nc, "Fsq%d" % i, [128, 4, 128], F32) for i in range(2)]
        scl = sb(es, nc, "Fscl", [128, 128], F32)
        Fq = alloc_fft_tiles(C, es, "Fq")
        XF = sb(es, nc, "FXF", [64, 2, 2, 8, 65], F32)
        XB = sb(es, nc, "FXB", [64, 2, 2, 8, 65], F32)
        Go = [[sb(es, nc, "FGo%d_%d" % (w_, 0), [64, 2, 8, 65], F32)] * 2 for w_ in range(3)]
        t65 = [Fq.tmpA[0:64, 0:1040].rearrange("p (r a c) -> p r a c", r=2, a=8),
               Fq.tmpB[0:64, 0:1040].rearrange("p (r a c) -> p r a c", r=2, a=8)]
        nflag = sb(es, nc, "Fnflag", [128, 1], F32)
        S.I("dve", "tensor_scalar", nflag[:, :], C.flagcol[:, 0:1], -1.0, None, op0=ALU.mult)
        wi = 0
        gi = 0
        for o in range(2):
            for cc in range(4):
                bss = C.psT[0]
                nmm = 0
                for d, dst in ((0, fw), (1, bw)):
                    colbase = (o * 2 + d) * 512 + cc * 128
                    for nb in range(16):
                        b = C.psM[C.psM_i % len(C.psM)]
                        C.psM_i += 1
                        wv = win[wi % 2]
                        wt = ("Fwin", wi % 2)
                        sv = sq[wi % 2]
                        st_ = ("Fsq", wi % 2)
                        wi += 1
                        for s in range(4):
                            n2 = nb * 4 + s
                            S.I("pe", "matmul", C.ps[b][:, s * 128:(s + 1) * 128], h3T[0:64, n2 * 128:(n2 + 1) * 128],
                                w4[0:64, colbase:colbase + 128], start=True, stop=True,
                                reads=[("Fh3T", n2 // 4)], writes=[("ps", b)])
                            S.I("act", "activation", out=wv[:, s, :], in_=absd[:, colbase:colbase + 128], func=AF.Exp,
                                scale=tneg[:, d * 64 + n2:d * 64 + n2 + 1], reads=[("Fabsd", 0)], writes=[wt])
                        S.I("dve", "tensor_tensor", dst[:, nb * 4:(nb + 1) * 4, :],
                            C.ps[b][:, :].rearrange("p (s c) -> p s c", s=4), wv[:, :, :], op=ALU.mult,
                            reads=[("ps", b), wt], writes=[("Ffilt", d, nb)])
                        S.I("pool", "tensor_tensor", sv[:, :, :], dst[:, nb * 4:(nb + 1) * 4, :],
                            dst[:, nb * 4:(nb + 1) * 4, :], op=ALU.mult, reads=[("Ffilt", d, nb)], writes=[st_])
                        for s in range(4):
                            S.I("pe", "matmul", C.ps[bss][:, 0:128], C.onesf[:, :], sv[:, s, :],
                                start=(nmm == 0), stop=(nmm == 127), reads=[st_], writes=[("ps", bss)])
                            nmm += 1
                S.I("act", "activation", out=scl[:, :], in_=C.ps[bss][:, 0:128], func=AF.Ln, bias=eps6[:, 0:1],
                    reads=[("ps", bss)], writes=[("Fscl", 0)])
                S.I("act", "activation", out=scl[:, :], in_=scl[:, :], func=AF.Exp, scale=-0.5,
                    reads=[("Fscl", 0)], writes=[("Fscl", 0)])
                sclb = scl[:, :].unsqueeze(1).broadcast_to([128, 64, 128])
                allf = [("Ffilt", 0, nb) for nb in range(16)]
                allb = [("Ffilt", 1, nb) for nb in range(16)]
                S.I("dve", "tensor_tensor", fw[:, :, :], fw[:, :, :], sclb, op=ALU.mult,
                    reads=[("Fscl", 0)] + allf, writes=allf + [("Ffw", 0)])
                S.I("pool", "tensor_tensor", bw[:, :, :], bw[:, :, :], sclb, op=ALU.mult,
                    reads=[("Fscl", 0)] + allb, writes=allb + [("Fbw", 0)])
                if C.dbg.get("fstop") == 3:
                    S.barrier()
                    S.flush()
                    return
                sg = hc(C, "sgn2")[0:64, :].rearrange("p (r c) -> p r c", r=2).unsqueeze(2).broadcast_to([64, 2, 8, 65])
                for g in range(16):
                    ch0 = g * 8
                    fft_fwd_group(C, Fq, fw, ("Ffw", 0), ch0, XF, ("FXF", 0))
                    fft_fwd_group(C, Fq, bw, ("Fbw", 0), ch0, XB, ("FXB", 0))
                    if C.dbg.get("fstop") == 4:
                        S.barrier()
                        S.flush()
                        return
                    k2 = 0
                    Gaa, Gab, Gba = Go[0][k2], Go[1][k2], Go[2][k2]
                    Fl, Fh = XF[:, :, 0, :, :], XF[:, :, 1, :, :]
                    Bl, Bh = XB[:, :, 0, :, :], XB[:, :, 1, :, :]
                    S.I("dve", "tensor_tensor", Gaa[:, 0, :, :], Fl[:, 0, :, :], Bl[:, 0, :, :], op=ALU.add,
                        reads=[("FXF", 0), ("FXB", 0)], writes=[("FGo", 0, k2)])
                    S.I("dve", "tensor_tensor", Gaa[:, 1, :, :], Fl[:, 1, :, :], Bl[:, 1, :, :], op=ALU.subtract,
                        reads=[("FXF", 0), ("FXB", 0)], writes=[("FGo", 0, k2)])
                    S.I("pool", "tensor_tensor", t65[0], Fl, sg, op=ALU.mult,
                        reads=[("FXF", 0)], writes=[("FqtmpA", 0)])
                    S.I("pool", "tensor_tensor", t65[0], t65[0], Fh, op=ALU.add,
                        reads=[("FXF", 0), ("FqtmpA", 0)], writes=[("FqtmpA", 0)])
                    S.I("pool", "tensor_scalar", Gba[:, :, :, :], t65[0], C.flagcol[0:64, 0:1], None,
                        op0=ALU.mult, reads=[("FqtmpA", 0)], writes=[("FGo", 2, k2)])
                    S.I("dve", "tensor_tensor", t65[1], Bl, sg, op=ALU.mult,
                        reads=[("FXB", 0)], writes=[("FqtmpB", 0)])
                    S.I("dve", "tensor_tensor", t65[1], t65[1], Bh, op=ALU.add,
                        reads=[("FXB", 0), ("FqtmpB", 0)], writes=[("FqtmpB", 0)])
                    S.I("dve", "tensor_scalar", Gab[:, 0, :, :], t65[1][:, 0, :, :], C.flagcol[0:64, 0:1], None,
                        op0=ALU.mult, reads=[("FqtmpB", 0)], writes=[("FGo", 1, k2)])
                    S.I("dve", "tensor_scalar", Gab[:, 1, :, :], t65[1][:, 1, :, :], nflag[0:64, 0:1], None,
                        op0=ALU.mult, reads=[("FqtmpB", 0)], writes=[("FGo", 1, k2)])
                    for w_ in range(3):
                        S.dma("sp", Gd[o, cc, w_, :, :, g * 8:(g + 1) * 8, :], Go[w_][k2][:, :, :, :],
                              reads=[("FGo", w_, k2)])
        S.barrier()
        S.flush()


def phase_hyena(C, uh, Gd, yb):
    nc, S = C.nc, C.S
    W = C.W
    with ExitStack() as es:
        skb = sb(es, nc, "Hskb", [128, 2, 512], F32)
        for o in range(2):
            S.dma("sp", skb[:, o, :], W["hy_skip"][o:o + 1, :].partition_broadcast(128), writes=[("Hskb", 0)])
        z = sb(es, nc, "Hz", [128, 64, 128], F32)
        xg = sb(es, nc, "Hxg", [128, 64, 128], F32)
        Fq = alloc_fft_tiles(C, es, "Hq")
        X = sb(es, nc, "HX", [64, 2, 2, 8, 65], F32)
        Y = sb(es, nc, "HY", [64, 2, 2, 8, 65], F32)
        Gt = [[sb(es, nc, "HG%d_%d" % (w_, i), [64, 2, 8, 65], F32) for i in range(2)] for w_ in range(3)]
        PQ = [sb(es, nc, "HPQ%d" % i, [64, 2, 8, 65], F32) for i in range(4)]
        C2 = sb(es, nc, "HC2", [65, 16, 128], F32)
        Dt = sb(es, nc, "HDt", [65, 2, 16, 64], F32)
        gt = [sb(es, nc, "Hgt%d" % i, [128, 64, 8], F32) for i in range(2)]
        gi = 0
        ztoks = [("Hz", g) for g in range(16)]
        for cc in range(4):
            for q4 in range(16):
                S.dma("sp", z[:, q4 * 4:(q4 + 1) * 4, :],
                      uh[:, 1024 + cc * 128:1024 + (cc + 1) * 128].rearrange("(p n) c -> p n c", n=64)[:, q4 * 4:(q4 + 1) * 4, :],
                      writes=ztoks)
            for o in range(2):
                for q4 in range(16):
                    S.dma("sp", xg[:, q4 * 4:(q4 + 1) * 4, :],
                          uh[:, o * 512 + cc * 128:o * 512 + (cc + 1) * 128].rearrange("(p n) c -> p n c", n=64)[:, q4 * 4:(q4 + 1) * 4, :],
                          writes=[("Hxg", 0)])
                for g in range(16):
                    ch0 = g * 8
                    k2 = gi % 2
                    gi += 1
                    for w_ in range(3):
                        S.dma("sp", Gt[w_][k2][:, :, :, :], Gd[o, cc, w_, :, :, g * 8:(g + 1) * 8, :],
                              writes=[("HG", w_, k2)])
                    fft_fwd_group(C, Fq, z, ("Hz", g), ch0, X, ("HX", 0))
                    Gaa, Gab, Gba = Gt[0][k2], Gt[1][k2], Gt[2][k2]
                    Xa, Xb = X[:, :, 0, :, :], X[:, :, 1, :, :]

                    def bc(Gx, r):
                        return Gx[:, r:r + 1, :, :].broadcast_to([64, 2, 8, 65])
                    for half, (GA, GB) in enumerate(((Gaa, Gab), (Gba, Gaa))):
                        ta = ("HG", (0, 1, 2)[[Gaa, Gab, Gba].index(GA)], k2)
                        tb = ("HG", (0, 1, 2)[[Gaa, Gab, Gba].index(GB)], k2)
                        S.I("dve", "tensor_tensor", PQ[0][:, :, :, :], Xa, bc(GA, 0), op=ALU.mult,
                            reads=[("HX", 0), ta], writes=[("HPQ", 0)])
                        S.I("dve", "tensor_tensor", PQ[1][:, :, :, :], Xa, bc(GA, 1), op=ALU.mult,
                            reads=[("HX", 0), ta], writes=[("HPQ", 1)])
                        S.I("pool", "tensor_tensor", PQ[2][:, :, :, :], Xb, bc(GB, 0), op=ALU.mult,
                            reads=[("HX", 0), tb], writes=[("HPQ", 2)])
                        S.I("pool", "tensor_tensor", PQ[3][:, :, :, :], Xb, bc(GB, 1), op=ALU.mult,
                            reads=[("HX", 0), tb], writes=[("HPQ", 3)])
                        S.I("pool", "tensor_tensor", PQ[0][:, :, :, :], PQ[0][:, :, :, :], PQ[2][:, :, :, :], op=ALU.add,
                            reads=[("HPQ", 0), ("HPQ", 2)], writes=[("HPQ", 0)])
                        S.I("pool", "tensor_tensor", PQ[1][:, :, :, :], PQ[1][:, :, :, :], PQ[3][:, :, :, :], op=ALU.add,
                            reads=[("HPQ", 1), ("HPQ", 3)], writes=[("HPQ", 1)])
                        S.I("dve", "tensor_tensor", Y[:, 0, half, :, :], PQ[0][:, 0, :, :], PQ[1][:, 1, :, :],
                            op=ALU.subtract, reads=[("HPQ", 0), ("HPQ", 1)], writes=[("HY", half)])
                        S.I("dve", "tensor_tensor", Y[:, 1, half, :, :], PQ[0][:, 1, :, :], PQ[1][:, 0, :, :],
                            op=ALU.add, reads=[("HPQ", 0), ("HPQ", 1)], writes=[("HY", half)])
                    rhsA, rhsB = hc(C, "rhsA")[0:64, :], hc(C, "rhsB")[0:64, :]
                    for s0 in range(0, 16, 4):
                        b = C.psM[C.psM_i % len(C.psM)]
                        C.psM_i += 1
                        for s in range(4):
                            i = s0 + s
                            h, chl = i // 8, i % 8
                            S.I("pe", "matmul", C.ps[b][0:65, s * 128:(s + 1) * 128], Y[:, 0, h, chl, :], rhsA,
                                start=True, stop=False, reads=[("HY", h)], writes=[("ps", b)])
                            S.I("pe", "matmul", C.ps[b][0:65, s * 128:(s + 1) * 128], Y[:, 1, h, chl, :], rhsB,
                                start=False, stop=True, reads=[("HY", h)], writes=[("ps", b)])
                        S.I("act", "activation", out=C2[0:65, s0:s0 + 4, :],
                            in_=C.ps[b][0:65, :].rearrange("p (s c) -> p s c", s=4), func=AF.Copy,
                            reads=[("ps", b)], writes=[("HC2", 0)])
                    Tci = hc(C, "Tci", 65).unsqueeze(1).broadcast_to([65, 16, 128])
                    Tsi = hc(C, "Tsi", 65).unsqueeze(1).broadcast_to([65, 16, 128])
                    P1 = Fq.tmpA[0:65, 0:2048].rearrange("p (a c) -> p a c", a=16)
                    P2 = Fq.tmpB[0:65, 0:2048].rearrange("p (a c) -> p a c", a=16)
                    S.I("dve", "tensor_tensor", P1, C2[0:65, :, :], Tci, op=ALU.mult,
                        reads=[("HC2", 0)], writes=[("HqtmpA", 0)])
                    S.I("pool", "tensor_tensor", P2, C2[0:65, :, :], Tsi, op=ALU.mult,
                        reads=[("HC2", 0)], writes=[("HqtmpB", 0)])
                    S.I("dve", "tensor_tensor", Dt[0:65, 0, :, :], P1[:, :, 0:64], P2[:, :, 64:128], op=ALU.subtract,
                        reads=[("HqtmpA", 0), ("HqtmpB", 0)], writes=[("HDt", 0)])
                    S.I("pool", "tensor_tensor", Dt[0:65, 1, :, :], P2[:, :, 0:64], P1[:, :, 64:128], op=ALU.add,
                        reads=[("HqtmpA", 0), ("HqtmpB", 0)], writes=[("HDt", 1)])
                    b = C.psM[C.psM_i % len(C.psM)]
                    C.psM_i += 1
                    fin = [("Gc_a", 0, 0), ("Gs_a", 1, 0), ("Gc_b", 0, 1), ("Gs_b", 1, 1)]
                    for i, (gn, ri, h) in enumerate(fin):
                        S.I("pe", "matmul", C.ps[b][:, :], hc(C, gn, 65), Dt[0:65, ri, 8 * h:8 * h + 8, :],
                            start=(i == 0), stop=(i == 3), reads=[("HDt", ri)], writes=[("ps", b)])
                    yv = C.ps[b][:, :].rearrange("p (c n) -> p n c", c=8)
                    zv = z[:, :, ch0:ch0 + 8]
                    xv = xg[:, :, ch0:ch0 + 8]
                    skv = skb[:, o, cc * 128 + ch0:cc * 128 + ch0 + 8].unsqueeze(1).broadcast_to([128, 64, 8])
                    gk = gt[g % 2]
                    S.I("pool", "tensor_tensor", gk[:, :, :], zv, skv, op=ALU.mult,
                        reads=[("Hz", g), ("Hskb", 0)], writes=[("Hgt", g % 2)])
                    S.I("dve", "tensor_tensor", gk[:, :, :], yv, gk[:, :, :], op=ALU.add,
                        reads=[("ps", b), ("Hgt", g % 2)], writes=[("Hgt", g % 2)])
                    S.I("pool", "tensor_tensor", zv, gk[:, :, :], xv, op=ALU.mult,
                        reads=[("Hgt", g % 2), ("Hxg", 0)], writes=[("Hz", g)])
            for q4 in range(16):
                S.dma("sp", yb[:, cc * 128:(cc + 1) * 128].rearrange("(p n) c -> p n c", n=64)[:, q4 * 4:(q4 + 1) * 4, :],
                      z[:, q4 * 4:(q4 + 1) * 4, :], reads=ztoks)
        S.barrier()
        S.flush()


def phase_outproj(C, x_src, ya, yb, w_ap, x_dst):
    nc, S = C.nc, C.S
    tag = "O"
    with ExitStack() as es:
        wb = load_weight_bf16(C, es, w_ap, D, D, "Ow")
        xt = [sb(es, nc, "Oxt%d" % i, [128, 4, D], F32) for i in range(2)]
        yaT = [sb(es, nc, "OyaT%d" % i, [128, 4, TT], BF16) for i in range(2)]
        ybt = [sb(es, nc, "Oybt%d" % i, [128, 4, 512], F32) for i in range(2)]
        ybb = sb(es, nc, "Oybb", [128, 4, 512], BF16)
        ybT = sb(es, nc, "OybT", [128, 4, TT], BF16)
        for ti in range(NT):
            t0 = ti * TT
            sl = ti % 2
            S.dma("sp", xt[sl][:, :, :], x_src[t0:t0 + TT, :].rearrange("(s p) d -> p s d", p=128),
                  writes=[("Oxt", sl)])
            S.dma("sp", yaT[sl][:, :, :], ya[:, t0:t0 + TT].rearrange("(k p) t -> p k t", p=128),
                  writes=[("OyaT", sl)])
            S.dma("sp", ybt[sl][:, :, :], yb[t0:t0 + TT, :].rearrange("(s p) c -> p s c", p=128),
                  writes=[("Oybt", sl)])
            S.I("pool", "tensor_copy", ybb[:, :, :], ybt[sl][:, :, :], reads=[("Oybt", sl)], writes=[("Oybb", 0)])
            for kc in range(4):
                b = C.psT[C.psT_i % len(C.psT)]
                C.psT_i += 1
                for s in range(4):
                    S.I("pe", "matmul", C.ps[b][:, s * 128:(s + 1) * 128], ybb[:, s, kc * 128:(kc + 1) * 128],
                        C.ident[:, :], start=True, stop=True, reads=[("Oybb", 0)], writes=[("ps", b)])
                S.I("act", "activation", out=ybT[:, kc, :], in_=C.ps[b][:, :], func=AF.Copy,
                    reads=[("ps", b)], writes=[("OybT", kc)])
            for s in range(4):
                for nh in range(2):
                    b = C.psM[C.psM_i % len(C.psM)]
                    C.psM_i += 1
                    for kc in range(8):
                        if kc < 4:
                            lt, tok = yaT[sl][:, kc, s * 128:(s + 1) * 128], ("OyaT", sl)
                        else:
                            lt, tok = ybT[:, kc - 4, s * 128:(s + 1) * 128], ("OybT", kc - 4)
                        S.I("pe", "matmul", C.ps[b][:, :], lt, wb[:, kc, nh * 512:(nh + 1) * 512],
                            start=(kc == 0), stop=(kc == 7), reads=[tok], writes=[("ps", b)])
                    S.I("dve", "tensor_tensor", xt[sl][:, s, nh * 512:(nh + 1) * 512], C.ps[b][:, :],
                        xt[sl][:, s, nh * 512:(nh + 1) * 512], op=ALU.add,
                        reads=[("ps", b)], writes=[("Oxt", sl)])
            S.dma("sp", x_dst[t0:t0 + TT, :].rearrange("(s p) d -> p s d", p=128), xt[sl][:, :, :],
                  reads=[("Oxt", sl)])
        S.barrier()
        S.flush()


def attn_consts(is_prompt):
    d = np.arange(128) % 64
    inv = 500000.0 ** (-(np.arange(0, 16, 2, dtype=np.float64) / 16.0))
    tau = np.arange(T)
    pos = tau if is_prompt else tau % 4096
    cos = np.ones((128, T), np.float64)
    sin = np.zeros((128, T), np.float64)
    for p in range(128):
        dd = d[p]
        if dd < 16:
            ang = pos * inv[dd % 8]
            cos[p] = np.cos(ang)
            sin[p] = np.sin(ang)
    P = np.zeros((128, 128), np.float32)
    for m in range(128):
        dd = m % 64
        if dd < 8:
            P[m + 8, m] = -1.0
        elif dd < 16:
            P[m - 8, m] = 1.0
    q = np.arange(128)[:, None]
    s = np.arange(128)[None, :]
    NEG = -30000.0
    prev = np.where(s >= q, 0.0, NEG)
    cur = np.zeros((128, 128))
    nxt = np.where(s <= q, 0.0, NEG)
    band = np.concatenate([prev, cur, nxt], 1).astype(np.float32)
    full = np.full((128, 128), NEG)
    if is_prompt:
        m31, m32 = band, band
    else:
        m31 = np.concatenate([prev, cur, full], 1).astype(np.float32)
        m32 = np.concatenate([full, cur, nxt], 1).astype(np.float32)
    masks = np.stack([band, m31, m32], 0).astype(np.float32)
    return cos.astype(np.float32), sin.astype(np.float32), P, masks


def phase_qkv(C, x_src, g_row, w_ap, qd, kd, vd, cos_in, sin_in, prot_in):
    nc, S = C.nc, C.S
    tag = "Q"
    with ExitStack() as es:
        gcol = load_bcast_row(C, es, g_row, D, "Qg")
        wb = load_weight_bf16(C, es, w_ap, D, 1536, "Qw")
        prf = sb(es, nc, "Qprf", [128, 128], F32)
        prb = sb(es, nc, "Qprb", [128, 128], BF16)
        S.dma("sp", prf[:, :], prot_in, writes=[("Qprf", 0)])
        S.I("dve", "tensor_copy", prb[:, :], prf[:, :], reads=[("Qprf", 0)], writes=[("Qprb", 0)])
        N = alloc_norm_tiles(C, es, tag, TT)
        cs = [sb(es, nc, "Qcs%d" % i, [128, TT], F32) for i in range(2)]
        sn = [sb(es, nc, "Qsn%d" % i, [128, TT], F32) for i in range(2)]
        xb = [sb(es, nc, "Qxb%d" % i, [128, TT], BF16) for i in range(2)]
        t1 = [sb(es, nc, "Qt1%d" % i, [128, TT], F32) for i in range(2)]
        t2 = [sb(es, nc, "Qt2%d" % i, [128, TT], F32) for i in range(2)]
        qo = [sb(es, nc, "Qqo%d" % i, [128, TT], BF16) for i in range(2)]
        vo = [sb(es, nc, "Qvo%d" % i, [128, 256], F32) for i in range(2)]
        k = 0
        norm_load(C, N, x_src, 0)
        nxt = norm_tile(C, N, gcol, 0)
        for ti in range(NT):
            t0 = ti * TT
            if ti + 1 < NT:
                norm_load(C, N, x_src, ti + 1)
            c2 = ti % 2
            S.dma("sp", cs[c2][:, :], cos_in[:, t0:t0 + TT], writes=[("Qcs", c2)])
            S.dma("sp", sn[c2][:, :], sin_in[:, t0:t0 + TT], writes=[("Qsn", c2)])
            xs, hT, slot, tslot = nxt
            for oc in range(10):
                if oc == 6 and ti + 1 < NT:
                    nxt = norm_tile(C, N, gcol, ti + 1)
                b = C.psM[C.psM_i % len(C.psM)]
                C.psM_i += 1
                for kc in range(8):
                    S.I("pe", "matmul", C.ps[b][:, :], wb[:, kc, oc * 128:(oc + 1) * 128], hT[:, kc, :],
                        start=(kc == 0), stop=(kc == 7), reads=[("QhnT", tslot, kc)], writes=[("ps", b)])
                k2 = k % 2
                k += 1
                if C.dbg.get("qstop") == 1:
                    S.I("act", "activation", out=qo[k2][:, :], in_=C.ps[b][:, :], func=AF.Copy,
                        reads=[("ps", b)], writes=[("Qqo", k2)])
                    S.dma("sp", qd[oc * 128:(oc + 1) * 128, t0:t0 + TT], qo[k2][:, :], reads=[("Qqo", k2)])
                    continue
                S.I("dve", "tensor_copy", xb[k2][:, :], C.ps[b][:, :],
                    reads=[("ps", b)], writes=[("Qxb", k2)])
                b2 = C.psM[C.psM_i % len(C.psM)]
                C.psM_i += 1
                S.I("pe", "matmul", C.ps[b2][:, :], prb[:, :], xb[k2][:, :], start=True, stop=True,
                    reads=[("Qxb", k2), ("Qprb", 0)], writes=[("ps", b2)])
                S.I("dve", "tensor_tensor", t1[k2][:, :], C.ps[b][:, :], cs[c2][:, :], op=ALU.mult,
                    reads=[("ps", b), ("Qcs", c2)], writes=[("Qt1", k2)])
                S.I("dve", "tensor_tensor", t2[k2][:, :], C.ps[b2][:, :], sn[c2][:, :], op=ALU.mult,
                    reads=[("ps", b2), ("Qsn", c2)], writes=[("Qt2", k2)])
                S.I("pool", "tensor_tensor", qo[k2][:, :], t1[k2][:, :], t2[k2][:, :], op=ALU.add,
                    reads=[("Qt1", k2), ("Qt2", k2)], writes=[("Qqo", k2)])
                if oc < 8:
                    S.dma("sp", qd[oc * 128:(oc + 1) * 128, t0:t0 + TT], qo[k2][:, :], reads=[("Qqo", k2)])
                else:
                    S.dma("sp", kd[(oc - 8) * 128:(oc - 7) * 128, t0:t0 + TT], qo[k2][:, :], reads=[("Qqo", k2)])
            for s in range(4):
                if C.dbg.get("qstop") == 2:
                    break
                b = C.psM[C.psM_i % len(C.psM)]
                C.psM_i += 1
                for kc in range(8):
                    S.I("pe", "matmul", C.ps[b][:, 0:256], hT[:, kc, s * 128:(s + 1) * 128], wb[:, kc, 1280:1536],
                        start=(kc == 0), stop=(kc == 7), reads=[("QhnT", tslot, kc)], writes=[("ps", b)])
                S.I("act", "activation", out=vo[s % 2][:, :], in_=C.ps[b][:, 0:256], func=AF.Copy,
                    reads=[("ps", b)], writes=[("Qvo", s % 2)])
                S.dma("sp", vd[t0 + s * 128:t0 + (s + 1) * 128, :], vo[s % 2][:, :], reads=[("Qvo", s % 2)])
        S.barrier()
        S.flush()


def phase_attn(C, x_src, qd, kd, vd, wo_ap, x_dst, masks_in):
    nc, S = C.nc, C.S
    W = C.W
    NB = T // 128
    with ExitStack() as es:
        wo = load_weight_bf16(C, es, wo_ap, D, D, "Two")
        sinkb = load_bcast_row(C, es, W["at_sink"], 16, "Tsink")
        mk = sb(es, nc, "Tmk", [128, 3, 384], F32)
        S.dma("sp", mk[:, :, :], masks_in.rearrange("m p s -> p m s"), writes=[("Tmk", 0)])
        qT = [sb(es, nc, "TqT%d" % i, [128, 8, 128], BF16) for i in range(2)]
        kT = [sb(es, nc, "TkT%d" % i, [128, 4, 384], BF16) for i in range(2)]
        vt = [sb(es, nc, "Tvt%d" % i, [128, 3, 256], BF16) for i in range(2)]
        vf = [sb(es, nc, "Tvf%d" % i, [128, 3, 256], F32) for i in range(2)]
        xt = [sb(es, nc, "Txt%d" % i, [128, D], F32) for i in range(2)]
        sm = [sb(es, nc, "Tsm%d" % i, [128, 384], F32) for i in range(4)]
        pb = [sb(es, nc, "Tpb%d" % i, [128, 384], BF16) for i in range(4)]
        pT = [sb(es, nc, "TpT%d" % i, [128, 3, 128], BF16) for i in range(4)]
        st = [sb(es, nc, "Tst%d" % i, [128, 8], F32) for i in range(4)]
        sbanks = [4, 5, 0, 1]
        tbanks = [2, 3]
        tbi = 0
        rdn = sb(es, nc, "Trdn", [128, 16], F32)
        ob = sb(es, nc, "Tob", [128, 16, 64], BF16)
        oT = sb(es, nc, "ToT", [128, 8, 128], BF16)
        hh = 0
        for n in range(NB):
            sl = n % 2
            kb0 = max(n - 1, 0)
            kb1 = min(n + 1, NB - 1)
            nk = kb1 - kb0 + 1
            mo = (kb0 - (n - 1)) * 128
            mi = 1 if n == NB // 2 - 1 else (2 if n == NB // 2 else 0)
            S.dma("sp", qT[sl][:, :, :], qd[:, n * 128:(n + 1) * 128].rearrange("(k p) t -> p k t", p=128),
                  writes=[("TqT", sl)])
            for kv in range(4):
                for dup in range(2):
                    S.dma("sp", kT[sl][dup * 64:(dup + 1) * 64, kv, 0:nk * 128],
                          kd[kv * 64:(kv + 1) * 64, kb0 * 128:(kb1 + 1) * 128], writes=[("TkT", sl)])
            S.dma("sp", vf[sl][:, 0:nk, :], vd[kb0 * 128:(kb1 + 1) * 128, :].rearrange("(k p) c -> p k c", p=128),
                  writes=[("Tvf", sl)])
            S.I("pool", "tensor_copy", vt[sl][:, 0:nk, :], vf[sl][:, 0:nk, :], reads=[("Tvf", sl)], writes=[("Tvt", sl)])
            S.dma("sp", xt[sl][:, :], x_src[n * 128:(n + 1) * 128, :], writes=[("Txt", sl)])
            bo = [6, 7]
            hp_ = []
            for h in range(16):
                h2 = hh % 4
                hh += 1
                bt = tbanks[tbi % 2]
                tbi += 1
                hp_.append((h2, sbanks[hh % 4], bt))

            def stA(h):
                kv = h // 4
                qc, hp = h // 2, h % 2
                h2, b, bt = hp_[h]
                smh, sth = sm[h2], st[h2]
                S.I("pe", "matmul", C.ps[b][:, 0:nk * 128], qT[sl][hp * 64:(hp + 1) * 64, qc, :],
                    kT[sl][hp * 64:(hp + 1) * 64, kv, 0:nk * 128], start=True, stop=True,
                    reads=[("TqT", sl), ("TkT", sl)], writes=[("ps", b)])
                S.I("dve", "scalar_tensor_tensor", out=smh[:, 0:nk * 128], in0=C.ps[b][:, 0:nk * 128], scalar=0.125,
                    in1=mk[:, mi, mo:mo + nk * 128], op0=ALU.mult, op1=ALU.add,
                    reads=[("ps", b), ("Tmk", 0)], writes=[("Tsm", h2)])
                S.I("dve", "reduce_max", sth[:, 0:1], smh[:, 0:nk * 128], axis=AX.X,
                    reads=[("Tsm", h2)], writes=[("Tst", h2, 0)])
                S.I("dve", "tensor_tensor", sth[:, 1:2], sth[:, 0:1], sinkb[:, h:h + 1], op=ALU.max,
                    reads=[("Tst", h2, 0), ("Tsink", 0)], writes=[("Tst", h2, 1)])
                S.I("dve", "tensor_scalar", sth[:, 2:3], sth[:, 1:2], -1.0, None, op0=ALU.mult,
                    reads=[("Tst", h2, 1)], writes=[("Tst", h2, 2)])

            def stB(h):
                h2, b, bt = hp_[h]
                smh, pbh, sth = sm[h2], pb[h2], st[h2]
                S.I("act", "activation", out=pbh[:, 0:nk * 128], in_=smh[:, 0:nk * 128], func=AF.Exp,
                    bias=sth[:, 2:3], accum_out=sth[:, 3:4],
                    reads=[("Tsm", h2), ("Tst", h2, 2)], writes=[("Tpb", h2), ("Tst", h2, 3)])
                S.I("act", "activation", out=sth[:, 4:5], in_=sinkb[:, h:h + 1], func=AF.Exp, bias=sth[:, 2:3],
                    reads=[("Tst", h2, 2), ("Tsink", 0)], writes=[("Tst", h2, 4)])
                S.I("dve", "tensor_tensor", sth[:, 5:6], sth[:, 3:4], sth[:, 4:5], op=ALU.add,
                    reads=[("Tst", h2, 3), ("Tst", h2, 4)], writes=[("Tst", h2, 5)])
                S.I("dve", "reciprocal", rdn[:, h:h + 1], sth[:, 5:6], reads=[("Tst", h2, 5)], writes=[("Trdn", h)])
                for kb in range(nk):
                    S.I("pe", "matmul", C.ps[bt][:, kb * 128:(kb + 1) * 128], pbh[:, kb * 128:(kb + 1) * 128],
                        C.ident[:, :], start=True, stop=True, reads=[("Tpb", h2)], writes=[("ps", bt)])

            def stC(h):
                kv = h // 4
                h2, b, bt = hp_[h]
                pTh = pT[h2]
                S.I("act", "activation", out=pTh[:, 0:nk, :],
                    in_=C.ps[bt][:, 0:nk * 128].rearrange("p (k c) -> p k c", k=nk), func=AF.Copy,
                    reads=[("ps", bt)], writes=[("TpT", h2)])
                bb = bo[h // 8]
                for kb in range(nk):
                    S.I("pe", "matmul", C.ps[bb][:, (h % 8) * 64:(h % 8 + 1) * 64], pTh[:, kb, :],
                        vt[sl][:, kb, kv * 64:(kv + 1) * 64], start=(kb == 0), stop=(kb == nk - 1),
                        reads=[("TpT", h2), ("Tvt", sl)], writes=[("ps", bb)])

            for i in range(18):
                if i < 16:
                    stA(i)
                if 0 <= i - 1 < 16:
                    stB(i - 1)
                if 0 <= i - 2 < 16:
                    stC(i - 2)
            for j in range(2):
                S.I("dve", "tensor_tensor", ob[:, j * 8:(j + 1) * 8, :],
                    C.ps[bo[j]][:, :].rearrange("p (h d) -> p h d", h=8),
                    rdn[:, j * 8:(j + 1) * 8].unsqueeze(2).broadcast_to([128, 8, 64]), op=ALU.mult,
                    reads=[("ps", bo[j])] + [("Trdn", hq) for hq in range(j * 8, j * 8 + 8)], writes=[("Tob", j)])
            for half in range(2):
                bt = tbanks[tbi % 2]
                tbi += 1
                for jj in range(4):
                    kc = half * 4 + jj
                    S.I("pe", "matmul", C.ps[bt][:, jj * 128:(jj + 1) * 128],
                        ob[:, 2 * kc:2 * kc + 2, :].rearrange("p h d -> p (h d)"),
                        C.ident[:, :], start=True, stop=True, reads=[("Tob", kc // 4)], writes=[("ps", bt)])
                S.I("act", "activation", out=oT[:, half * 4:(half + 1) * 4, :],
                    in_=C.ps[bt][:, :].rearrange("p (k c) -> p k c", k=4), func=AF.Copy,
                    reads=[("ps", bt)], writes=[("ToT", half)])
            for nh in range(2):
                b = sbanks[(hh + 1 + nh) % 4]
                for kc in range(8):
                    S.I("pe", "matmul", C.ps[b][:, :], oT[:, kc, :], wo[:, kc, nh * 512:(nh + 1) * 512],
                        start=(kc == 0), stop=(kc == 7), reads=[("ToT", kc // 4)], writes=[("ps", b)])
                S.I("dve", "tensor_tensor", xt[sl][:, nh * 512:(nh + 1) * 512], C.ps[b][:, :],
                    xt[sl][:, nh * 512:(nh + 1) * 512], op=ALU.add, reads=[("ps", b)], writes=[("Txt", sl)])
            S.dma("pool", x_dst[n * 128:(n + 1) * 128, :], xt[sl][:, :], reads=[("Txt", sl)])
        S.barrier()
        S.flush()


def phase_mlp(C, x_src, g_row, wup_ap, wdn_ap, x_dst, tag, final_g_row=None):
    nc, S = C.nc, C.S
    DFF = 4096
    TK = 256
    with ExitStack() as es:
        gcol = load_bcast_row(C, es, g_row, D, tag + "g")
        gfin = load_bcast_row(C, es, final_g_row, D, tag + "gf") if final_g_row is not None else None
        wup = load_weight_bf16(C, es, wup_ap, D, DFF, tag + "wu")
        wdn = load_weight_bf16(C, es, wdn_ap, DFF, D, tag + "wd")
        N = alloc_norm_tiles(C, es, tag, TK, nx=2, nT=2)
        hT = sb(es, nc, tag + "hT", [128, 32, TK], BF16)
        rl = [sb(es, nc, tag + "rl%d" % i, [128, TK], F32) for i in range(2)]
        ss2 = sb(es, nc, tag + "ss2", [128, 2], F32)
        rs2 = sb(es, nc, tag + "rs2", [128, 2], F32)
        norm_load(C, N, x_src, 0)
        nxt = norm_tile(C, N, gcol, 0)
        for ti in range(T // TK):
            if ti + 1 < T // TK:
                norm_load(C, N, x_src, ti + 1)
            xs, hTn, slot, tslot = nxt
            for fc in range(32):
                b = C.psM[C.psM_i % len(C.psM)]
                C.psM_i += 1
                for kc in range(8):
                    S.I("pe", "matmul", C.ps[b][:, 0:TK], wup[:, kc, fc * 128:(fc + 1) * 128], hTn[:, kc, :],
                        start=(kc == 0), stop=(kc == 7),
                        reads=[(tag + "hnT", tslot, kc)], writes=[("ps", b)])
                j = fc % 2
                if j == 0:
                    S.I("act", "activation", out=rl[0][:, :], in_=C.ps[b][:, 0:TK], func=AF.Relu,
                        reads=[("ps", b)], writes=[(tag + "rl", 0)])
                else:
                    S.I("dve", "tensor_scalar", rl[1][:, :], C.ps[b][:, 0:TK], 0.0, None, op0=ALU.max,
                        reads=[("ps", b)], writes=[(tag + "rl", 1)])
                S.I("pool", "tensor_tensor", hT[:, fc, :], rl[j][:, :], rl[j][:, :], op=ALU.mult,
                    reads=[(tag + "rl", j)], writes=[(tag + "hT", fc)])
            if ti + 1 < T // TK:
                nxt = norm_tile(C, N, gcol, ti + 1)
            for s in range(N.NS):
                xr = (tag + "xt", slot)
                for nh in range(2):
                    b = C.psM[C.psM_i % len(C.psM)]
                    C.psM_i += 1
                    for fc in range(32):
                        S.I("pe", "matmul", C.ps[b][:, :], hT[:, fc, s * 128:(s + 1) * 128],
                            wdn[:, fc, nh * 512:(nh + 1) * 512], start=(fc == 0), stop=(fc == 31),
                            reads=[(tag + "hT", fc)], writes=[("ps", b)])
                    S.I("dve", "tensor_tensor", xs[:, s, nh * 512:(nh + 1) * 512], C.ps[b][:, :],
                        xs[:, s, nh * 512:(nh + 1) * 512], op=ALU.add,
                        reads=[("ps", b)], writes=[xr])
                r0 = ti * TK + s * 128
                if gfin is not None:
                    S.I("pool", "memset", ss2[:, 0:1], 0.0, writes=[(tag + "ss2", 0)])
                    S.I("act", "activation", out=N.junk[:, :], in_=xs[:, s, :], func=AF.Square,
                        accum_out=ss2[:, 0:1], reads=[xr], writes=[(tag + "ss2", 0), (tag + "junk", 0)])
                    S.I("act", "activation", out=rs2[:, 0:1], in_=ss2[:, 0:1], func=AF.Ln, scale=1.0 / D,
                        bias=C.eps5[:, 0:1], reads=[(tag + "ss2", 0)], writes=[(tag + "rs2", 0)])
                    S.I("act", "activation", out=rs2[:, 0:1], in_=rs2[:, 0:1], func=AF.Exp, scale=-0.5,
                        reads=[(tag + "rs2", 0)], writes=[(tag + "rs2", 0)])
                    S.I("dve", "scalar_tensor_tensor", out=xs[:, s, :], in0=xs[:, s, :], scalar=rs2[:, 0:1],
                        in1=gfin[:, :], op0=ALU.mult, op1=ALU.mult,
                        reads=[(tag + "rs2", 0)], writes=[xr])
                S.dma("sp", x_dst[r0:r0 + 128, :], xs[:, s, :], reads=[xr])
        S.barrier()
        S.flush()


WEIGHT_SPECS = [
    ("norm_mix", (2, 1024)), ("norm_mlp", (2, 1024)), ("norm_final", (1, 1024)),
    ("ab_w_in", (1024, 2560)), ("ab_w_out", (1024, 1024)),
    ("cv_dw_w", (31, 512)), ("cv_dw_b", (1, 512)), ("cv_ln_g", (1, 512)), ("cv_ln_b", (1, 512)),
    ("hy_short_w", (3, 1536)), ("hy_short_b", (1, 1536)),
    ("hy_w1", (33, 64)), ("hy_b1", (1, 64)), ("hy_w2", (64, 64)), ("hy_b2", (1, 64)),
    ("hy_w3", (64, 64)), ("hy_b3", (1, 64)), ("hy_w4", (64, 2048)),
    ("hy_freq", (3, 64)), ("hy_decay", (1, 2048)), ("hy_skip", (2, 512)),
    ("at_w_qkv", (1024, 1536)), ("at_sink", (1, 16)), ("at_w_o", (1024, 1024)),
    ("mlp_w_up", (2, 1024, 4096)), ("mlp_w_down", (2, 4096, 1024)),
]


def build_program(dbg=None):
    nc = bass.Bass("TRN2", target_bir_lowering=False)
    C = Ctx()
    C.nc = nc
    C.dbg = dbg or {}
    W = {}
    x_in = nc.dram_tensor("x", [T, D], F32, kind="ExternalInput").ap()
    for name, shp in WEIGHT_SPECS:
        W[name] = nc.dram_tensor(name, list(shp), F32, kind="ExternalInput").ap()
    ident_in = nc.dram_tensor("ident", [128, 128], F32, kind="ExternalInput").ap()
    flag_in = nc.dram_tensor("flag", [1, 8], F32, kind="ExternalInput").ap()
    hc128_np, hc65_np = hyena_consts()
    hc128_in = nc.dram_tensor("hc128", list(hc128_np.shape), F32, kind="ExternalInput").ap()
    hc65_in = nc.dram_tensor("hc65", list(hc65_np.shape), F32, kind="ExternalInput").ap()
    zT_in = nc.dram_tensor("zT", [33, 8192], F32, kind="ExternalInput").ap()
    tneg_in = nc.dram_tensor("tneg", [128, 128], F32, kind="ExternalInput").ap()
    cos_in = nc.dram_tensor("ropecos", [128, T], F32, kind="ExternalInput").ap()
    sin_in = nc.dram_tensor("ropesin", [128, T], F32, kind="ExternalInput").ap()
    prot_in = nc.dram_tensor("prot", [128, 128], F32, kind="ExternalInput").ap()
    masks_in = nc.dram_tensor("amasks", [3, 128, 384], F32, kind="ExternalInput").ap()
    y_out = nc.dram_tensor("y", [T, D], F32, kind="ExternalOutput").ap()
    C.W = W

    def scratch(name, shape, dt=F32):
        kind = "ExternalOutput" if name in C.dbg.get("out", ()) else "Internal"
        return nc.dram_tensor(name, list(shape), dt, kind=kind).ap()

    u = scratch("u", [2560, T])
    ya = scratch("ya", [512, T], BF16)
    uh = scratch("uh", [T, 1536])
    Gd = scratch("Gd", [2, 4, 3, 64, 2, 128, 65])
    yb = scratch("yb", [T, 512])
    x1 = scratch("x1", [T, D])
    x2 = scratch("x2", [T, D])
    x3 = scratch("x3", [T, D])
    qd = scratch("qd", [1024, T], BF16)
    kd = scratch("kd", [256, T], BF16)
    vd = scratch("vd", [T, 256])
    stop = C.dbg.get("stop")

    with ExitStack() as es:
        S = Sched(nc, es)
        C.S = S
        C.ps = [es.enter_context(nc.psum_tensor("ps%d" % i, [128, 512], F32)) for i in range(8)]
        C.psT = [0, 1, 2, 3]
        C.psM = [4, 5, 6, 7]
        C.psT_i = 0
        C.psM_i = 0
        C.ident = sb(es, nc, "ident_b", [128, 128], BF16)
        C.identf = sb(es, nc, "ident_f", [128, 128], F32)
        C.eps5 = sb(es, nc, "eps5", [128, 1], F32)
        S.I("pool", "memset", C.eps5[:, :], 1e-5)
        S.dma("sp", C.identf[:, :], ident_in[:, :], writes=[("identf", 0)])
        S.I("dve", "tensor_copy", C.ident[:, :], C.identf[:, :], reads=[("identf", 0)], writes=[("ident", 0)])
        C.onesf = sb(es, nc, "onesf", [128, 128], F32)
        S.I("pool", "memset", C.onesf[:, :], 1.0)
        C.flagcol = sb(es, nc, "flagcol", [128, 8], F32)
        S.dma("sp", C.flagcol[:, :], flag_in.partition_broadcast(128))
        C.hc128 = sb(es, nc, "hc128_t", list(hc128_np.shape), F32)
        C.hc65 = sb(es, nc, "hc65_t", list(hc65_np.shape), F32)
        S.dma("sp", C.hc128[:, :], hc128_in)
        S.dma("sp", C.hc65[:, :], hc65_in)
        S.barrier()
        S.flush()

        skip = C.dbg.get("skip", "")
        if stop != "M" and "A" not in skip:
            phase_inproj(C, x_in, W["norm_mix"][0:1, :], W["ab_w_in"], 2560, u, "A")
        if stop == "A":
            return nc
        if "B" not in skip:
            phase_conformer(C, u, ya)
        if stop == "B":
            return nc
        if "S" not in skip:
            phase_shortconv(C, u, uh)
        if stop == "S":
            return nc
        if "F" not in skip:
            phase_filters(C, Gd, zT_in, tneg_in)
        if stop == "F":
            return nc
        if "H" not in skip:
            phase_hyena(C, uh, Gd, yb)
        if stop == "H":
            return nc
        if "O" not in skip:
            phase_outproj(C, x_in, ya, yb, W["ab_w_out"], x1)
        if stop == "O":
            return nc
        if "M0" not in skip:
            phase_mlp(C, x1, W["norm_mlp"][0:1, :], W["mlp_w_up"][0], W["mlp_w_down"][0], x2, "M0")
        if stop == "M0":
            return nc
        xa = x_in if "X" in skip else x2
        phase_qkv(C, xa, W["norm_mix"][1:2, :], W["at_w_qkv"], qd, kd, vd, cos_in, sin_in, prot_in)
        if stop == "Q":
            return nc
        phase_attn(C, xa, qd, kd, vd, W["at_w_o"], x3, masks_in)
        if stop == "T":
            return nc
        phase_mlp(C, x3, W["norm_mlp"][1:2, :], W["mlp_w_up"][1], W["mlp_w_down"][1], y_out, "M1",
                  final_g_row=W["norm_final"][0:1, :])
        return nc
        phase_mlp(C, x_in, W["norm_mlp"][0:1, :], W["mlp_w_up"][0], W["mlp_w_down"][0], y_out, "M0",
                  final_g_row=W["norm_final"][0:1, :])
    return nc


_CACHE = {}


def make_in_maps(inputs):
    xp = np.asarray(inputs["x_prompt"], np.float32)
    xs = np.asarray(inputs["x_sample"], np.float32)
    base = {}
    for name, shp in WEIGHT_SPECS:
        base[name] = np.ascontiguousarray(np.asarray(inputs[name], np.float32)).reshape(shp)
    base["ident"] = np.eye(128, dtype=np.float32)
    base["hc128"], base["hc65"] = hyena_consts()
    ptab = {True: hyena_pos_tables(True), False: hyena_pos_tables(False)}
    atab = {True: attn_consts(True), False: attn_consts(False)}
    maps = []
    for c in range(NCORES):
        m = dict(base)
        m["flag"] = np.full((1, 8), 1.0 if c < 4 else 0.0, np.float32)
        m["zT"], m["tneg"] = ptab[c < 4]
        m["ropecos"], m["ropesin"], m["prot"], m["amasks"] = atab[c < 4]
        if c < 4:
            m["x"] = np.ascontiguousarray(xp[c])
        else:
            m["x"] = np.ascontiguousarray(xs[2 * (c - 4):2 * (c - 4) + 2].reshape(T, D))
        maps.append(m)
    return maps


def kernel(**inputs):
    if "nc" not in _CACHE:
        _CACHE["nc"] = build_program()
    nc = _CACHE["nc"]
    maps = make_in_maps(inputs)
    res = run_bass_kernel_spmd(nc, maps, core_ids=list(range(NCORES)))
    ys = [np.asarray(r["y"], np.float32) for r in res.results]
    y_prompt = np.stack(ys[0:4], axis=0)
    y_sample = np.concatenate([y.reshape(2, 4096, D) for y in ys[4:8]], axis=0)
    return (y_prompt, y_sample)
```

```python
import math
from contextlib import ExitStack

import numpy as np
import concourse.bass as bass
import concourse.mybir as mybir
from concourse.bass_utils import run_bass_kernel_spmd

F32 = mybir.dt.float32
BF16 = mybir.dt.bfloat16
AF = mybir.ActivationFunctionType
ALU = mybir.AluOpType
AX = mybir.AxisListType

NCORES = 8
T = 8192
D = 1024
TT = 512
NT = T // TT
ENGS = ("pe", "act", "dve", "pool", "sp")


class Op:
    __slots__ = ("eng", "fn", "dma", "deps", "sig", "count", "sem", "semval")

    def __init__(self, eng, fn, dma):
        self.eng = eng
        self.fn = fn
        self.dma = dma
        self.deps = set()
        self.sig = False
        self.count = 0
        self.sem = None
        self.semval = 0


class Sched:
    NDMA_SEM = 12

    def __init__(self, nc, es):
        self.nc = nc
        self.streams = {k: [] for k in ENGS}
        self.w = {}
        self.r = {}
        self.dma_rr = {k: 0 for k in ENGS}
        self.dma_last = {}
        self.dma_cnt = {}
        self.esem = {k: es.enter_context(nc.semaphore("e_" + k)) for k in ENGS}
        self.dsem = {}
        for k in ("sp", "act", "pool"):
            for i in range(self.NDMA_SEM):
                self.dsem[(k, i)] = es.enter_context(nc.semaphore("d_%s%d" % (k, i)))
        self.ecount = {k: 0 for k in ENGS}
        self.seen = {k: {} for k in ENGS}
        self.lastc = {}

    def op(self, eng, fn, reads=(), writes=(), dma=False, deps=()):
        o = Op(eng, fn, dma)
        for d in deps:
            if d is not None:
                o.deps.add(d)
        for r in reads:
            lw = self.w.get(r)
            if lw is not None:
                o.deps.add(lw)
        for w_ in writes:
            lw = self.w.get(w_)
            if lw is not None:
                o.deps.add(lw)
            for rd in self.r.get(w_, ()):
                o.deps.add(rd)
        for r in reads:
            self.r.setdefault(r, []).append(o)
        for w_ in writes:
            self.w[w_] = o
            self.r[w_] = []
        if dma:
            i = self.dma_rr[eng]
            self.dma_rr[eng] = (i + 1) % self.NDMA_SEM
            key = (eng, i)
            prev = self.dma_last.get(key)
            if prev is not None:
                o.deps.add(prev)
            self.dma_last[key] = o
            self.dma_cnt[key] = self.dma_cnt.get(key, 0) + 1
            o.sem = key
            o.semval = 16 * self.dma_cnt[key]
        else:
            self.lastc[eng] = o
        o.deps.discard(o)
        self.streams[eng].append(o)
        return o

    def I(self, eng, name, *args, reads=(), writes=(), deps=(), **kw):
        return self.op(eng, lambda e: getattr(e, name)(*args, **kw), reads, writes, deps=deps)

    def dma(self, eng, out, in_, reads=(), writes=(), deps=(), **kw):
        return self.op(eng, lambda e: e.dma_start(out=out, in_=in_, **kw), reads, writes, dma=True, deps=deps)

    def barrier(self):
        dmas = list(self.dma_last.values())
        lastc = dict(self.lastc)
        for k in ENGS:
            o = Op(k, None, False)
            for kk, lo in lastc.items():
                if kk != k:
                    o.deps.add(lo)
            for d in dmas:
                o.deps.add(d)
            self.streams[k].append(o)
        self.w = {}
        self.r = {}

    def flush(self):
        nc = self.nc
        for k in ENGS:
            for o in self.streams[k]:
                for d in o.deps:
                    if not d.dma and (d.eng != o.eng or o.dma or o.eng != "pe"):
                        d.sig = True
        for k in ENGS:
            for o in self.streams[k]:
                if o.sig and o.count == 0:
                    self.ecount[k] += 1
                    o.count = self.ecount[k]
        streams = self.streams
        self.streams = {k: [] for k in ENGS}
        with nc.Block() as block:
            def run(k, e):
                seen = self.seen[k]
                for o in streams[k]:
                    need = {}
                    for d in o.deps:
                        if d.dma:
                            s, v = self.dsem[d.sem], d.semval
                        elif d.eng == k and not o.dma and k == "pe":
                            continue
                        else:
                            assert d.count > 0
                            s, v = self.esem[d.eng], d.count
                        if need.get(id(s), (None, 0))[1] < v:
                            need[id(s)] = (s, v)
                    for s, v in need.values():
                        if seen.get(id(s), 0) < v:
                            e.wait_ge(s, v)
                            seen[id(s)] = v
                    if o.fn is None:
                        continue
                    ins = o.fn(e)
                    if o.dma:
                        ins.then_inc(self.dsem[o.sem], 16)
                    elif o.sig:
                        ins.then_inc(self.esem[k], 1)

            @block.tensor
            def _(e):
                run("pe", e)

            @block.scalar
            def _(e):
                run("act", e)

            @block.vector
            def _(e):
                run("dve", e)

            @block.gpsimd
            def _(e):
                run("pool", e)

            @block.sync
            def _(e):
                run("sp", e)


class Ctx:
    pass


def sb(es, nc, name, shape, dt):
    return es.enter_context(nc.sbuf_tensor(name, list(shape), dt))


def load_weight_bf16(C, es, w_ap, K, N, name, eng_cast=("pool", "act")):
    nc, S = C.nc, C.S
    KC = K // 128
    wb = sb(es, nc, name, [128, KC, N], BF16)
    CW = min(N, 2048)
    with ExitStack() as es2:
        st = [sb(es2, nc, name + "_st%d" % i, [128, CW], F32) for i in range(2)]
        j = 0
        for kc in range(KC):
            for c0 in range(0, N, CW):
                cw = min(CW, N - c0)
                s = st[j % 2]
                S.dma("sp", s[:, 0:cw], w_ap[kc * 128:(kc + 1) * 128, c0:c0 + cw],
                      writes=[(name + "st", j % 2)])
                eng = eng_cast[j % len(eng_cast)]
                if eng == "act":
                    S.I("act", "activation", out=wb[:, kc, c0:c0 + cw], in_=s[:, 0:cw], func=AF.Copy,
                        reads=[(name + "st", j % 2)], writes=[(name, kc)])
                else:
                    S.I(eng, "tensor_copy", wb[:, kc, c0:c0 + cw], s[:, 0:cw],
                        reads=[(name + "st", j % 2)], writes=[(name, kc)])
                j += 1
        S.barrier()
        S.flush()
    return wb


def alloc_norm_tiles(C, es, tag, TK, nx=2, nT=2):
    nc = C.nc
    NS = TK // 128
    N = Ctx()
    N.TK, N.NS, N.tag = TK, NS, tag
    N.xt = [sb(es, nc, tag + "xt%d" % i, [128, NS, D], F32) for i in range(nx)]
    N.hn = sb(es, nc, tag + "hn", [128, NS, D], BF16)
    N.hnT = [sb(es, nc, tag + "hnT%d" % i, [128, 8, TK], BF16) for i in range(nT)]
    N.ss = [sb(es, nc, tag + "ss%d" % i, [128, NS], F32) for i in range(nx)]
    N.rstd = [sb(es, nc, tag + "rstd%d" % i, [128, NS], F32) for i in range(nx)]
    N.junk = sb(es, nc, tag + "junk", [128, D], BF16)
    return N


def norm_load(C, N, x_src, ti):
    S = C.S
    slot = ti % len(N.xt)
    S.dma("sp", N.xt[slot][:, :, :],
          x_src[ti * N.TK:(ti + 1) * N.TK, :].rearrange("(s p) d -> p s d", p=128),
          writes=[(N.tag + "xt", slot)])


def norm_tile(C, N, gcol, ti):
    S = C.S
    tag = N.tag
    slot = ti % len(N.xt)
    tslot = ti % len(N.hnT)
    xs = N.xt[slot]
    ss, rstd = N.ss[slot], N.rstd[slot]
    S.I("pool", "memset", ss[:, :], 0.0, writes=[(tag + "ss", slot)])
    for s in range(N.NS):
        S.I("act", "activation", out=N.junk[:, :], in_=xs[:, s, :], func=AF.Square,
            accum_out=ss[:, s:s + 1],
            reads=[(tag + "xt", slot)], writes=[(tag + "ss", slot), (tag + "junk", 0)])
    S.I("act", "activation", out=rstd[:, :], in_=ss[:, :], func=AF.Ln, scale=1.0 / D, bias=C.eps5[:, 0:1],
        reads=[(tag + "ss", slot)], writes=[(tag + "rstd", slot)])
    S.I("act", "activation", out=rstd[:, :], in_=rstd[:, :], func=AF.Exp, scale=-0.5,
        reads=[(tag + "rstd", slot)], writes=[(tag + "rstd", slot)])
    for s in range(N.NS):
        S.I("dve", "scalar_tensor_tensor", out=N.hn[:, s, :], in0=xs[:, s, :], scalar=rstd[:, s:s + 1],
            in1=gcol[:, :], op0=ALU.mult, op1=ALU.mult,
            reads=[(tag + "xt", slot), (tag + "rstd", slot)], writes=[(tag + "hn", s)])
    hT = N.hnT[tslot]
    for kc in range(8):
        b = C.psT[C.psT_i % len(C.psT)]
        C.psT_i += 1
        for s in range(N.NS):
            S.I("pe", "matmul", C.ps[b][:, s * 128:(s + 1) * 128], N.hn[:, s, kc * 128:(kc + 1) * 128],
                C.ident[:, :], start=True, stop=True,
                reads=[(tag + "hn", s)], writes=[("ps", b)])
        if kc % 2 == 0:
            S.I("act", "activation", out=hT[:, kc, :], in_=C.ps[b][:, 0:N.TK], func=AF.Copy,
                reads=[("ps", b)], writes=[(tag + "hnT", tslot, kc)])
        else:
            S.I("dve", "tensor_copy", hT[:, kc, :], C.ps[b][:, 0:N.TK],
                reads=[("ps", b)], writes=[(tag + "hnT", tslot, kc)])
    return xs, hT, slot, tslot


def load_bcast_row(C, es, row_ap, n, name):
    t = sb(es, C.nc, name, [128, n], F32)
    C.S.dma("sp", t[:, :], row_ap.partition_broadcast(128), writes=[(name, 0)])
    return t


def phase_inproj(C, x_src, g_row, w_ap, NOUT, u_dst, tag):
    nc, S = C.nc, C.S
    with ExitStack() as es:
        gcol = load_bcast_row(C, es, g_row, D, tag + "g")
        wb = load_weight_bf16(C, es, w_ap, D, NOUT, tag + "w")
        N = alloc_norm_tiles(C, es, tag, TT)
        ost = [sb(es, nc, tag + "ost%d" % i, [128, TT], F32) for i in range(4)]
        oi = 0
        norm_load(C, N, x_src, 0)
        nxt = norm_tile(C, N, gcol, 0)
        for ti in range(NT):
            if ti + 1 < NT:
                norm_load(C, N, x_src, ti + 1)
            xs, hT, slot, tslot = nxt
            for oc in range(NOUT // 128):
                if oc == (NOUT // 128) // 2 and ti + 1 < NT:
                    nxt = norm_tile(C, N, gcol, ti + 1)
                b = C.psM[C.psM_i % len(C.psM)]
                C.psM_i += 1
                for kc in range(8):
                    S.I("pe", "matmul", C.ps[b][:, :], wb[:, kc, oc * 128:(oc + 1) * 128], hT[:, kc, :],
                        start=(kc == 0), stop=(kc == 7),
                        reads=[(tag + "hnT", tslot, kc)], writes=[("ps", b)])
                o = oi % 4
                oi += 1
                if oc % 2 == 0:
                    S.I("act", "activation", out=ost[o][:, :], in_=C.ps[b][:, :], func=AF.Copy,
                        reads=[("ps", b)], writes=[(tag + "ost", o)])
                else:
                    S.I("dve", "tensor_copy", ost[o][:, :], C.ps[b][:, :],
                        reads=[("ps", b)], writes=[(tag + "ost", o)])
                S.dma("sp", u_dst[oc * 128:(oc + 1) * 128, ti * TT:(ti + 1) * TT], ost[o][:, :],
                      reads=[(tag + "ost", o)])
        S.barrier()
        S.flush()


def colvec(C, es, ap, R, NCOL, name):
    nc, S = C.nc, C.S
    NCH = NCOL // 128
    out = sb(es, nc, name, [128, NCH, R], F32)
    with ExitStack() as es2:
        rows = sb(es2, nc, name + "_rows", [R, NCOL], F32)
        S.dma("sp", rows[:, :], ap, writes=[(name + "rows", 0)])
        for c in range(NCH):
            b = C.psT[C.psT_i % len(C.psT)]
            C.psT_i += 1
            S.I("pe", "matmul", C.ps[b][:, 0:R], rows[0:R, c * 128:(c + 1) * 128], C.identf[0:R, 0:R],
                start=True, stop=True, reads=[(name + "rows", 0)], writes=[("ps", b)])
            S.I("dve", "tensor_copy", out[:, c, :], C.ps[b][:, 0:R], reads=[("ps", b)], writes=[(name, c)])
        S.barrier()
        S.flush()
    return out


def phase_conformer(C, u, ya_dst):
    nc, S = C.nc, C.S
    W = C.W
    tag = "B"
    HW = 15
    with ExitStack() as es:
        wcol = colvec(C, es, W["cv_dw_w"], 31, 512, "Bw")
        bcol = colvec(C, es, W["cv_dw_b"], 1, 512, "Bb")
        gcol = colvec(C, es, W["cv_ln_g"], 1, 512, "Bg")
        becol = colvec(C, es, W["cv_ln_b"], 1, 512, "Bbe")
        at = [sb(es, nc, "Bat%d" % i, [128, TT + 2 * HW], F32) for i in range(4)]
        gt = [sb(es, nc, "Bgt%d" % i, [128, TT + 2 * HW], F32) for i in range(4)]
        ht = [sb(es, nc, "Bht%d" % i, [128, TT + 2 * HW], F32) for i in range(4)]
        cv = [sb(es, nc, "Bcv%d" % i, [128, TT], F32) for i in range(4)]
        sq = [sb(es, nc, "Bsq%d" % i, [128, TT], F32) for i in range(2)]
        mean = sb(es, nc, "Bmean", [128, TT], F32)
        msq = sb(es, nc, "Bmsq", [128, TT], F32)
        rstd = sb(es, nc, "Brstd", [128, TT], F32)
        t1 = [sb(es, nc, "Bt1%d" % i, [128, TT], F32) for i in range(2)]
        yo = [sb(es, nc, "Byo%d" % i, [128, TT], BF16) for i in range(2)]
        for i in range(4):
            S.I("pool", "memset", at[i][:, :], 0.0, writes=[("Bat", i)])
            S.I("pool", "memset", gt[i][:, :], 0.0, writes=[("Bgt", i)])
        for ti in range(NT):
            t0 = ti * TT
            lo = max(t0 - HW, 0)
            hi = min(t0 + TT + HW, T)
            c0 = lo - (t0 - HW)
            c1 = c0 + (hi - lo)
            for c in range(4):
                if ti == NT - 1:
                    S.I("pool", "memset", at[c][:, c1:], 0.0, writes=[("Bat", c)])
                    S.I("pool", "memset", gt[c][:, c1:], 0.0, writes=[("Bgt", c)])
                S.dma("sp", at[c][:, c0:c1], u[c * 128:(c + 1) * 128, lo:hi], writes=[("Bat", c)])
                S.dma("sp", gt[c][:, c0:c1], u[512 + c * 128:512 + (c + 1) * 128, lo:hi], writes=[("Bgt", c)])
            for c in range(4):
                S.I("act", "activation", out=gt[c][:, :], in_=gt[c][:, :], func=AF.Sigmoid,
                    reads=[("Bgt", c)], writes=[("Bgt", c)])
                eng = "dve" if c < 2 else "pool"
                S.I(eng, "tensor_tensor", ht[c][:, :], at[c][:, :], gt[c][:, :], op=ALU.mult,
                    reads=[("Bat", c), ("Bgt", c)], writes=[("Bht", c)])
                if ti == NT // 2 - 1:
                    S.I(eng, "tensor_scalar", ht[c][:, TT + HW:], ht[c][:, TT + HW:], C.flagcol[:, 0:1], None,
                        op0=ALU.mult, reads=[("Bht", c)], writes=[("Bht", c)])
                if ti == NT // 2:
                    S.I(eng, "tensor_scalar", ht[c][:, 0:HW], ht[c][:, 0:HW], C.flagcol[:, 0:1], None,
                        op0=ALU.mult, reads=[("Bht", c)], writes=[("Bht", c)])
            for j in range(31):
                for c in range(4):
                    eng = "dve"
                    if j == 0:
                        S.I(eng, "tensor_scalar", cv[c][:, :], ht[c][:, 0:TT], wcol[:, c, 0:1], bcol[:, c, 0:1],
                            op0=ALU.mult, op1=ALU.add, reads=[("Bht", c)], writes=[("Bcv", c)])
                    else:
                        S.I(eng, "scalar_tensor_tensor", out=cv[c][:, :], in0=ht[c][:, j:j + TT],
                            scalar=wcol[:, c, j:j + 1], in1=cv[c][:, :], op0=ALU.mult, op1=ALU.add,
                            reads=[("Bht", c)], writes=[("Bcv", c)])
            b1 = C.psM[C.psM_i % len(C.psM)]
            C.psM_i += 1
            b2 = C.psM[C.psM_i % len(C.psM)]
            C.psM_i += 1
            for c in range(4):
                S.I("pe", "matmul", C.ps[b1][:, :], C.onesf[:, :], cv[c][:, :], start=(c == 0), stop=(c == 3),
                    reads=[("Bcv", c)], writes=[("ps", b1)])
            for c in range(4):
                S.I("act", "activation", out=sq[c % 2][:, :], in_=cv[c][:, :], func=AF.Square,
                    reads=[("Bcv", c)], writes=[("Bsq", c % 2)])
                S.I("pe", "matmul", C.ps[b2][:, :], C.onesf[:, :], sq[c % 2][:, :], start=(c == 0), stop=(c == 3),
                    reads=[("Bsq", c % 2)], writes=[("ps", b2)])
            S.I("act", "activation", out=mean[:, :], in_=C.ps[b1][:, :], func=AF.Copy, scale=1.0 / 512,
                reads=[("ps", b1)], writes=[("Bmean", 0)])
            S.I("dve", "tensor_tensor", msq[:, :], mean[:, :], mean[:, :], op=ALU.mult,
                reads=[("Bmean", 0)], writes=[("Bmsq", 0)])
            S.I("dve", "scalar_tensor_tensor", out=rstd[:, :], in0=C.ps[b2][:, :], scalar=1.0 / 512, in1=msq[:, :],
                op0=ALU.mult, op1=ALU.subtract, reads=[("ps", b2), ("Bmsq", 0)], writes=[("Brstd", 0)])
            S.I("act", "activation", out=rstd[:, :], in_=rstd[:, :], func=AF.Ln, bias=C.eps5[:, 0:1],
                reads=[("Brstd", 0)], writes=[("Brstd", 0)])
            S.I("act", "activation", out=rstd[:, :], in_=rstd[:, :], func=AF.Exp, scale=-0.5,
                reads=[("Brstd", 0)], writes=[("Brstd", 0)])
            for c in range(4):
                k = c % 2
                S.I("dve", "tensor_tensor", t1[k][:, :], cv[c][:, :], mean[:, :], op=ALU.subtract,
                    reads=[("Bcv", c), ("Bmean", 0)], writes=[("Bt1", k)])
                S.I("pool", "tensor_tensor", t1[k][:, :], t1[k][:, :], rstd[:, :], op=ALU.mult,
                    reads=[("Bt1", k), ("Brstd", 0)], writes=[("Bt1", k)])
                S.I("act", "activation", out=yo[k][:, :], in_=t1[k][:, :], func=AF.Silu,
                    scale=gcol[:, c, 0:1], bias=becol[:, c, 0:1],
                    reads=[("Bt1", k)], writes=[("Byo", k)])
                S.dma("act", ya_dst[c * 128:(c + 1) * 128, t0:t0 + TT], yo[k][:, :], reads=[("Byo", k)])
        S.barrier()
        S.flush()


HC128_COLS = {}
HC65_COLS = {}


def hyena_consts():
    p = np.arange(128)
    n1 = (p % 64)[:, None].astype(np.float64)
    k1 = np.arange(65)[None, :].astype(np.float64)
    th = 2 * np.pi * n1 * k1 / 128.0
    F1cat = np.concatenate([np.cos(th), -np.sin(th)], 1)
    n2 = (p % 64)[:, None].astype(np.float64)
    ph = 2 * np.pi * n2 * k1 / 8192.0
    Tc2 = np.concatenate([np.cos(ph), np.cos(ph)], 1)
    Ts2 = np.concatenate([np.sin(ph), np.sin(ph)], 1)
    q64 = np.arange(64)
    psi = 2 * np.pi * (p % 64)[:, None] * q64[None, :] / 64.0
    Mc = np.cos(psi)
    Ms = np.sin(psi)
    rhsA = np.concatenate([Mc, Ms], 1)
    rhsB = np.concatenate([-Ms, Mc], 1)
    sg = ((-1.0) ** np.arange(65))[None, :].repeat(128, 0)
    sgn2 = np.concatenate([sg, sg], 1)
    parts = [("F1cat", F1cat), ("Tc2", Tc2), ("Ts2", Ts2), ("Mc", Mc), ("Ms", Ms), ("nMs", -Ms),
             ("rhsA", rhsA), ("rhsB", rhsB), ("sgn2", sgn2)]
    off = 0
    for nm, a in parts:
        HC128_COLS[nm] = (off, a.shape[1])
        off += a.shape[1]
    hc128 = np.concatenate([a for _, a in parts], 1).astype(np.float32)
    kk = np.arange(65)[:, None].astype(np.float64)
    col = np.arange(128)[None, :]
    phi = 2 * np.pi * kk * (col % 64) / 8192.0
    Tci = np.cos(phi)
    Tsi = np.sin(phi)
    wk = np.full((65, 1), 2.0)
    wk[0, 0] = 1.0
    wk[64, 0] = 1.0
    n1c = np.arange(64)[None, :].astype(np.float64)
    thi = 2 * np.pi * kk * n1c / 128.0
    gc = wk * np.cos(thi) / 8192.0
    gs = -wk * np.sin(thi) / 8192.0
    zz = np.zeros((65, 64))
    parts = [("Tci", Tci), ("Tsi", Tsi), ("Gc_a", np.concatenate([gc, zz], 1)), ("Gs_a", np.concatenate([gs, zz], 1)),
             ("Gc_b", np.concatenate([zz, gc], 1)), ("Gs_b", np.concatenate([zz, gs], 1))]
    off = 0
    for nm, a in parts:
        HC65_COLS[nm] = (off, a.shape[1])
        off += a.shape[1]
    hc65 = np.concatenate([a for _, a in parts], 1).astype(np.float32)
    return hc128, hc65


def hyena_pos_tables(is_prompt):
    L = 8192 if is_prompt else 4096
    q = np.arange(8192)
    n2 = q // 128
    hp = q % 128
    half = hp // 64
    n1 = hp % 64
    pl = n1 * 64 + n2
    pos = pl + (4096 * half if is_prompt else 0)
    t = pos.astype(np.float64) / (L - 1)
    bands = 16
    f = np.linspace(1e-4, bands - 1, bands)
    w = 2 * np.pi * pos.astype(np.float64) / L
    fw = w[None, :] * f[:, None]
    zT = np.concatenate([t[None, :], np.cos(fw), -np.sin(fw)], 0).astype(np.float32)
    tn = -t.reshape(64, 128).T.copy()
    if not is_prompt:
        tn[64:, :] = -1e4
    tnb = tn.copy()
    tnb[0, 0] = -1e4
    return zT, np.concatenate([tn, tnb], 1).astype(np.float32)


def hc(C, name, rows=128):
    off, n = (HC128_COLS if rows == 128 else HC65_COLS)[name]
    t = C.hc128 if rows == 128 else C.hc65
    return t[0:rows, off:off + n]


def phase_shortconv(C, u, uh):
    nc, S = C.nc, C.S
    W = C.W
    with ExitStack() as es:
        wcol = colvec(C, es, W["hy_short_w"], 3, 1536, "Sw")
        bcol = colvec(C, es, W["hy_short_b"], 1, 1536, "Sb")
        it = [sb(es, nc, "Sit%d" % i, [128, TT + 2], F32) for i in range(3)]
        cv = [sb(es, nc, "Scv%d" % i, [128, TT], F32) for i in range(2)]
        ot = [sb(es, nc, "Sot%d" % i, [128, 4, 128], F32) for i in range(2)]
        k = 0
        for c in range(12):
            for ti in range(NT):
                t0 = ti * TT
                lo = max(t0 - 1, 0)
                hi = min(t0 + TT + 1, T)
                c0 = lo - (t0 - 1)
                c1 = c0 + (hi - lo)
                i3 = k % 3
                i2 = k % 2
                k += 1
                if ti == 0:
                    S.I("pool", "memset", it[i3][:, 0:1], 0.0, writes=[("Sit", i3)])
                if ti == NT - 1:
                    S.I("pool", "memset", it[i3][:, TT + 1:TT + 2], 0.0, writes=[("Sit", i3)])
                S.dma("sp", it[i3][:, c0:c1], u[1024 + c * 128:1024 + (c + 1) * 128, lo:hi], writes=[("Sit", i3)])
                if ti == NT // 2 - 1:
                    S.I("pool", "tensor_scalar", it[i3][:, TT + 1:TT + 2], it[i3][:, TT + 1:TT + 2],
                        C.flagcol[:, 0:1], None, op0=ALU.mult, reads=[("Sit", i3)], writes=[("Sit", i3)])
                if ti == NT // 2:
                    S.I("pool", "tensor_scalar", it[i3][:, 0:1], it[i3][:, 0:1],
                        C.flagcol[:, 0:1], None, op0=ALU.mult, reads=[("Sit", i3)], writes=[("Sit", i3)])
                S.I("dve", "tensor_scalar", cv[i2][:, :], it[i3][:, 0:TT], wcol[:, c, 0:1], bcol[:, c, 0:1],
                    op0=ALU.mult, op1=ALU.add, reads=[("Sit", i3)], writes=[("Scv", i2)])
                for j in (1, 2):
                    S.I("dve", "scalar_tensor_tensor", out=cv[i2][:, :], in0=it[i3][:, j:j + TT],
                        scalar=wcol[:, c, j:j + 1], in1=cv[i2][:, :], op0=ALU.mult, op1=ALU.add,
                        reads=[("Sit", i3)], writes=[("Scv", i2)])
                b = C.psT[C.psT_i % len(C.psT)]
                C.psT_i += 1
                for s in range(4):
                    S.I("pe", "matmul", C.ps[b][:, s * 128:(s + 1) * 128], cv[i2][:, s * 128:(s + 1) * 128],
                        C.identf[:, :], start=True, stop=True, reads=[("Scv", i2)], writes=[("ps", b)])
                S.I("act", "activation", out=ot[i2][:, :, :], in_=C.ps[b][:, :].rearrange("p (s c) -> p s c", s=4),
                    func=AF.Copy, reads=[("ps", b)], writes=[("Sot", i2)])
                S.dma("act", uh[t0:t0 + TT, c * 128:(c + 1) * 128].rearrange("(s p) c -> p s c", p=128),
                      ot[i2][:, :, :], reads=[("Sot", i2)])
        S.barrier()
        S.flush()


def alloc_fft_tiles(C, es, pfx):
    nc = C.nc
    Fq = Ctx()
    Fq.A2 = sb(es, nc, pfx + "A2", [64, 16, 130], F32)
    Fq.B = sb(es, nc, pfx + "B", [64, 2, 16, 65], F32)
    Fq.tmpA = sb(es, nc, pfx + "tmpA", [128, 2080], F32)
    Fq.tmpB = sb(es, nc, pfx + "tmpB", [128, 2080], F32)
    Fq.pfx = pfx
    return Fq


def fft_fwd_group(C, Fq, src, src_tok, ch0, X, X_tok):
    S = C.S
    pfx = Fq.pfx
    F1 = hc(C, "F1cat")
    slot = 0
    while slot < 16:
        nb = 2
        b = C.psM[C.psM_i % len(C.psM)]
        C.psM_i += 1
        for s in range(nb):
            i = slot + s
            h, chl = i // 8, i % 8
            S.I("pe", "matmul", C.ps[b][0:64, s * 256:s * 256 + 130],
                src[h * 64:(h + 1) * 64, :, ch0 + chl], F1[h * 64:(h + 1) * 64, :],
                start=True, stop=True, reads=[src_tok], writes=[("ps", b)])
        S.I("act", "activation", out=Fq.A2[:, slot:slot + nb, :],
            in_=C.ps[b][0:64, :].rearrange("p (s c) -> p s c", s=2)[:, :, 0:130], func=AF.Copy,
            reads=[("ps", b)], writes=[(pfx + "A2", 0)])
        slot += nb
    if C.dbg.get("ffstop") == 1:
        return
    Tc = hc(C, "Tc2")[0:64, :].unsqueeze(1).broadcast_to([64, 16, 130])
    Ts = hc(C, "Ts2")[0:64, :].unsqueeze(1).broadcast_to([64, 16, 130])
    P1 = Fq.tmpA[0:64, 0:2080].rearrange("p (a c) -> p a c", a=16)
    P2 = Fq.tmpB[0:64, 0:2080].rearrange("p (a c) -> p a c", a=16)
    S.I("dve", "tensor_tensor", P1, Fq.A2[:, :, :], Tc, op=ALU.mult,
        reads=[(pfx + "A2", 0)], writes=[(pfx + "tmpA", 0)])
    S.I("pool", "tensor_tensor", P2, Fq.A2[:, :, :], Ts, op=ALU.mult,
        reads=[(pfx + "A2", 0)], writes=[(pfx + "tmpB", 0)])
    S.I("dve", "tensor_tensor", Fq.B[:, 0, :, :], P1[:, :, 0:65], P2[:, :, 65:130], op=ALU.add,
        reads=[(pfx + "tmpA", 0), (pfx + "tmpB", 0)], writes=[(pfx + "B", 0)])
    S.I("pool", "tensor_tensor", Fq.B[:, 1, :, :], P1[:, :, 65:130], P2[:, :, 0:65], op=ALU.subtract,
        reads=[(pfx + "tmpA", 0), (pfx + "tmpB", 0)], writes=[(pfx + "B", 1)])
    if C.dbg.get("ffstop") == 2:
        return
    Mc, Ms, nMs = hc(C, "Mc")[0:64, :], hc(C, "Ms")[0:64, :], hc(C, "nMs")[0:64, :]
    for q in range(4):
        Br = Fq.B[:, 0, 4 * q:4 * q + 4, :]
        Bi = Fq.B[:, 1, 4 * q:4 * q + 4, :]
        b1 = C.psM[C.psM_i % len(C.psM)]
        C.psM_i += 1
        b2 = C.psM[C.psM_i % len(C.psM)]
        C.psM_i += 1
        S.I("pe", "matmul", C.ps[b1][0:64, 0:260], Mc, Br, start=True, stop=False,
            reads=[(pfx + "B", 0)], writes=[("ps", b1)])
        S.I("pe", "matmul", C.ps[b1][0:64, 0:260], Ms, Bi, start=False, stop=True,
            reads=[(pfx + "B", 1)], writes=[("ps", b1)])
        S.I("pe", "matmul", C.ps[b2][0:64, 0:260], Mc, Bi, start=True, stop=False,
            reads=[(pfx + "B", 1)], writes=[("ps", b2)])
        S.I("pe", "matmul", C.ps[b2][0:64, 0:260], nMs, Br, start=False, stop=True,
            reads=[(pfx + "B", 0)], writes=[("ps", b2)])
        h, c4 = q // 2, 4 * (q % 2)
        S.I("act", "activation", out=X[:, 0, h, c4:c4 + 4, :],
            in_=C.ps[b1][0:64, 0:260].rearrange("p (a c) -> p a c", a=4), func=AF.Copy,
            reads=[("ps", b1)], writes=[X_tok])
        S.I("act", "activation", out=X[:, 1, h, c4:c4 + 4, :],
            in_=C.ps[b2][0:64, 0:260].rearrange("p (a c) -> p a c", a=4), func=AF.Copy,
            reads=[("ps", b2)], writes=[X_tok])


def phase_filters(C, Gd, zT_in, tneg_in):
    nc, S = C.nc, C.S
    W = C.W
    with ExitStack() as es:
        fcol = sb(es, nc, "Ffcol", [64, 3], F32)
        bcol = sb(es, nc, "Fbcol", [64, 3], F32)
        f8 = sb(es, nc, "Ff8", [64, 3], F32)
        b8 = sb(es, nc, "Fb8", [64, 3], F32)
        f4 = sb(es, nc, "Ff4", [64, 3], F32)
        b4 = sb(es, nc, "Fb4", [64, 3], F32)
        with ExitStack() as es2:
            rows = sb(es2, nc, "Frows", [6, 64], F32)
            S.dma("sp", rows[0:3, :], W["hy_freq"], writes=[("Frows", 0)])
            for i, nm in enumerate(("hy_b1", "hy_b2", "hy_b3")):
                S.dma("sp", rows[3 + i:4 + i, :], W[nm], writes=[("Frows", 0)])
            b = C.psT[C.psT_i % len(C.psT)]
            C.psT_i += 1
            S.I("pe", "matmul", C.ps[b][0:64, 0:6], rows[0:6, 0:64], C.identf[0:6, 0:6], start=True, stop=True,
                reads=[("Frows", 0)], writes=[("ps", b)])
            S.I("dve", "tensor_copy", fcol[:, :], C.ps[b][0:64, 0:3], reads=[("ps", b)], writes=[("Ffcol", 0)])
            S.I("dve", "tensor_copy", bcol[:, :], C.ps[b][0:64, 3:6], reads=[("ps", b)], writes=[("Fbcol", 0)])
            S.I("dve", "tensor_tensor", bcol[:, :], bcol[:, :], fcol[:, :], op=ALU.mult,
                reads=[("Ffcol", 0), ("Fbcol", 0)], writes=[("Fbcol", 0)])
            S.I("dve", "tensor_scalar", f8[:, :], fcol[:, :], 0.125, None, op0=ALU.mult,
                reads=[("Ffcol", 0)], writes=[("Ff8", 0)])
            S.I("dve", "tensor_scalar", b8[:, :], bcol[:, :], 0.125, None, op0=ALU.mult,
                reads=[("Fbcol", 0)], writes=[("Fb8", 0)])
            S.I("dve", "tensor_scalar", f4[:, :], fcol[:, :], 0.25, None, op0=ALU.mult,
                reads=[("Ffcol", 0)], writes=[("Ff4", 0)])
            S.I("dve", "tensor_scalar", b4[:, :], bcol[:, :], 0.25, None, op0=ALU.mult,
                reads=[("Fbcol", 0)], writes=[("Fb4", 0)])
            S.barrier()
            S.flush()
        w1 = sb(es, nc, "Fw1", [33, 64], F32)
        w2 = sb(es, nc, "Fw2", [64, 64], F32)
        w3 = sb(es, nc, "Fw3", [64, 64], F32)
        w4 = sb(es, nc, "Fw4", [64, 2048], F32)
        S.dma("sp", w1[:, :], W["hy_w1"])
        S.dma("sp", w2[:, :], W["hy_w2"])
        S.dma("sp", w3[:, :], W["hy_w3"])
        S.dma("sp", w4[:, :], W["hy_w4"])
        absd = sb(es, nc, "Fabsd", [128, 2048], F32)
        S.dma("sp", absd[:, :], W["hy_decay"].partition_broadcast(128), writes=[("Fabsd", 0)])
        S.I("act", "activation", out=absd[:, :], in_=absd[:, :], func=AF.Abs,
            reads=[("Fabsd", 0)], writes=[("Fabsd", 0)])
        tneg = sb(es, nc, "Ftneg", [128, 128], F32)
        S.dma("sp", tneg[:, :], tneg_in)
        negpi = sb(es, nc, "Fnegpi", [128, 1], F32)
        S.I("pool", "memset", negpi[:, :], -math.pi)
        eps6 = sb(es, nc, "Feps6", [128, 1], F32)
        S.I("pool", "memset", eps6[:, :], 1e-6)
        h3T = sb(es, nc, "Fh3T", [64, 8192], F32)
        S.barrier()
        S.flush()
        if C.dbg.get("fstop") == 1:
            return
        with ExitStack() as es2:
            zt = [sb(es2, nc, "Fzt%d" % i, [33, 512], F32) for i in range(2)]
            ha = [sb(es2, nc, "Fha%d" % i, [64, 512], F32) for i in range(2)]
            hb = [sb(es2, nc, "Fhb%d" % i, [64, 512], F32) for i in range(2)]
            hc_ = [sb(es2, nc, "Fhc%d" % i, [64, 512], F32) for i in range(2)]
            for blk in range(16):
                i2 = blk % 2
                S.dma("sp", zt[i2][:, :], zT_in[:, blk * 512:(blk + 1) * 512], writes=[("Fzt", i2)])
                cur = zt[i2][0:33, :]
                cur_tok = ("Fzt", i2)
                for l, wl in enumerate((w1, w2, w3)):
                    K = 33 if l == 0 else 64
                    b = C.psM[C.psM_i % len(C.psM)]
                    C.psM_i += 1
                    S.I("pe", "matmul", C.ps[b][0:64, :], wl[0:K, :], cur, start=True, stop=True,
                        reads=[cur_tok], writes=[("ps", b)])
                    sa, sb_ = ha[i2], hc_[i2]
                    S.I("act", "activation", out=sa[:, :], in_=C.ps[b][0:64, :], func=AF.Sin,
                        scale=f8[:, l:l + 1], bias=b8[:, l:l + 1], reads=[("ps", b)], writes=[("Fha", i2)])
                    S.I("act", "activation", out=sb_[:, :], in_=C.ps[b][0:64, :], func=AF.Sin,
                        scale=f4[:, l:l + 1], bias=b4[:, l:l + 1], reads=[("ps", b)], writes=[("Fhc", i2)])
                    S.I("dve", "tensor_tensor", sa[:, :], sa[:, :], sa[:, :], op=ALU.mult,
                        reads=[("Fha", i2)], writes=[("Fha", i2)])
                    S.I("dve", "tensor_scalar", sa[:, :], sa[:, :], -2.0, 1.0, op0=ALU.mult, op1=ALU.add,
                        reads=[("Fha", i2)], writes=[("Fha", i2)])
                    S.I("dve", "scalar_tensor_tensor", out=sa[:, :], in0=sb_[:, :], scalar=2.0, in1=sa[:, :],
                        op0=ALU.mult, op1=ALU.mult, reads=[("Fha", i2), ("Fhc", i2)], writes=[("Fha", i2)])
                    S.I("dve", "tensor_tensor", sb_[:, :], sb_[:, :], sb_[:, :], op=ALU.mult,
                        reads=[("Fhc", i2)], writes=[("Fhc", i2)])
                    S.I("dve", "tensor_scalar", sb_[:, :], sb_[:, :], -2.0, 1.0, op0=ALU.mult, op1=ALU.add,
                        reads=[("Fhc", i2)], writes=[("Fhc", i2)])
                    dst = h3T[:, blk * 512:(blk + 1) * 512] if l == 2 else hb[i2][:, :]
                    dtok = ("Fh3T", blk) if l == 2 else ("Fhb", i2)
                    S.I("dve", "scalar_tensor_tensor", out=dst, in0=sa[:, :], scalar=2.0, in1=sb_[:, :],
                        op0=ALU.mult, op1=ALU.mult, reads=[("Fha", i2), ("Fhc", i2)], writes=[dtok])
                    cur = hb[i2][:, :]
                    cur_tok = ("Fhb", i2)
            S.barrier()
            S.flush()
        if C.dbg.get("fstop") == 2:
            return
        fw = sb(es, nc, "Ffw", [128, 64, 128], F32)
        bw = sb(es, nc, "Fbw", [128, 64, 128], F32)
        win = [sb(es, nc, "Fwin%d" % i, [128, 4, 128], F32) for i in range(2)]
        sq = [sb(es, nc, "Fsq%d" % i, [128, 4, 128], F32) for i in range(2)]
        scl = sb(es, nc, "Fscl", [128, 128], F32)
        Fq = alloc_fft_tiles(C, es, "Fq")
        XF = sb(es, nc, "FXF", [64, 2, 2, 8, 65], F32)
        XB = sb(es, nc, "FXB", [64, 2, 2, 8, 65], F32)
        Go = [[sb(es, nc, "FGo%d_%d" % (w_, 0), [64, 2, 8, 65], F32)] * 2 for w_ in range(3)]
        t65 = [Fq.tmpA[0:64, 0:1040].rearrange("p (r a c) -> p r a c", r=2, a=8),
               Fq.tmpB[0:64, 0:1040].rearrange("p (r a c) -> p r a c", r=2, a=8)]
        nflag = sb(es, nc, "Fnflag", [128, 1], F32)
        S.I("dve", "tensor_scalar", nflag[:, :], C.flagcol[:, 0:1], -1.0, None, op0=ALU.mult)
        wi = 0
        gi = 0
        for o in range(2):
            for cc in range(4):
                bss = C.psT[0]
                nmm = 0
                for d, dst in ((0, fw), (1, bw)):
                    colbase = (o * 2 + d) * 512 + cc * 128
                    for nb in range(16):
                        b = C.psM[C.psM_i % len(C.psM)]
                        C.psM_i += 1
                        wv = win[wi % 2]
                        wt = ("Fwin", wi % 2)
                        sv = sq[wi % 2]
                        st_ = ("Fsq", wi % 2)
                        wi += 1
                        for s in range(4):
                            n2 = nb * 4 + s
                            S.I("pe", "matmul", C.ps[b][:, s * 128:(s + 1) * 128], h3T[0:64, n2 * 128:(n2 + 1) * 128],
                                w4[0:64, colbase:colbase + 128], start=True, stop=True,
                                reads=[("Fh3T", n2 // 4)], writes=[("ps", b)])
                            S.I("act", "activation", out=wv[:, s, :], in_=absd[:, colbase:colbase + 128], func=AF.Exp,
                                scale=tneg[:, d * 64 + n2:d * 64 + n2 + 1], reads=[("Fabsd", 0)], writes=[wt])
                        S.I("dve", "tensor_tensor", dst[:, nb * 4:(nb + 1) * 4, :],
                            C.ps[b][:, :].rearrange("p (s c) -> p s c", s=4), wv[:, :, :], op=ALU.mult,
                            reads=[("ps", b), wt], writes=[("Ffilt", d, nb)])
                        S.I("pool", "tensor_tensor", sv[:, :, :], dst[:, nb * 4:(nb + 1) * 4, :],
                            dst[:, nb * 4:(nb + 1) * 4, :], op=ALU.mult, reads=[("Ffilt", d, nb)], writes=[st_])
                        for s in range(4):
                            S.I("pe", "matmul", C.ps[bss][:, 0:128], C.onesf[:, :], sv[:, s, :],
                                start=(nmm == 0), stop=(nmm == 127), reads=[st_], writes=[("ps", bss)])
                            nmm += 1
                S.I("act", "activation", out=scl[:, :], in_=C.ps[bss][:, 0:128], func=AF.Ln, bias=eps6[:, 0:1],
                    reads=[("ps", bss)], writes=[("Fscl", 0)])
                S.I("act", "activation", out=scl[:, :], in_=scl[:, :], func=AF.Exp, scale=-0.5,
                    reads=[("Fscl", 0)], writes=[("Fscl", 0)])
                sclb = scl[:, :].unsqueeze(1).broadcast_to([128, 64, 128])
                allf = [("Ffilt", 0, nb) for nb in range(16)]
                allb = [("Ffilt", 1, nb) for nb in range(16)]
                S.I("dve", "tensor_tensor", fw[:, :, :], fw[:, :, :], sclb, op=ALU.mult,
                    reads=[("Fscl", 0)] + allf, writes=allf + [("Ffw", 0)])
                S.I("pool", "tensor_tensor", bw[:, :, :], bw[:, :, :], sclb, op=ALU.mult,
                    reads=[("Fscl", 0)] + allb, writes=allb + [("Fbw", 0)])
                if C.dbg.get("fstop") == 3:
                    S.barrier()
                    S.flush()
                    return
                sg = hc(C, "sgn2")[0:64, :].rearrange("p (r c) -> p r c", r=2).unsqueeze(2).broadcast_to([64, 2, 8, 65])
                for g in range(16):
                    ch0 = g * 8
                    fft_fwd_group(C, Fq, fw, ("Ffw", 0), ch0, XF, ("FXF", 0))
                    fft_fwd_group(C, Fq, bw, ("Fbw", 0), ch0, XB, ("FXB", 0))
                    if C.dbg.get("fstop") == 4:
                        S.barrier()
                        S.flush()
                        return
                    k2 = 0
                    Gaa, Gab, Gba = Go[0][k2], Go[1][k2], Go[2][k2]
                    Fl, Fh = XF[:, :, 0, :, :], XF[:, :, 1, :, :]
                    Bl, Bh = XB[:, :, 0, :, :], XB[:, :, 1, :, :]
                    S.I("dve", "tensor_tensor", Gaa[:, 0, :, :], Fl[:, 0, :, :], Bl[:, 0, :, :], op=ALU.add,
                        reads=[("FXF", 0), ("FXB", 0)], writes=[("FGo", 0, k2)])
                    S.I("dve", "tensor_tensor", Gaa[:, 1, :, :], Fl[:, 1, :, :], Bl[:, 1, :, :], op=ALU.subtract,
                        reads=[("FXF", 0), ("FXB", 0)], writes=[("FGo", 0, k2)])
                    S.I("pool", "tensor_tensor", t65[0], Fl, sg, op=ALU.mult,
                        reads=[("FXF", 0)], writes=[("FqtmpA", 0)])
                    S.I("pool", "tensor_tensor", t65[0], t65[0], Fh, op=ALU.add,
                        reads=[("FXF", 0), ("FqtmpA", 0)], writes=[("FqtmpA", 0)])
                    S.I("pool", "tensor_scalar", Gba[:, :, :, :], t65[0], C.flagcol[0:64, 0:1], None,
                        op0=ALU.mult, reads=[("FqtmpA", 0)], writes=[("FGo", 2, k2)])
                    S.I("dve", "tensor_tensor", t65[1], Bl, sg, op=ALU.mult,
                        reads=[("FXB", 0)], writes=[("FqtmpB", 0)])
                    S.I("dve", "tensor_tensor", t65[1], t65[1], Bh, op=ALU.add,
                        reads=[("FXB", 0), ("FqtmpB", 0)], writes=[("FqtmpB", 0)])
                    S.I("dve", "tensor_scalar", Gab[:, 0, :, :], t65[1][:, 0, :, :], C.flagcol[0:64, 0:1], None,
                        op0=ALU.mult, reads=[("FqtmpB", 0)], writes=[("FGo", 1, k2)])
                    S.I("dve", "tensor_scalar", Gab[:, 1, :, :], t65[1][:, 1, :, :], nflag[0:64, 0:1], None,
                        op0=ALU.mult, reads=[("FqtmpB", 0)], writes=[("FGo", 1, k2)])
                    for w_ in range(3):
                        S.dma("sp", Gd[o, cc, w_, :, :, g * 8:(g + 1) * 8, :], Go[w_][k2][:, :, :, :],
                              reads=[("FGo", w_, k2)])
        S.barrier()
        S.flush()


def phase_hyena(C, uh, Gd, yb):
    nc, S = C.nc, C.S
    W = C.W
    with ExitStack() as es:
        skb = sb(es, nc, "Hskb", [128, 2, 512], F32)
        for o in range(2):
            S.dma("sp", skb[:, o, :], W["hy_skip"][o:o + 1, :].partition_broadcast(128), writes=[("Hskb", 0)])
        z = sb(es, nc, "Hz", [128, 64, 128], F32)
        xg = sb(es, nc, "Hxg", [128, 64, 128], F32)
        Fq = alloc_fft_tiles(C, es, "Hq")
        X = sb(es, nc, "HX", [64, 2, 2, 8, 65], F32)
        Y = sb(es, nc, "HY", [64, 2, 2, 8, 65], F32)
        Gt = [[sb(es, nc, "HG%d_%d" % (w_, i), [64, 2, 8, 65], F32) for i in range(2)] for w_ in range(3)]
        PQ = [sb(es, nc, "HPQ%d" % i, [64, 2, 8, 65], F32) for i in range(4)]
        C2 = sb(es, nc, "HC2", [65, 16, 128], F32)
        Dt = sb(es, nc, "HDt", [65, 2, 16, 64], F32)
        gt = [sb(es, nc, "Hgt%d" % i, [128, 64, 8], F32) for i in range(2)]
        gi = 0
        ztoks = [("Hz", g) for g in range(16)]
        for cc in range(4):
            for q4 in range(16):
                S.dma("sp", z[:, q4 * 4:(q4 + 1) * 4, :],
                      uh[:, 1024 + cc * 128:1024 + (cc + 1) * 128].rearrange("(p n) c -> p n c", n=64)[:, q4 * 4:(q4 + 1) * 4, :],
                      writes=ztoks)
            for o in range(2):
                for q4 in range(16):
                    S.dma("sp", xg[:, q4 * 4:(q4 + 1) * 4, :],
                          uh[:, o * 512 + cc * 128:o * 512 + (cc + 1) * 128].rearrange("(p n) c -> p n c", n=64)[:, q4 * 4:(q4 + 1) * 4, :],
                          writes=[("Hxg", 0)])
                for g in range(16):
                    ch0 = g * 8
                    k2 = gi % 2
                    gi += 1
                    for w_ in range(3):
                        S.dma("sp", Gt[w_][k2][:, :, :, :], Gd[o, cc, w_, :, :, g * 8:(g + 1) * 8, :],
                              writes=[("HG", w_, k2)])
                    fft_fwd_group(C, Fq, z, ("Hz", g), ch0, X, ("HX", 0))
                    Gaa, Gab, Gba = Gt[0][k2], Gt[1][k2], Gt[2][k2]
                    Xa, Xb = X[:, :, 0, :, :], X[:, :, 1, :, :]

                    def bc(Gx, r):
                        return Gx[:, r:r + 1, :, :].broadcast_to([64, 2, 8, 65])
                    for half, (GA, GB) in enumerate(((Gaa, Gab), (Gba, Gaa))):
                        ta = ("HG", (0, 1, 2)[[Gaa, Gab, Gba].index(GA)], k2)
                        tb = ("HG", (0, 1, 2)[[Gaa, Gab, Gba].index(GB)], k2)
                        S.I("dve", "tensor_tensor", PQ[0][:, :, :, :], Xa, bc(GA, 0), op=ALU.mult,
                            reads=[("HX", 0), ta], writes=[("HPQ", 0)])
                        S.I("dve", "tensor_tensor", PQ[1][:, :, :, :], Xa, bc(GA, 1), op=ALU.mult,
                            reads=[("HX", 0), ta], writes=[("HPQ", 1)])
                        S.I("pool", "tensor_tensor", PQ[2][:, :, :, :], Xb, bc(GB, 0), op=ALU.mult,
                            reads=[("HX", 0), tb], writes=[("HPQ", 2)])
                        S.I("pool", "tensor_tensor", PQ[3][:, :, :, :], Xb, bc(GB, 1), op=ALU.mult,
                            reads=[("HX", 0), tb], writes=[("HPQ", 3)])
                        S.I("pool", "tensor_tensor", PQ[0][:, :, :, :], PQ[0][:, :, :, :], PQ[2][:, :, :, :], op=ALU.add,
                            reads=[("HPQ", 0), ("HPQ", 2)], writes=[("HPQ", 0)])
                        S.I("pool", "tensor_tensor", PQ[1][:, :, :, :], PQ[1][:, :, :, :], PQ[3][:, :, :, :], op=ALU.add,
                            reads=[("HPQ", 1), ("HPQ", 3)], writes=[("HPQ", 1)])
                        S.I("dve", "tensor_tensor", Y[:, 0, half, :, :], PQ[0][:, 0, :, :], PQ[1][:, 1, :, :],
                            op=ALU.subtract, reads=[("HPQ", 0), ("HPQ", 1)], writes=[("HY", half)])
                        S.I("dve", "tensor_tensor", Y[:, 1, half, :, :], PQ[0][:, 1, :, :], PQ[1][:, 0, :, :],
                            op=ALU.add, reads=[("HPQ", 0), ("HPQ", 1)], writes=[("HY", half)])
                    rhsA, rhsB = hc(C, "rhsA")[0:64, :], hc(C, "rhsB")[0:64, :]
                    for s0 in range(0, 16, 4):
                        b = C.psM[C.psM_i % len(C.psM)]
                        C.psM_i += 1
                        for s in range(4):
                            i = s0 + s
                            h, chl = i // 8, i % 8
                            S.I("pe", "matmul", C.ps[b][0:65, s * 128:(s + 1) * 128], Y[:, 0, h, chl, :], rhsA,
                                start=True, stop=False, reads=[("HY", h)], writes=[("ps", b)])
                            S.I("pe", "matmul", C.ps[b][0:65, s * 128:(s + 1) * 128], Y[:, 1, h, chl, :], rhsB,
                                start=False, stop=True, reads=[("HY", h)], writes=[("ps", b)])
                        S.I("act", "activation", out=C2[0:65, s0:s0 + 4, :],
                            in_=C.ps[b][0:65, :].rearrange("p (s c) -> p s c", s=4), func=AF.Copy,
                            reads=[("ps", b)], writes=[("HC2", 0)])
                    Tci = hc(C, "Tci", 65).unsqueeze(1).broadcast_to([65, 16, 128])
                    Tsi = hc(C, "Tsi", 65).unsqueeze(1).broadcast_to([65, 16, 128])
                    P1 = Fq.tmpA[0:65, 0:2048].rearrange("p (a c) -> p a c", a=16)
                    P2 = Fq.tmpB[0:65, 0:2048].rearrange("p (a c) -> p a c", a=16)
                    S.I("dve", "tensor_tensor", P1, C2[0:65, :, :], Tci, op=ALU.mult,
                        reads=[("HC2", 0)], writes=[("HqtmpA", 0)])
                    S.I("pool", "tensor_tensor", P2, C2[0:65, :, :], Tsi, op=ALU.mult,
                        reads=[("HC2", 0)], writes=[("HqtmpB", 0)])
                    S.I("dve", "tensor_tensor", Dt[0:65, 0, :, :], P1[:, :, 0:64], P2[:, :, 64:128], op=ALU.subtract,
                        reads=[("HqtmpA", 0), ("HqtmpB", 0)], writes=[("HDt", 0)])
                    S.I("pool", "tensor_tensor", Dt[0:65, 1, :, :], P2[:, :, 0:64], P1[:, :, 64:128], op=ALU.add,
                        reads=[("HqtmpA", 0), ("HqtmpB", 0)], writes=[("HDt", 1)])
                    b = C.psM[C.psM_i % len(C.psM)]
                    C.psM_i += 1
                    fin = [("Gc_a", 0, 0), ("Gs_a", 1, 0), ("Gc_b", 0, 1), ("Gs_b", 1, 1)]
                    for i, (gn, ri, h) in enumerate(fin):
                        S.I("pe", "matmul", C.ps[b][:, :], hc(C, gn, 65), Dt[0:65, ri, 8 * h:8 * h + 8, :],
                            start=(i == 0), stop=(i == 3), reads=[("HDt", ri)], writes=[("ps", b)])
                    yv = C.ps[b][:, :].rearrange("p (c n) -> p n c", c=8)
                    zv = z[:, :, ch0:ch0 + 8]
                    xv = xg[:, :, ch0:ch0 + 8]
                    skv = skb[:, o, cc * 128 + ch0:cc * 128 + ch0 + 8].unsqueeze(1).broadcast_to([128, 64, 8])
                    gk = gt[g % 2]
                    S.I("pool", "tensor_tensor", gk[:, :, :], zv, skv, op=ALU.mult,
                        reads=[("Hz", g), ("Hskb", 0)], writes=[("Hgt", g % 2)])
                    S.I("dve", "tensor_tensor", gk[:, :, :], yv, gk[:, :, :], op=ALU.add,
                        reads=[("ps", b), ("Hgt", g % 2)], writes=[("Hgt", g % 2)])
                    S.I("pool", "tensor_tensor", zv, gk[:, :, :], xv, op=ALU.mult,
                        reads=[("Hgt", g % 2), ("Hxg", 0)], writes=[("Hz", g)])
            for q4 in range(16):
                S.dma("sp", yb[:, cc * 128:(cc + 1) * 128].rearrange("(p n) c -> p n c", n=64)[:, q4 * 4:(q4 + 1) * 4, :],
                      z[:, q4 * 4:(q4 + 1) * 4, :], reads=ztoks)
        S.barrier()
        S.flush()


def phase_outproj(C, x_src, ya, yb, w_ap, x_dst):
    nc, S = C.nc, C.S
    tag = "O"
    with ExitStack() as es:
        wb = load_weight_bf16(C, es, w_ap, D, D, "Ow")
        xt = [sb(es, nc, "Oxt%d" % i, [128, 4, D], F32) for i in range(2)]
        yaT = [sb(es, nc, "OyaT%d" % i, [128, 4, TT], BF16) for i in range(2)]
        ybt = [sb(es, nc, "Oybt%d" % i, [128, 4, 512], F32) for i in range(2)]
        ybb = sb(es, nc, "Oybb", [128, 4, 512], BF16)
        ybT = sb(es, nc, "OybT", [128, 4, TT], BF16)
        for ti in range(NT):
            t0 = ti * TT
            sl = ti % 2
            S.dma("sp", xt[sl][:, :, :], x_src[t0:t0 + TT, :].rearrange("(s p) d -> p s d", p=128),
                  writes=[("Oxt", sl)])
            S.dma("sp", yaT[sl][:, :, :], ya[:, t0:t0 + TT].rearrange("(k p) t -> p k t", p=128),
                  writes=[("OyaT", sl)])
            S.dma("sp", ybt[sl][:, :, :], yb[t0:t0 + TT, :].rearrange("(s p) c -> p s c", p=128),
                  writes=[("Oybt", sl)])
            S.I("pool", "tensor_copy", ybb[:, :, :], ybt[sl][:, :, :], reads=[("Oybt", sl)], writes=[("Oybb", 0)])
            for kc in range(4):
                b = C.psT[C.psT_i % len(C.psT)]
                C.psT_i += 1
                for s in range(4):
                    S.I("pe", "matmul", C.ps[b][:, s * 128:(s + 1) * 128], ybb[:, s, kc * 128:(kc + 1) * 128],
                        C.ident[:, :], start=True, stop=True, reads=[("Oybb", 0)], writes=[("ps", b)])
                S.I("act", "activation", out=ybT[:, kc, :], in_=C.ps[b][:, :], func=AF.Copy,
                    reads=[("ps", b)], writes=[("OybT", kc)])
            for s in range(4):
                for nh in range(2):
                    b = C.psM[C.psM_i % len(C.psM)]
                    C.psM_i += 1
                    for kc in range(8):
                        if kc < 4:
                            lt, tok = yaT[sl][:, kc, s * 128:(s + 1) * 128], ("OyaT", sl)
                        else:
                            lt, tok = ybT[:, kc - 4, s * 128:(s + 1) * 128], ("OybT", kc - 4)
                        S.I("pe", "matmul", C.ps[b][:, :], lt, wb[:, kc, nh * 512:(nh + 1) * 512],
                            start=(kc == 0), stop=(kc == 7), reads=[tok], writes=[("ps", b)])
                    S.I("dve", "tensor_tensor", xt[sl][:, s, nh * 512:(nh + 1) * 512], C.ps[b][:, :],
                        xt[sl][:, s, nh * 512:(nh + 1) * 512], op=ALU.add,
                        reads=[("ps", b)], writes=[("Oxt", sl)])
            S.dma("pool", x_dst[t0:t0 + TT, :].rearrange("(s p) d -> p s d", p=128), xt[sl][:, :, :],
                  reads=[("Oxt", sl)])
        S.barrier()
        S.flush()


def attn_consts(is_prompt):
    d = np.arange(128) % 64
    inv = 500000.0 ** (-(np.arange(0, 16, 2, dtype=np.float64) / 16.0))
    tau = np.arange(T)
    pos = tau if is_prompt else tau % 4096
    cos = np.ones((128, T), np.float64)
    sin = np.zeros((128, T), np.float64)
    for p in range(128):
        dd = d[p]
        if dd < 16:
            ang = pos * inv[dd % 8]
            cos[p] = np.cos(ang)
            sin[p] = np.sin(ang)
    P = np.zeros((128, 128), np.float32)
    for m in range(128):
        dd = m % 64
        if dd < 8:
            P[m + 8, m] = -1.0
        elif dd < 16:
            P[m - 8, m] = 1.0
    q = np.arange(128)[:, None]
    s = np.arange(128)[None, :]
    NEG = -30000.0
    prev = np.where(s >= q, 0.0, NEG)
    cur = np.zeros((128, 128))
    nxt = np.where(s <= q, 0.0, NEG)
    band = np.concatenate([prev, cur, nxt], 1).astype(np.float32)
    full = np.full((128, 128), NEG)
    if is_prompt:
        m31, m32 = band, band
    else:
        m31 = np.concatenate([prev, cur, full], 1).astype(np.float32)
        m32 = np.concatenate([full, cur, nxt], 1).astype(np.float32)
    masks = np.stack([band, m31, m32], 0).astype(np.float32)
    return cos.astype(np.float32), sin.astype(np.float32), P, masks


def phase_qkv(C, x_src, g_row, w_ap, qd, kd, vd, cos_in, sin_in, prot_in):
    nc, S = C.nc, C.S
    tag = "Q"
    with ExitStack() as es:
        gcol = load_bcast_row(C, es, g_row, D, "Qg")
        wb = load_weight_bf16(C, es, w_ap, D, 1536, "Qw")
        prf = sb(es, nc, "Qprf", [128, 128], F32)
        prb = sb(es, nc, "Qprb", [128, 128], BF16)
        S.dma("sp", prf[:, :], prot_in, writes=[("Qprf", 0)])
        S.I("dve", "tensor_copy", prb[:, :], prf[:, :], reads=[("Qprf", 0)], writes=[("Qprb", 0)])
        N = alloc_norm_tiles(C, es, tag, TT)
        cs = [sb(es, nc, "Qcs%d" % i, [128, TT], F32) for i in range(2)]
        sn = [sb(es, nc, "Qsn%d" % i, [128, TT], F32) for i in range(2)]
        xb = [sb(es, nc, "Qxb%d" % i, [128, TT], BF16) for i in range(2)]
        t1 = [sb(es, nc, "Qt1%d" % i, [128, TT], F32) for i in range(2)]
        t2 = [sb(es, nc, "Qt2%d" % i, [128, TT], F32) for i in range(2)]
        qo = [sb(es, nc, "Qqo%d" % i, [128, TT], BF16) for i in range(2)]
        vo = [sb(es, nc, "Qvo%d" % i, [128, 256], F32) for i in range(2)]
        k = 0
        norm_load(C, N, x_src, 0)
        nxt = norm_tile(C, N, gcol, 0)
        for ti in range(NT):
            t0 = ti * TT
            if ti + 1 < NT:
                norm_load(C, N, x_src, ti + 1)
            c2 = ti % 2
            S.dma("sp", cs[c2][:, :], cos_in[:, t0:t0 + TT], writes=[("Qcs", c2)])
            S.dma("sp", sn[c2][:, :], sin_in[:, t0:t0 + TT], writes=[("Qsn", c2)])
            xs, hT, slot, tslot = nxt
            for oc in range(10):
                if oc == 6 and ti + 1 < NT:
                    nxt = norm_tile(C, N, gcol, ti + 1)
                b = C.psM[C.psM_i % len(C.psM)]
                C.psM_i += 1
                for kc in range(8):
                    S.I("pe", "matmul", C.ps[b][:, :], wb[:, kc, oc * 128:(oc + 1) * 128], hT[:, kc, :],
                        start=(kc == 0), stop=(kc == 7), reads=[("QhnT", tslot, kc)], writes=[("ps", b)])
                k2 = k % 2
                k += 1
                if C.dbg.get("qstop") == 1:
                    S.I("act", "activation", out=qo[k2][:, :], in_=C.ps[b][:, :], func=AF.Copy,
                        reads=[("ps", b)], writes=[("Qqo", k2)])
                    S.dma("sp", qd[oc * 128:(oc + 1) * 128, t0:t0 + TT], qo[k2][:, :], reads=[("Qqo", k2)])
                    continue
                S.I("dve", "tensor_copy", xb[k2][:, :], C.ps[b][:, :],
                    reads=[("ps", b)], writes=[("Qxb", k2)])
                b2 = C.psM[C.psM_i % len(C.psM)]
                C.psM_i += 1
                S.I("pe", "matmul", C.ps[b2][:, :], prb[:, :], xb[k2][:, :], start=True, stop=True,
                    reads=[("Qxb", k2), ("Qprb", 0)], writes=[("ps", b2)])
                S.I("dve", "tensor_tensor", t1[k2][:, :], C.ps[b][:, :], cs[c2][:, :], op=ALU.mult,
                    reads=[("ps", b), ("Qcs", c2)], writes=[("Qt1", k2)])
                S.I("dve", "tensor_tensor", t2[k2][:, :], C.ps[b2][:, :], sn[c2][:, :], op=ALU.mult,
                    reads=[("ps", b2), ("Qsn", c2)], writes=[("Qt2", k2)])
                S.I("pool", "tensor_tensor", qo[k2][:, :], t1[k2][:, :], t2[k2][:, :], op=ALU.add,
                    reads=[("Qt1", k2), ("Qt2", k2)], writes=[("Qqo", k2)])
                if oc < 8:
                    S.dma("pool", qd[oc * 128:(oc + 1) * 128, t0:t0 + TT], qo[k2][:, :], reads=[("Qqo", k2)])
                else:
                    S.dma("pool", kd[(oc - 8) * 128:(oc - 7) * 128, t0:t0 + TT], qo[k2][:, :], reads=[("Qqo", k2)])
            for s in range(4):
                if C.dbg.get("qstop") == 2:
                    break
                b = C.psM[C.psM_i % len(C.psM)]
                C.psM_i += 1
                for kc in range(8):
                    S.I("pe", "matmul", C.ps[b][:, 0:256], hT[:, kc, s * 128:(s + 1) * 128], wb[:, kc, 1280:1536],
                        start=(kc == 0), stop=(kc == 7), reads=[("QhnT", tslot, kc)], writes=[("ps", b)])
                S.I("act", "activation", out=vo[s % 2][:, :], in_=C.ps[b][:, 0:256], func=AF.Copy,
                    reads=[("ps", b)], writes=[("Qvo", s % 2)])
                S.dma("act", vd[t0 + s * 128:t0 + (s + 1) * 128, :], vo[s % 2][:, :], reads=[("Qvo", s % 2)])
        S.barrier()
        S.flush()


def phase_attn(C, x_src, qd, kd, vd, wo_ap, x_dst, masks_in):
    nc, S = C.nc, C.S
    W = C.W
    NB = T // 128
    with ExitStack() as es:
        wo = load_weight_bf16(C, es, wo_ap, D, D, "Two")
        sinkb = load_bcast_row(C, es, W["at_sink"], 16, "Tsink")
        mk = sb(es, nc, "Tmk", [128, 3, 384], F32)
        S.dma("sp", mk[:, :, :], masks_in.rearrange("m p s -> p m s"), writes=[("Tmk", 0)])
        qT = [sb(es, nc, "TqT%d" % i, [128, 8, 128], BF16) for i in range(2)]
        kT = [sb(es, nc, "TkT%d" % i, [128, 4, 384], BF16) for i in range(2)]
        vt = [sb(es, nc, "Tvt%d" % i, [128, 3, 256], BF16) for i in range(2)]
        vf = [sb(es, nc, "Tvf%d" % i, [128, 3, 256], F32) for i in range(2)]
        xt = [sb(es, nc, "Txt%d" % i, [128, D], F32) for i in range(2)]
        sm = [sb(es, nc, "Tsm%d" % i, [128, 384], F32) for i in range(4)]
        pb = [sb(es, nc, "Tpb%d" % i, [128, 384], BF16) for i in range(4)]
        pT = [sb(es, nc, "TpT%d" % i, [128, 3, 128], BF16) for i in range(4)]
        st = [sb(es, nc, "Tst%d" % i, [128, 8], F32) for i in range(4)]
        sbanks = [4, 5, 0, 1]
        tbanks = [2, 3]
        tbi = 0
        rdn = sb(es, nc, "Trdn", [128, 16], F32)
        ob = sb(es, nc, "Tob", [128, 16, 64], BF16)
        oT = sb(es, nc, "ToT", [128, 8, 128], BF16)
        hh = 0
        for n in range(NB):
            sl = n % 2
            kb0 = max(n - 1, 0)
            kb1 = min(n + 1, NB - 1)
            nk = kb1 - kb0 + 1
            mo = (kb0 - (n - 1)) * 128
            mi = 1 if n == NB // 2 - 1 else (2 if n == NB // 2 else 0)
            S.dma("sp", qT[sl][:, :, :], qd[:, n * 128:(n + 1) * 128].rearrange("(k p) t -> p k t", p=128),
                  writes=[("TqT", sl)])
            for kv in range(4):
                for dup in range(2):
                    S.dma("sp", kT[sl][dup * 64:(dup + 1) * 64, kv, 0:nk * 128],
                          kd[kv * 64:(kv + 1) * 64, kb0 * 128:(kb1 + 1) * 128], writes=[("TkT", sl)])
            S.dma("sp", vf[sl][:, 0:nk, :], vd[kb0 * 128:(kb1 + 1) * 128, :].rearrange("(k p) c -> p k c", p=128),
                  writes=[("Tvf", sl)])
            S.I("pool", "tensor_copy", vt[sl][:, 0:nk, :], vf[sl][:, 0:nk, :], reads=[("Tvf", sl)], writes=[("Tvt", sl)])
            S.dma("sp", xt[sl][:, :], x_src[n * 128:(n + 1) * 128, :], writes=[("Txt", sl)])
            bo = [6, 7]
            hp_ = []
            for h in range(16):
                h2 = hh % 4
                hh += 1
                bt = tbanks[tbi % 2]
                tbi += 1
                hp_.append((h2, sbanks[hh % 4], bt))

            def stA(h):
                kv = h // 4
                qc, hp = h // 2, h % 2
                h2, b, bt = hp_[h]
                smh, sth = sm[h2], st[h2]
                S.I("pe", "matmul", C.ps[b][:, 0:nk * 128], qT[sl][hp * 64:(hp + 1) * 64, qc, :],
                    kT[sl][hp * 64:(hp + 1) * 64, kv, 0:nk * 128], start=True, stop=True,
                    reads=[("TqT", sl), ("TkT", sl)], writes=[("ps", b)])
                S.I("dve", "scalar_tensor_tensor", out=smh[:, 0:nk * 128], in0=C.ps[b][:, 0:nk * 128], scalar=0.125,
                    in1=mk[:, mi, mo:mo + nk * 128], op0=ALU.mult, op1=ALU.add,
                    reads=[("ps", b), ("Tmk", 0)], writes=[("Tsm", h2)])
                S.I("dve", "reduce_max", sth[:, 0:1], smh[:, 0:nk * 128], axis=AX.X,
                    reads=[("Tsm", h2)], writes=[("Tst", h2, 0)])
                S.I("dve", "tensor_tensor", sth[:, 1:2], sth[:, 0:1], sinkb[:, h:h + 1], op=ALU.max,
                    reads=[("Tst", h2, 0), ("Tsink", 0)], writes=[("Tst", h2, 1)])
                S.I("dve", "tensor_scalar", sth[:, 2:3], sth[:, 1:2], -1.0, None, op0=ALU.mult,
                    reads=[("Tst", h2, 1)], writes=[("Tst", h2, 2)])

            def stB(h):
                h2, b, bt = hp_[h]
                smh, pbh, sth = sm[h2], pb[h2], st[h2]
                S.I("act", "activation", out=pbh[:, 0:nk * 128], in_=smh[:, 0:nk * 128], func=AF.Exp,
                    bias=sth[:, 2:3], accum_out=sth[:, 3:4],
                    reads=[("Tsm", h2), ("Tst", h2, 2)], writes=[("Tpb", h2), ("Tst", h2, 3)])
                S.I("act", "activation", out=sth[:, 4:5], in_=sinkb[:, h:h + 1], func=AF.Exp, bias=sth[:, 2:3],
                    reads=[("Tst", h2, 2), ("Tsink", 0)], writes=[("Tst", h2, 4)])
                S.I("dve", "tensor_tensor", sth[:, 5:6], sth[:, 3:4], sth[:, 4:5], op=ALU.add,
                    reads=[("Tst", h2, 3), ("Tst", h2, 4)], writes=[("Tst", h2, 5)])
                S.I("dve", "reciprocal", rdn[:, h:h + 1], sth[:, 5:6], reads=[("Tst", h2, 5)], writes=[("Trdn", h)])
                for kb in range(nk):
                    S.I("pe", "matmul", C.ps[bt][:, kb * 128:(kb + 1) * 128], pbh[:, kb * 128:(kb + 1) * 128],
                        C.ident[:, :], start=True, stop=True, reads=[("Tpb", h2)], writes=[("ps", bt)])

            def stC(h):
                kv = h // 4
                h2, b, bt = hp_[h]
                pTh = pT[h2]
                S.I("act", "activation", out=pTh[:, 0:nk, :],
                    in_=C.ps[bt][:, 0:nk * 128].rearrange("p (k c) -> p k c", k=nk), func=AF.Copy,
                    reads=[("ps", bt)], writes=[("TpT", h2)])
                bb = bo[h // 8]
                for kb in range(nk):
                    S.I("pe", "matmul", C.ps[bb][:, (h % 8) * 64:(h % 8 + 1) * 64], pTh[:, kb, :],
                        vt[sl][:, kb, kv * 64:(kv + 1) * 64], start=(kb == 0), stop=(kb == nk - 1),
                        reads=[("TpT", h2), ("Tvt", sl)], writes=[("ps", bb)])

            for i in range(18):
                if i < 16:
                    stA(i)
                if 0 <= i - 1 < 16:
                    stB(i - 1)
                if 0 <= i - 2 < 16:
                    stC(i - 2)
            for j in range(2):
                S.I("dve", "tensor_tensor", ob[:, j * 8:(j + 1) * 8, :],
                    C.ps[bo[j]][:, :].rearrange("p (h d) -> p h d", h=8),
                    rdn[:, j * 8:(j + 1) * 8].unsqueeze(2).broadcast_to([128, 8, 64]), op=ALU.mult,
                    reads=[("ps", bo[j])] + [("Trdn", hq) for hq in range(j * 8, j * 8 + 8)], writes=[("Tob", j)])
            for half in range(2):
                bt = tbanks[tbi % 2]
                tbi += 1
                for jj in range(4):
                    kc = half * 4 + jj
                    S.I("pe", "matmul", C.ps[bt][:, jj * 128:(jj + 1) * 128],
                        ob[:, 2 * kc:2 * kc + 2, :].rearrange("p h d -> p (h d)"),
                        C.ident[:, :], start=True, stop=True, reads=[("Tob", kc // 4)], writes=[("ps", bt)])
                S.I("act", "activation", out=oT[:, half * 4:(half + 1) * 4, :],
                    in_=C.ps[bt][:, :].rearrange("p (k c) -> p k c", k=4), func=AF.Copy,
                    reads=[("ps", bt)], writes=[("ToT", half)])
            for nh in range(2):
                b = sbanks[(hh + 1 + nh) % 4]
                for kc in range(8):
                    S.I("pe", "matmul", C.ps[b][:, :], oT[:, kc, :], wo[:, kc, nh * 512:(nh + 1) * 512],
                        start=(kc == 0), stop=(kc == 7), reads=[("ToT", kc // 4)], writes=[("ps", b)])
                S.I("dve", "tensor_tensor", xt[sl][:, nh * 512:(nh + 1) * 512], C.ps[b][:, :],
                    xt[sl][:, nh * 512:(nh + 1) * 512], op=ALU.add, reads=[("ps", b)], writes=[("Txt", sl)])
            S.dma("pool", x_dst[n * 128:(n + 1) * 128, :], xt[sl][:, :], reads=[("Txt", sl)])
        S.barrier()
        S.flush()


def phase_mlp(C, x_src, g_row, wup_ap, wdn_ap, x_dst, tag, final_g_row=None):
    nc, S = C.nc, C.S
    DFF = 4096
    TK = 256
    with ExitStack() as es:
        gcol = load_bcast_row(C, es, g_row, D, tag + "g")
        gfin = load_bcast_row(C, es, final_g_row, D, tag + "gf") if final_g_row is not None else None
        wup = load_weight_bf16(C, es, wup_ap, D, DFF, tag + "wu")
        wdn = load_weight_bf16(C, es, wdn_ap, DFF, D, tag + "wd")
        N = alloc_norm_tiles(C, es, tag, TK, nx=2, nT=2)
        hT = sb(es, nc, tag + "hT", [128, 32, TK], BF16)
        rl = [sb(es, nc, tag + "rl%d" % i, [128, TK], F32) for i in range(2)]
        ss2 = sb(es, nc, tag + "ss2", [128, 2], F32)
        rs2 = sb(es, nc, tag + "rs2", [128, 2], F32)
        norm_load(C, N, x_src, 0)
        nxt = norm_tile(C, N, gcol, 0)
        for ti in range(T // TK):
            if ti + 1 < T // TK:
                norm_load(C, N, x_src, ti + 1)
            xs, hTn, slot, tslot = nxt
            for fc in range(32):
                b = C.psM[C.psM_i % len(C.psM)]
                C.psM_i += 1
                for kc in range(8):
                    S.I("pe", "matmul", C.ps[b][:, 0:TK], wup[:, kc, fc * 128:(fc + 1) * 128], hTn[:, kc, :],
                        start=(kc == 0), stop=(kc == 7),
                        reads=[(tag + "hnT", tslot, kc)], writes=[("ps", b)])
                j = fc % 2
                if j == 0:
                    S.I("act", "activation", out=rl[0][:, :], in_=C.ps[b][:, 0:TK], func=AF.Relu,
                        reads=[("ps", b)], writes=[(tag + "rl", 0)])
                else:
                    S.I("dve", "tensor_scalar", rl[1][:, :], C.ps[b][:, 0:TK], 0.0, None, op0=ALU.max,
                        reads=[("ps", b)], writes=[(tag + "rl", 1)])
                S.I("pool", "tensor_tensor", hT[:, fc, :], rl[j][:, :], rl[j][:, :], op=ALU.mult,
                    reads=[(tag + "rl", j)], writes=[(tag + "hT", fc)])
            if ti + 1 < T // TK:
                nxt = norm_tile(C, N, gcol, ti + 1)
            for s in range(N.NS):
                xr = (tag + "xt", slot)
                for nh in range(2):
                    b = C.psM[C.psM_i % len(C.psM)]
                    C.psM_i += 1
                    for fc in range(32):
                        S.I("pe", "matmul", C.ps[b][:, :], hT[:, fc, s * 128:(s + 1) * 128],
                            wdn[:, fc, nh * 512:(nh + 1) * 512], start=(fc == 0), stop=(fc == 31),
                            reads=[(tag + "hT", fc)], writes=[("ps", b)])
                    S.I("dve", "tensor_tensor", xs[:, s, nh * 512:(nh + 1) * 512], C.ps[b][:, :],
                        xs[:, s, nh * 512:(nh + 1) * 512], op=ALU.add,
                        reads=[("ps", b)], writes=[xr])
                r0 = ti * TK + s * 128
                if gfin is not None:
                    S.I("pool", "memset", ss2[:, 0:1], 0.0, writes=[(tag + "ss2", 0)])
                    S.I("act", "activation", out=N.junk[:, :], in_=xs[:, s, :], func=AF.Square,
                        accum_out=ss2[:, 0:1], reads=[xr], writes=[(tag + "ss2", 0), (tag + "junk", 0)])
                    S.I("act", "activation", out=rs2[:, 0:1], in_=ss2[:, 0:1], func=AF.Ln, scale=1.0 / D,
                        bias=C.eps5[:, 0:1], reads=[(tag + "ss2", 0)], writes=[(tag + "rs2", 0)])
                    S.I("act", "activation", out=rs2[:, 0:1], in_=rs2[:, 0:1], func=AF.Exp, scale=-0.5,
                        reads=[(tag + "rs2", 0)], writes=[(tag + "rs2", 0)])
                    S.I("dve", "scalar_tensor_tensor", out=xs[:, s, :], in0=xs[:, s, :], scalar=rs2[:, 0:1],
                        in1=gfin[:, :], op0=ALU.mult, op1=ALU.mult,
                        reads=[(tag + "rs2", 0)], writes=[xr])
                S.dma("sp", x_dst[r0:r0 + 128, :], xs[:, s, :], reads=[xr])
        S.barrier()
        S.flush()


WEIGHT_SPECS = [
    ("norm_mix", (2, 1024)), ("norm_mlp", (2, 1024)), ("norm_final", (1, 1024)),
    ("ab_w_in", (1024, 2560)), ("ab_w_out", (1024, 1024)),
    ("cv_dw_w", (31, 512)), ("cv_dw_b", (1, 512)), ("cv_ln_g", (1, 512)), ("cv_ln_b", (1, 512)),
    ("hy_short_w", (3, 1536)), ("hy_short_b", (1, 1536)),
    ("hy_w1", (33, 64)), ("hy_b1", (1, 64)), ("hy_w2", (64, 64)), ("hy_b2", (1, 64)),
    ("hy_w3", (64, 64)), ("hy_b3", (1, 64)), ("hy_w4", (64, 2048)),
    ("hy_freq", (3, 64)), ("hy_decay", (1, 2048)), ("hy_skip", (2, 512)),
    ("at_w_qkv", (1024, 1536)), ("at_sink", (1, 16)), ("at_w_o", (1024, 1024)),
    ("mlp_w_up", (2, 1024, 4096)), ("mlp_w_down", (2, 4096, 1024)),
]


def build_program(dbg=None):
    nc = bass.Bass("TRN2", target_bir_lowering=False)
    C = Ctx()
    C.nc = nc
    C.dbg = dbg or {}
    W = {}
    x_in = nc.dram_tensor("x", [T, D], F32, kind="ExternalInput").ap()
    for name, shp in WEIGHT_SPECS:
        W[name] = nc.dram_tensor(name, list(shp), F32, kind="ExternalInput").ap()
    ident_in = nc.dram_tensor("ident", [128, 128], F32, kind="ExternalInput").ap()
    flag_in = nc.dram_tensor("flag", [1, 8], F32, kind="ExternalInput").ap()
    hc128_np, hc65_np = hyena_consts()
    hc128_in = nc.dram_tensor("hc128", list(hc128_np.shape), F32, kind="ExternalInput").ap()
    hc65_in = nc.dram_tensor("hc65", list(hc65_np.shape), F32, kind="ExternalInput").ap()
    zT_in = nc.dram_tensor("zT", [33, 8192], F32, kind="ExternalInput").ap()
    tneg_in = nc.dram_tensor("tneg", [128, 128], F32, kind="ExternalInput").ap()
    cos_in = nc.dram_tensor("ropecos", [128, T], F32, kind="ExternalInput").ap()
    sin_in = nc.dram_tensor("ropesin", [128, T], F32, kind="ExternalInput").ap()
    prot_in = nc.dram_tensor("prot", [128, 128], F32, kind="ExternalInput").ap()
    masks_in = nc.dram_tensor("amasks", [3, 128, 384], F32, kind="ExternalInput").ap()
    y_out = nc.dram_tensor("y", [T, D], F32, kind="ExternalOutput").ap()
    C.W = W

    def scratch(name, shape, dt=F32):
        kind = "ExternalOutput" if name in C.dbg.get("out", ()) else "Internal"
        return nc.dram_tensor(name, list(shape), dt, kind=kind).ap()

    u = scratch("u", [2560, T])
    ya = scratch("ya", [512, T], BF16)
    uh = scratch("uh", [T, 1536])
    Gd = scratch("Gd", [2, 4, 3, 64, 2, 128, 65])
    yb = scratch("yb", [T, 512])
    x1 = scratch("x1", [T, D])
    x2 = scratch("x2", [T, D])
    x3 = scratch("x3", [T, D])
    qd = scratch("qd", [1024, T], BF16)
    kd = scratch("kd", [256, T], BF16)
    vd = scratch("vd", [T, 256])
    stop = C.dbg.get("stop")

    with ExitStack() as es:
        S = Sched(nc, es)
        C.S = S
        C.ps = [es.enter_context(nc.psum_tensor("ps%d" % i, [128, 512], F32)) for i in range(8)]
        C.psT = [0, 1, 2, 3]
        C.psM = [4, 5, 6, 7]
        C.psT_i = 0
        C.psM_i = 0
        C.ident = sb(es, nc, "ident_b", [128, 128], BF16)
        C.identf = sb(es, nc, "ident_f", [128, 128], F32)
        C.eps5 = sb(es, nc, "eps5", [128, 1], F32)
        S.I("pool", "memset", C.eps5[:, :], 1e-5)
        S.dma("sp", C.identf[:, :], ident_in[:, :], writes=[("identf", 0)])
        S.I("dve", "tensor_copy", C.ident[:, :], C.identf[:, :], reads=[("identf", 0)], writes=[("ident", 0)])
        C.onesf = sb(es, nc, "onesf", [128, 128], F32)
        S.I("pool", "memset", C.onesf[:, :], 1.0)
        C.flagcol = sb(es, nc, "flagcol", [128, 8], F32)
        S.dma("sp", C.flagcol[:, :], flag_in.partition_broadcast(128))
        C.hc128 = sb(es, nc, "hc128_t", list(hc128_np.shape), F32)
        C.hc65 = sb(es, nc, "hc65_t", list(hc65_np.shape), F32)
        S.dma("sp", C.hc128[:, :], hc128_in)
        S.dma("sp", C.hc65[:, :], hc65_in)
        S.barrier()
        S.flush()

        skip = C.dbg.get("skip", "")
        if stop != "M" and "A" not in skip:
            phase_inproj(C, x_in, W["norm_mix"][0:1, :], W["ab_w_in"], 2560, u, "A")
        if stop == "A":
            return nc
        if "B" not in skip:
            phase_conformer(C, u, ya)
        if stop == "B":
            return nc
        if "S" not in skip:
            phase_shortconv(C, u, uh)
        if stop == "S":
            return nc
        if "F" not in skip:
            phase_filters(C, Gd, zT_in, tneg_in)
        if stop == "F":
            return nc
        if "H" not in skip:
            phase_hyena(C, uh, Gd, yb)
        if stop == "H":
            return nc
        if "O" not in skip:
            phase_outproj(C, x_in, ya, yb, W["ab_w_out"], x1)
        if stop == "O":
            return nc
        if "M0" not in skip:
            phase_mlp(C, x1, W["norm_mlp"][0:1, :], W["mlp_w_up"][0], W["mlp_w_down"][0], x2, "M0")
        if stop == "M0":
            return nc
        xa = x_in if "X" in skip else x2
        phase_qkv(C, xa, W["norm_mix"][1:2, :], W["at_w_qkv"], qd, kd, vd, cos_in, sin_in, prot_in)
        if stop == "Q":
            return nc
        phase_attn(C, xa, qd, kd, vd, W["at_w_o"], x3, masks_in)
        if stop == "T":
            return nc
        phase_mlp(C, x3, W["norm_mlp"][1:2, :], W["mlp_w_up"][1], W["mlp_w_down"][1], y_out, "M1",
                  final_g_row=W["norm_final"][0:1, :])
        return nc
        phase_mlp(C, x_in, W["norm_mlp"][0:1, :], W["mlp_w_up"][0], W["mlp_w_down"][0], y_out, "M0",
                  final_g_row=W["norm_final"][0:1, :])
    return nc


_CACHE = {}


def make_in_maps(inputs):
    xp = np.asarray(inputs["x_prompt"], np.float32)
    xs = np.asarray(inputs["x_sample"], np.float32)
    base = {}
    for name, shp in WEIGHT_SPECS:
        base[name] = np.ascontiguousarray(np.asarray(inputs[name], np.float32)).reshape(shp)
    base["ident"] = np.eye(128, dtype=np.float32)
    base["hc128"], base["hc65"] = hyena_consts()
    ptab = {True: hyena_pos_tables(True), False: hyena_pos_tables(False)}
    atab = {True: attn_consts(True), False: attn_consts(False)}
    maps = []
    for c in range(NCORES):
        m = dict(base)
        m["flag"] = np.full((1, 8), 1.0 if c < 4 else 0.0, np.float32)
        m["zT"], m["tneg"] = ptab[c < 4]
        m["ropecos"], m["ropesin"], m["prot"], m["amasks"] = atab[c < 4]
        if c < 4:
            m["x"] = np.ascontiguousarray(xp[c])
        else:
            m["x"] = np.ascontiguousarray(xs[2 * (c - 4):2 * (c - 4) + 2].reshape(T, D))
        maps.append(m)
    return maps


def kernel(**inputs):
    if "nc" not in _CACHE:
        _CACHE["nc"] = build_program()
    nc = _CACHE["nc"]
    maps = make_in_maps(inputs)
    res = run_bass_kernel_spmd(nc, maps, core_ids=list(range(NCORES)))
    ys = [np.asarray(r["y"], np.float32) for r in res.results]
    y_prompt = np.stack(ys[0:4], axis=0)
    y_sample = np.concatenate([y.reshape(2, 4096, D) for y in ys[4:8]], axis=0)
    return (y_prompt, y_sample)
```

```python
import math
from contextlib import ExitStack

import numpy as np
import concourse.bass as bass
import concourse.mybir as mybir
from concourse.bass_utils import run_bass_kernel_spmd

F32 = mybir.dt.float32
BF16 = mybir.dt.bfloat16
AF = mybir.ActivationFunctionType
ALU = mybir.AluOpType
AX = mybir.AxisListType

NCORES = 8
T = 8192
D = 1024
TT = 512
NT = T // TT
ENGS = ("pe", "act", "dve", "pool", "sp")


class Op:
    __slots__ = ("eng", "fn", "dma", "deps", "sig", "count", "sem", "semval")

    def __init__(self, eng, fn, dma):
        self.eng = eng
        self.fn = fn
        self.dma = dma
        self.deps = set()
        self.sig = False
        self.count = 0
        self.sem = None
        self.semval = 0


class Sched:
    NDMA_SEM = 12

    def __init__(self, nc, es):
        self.nc = nc
        self.streams = {k: [] for k in ENGS}
        self.w = {}
        self.r = {}
        self.dma_rr = {k: 0 for k in ENGS}
        self.dma_last = {}
        self.dma_cnt = {}
        self.esem = {k: es.enter_context(nc.semaphore("e_" + k)) for k in ENGS}
        self.dsem = {}
        for k in ("sp", "act", "pool"):
            for i in range(self.NDMA_SEM):
                self.dsem[(k, i)] = es.enter_context(nc.semaphore("d_%s%d" % (k, i)))
        self.ecount = {k: 0 for k in ENGS}
        self.seen = {k: {} for k in ENGS}
        self.lastc = {}

    def op(self, eng, fn, reads=(), writes=(), dma=False, deps=()):
        o = Op(eng, fn, dma)
        for d in deps:
            if d is not None:
                o.deps.add(d)
        for r in reads:
            lw = self.w.get(r)
            if lw is not None:
                o.deps.add(lw)
        for w_ in writes:
            lw = self.w.get(w_)
            if lw is not None:
                o.deps.add(lw)
            for rd in self.r.get(w_, ()):
                o.deps.add(rd)
        for r in reads:
            self.r.setdefault(r, []).append(o)
        for w_ in writes:
            self.w[w_] = o
            self.r[w_] = []
        if dma:
            i = self.dma_rr[eng]
            self.dma_rr[eng] = (i + 1) % self.NDMA_SEM
            key = (eng, i)
            prev = self.dma_last.get(key)
            if prev is not None:
                o.deps.add(prev)
            self.dma_last[key] = o
            self.dma_cnt[key] = self.dma_cnt.get(key, 0) + 1
            o.sem = key
            o.semval = 16 * self.dma_cnt[key]
        else:
            self.lastc[eng] = o
        o.deps.discard(o)
        self.streams[eng].append(o)
        return o

    def I(self, eng, name, *args, reads=(), writes=(), deps=(), **kw):
        return self.op(eng, lambda e: getattr(e, name)(*args, **kw), reads, writes, deps=deps)

    def dma(self, eng, out, in_, reads=(), writes=(), deps=(), **kw):
        return self.op(eng, lambda e: e.dma_start(out=out, in_=in_, **kw), reads, writes, dma=True, deps=deps)

    def barrier(self):
        dmas = list(self.dma_last.values())
        lastc = dict(self.lastc)
        for k in ENGS:
            o = Op(k, None, False)
            for kk, lo in lastc.items():
                if kk != k:
                    o.deps.add(lo)
            for d in dmas:
                o.deps.add(d)
            self.streams[k].append(o)
        self.w = {}
        self.r = {}

    def flush(self):
        nc = self.nc
        for k in ENGS:
            for o in self.streams[k]:
                for d in o.deps:
                    if not d.dma and (d.eng != o.eng or o.dma or o.eng != "pe"):
                        d.sig = True
        for k in ENGS:
            for o in self.streams[k]:
                if o.sig and o.count == 0:
                    self.ecount[k] += 1
                    o.count = self.ecount[k]
        streams = self.streams
        self.streams = {k: [] for k in ENGS}
        with nc.Block() as block:
            def run(k, e):
                seen = self.seen[k]
                for o in streams[k]:
                    need = {}
                    for d in o.deps:
                        if d.dma:
                            s, v = self.dsem[d.sem], d.semval
                        elif d.eng == k and not o.dma and k == "pe":
                            continue
                        else:
                            assert d.count > 0
                            s, v = self.esem[d.eng], d.count
                        if need.get(id(s), (None, 0))[1] < v:
                            need[id(s)] = (s, v)
                    for s, v in need.values():
                        if seen.get(id(s), 0) < v:
                            e.wait_ge(s, v)
                            seen[id(s)] = v
                    if o.fn is None:
                        continue
                    ins = o.fn(e)
                    if o.dma:
                        ins.then_inc(self.dsem[o.sem], 16)
                    elif o.sig:
                        ins.then_inc(self.esem[k], 1)

            @block.tensor
            def _(e):
                run("pe", e)

            @block.scalar
            def _(e):
                run("act", e)

            @block.vector
            def _(e):
                run("dve", e)

            @block.gpsimd
            def _(e):
                run("pool", e)

            @block.sync
            def _(e):
                run("sp", e)


class Ctx:
    pass


def sb(es, nc, name, shape, dt):
    return es.enter_context(nc.sbuf_tensor(name, list(shape), dt))


def load_weight_bf16(C, es, w_ap, K, N, name, eng_cast=("pool", "act")):
    nc, S = C.nc, C.S
    KC = K // 128
    wb = sb(es, nc, name, [128, KC, N], BF16)
    CW = min(N, 2048)
    with ExitStack() as es2:
        st = [sb(es2, nc, name + "_st%d" % i, [128, CW], F32) for i in range(2)]
        j = 0
        for kc in range(KC):
            for c0 in range(0, N, CW):
                cw = min(CW, N - c0)
                s = st[j % 2]
                S.dma("sp", s[:, 0:cw], w_ap[kc * 128:(kc + 1) * 128, c0:c0 + cw],
                      writes=[(name + "st", j % 2)])
                eng = eng_cast[j % len(eng_cast)]
                if eng == "act":
                    S.I("act", "activation", out=wb[:, kc, c0:c0 + cw], in_=s[:, 0:cw], func=AF.Copy,
                        reads=[(name + "st", j % 2)], writes=[(name, kc)])
                else:
                    S.I(eng, "tensor_copy", wb[:, kc, c0:c0 + cw], s[:, 0:cw],
                        reads=[(name + "st", j % 2)], writes=[(name, kc)])
                j += 1
        S.barrier()
        S.flush()
    return wb


def alloc_norm_tiles(C, es, tag, TK, nx=2, nT=2):
    nc = C.nc
    NS = TK // 128
    N = Ctx()
    N.TK, N.NS, N.tag = TK, NS, tag
    N.xt = [sb(es, nc, tag + "xt%d" % i, [128, NS, D], F32) for i in range(nx)]
    N.hn = sb(es, nc, tag + "hn", [128, NS, D], BF16)
    N.hnT = [sb(es, nc, tag + "hnT%d" % i, [128, 8, TK], BF16) for i in range(nT)]
    N.ss = [sb(es, nc, tag + "ss%d" % i, [128, NS], F32) for i in range(nx)]
    N.rstd = [sb(es, nc, tag + "rstd%d" % i, [128, NS], F32) for i in range(nx)]
    N.junk = sb(es, nc, tag + "junk", [128, D], BF16)
    return N


def norm_load(C, N, x_src, ti):
    S = C.S
    slot = ti % len(N.xt)
    S.dma("sp", N.xt[slot][:, :, :],
          x_src[ti * N.TK:(ti + 1) * N.TK, :].rearrange("(s p) d -> p s d", p=128),
          writes=[(N.tag + "xt", slot)])


def norm_tile(C, N, gcol, ti):
    S = C.S
    tag = N.tag
    slot = ti % len(N.xt)
    tslot = ti % len(N.hnT)
    xs = N.xt[slot]
    ss, rstd = N.ss[slot], N.rstd[slot]
    S.I("pool", "memset", ss[:, :], 0.0, writes=[(tag + "ss", slot)])
    for s in range(N.NS):
        S.I("act", "activation", out=N.junk[:, :], in_=xs[:, s, :], func=AF.Square,
            accum_out=ss[:, s:s + 1],
            reads=[(tag + "xt", slot)], writes=[(tag + "ss", slot), (tag + "junk", 0)])
    S.I("act", "activation", out=rstd[:, :], in_=ss[:, :], func=AF.Ln, scale=1.0 / D, bias=C.eps5[:, 0:1],
        reads=[(tag + "ss", slot)], writes=[(tag + "rstd", slot)])
    S.I("act", "activation", out=rstd[:, :], in_=rstd[:, :], func=AF.Exp, scale=-0.5,
        reads=[(tag + "rstd", slot)], writes=[(tag + "rstd", slot)])
    for s in range(N.NS):
        S.I("dve", "scalar_tensor_tensor", out=N.hn[:, s, :], in0=xs[:, s, :], scalar=rstd[:, s:s + 1],
            in1=gcol[:, :], op0=ALU.mult, op1=ALU.mult,
            reads=[(tag + "xt", slot), (tag + "rstd", slot)], writes=[(tag + "hn", s)])
    hT = N.hnT[tslot]
    for kc in range(8):
        b = C.psT[C.psT_i % len(C.psT)]
        C.psT_i += 1
        for s in range(N.NS):
            S.I("pe", "matmul", C.ps[b][:, s * 128:(s + 1) * 128], N.hn[:, s, kc * 128:(kc + 1) * 128],
                C.ident[:, :], start=True, stop=True,
                reads=[(tag + "hn", s)], writes=[("ps", b)])
        if kc % 2 == 0:
            S.I("act", "activation", out=hT[:, kc, :], in_=C.ps[b][:, 0:N.TK], func=AF.Copy,
                reads=[("ps", b)], writes=[(tag + "hnT", tslot, kc)])
        else:
            S.I("dve", "tensor_copy", hT[:, kc, :], C.ps[b][:, 0:N.TK],
                reads=[("ps", b)], writes=[(tag + "hnT", tslot, kc)])
    return xs, hT, slot, tslot


def load_bcast_row(C, es, row_ap, n, name):
    t = sb(es, C.nc, name, [128, n], F32)
    C.S.dma("sp", t[:, :], row_ap.partition_broadcast(128), writes=[(name, 0)])
    return t


def phase_inproj(C, x_src, g_row, w_ap, NOUT, u_dst, tag):
    nc, S = C.nc, C.S
    with ExitStack() as es:
        gcol = load_bcast_row(C, es, g_row, D, tag + "g")
        wb = load_weight_bf16(C, es, w_ap, D, NOUT, tag + "w")
        N = alloc_norm_tiles(C, es, tag, TT)
        ost = [sb(es, nc, tag + "ost%d" % i, [128, TT], F32) for i in range(4)]
        oi = 0
        norm_load(C, N, x_src, 0)
        nxt = norm_tile(C, N, gcol, 0)
        for ti in range(NT):
            if ti + 1 < NT:
                norm_load(C, N, x_src, ti + 1)
            xs, hT, slot, tslot = nxt
            for oc in range(NOUT // 128):
                if oc == (NOUT // 128) // 2 and ti + 1 < NT:
                    nxt = norm_tile(C, N, gcol, ti + 1)
                b = C.psM[C.psM_i % len(C.psM)]
                C.psM_i += 1
                for kc in range(8):
                    S.I("pe", "matmul", C.ps[b][:, :], wb[:, kc, oc * 128:(oc + 1) * 128], hT[:, kc, :],
                        start=(kc == 0), stop=(kc == 7),
                        reads=[(tag + "hnT", tslot, kc)], writes=[("ps", b)])
                o = oi % 4
                oi += 1
                if oc % 2 == 0:
                    S.I("act", "activation", out=ost[o][:, :], in_=C.ps[b][:, :], func=AF.Copy,
                        reads=[("ps", b)], writes=[(tag + "ost", o)])
                else:
                    S.I("dve", "tensor_copy", ost[o][:, :], C.ps[b][:, :],
                        reads=[("ps", b)], writes=[(tag + "ost", o)])
                S.dma("sp", u_dst[oc * 128:(oc + 1) * 128, ti * TT:(ti + 1) * TT], ost[o][:, :],
                      reads=[(tag + "ost", o)])
        S.barrier()
        S.flush()


def colvec(C, es, ap, R, NCOL, name):
    nc, S = C.nc, C.S
    NCH = NCOL // 128
    out = sb(es, nc, name, [128, NCH, R], F32)
    with ExitStack() as es2:
        rows = sb(es2, nc, name + "_rows", [R, NCOL], F32)
        S.dma("sp", rows[:, :], ap, writes=[(name + "rows", 0)])
        for c in range(NCH):
            b = C.psT[C.psT_i % len(C.psT)]
            C.psT_i += 1
            S.I("pe", "matmul", C.ps[b][:, 0:R], rows[0:R, c * 128:(c + 1) * 128], C.identf[0:R, 0:R],
                start=True, stop=True, reads=[(name + "rows", 0)], writes=[("ps", b)])
            S.I("dve", "tensor_copy", out[:, c, :], C.ps[b][:, 0:R], reads=[("ps", b)], writes=[(name, c)])
        S.barrier()
        S.flush()
    return out


def phase_conformer(C, u, ya_dst):
    nc, S = C.nc, C.S
    W = C.W
    tag = "B"
    HW = 15
    with ExitStack() as es:
        wcol = colvec(C, es, W["cv_dw_w"], 31, 512, "Bw")
        bcol = colvec(C, es, W["cv_dw_b"], 1, 512, "Bb")
        gcol = colvec(C, es, W["cv_ln_g"], 1, 512, "Bg")
        becol = colvec(C, es, W["cv_ln_b"], 1, 512, "Bbe")
        at = [sb(es, nc, "Bat%d" % i, [128, TT + 2 * HW], F32) for i in range(4)]
        gt = [sb(es, nc, "Bgt%d" % i, [128, TT + 2 * HW], F32) for i in range(4)]
        ht = [sb(es, nc, "Bht%d" % i, [128, TT + 2 * HW], F32) for i in range(4)]
        cv = [sb(es, nc, "Bcv%d" % i, [128, TT], F32) for i in range(4)]
        sq = [sb(es, nc, "Bsq%d" % i, [128, TT], F32) for i in range(2)]
        mean = sb(es, nc, "Bmean", [128, TT], F32)
        msq = sb(es, nc, "Bmsq", [128, TT], F32)
        rstd = sb(es, nc, "Brstd", [128, TT], F32)
        t1 = [sb(es, nc, "Bt1%d" % i, [128, TT], F32) for i in range(2)]
        yo = [sb(es, nc, "Byo%d" % i, [128, TT], BF16) for i in range(2)]
        for i in range(4):
            S.I("pool", "memset", at[i][:, :], 0.0, writes=[("Bat", i)])
            S.I("pool", "memset", gt[i][:, :], 0.0, writes=[("Bgt", i)])
        for ti in range(NT):
            t0 = ti * TT
            lo = max(t0 - HW, 0)
            hi = min(t0 + TT + HW, T)
            c0 = lo - (t0 - HW)
            c1 = c0 + (hi - lo)
            for c in range(4):
                if ti == NT - 1:
                    S.I("pool", "memset", at[c][:, c1:], 0.0, writes=[("Bat", c)])
                    S.I("pool", "memset", gt[c][:, c1:], 0.0, writes=[("Bgt", c)])
                S.dma("sp", at[c][:, c0:c1], u[c * 128:(c + 1) * 128, lo:hi], writes=[("Bat", c)])
                S.dma("sp", gt[c][:, c0:c1], u[512 + c * 128:512 + (c + 1) * 128, lo:hi], writes=[("Bgt", c)])
            for c in range(4):
                S.I("act", "activation", out=gt[c][:, :], in_=gt[c][:, :], func=AF.Sigmoid,
                    reads=[("Bgt", c)], writes=[("Bgt", c)])
                eng = "dve" if c < 2 else "pool"
                S.I(eng, "tensor_tensor", ht[c][:, :], at[c][:, :], gt[c][:, :], op=ALU.mult,
                    reads=[("Bat", c), ("Bgt", c)], writes=[("Bht", c)])
                if ti == NT // 2 - 1:
                    S.I(eng, "tensor_scalar", ht[c][:, TT + HW:], ht[c][:, TT + HW:], C.flagcol[:, 0:1], None,
                        op0=ALU.mult, reads=[("Bht", c)], writes=[("Bht", c)])
                if ti == NT // 2:
                    S.I(eng, "tensor_scalar", ht[c][:, 0:HW], ht[c][:, 0:HW], C.flagcol[:, 0:1], None,
                        op0=ALU.mult, reads=[("Bht", c)], writes=[("Bht", c)])
            for j in range(31):
                for c in range(4):
                    eng = "dve"
                    if j == 0:
                        S.I(eng, "tensor_scalar", cv[c][:, :], ht[c][:, 0:TT], wcol[:, c, 0:1], bcol[:, c, 0:1],
                            op0=ALU.mult, op1=ALU.add, reads=[("Bht", c)], writes=[("Bcv", c)])
                    else:
                        S.I(eng, "scalar_tensor_tensor", out=cv[c][:, :], in0=ht[c][:, j:j + TT],
                            scalar=wcol[:, c, j:j + 1], in1=cv[c][:, :], op0=ALU.mult, op1=ALU.add,
                            reads=[("Bht", c)], writes=[("Bcv", c)])
            b1 = C.psM[C.psM_i % len(C.psM)]
            C.psM_i += 1
            b2 = C.psM[C.psM_i % len(C.psM)]
            C.psM_i += 1
            for c in range(4):
                S.I("pe", "matmul", C.ps[b1][:, :], C.onesf[:, :], cv[c][:, :], start=(c == 0), stop=(c == 3),
                    reads=[("Bcv", c)], writes=[("ps", b1)])
            for c in range(4):
                S.I("act", "activation", out=sq[c % 2][:, :], in_=cv[c][:, :], func=AF.Square,
                    reads=[("Bcv", c)], writes=[("Bsq", c % 2)])
                S.I("pe", "matmul", C.ps[b2][:, :], C.onesf[:, :], sq[c % 2][:, :], start=(c == 0), stop=(c == 3),
                    reads=[("Bsq", c % 2)], writes=[("ps", b2)])
            S.I("act", "activation", out=mean[:, :], in_=C.ps[b1][:, :], func=AF.Copy, scale=1.0 / 512,
                reads=[("ps", b1)], writes=[("Bmean", 0)])
            S.I("dve", "tensor_tensor", msq[:, :], mean[:, :], mean[:, :], op=ALU.mult,
                reads=[("Bmean", 0)], writes=[("Bmsq", 0)])
            S.I("dve", "scalar_tensor_tensor", out=rstd[:, :], in0=C.ps[b2][:, :], scalar=1.0 / 512, in1=msq[:, :],
                op0=ALU.mult, op1=ALU.subtract, reads=[("ps", b2), ("Bmsq", 0)], writes=[("Brstd", 0)])
            S.I("act", "activation", out=rstd[:, :], in_=rstd[:, :], func=AF.Ln, bias=C.eps5[:, 0:1],
                reads=[("Brstd", 0)], writes=[("Brstd", 0)])
            S.I("act", "activation", out=rstd[:, :], in_=rstd[:, :], func=AF.Exp, scale=-0.5,
                reads=[("Brstd", 0)], writes=[("Brstd", 0)])
            for c in range(4):
                k = c % 2
                S.I("dve", "tensor_tensor", t1[k][:, :], cv[c][:, :], mean[:, :], op=ALU.subtract,
                    reads=[("Bcv", c), ("Bmean", 0)], writes=[("Bt1", k)])
                S.I("pool", "tensor_tensor", t1[k][:, :], t1[k][:, :], rstd[:, :], op=ALU.mult,
                    reads=[("Bt1", k), ("Brstd", 0)], writes=[("Bt1", k)])
                S.I("act", "activation", out=yo[k][:, :], in_=t1[k][:, :], func=AF.Silu,
                    scale=gcol[:, c, 0:1], bias=becol[:, c, 0:1],
                    reads=[("Bt1", k)], writes=[("Byo", k)])
                S.dma("act", ya_dst[c * 128:(c + 1) * 128, t0:t0 + TT], yo[k][:, :], reads=[("Byo", k)])
        S.barrier()
        S.flush()


HC128_COLS = {}
HC65_COLS = {}


def hyena_consts():
    p = np.arange(128)
    n1 = (p % 64)[:, None].astype(np.float64)
    k1 = np.arange(65)[None, :].astype(np.float64)
    th = 2 * np.pi * n1 * k1 / 128.0
    F1cat = np.concatenate([np.cos(th), -np.sin(th)], 1)
    n2 = (p % 64)[:, None].astype(np.float64)
    ph = 2 * np.pi * n2 * k1 / 8192.0
    Tc2 = np.concatenate([np.cos(ph), np.cos(ph)], 1)
    Ts2 = np.concatenate([np.sin(ph), np.sin(ph)], 1)
    q64 = np.arange(64)
    psi = 2 * np.pi * (p % 64)[:, None] * q64[None, :] / 64.0
    Mc = np.cos(psi)
    Ms = np.sin(psi)
    rhsA = np.concatenate([Mc, Ms], 1)
    rhsB = np.concatenate([-Ms, Mc], 1)
    sg = ((-1.0) ** np.arange(65))[None, :].repeat(128, 0)
    sgn2 = np.concatenate([sg, sg], 1)
    parts = [("F1cat", F1cat), ("Tc2", Tc2), ("Ts2", Ts2), ("Mc", Mc), ("Ms", Ms), ("nMs", -Ms),
             ("rhsA", rhsA), ("rhsB", rhsB), ("sgn2", sgn2)]
    off = 0
    for nm, a in parts:
        HC128_COLS[nm] = (off, a.shape[1])
        off += a.shape[1]
    hc128 = np.concatenate([a for _, a in parts], 1).astype(np.float32)
    kk = np.arange(65)[:, None].astype(np.float64)
    col = np.arange(128)[None, :]
    phi = 2 * np.pi * kk * (col % 64) / 8192.0
    Tci = np.cos(phi)
    Tsi = np.sin(phi)
    wk = np.full((65, 1), 2.0)
    wk[0, 0] = 1.0
    wk[64, 0] = 1.0
    n1c = np.arange(64)[None, :].astype(np.float64)
    thi = 2 * np.pi * kk * n1c / 128.0
    gc = wk * np.cos(thi) / 8192.0
    gs = -wk * np.sin(thi) / 8192.0
    zz = np.zeros((65, 64))
    parts = [("Tci", Tci), ("Tsi", Tsi), ("Gc_a", np.concatenate([gc, zz], 1)), ("Gs_a", np.concatenate([gs, zz], 1)),
             ("Gc_b", np.concatenate([zz, gc], 1)), ("Gs_b", np.concatenate([zz, gs], 1))]
    off = 0
    for nm, a in parts:
        HC65_COLS[nm] = (off, a.shape[1])
        off += a.shape[1]
    hc65 = np.concatenate([a for _, a in parts], 1).astype(np.float32)
    return hc128, hc65


def hyena_pos_tables(is_prompt):
    L = 8192 if is_prompt else 4096
    q = np.arange(8192)
    n2 = q // 128
    hp = q % 128
    half = hp // 64
    n1 = hp % 64
    pl = n1 * 64 + n2
    pos = pl + (4096 * half if is_prompt else 0)
    t = pos.astype(np.float64) / (L - 1)
    bands = 16
    f = np.linspace(1e-4, bands - 1, bands)
    w = 2 * np.pi * pos.astype(np.float64) / L
    fw = w[None, :] * f[:, None]
    zT = np.concatenate([t[None, :], np.cos(fw), -np.sin(fw)], 0).astype(np.float32)
    tn = -t.reshape(64, 128).T.copy()
    if not is_prompt:
        tn[64:, :] = -1e4
    tnb = tn.copy()
    tnb[0, 0] = -1e4
    return zT, np.concatenate([tn, tnb], 1).astype(np.float32)


def hc(C, name, rows=128):
    off, n = (HC128_COLS if rows == 128 else HC65_COLS)[name]
    t = C.hc128 if rows == 128 else C.hc65
    return t[0:rows, off:off + n]


def phase_shortconv(C, u, uh):
    nc, S = C.nc, C.S
    W = C.W
    with ExitStack() as es:
        wcol = colvec(C, es, W["hy_short_w"], 3, 1536, "Sw")
        bcol = colvec(C, es, W["hy_short_b"], 1, 1536, "Sb")
        it = [sb(es, nc, "Sit%d" % i, [128, TT + 2], F32) for i in range(3)]
        cv = [sb(es, nc, "Scv%d" % i, [128, TT], F32) for i in range(2)]
        ot = [sb(es, nc, "Sot%d" % i, [128, 4, 128], F32) for i in range(2)]
        k = 0
        for c in range(12):
            for ti in range(NT):
                t0 = ti * TT
                lo = max(t0 - 1, 0)
                hi = min(t0 + TT + 1, T)
                c0 = lo - (t0 - 1)
                c1 = c0 + (hi - lo)
                i3 = k % 3
                i2 = k % 2
                k += 1
                if ti == 0:
                    S.I("pool", "memset", it[i3][:, 0:1], 0.0, writes=[("Sit", i3)])
                if ti == NT - 1:
                    S.I("pool", "memset", it[i3][:, TT + 1:TT + 2], 0.0, writes=[("Sit", i3)])
                S.dma("sp", it[i3][:, c0:c1], u[1024 + c * 128:1024 + (c + 1) * 128, lo:hi], writes=[("Sit", i3)])
                if ti == NT // 2 - 1:
                    S.I("pool", "tensor_scalar", it[i3][:, TT + 1:TT + 2], it[i3][:, TT + 1:TT + 2],
                        C.flagcol[:, 0:1], None, op0=ALU.mult, reads=[("Sit", i3)], writes=[("Sit", i3)])
                if ti == NT // 2:
                    S.I("pool", "tensor_scalar", it[i3][:, 0:1], it[i3][:, 0:1],
                        C.flagcol[:, 0:1], None, op0=ALU.mult, reads=[("Sit", i3)], writes=[("Sit", i3)])
                S.I("dve", "tensor_scalar", cv[i2][:, :], it[i3][:, 0:TT], wcol[:, c, 0:1], bcol[:, c, 0:1],
                    op0=ALU.mult, op1=ALU.add, reads=[("Sit", i3)], writes=[("Scv", i2)])
                for j in (1, 2):
                    S.I("dve", "scalar_tensor_tensor", out=cv[i2][:, :], in0=it[i3][:, j:j + TT],
                        scalar=wcol[:, c, j:j + 1], in1=cv[i2][:, :], op0=ALU.mult, op1=ALU.add,
                        reads=[("Sit", i3)], writes=[("Scv", i2)])
                b = C.psT[C.psT_i % len(C.psT)]
                C.psT_i += 1
                for s in range(4):
                    S.I("pe", "matmul", C.ps[b][:, s * 128:(s + 1) * 128], cv[i2][:, s * 128:(s + 1) * 128],
                        C.identf[:, :], start=True, stop=True, reads=[("Scv", i2)], writes=[("ps", b)])
                S.I("act", "activation", out=ot[i2][:, :, :], in_=C.ps[b][:, :].rearrange("p (s c) -> p s c", s=4),
                    func=AF.Copy, reads=[("ps", b)], writes=[("Sot", i2)])
                S.dma("act", uh[t0:t0 + TT, c * 128:(c + 1) * 128].rearrange("(s p) c -> p s c", p=128),
                      ot[i2][:, :, :], reads=[("Sot", i2)])
        S.barrier()
        S.flush()


def alloc_fft_tiles(C, es, pfx):
    nc = C.nc
    Fq = Ctx()
    Fq.A2 = sb(es, nc, pfx + "A2", [64, 16, 130], F32)
    Fq.B = sb(es, nc, pfx + "B", [64, 2, 16, 65], F32)
    Fq.tmpA = sb(es, nc, pfx + "tmpA", [128, 2080], F32)
    Fq.tmpB = sb(es, nc, pfx + "tmpB", [128, 2080], F32)
    Fq.pfx = pfx
    return Fq


def fft_fwd_group(C, Fq, src, src_tok, ch0, X, X_tok):
    fft_s1(C, Fq, src, src_tok, ch0)
    fft_rest(C, Fq, X, X_tok)


def fft_s1(C, Fq, src, src_tok, ch0):
    S = C.S
    pfx = Fq.pfx
    F1 = hc(C, "F1cat")
    slot = 0
    while slot < 16:
        nb = 2
        b = C.psM[C.psM_i % len(C.psM)]
        C.psM_i += 1
        for s in range(nb):
            i = slot + s
            h, chl = i // 8, i % 8
            S.I("pe", "matmul", C.ps[b][0:64, s * 256:s * 256 + 130],
                src[h * 64:(h + 1) * 64, :, ch0 + chl], F1[h * 64:(h + 1) * 64, :],
                start=True, stop=True, reads=[src_tok], writes=[("ps", b)])
        S.I("act", "activation", out=Fq.A2[:, slot:slot + nb, :],
            in_=C.ps[b][0:64, :].rearrange("p (s c) -> p s c", s=2)[:, :, 0:130], func=AF.Copy,
            reads=[("ps", b)], writes=[(pfx + "A2", 0)])
        slot += nb


def fft_rest(C, Fq, X, X_tok):
    S = C.S
    pfx = Fq.pfx
    Tc = hc(C, "Tc2")[0:64, :].unsqueeze(1).broadcast_to([64, 16, 130])
    Ts = hc(C, "Ts2")[0:64, :].unsqueeze(1).broadcast_to([64, 16, 130])
    P1 = Fq.tmpA[0:64, 0:2080].rearrange("p (a c) -> p a c", a=16)
    P2 = Fq.tmpB[0:64, 0:2080].rearrange("p (a c) -> p a c", a=16)
    S.I("dve", "tensor_tensor", P1, Fq.A2[:, :, :], Tc, op=ALU.mult,
        reads=[(pfx + "A2", 0)], writes=[(pfx + "tmpA", 0)])
    S.I("pool", "tensor_tensor", P2, Fq.A2[:, :, :], Ts, op=ALU.mult,
        reads=[(pfx + "A2", 0)], writes=[(pfx + "tmpB", 0)])
    S.I("dve", "tensor_tensor", Fq.B[:, 0, :, :], P1[:, :, 0:65], P2[:, :, 65:130], op=ALU.add,
        reads=[(pfx + "tmpA", 0), (pfx + "tmpB", 0)], writes=[(pfx + "B", 0)])
    S.I("pool", "tensor_tensor", Fq.B[:, 1, :, :], P1[:, :, 65:130], P2[:, :, 0:65], op=ALU.subtract,
        reads=[(pfx + "tmpA", 0), (pfx + "tmpB", 0)], writes=[(pfx + "B", 1)])
    if C.dbg.get("ffstop") == 2:
        return
    Mc, Ms, nMs = hc(C, "Mc")[0:64, :], hc(C, "Ms")[0:64, :], hc(C, "nMs")[0:64, :]
    for q in range(4):
        Br = Fq.B[:, 0, 4 * q:4 * q + 4, :]
        Bi = Fq.B[:, 1, 4 * q:4 * q + 4, :]
        b1 = C.psM[C.psM_i % len(C.psM)]
        C.psM_i += 1
        b2 = C.psM[C.psM_i % len(C.psM)]
        C.psM_i += 1
        S.I("pe", "matmul", C.ps[b1][0:64, 0:260], Mc, Br, start=True, stop=False,
            reads=[(pfx + "B", 0)], writes=[("ps", b1)])
        S.I("pe", "matmul", C.ps[b1][0:64, 0:260], Ms, Bi, start=False, stop=True,
            reads=[(pfx + "B", 1)], writes=[("ps", b1)])
        S.I("pe", "matmul", C.ps[b2][0:64, 0:260], Mc, Bi, start=True, stop=False,
            reads=[(pfx + "B", 1)], writes=[("ps", b2)])
        S.I("pe", "matmul", C.ps[b2][0:64, 0:260], nMs, Br, start=False, stop=True,
            reads=[(pfx + "B", 0)], writes=[("ps", b2)])
        h, c4 = q // 2, 4 * (q % 2)
        S.I("act", "activation", out=X[:, 0, h, c4:c4 + 4, :],
            in_=C.ps[b1][0:64, 0:260].rearrange("p (a c) -> p a c", a=4), func=AF.Copy,
            reads=[("ps", b1)], writes=[X_tok])
        S.I("act", "activation", out=X[:, 1, h, c4:c4 + 4, :],
            in_=C.ps[b2][0:64, 0:260].rearrange("p (a c) -> p a c", a=4), func=AF.Copy,
            reads=[("ps", b2)], writes=[X_tok])


def phase_filters(C, Gd, zT_in, tneg_in):
    nc, S = C.nc, C.S
    W = C.W
    with ExitStack() as es:
        fcol = sb(es, nc, "Ffcol", [64, 3], F32)
        bcol = sb(es, nc, "Fbcol", [64, 3], F32)
        f8 = sb(es, nc, "Ff8", [64, 3], F32)
        b8 = sb(es, nc, "Fb8", [64, 3], F32)
        f4 = sb(es, nc, "Ff4", [64, 3], F32)
        b4 = sb(es, nc, "Fb4", [64, 3], F32)
        with ExitStack() as es2:
            rows = sb(es2, nc, "Frows", [6, 64], F32)
            S.dma("sp", rows[0:3, :], W["hy_freq"], writes=[("Frows", 0)])
            for i, nm in enumerate(("hy_b1", "hy_b2", "hy_b3")):
                S.dma("sp", rows[3 + i:4 + i, :], W[nm], writes=[("Frows", 0)])
            b = C.psT[C.psT_i % len(C.psT)]
            C.psT_i += 1
            S.I("pe", "matmul", C.ps[b][0:64, 0:6], rows[0:6, 0:64], C.identf[0:6, 0:6], start=True, stop=True,
                reads=[("Frows", 0)], writes=[("ps", b)])
            S.I("dve", "tensor_copy", fcol[:, :], C.ps[b][0:64, 0:3], reads=[("ps", b)], writes=[("Ffcol", 0)])
            S.I("dve", "tensor_copy", bcol[:, :], C.ps[b][0:64, 3:6], reads=[("ps", b)], writes=[("Fbcol", 0)])
            S.I("dve", "tensor_tensor", bcol[:, :], bcol[:, :], fcol[:, :], op=ALU.mult,
                reads=[("Ffcol", 0), ("Fbcol", 0)], writes=[("Fbcol", 0)])
            S.I("dve", "tensor_scalar", f8[:, :], fcol[:, :], 0.125, None, op0=ALU.mult,
                reads=[("Ffcol", 0)], writes=[("Ff8", 0)])
            S.I("dve", "tensor_scalar", b8[:, :], bcol[:, :], 0.125, None, op0=ALU.mult,
                reads=[("Fbcol", 0)], writes=[("Fb8", 0)])
            S.I("dve", "tensor_scalar", f4[:, :], fcol[:, :], 0.25, None, op0=ALU.mult,
                reads=[("Ffcol", 0)], writes=[("Ff4", 0)])
            S.I("dve", "tensor_scalar", b4[:, :], bcol[:, :], 0.25, None, op0=ALU.mult,
                reads=[("Fbcol", 0)], writes=[("Fb4", 0)])
            S.barrier()
            S.flush()
        w1 = sb(es, nc, "Fw1", [33, 64], F32)
        w2 = sb(es, nc, "Fw2", [64, 64], F32)
        w3 = sb(es, nc, "Fw3", [64, 64], F32)
        w4 = sb(es, nc, "Fw4", [64, 2048], F32)
        S.dma("sp", w1[:, :], W["hy_w1"])
        S.dma("sp", w2[:, :], W["hy_w2"])
        S.dma("sp", w3[:, :], W["hy_w3"])
        S.dma("sp", w4[:, :], W["hy_w4"])
        absd = sb(es, nc, "Fabsd", [128, 2048], F32)
        S.dma("sp", absd[:, :], W["hy_decay"].partition_broadcast(128), writes=[("Fabsd", 0)])
        S.I("act", "activation", out=absd[:, :], in_=absd[:, :], func=AF.Abs,
            reads=[("Fabsd", 0)], writes=[("Fabsd", 0)])
        tneg = sb(es, nc, "Ftneg", [128, 128], F32)
        S.dma("sp", tneg[:, :], tneg_in)
        negpi = sb(es, nc, "Fnegpi", [128, 1], F32)
        S.I("pool", "memset", negpi[:, :], -math.pi)
        eps6 = sb(es, nc, "Feps6", [128, 1], F32)
        S.I("pool", "memset", eps6[:, :], 1e-6)
        h3T = sb(es, nc, "Fh3T", [64, 8192], F32)
        S.barrier()
        S.flush()
        if C.dbg.get("fstop") == 1:
            return
        with ExitStack() as es2:
            zt = [sb(es2, nc, "Fzt%d" % i, [33, 512], F32) for i in range(2)]
            ha = [sb(es2, nc, "Fha%d" % i, [64, 512], F32) for i in range(2)]
            hb = [sb(es2, nc, "Fhb%d" % i, [64, 512], F32) for i in range(2)]
            hc_ = [sb(es2, nc, "Fhc%d" % i, [64, 512], F32) for i in range(2)]
            for blk in range(16):
                i2 = blk % 2
                S.dma("sp", zt[i2][:, :], zT_in[:, blk * 512:(blk + 1) * 512], writes=[("Fzt", i2)])
                cur = zt[i2][0:33, :]
                cur_tok = ("Fzt", i2)
                for l, wl in enumerate((w1, w2, w3)):
                    K = 33 if l == 0 else 64
                    b = C.psM[C.psM_i % len(C.psM)]
                    C.psM_i += 1
                    S.I("pe", "matmul", C.ps[b][0:64, :], wl[0:K, :], cur, start=True, stop=True,
                        reads=[cur_tok], writes=[("ps", b)])
                    sa, sb_ = ha[i2], hc_[i2]
                    S.I("act", "activation", out=sa[:, :], in_=C.ps[b][0:64, :], func=AF.Sin,
                        scale=f8[:, l:l + 1], bias=b8[:, l:l + 1], reads=[("ps", b)], writes=[("Fha", i2)])
                    S.I("act", "activation", out=sb_[:, :], in_=C.ps[b][0:64, :], func=AF.Sin,
                        scale=f4[:, l:l + 1], bias=b4[:, l:l + 1], reads=[("ps", b)], writes=[("Fhc", i2)])
                    S.I("dve", "tensor_tensor", sa[:, :], sa[:, :], sa[:, :], op=ALU.mult,
                        reads=[("Fha", i2)], writes=[("Fha", i2)])
                    S.I("dve", "tensor_scalar", sa[:, :], sa[:, :], -2.0, 1.0, op0=ALU.mult, op1=ALU.add,
                        reads=[("Fha", i2)], writes=[("Fha", i2)])
                    S.I("dve", "scalar_tensor_tensor", out=sa[:, :], in0=sb_[:, :], scalar=2.0, in1=sa[:, :],
                        op0=ALU.mult, op1=ALU.mult, reads=[("Fha", i2), ("Fhc", i2)], writes=[("Fha", i2)])
                    S.I("dve", "tensor_tensor", sb_[:, :], sb_[:, :], sb_[:, :], op=ALU.mult,
                        reads=[("Fhc", i2)], writes=[("Fhc", i2)])
                    S.I("dve", "tensor_scalar", sb_[:, :], sb_[:, :], -2.0, 1.0, op0=ALU.mult, op1=ALU.add,
                        reads=[("Fhc", i2)], writes=[("Fhc", i2)])
                    dst = h3T[:, blk * 512:(blk + 1) * 512] if l == 2 else hb[i2][:, :]
                    dtok = ("Fh3T", blk) if l == 2 else ("Fhb", i2)
                    S.I("dve", "scalar_tensor_tensor", out=dst, in0=sa[:, :], scalar=2.0, in1=sb_[:, :],
                        op0=ALU.mult, op1=ALU.mult, reads=[("Fha", i2), ("Fhc", i2)], writes=[dtok])
                    cur = hb[i2][:, :]
                    cur_tok = ("Fhb", i2)
            S.barrier()
            S.flush()
        if C.dbg.get("fstop") == 2:
            return
        fw = sb(es, nc, "Ffw", [128, 64, 128], F32)
        bw = sb(es, nc, "Fbw", [128, 64, 128], F32)
        win = [sb(es, nc, "Fwin%d" % i, [128, 4, 128], F32) for i in range(2)]
        sq = [sb(es, nc, "Fsq%d" % i, [128, 4, 128], F32) for i in range(2)]
        scl = sb(es, nc, "Fscl", [128, 128], F32)
        Fq = alloc_fft_tiles(C, es, "Fq")
        XF = sb(es, nc, "FXF", [64, 2, 2, 8, 65], F32)
        XB = sb(es, nc, "FXB", [64, 2, 2, 8, 65], F32)
        Go = [[sb(es, nc, "FGo%d_%d" % (w_, 0), [64, 2, 8, 65], F32)] * 2 for w_ in range(3)]
        t65 = [Fq.tmpA[0:64, 0:1040].rearrange("p (r a c) -> p r a c", r=2, a=8),
               Fq.tmpB[0:64, 0:1040].rearrange("p (r a c) -> p r a c", r=2, a=8)]
        nflag = sb(es, nc, "Fnflag", [128, 1], F32)
        S.I("dve", "tensor_scalar", nflag[:, :], C.flagcol[:, 0:1], -1.0, None, op0=ALU.mult)
        wi = 0
        gi = 0
        for o in range(2):
            for cc in range(4):
                bss = C.psT[0]
                nmm = 0
                for d, dst in ((0, fw), (1, bw)):
                    colbase = (o * 2 + d) * 512 + cc * 128
                    for nb in range(16):
                        b = C.psM[C.psM_i % len(C.psM)]
                        C.psM_i += 1
                        wv = win[wi % 2]
                        wt = ("Fwin", wi % 2)
                        sv = sq[wi % 2]
                        st_ = ("Fsq", wi % 2)
                        wi += 1
                        for s in range(4):
                            n2 = nb * 4 + s
                            S.I("pe", "matmul", C.ps[b][:, s * 128:(s + 1) * 128], h3T[0:64, n2 * 128:(n2 + 1) * 128],
                                w4[0:64, colbase:colbase + 128], start=True, stop=True,
                                reads=[("Fh3T", n2 // 4)], writes=[("ps", b)])
                            S.I("act", "activation", out=wv[:, s, :], in_=absd[:, colbase:colbase + 128], func=AF.Exp,
                                scale=tneg[:, d * 64 + n2:d * 64 + n2 + 1], reads=[("Fabsd", 0)], writes=[wt])
                        S.I("dve", "tensor_tensor", dst[:, nb * 4:(nb + 1) * 4, :],
                            C.ps[b][:, :].rearrange("p (s c) -> p s c", s=4), wv[:, :, :], op=ALU.mult,
                            reads=[("ps", b), wt], writes=[("Ffilt", d, nb)])
                        S.I("pool", "tensor_tensor", sv[:, :, :], dst[:, nb * 4:(nb + 1) * 4, :],
                            dst[:, nb * 4:(nb + 1) * 4, :], op=ALU.mult, reads=[("Ffilt", d, nb)], writes=[st_])
                        for s in range(4):
                            S.I("pe", "matmul", C.ps[bss][:, 0:128], C.onesf[:, :], sv[:, s, :],
                                start=(nmm == 0), stop=(nmm == 127), reads=[st_], writes=[("ps", bss)])
                            nmm += 1
                S.I("act", "activation", out=scl[:, :], in_=C.ps[bss][:, 0:128], func=AF.Ln, bias=eps6[:, 0:1],
                    reads=[("ps", bss)], writes=[("Fscl", 0)])
                S.I("act", "activation", out=scl[:, :], in_=scl[:, :], func=AF.Exp, scale=-0.5,
                    reads=[("Fscl", 0)], writes=[("Fscl", 0)])
                sclb = scl[:, :].unsqueeze(1).broadcast_to([128, 64, 128])
                allf = [("Ffilt", 0, nb) for nb in range(16)]
                allb = [("Ffilt", 1, nb) for nb in range(16)]
                S.I("dve", "tensor_tensor", fw[:, :, :], fw[:, :, :], sclb, op=ALU.mult,
                    reads=[("Fscl", 0)] + allf, writes=allf + [("Ffw", 0)])
                S.I("pool", "tensor_tensor", bw[:, :, :], bw[:, :, :], sclb, op=ALU.mult,
                    reads=[("Fscl", 0)] + allb, writes=allb + [("Fbw", 0)])
                if C.dbg.get("fstop") == 3:
                    S.barrier()
                    S.flush()
                    return
                sg = hc(C, "sgn2")[0:64, :].rearrange("p (r c) -> p r c", r=2).unsqueeze(2).broadcast_to([64, 2, 8, 65])
                for g in range(16):
                    ch0 = g * 8
                    if g == 0:
                        fft_s1(C, Fq, fw, ("Ffw", 0), ch0)
                    fft_rest(C, Fq, XF, ("FXF", 0))
                    fft_s1(C, Fq, bw, ("Fbw", 0), ch0)
                    fft_rest(C, Fq, XB, ("FXB", 0))
                    if g + 1 < 16:
                        fft_s1(C, Fq, fw, ("Ffw", 0), ch0 + 8)
                    if C.dbg.get("fstop") == 4:
                        S.barrier()
                        S.flush()
                        return
                    k2 = 0
                    Gaa, Gab, Gba = Go[0][k2], Go[1][k2], Go[2][k2]
                    Fl, Fh = XF[:, :, 0, :, :], XF[:, :, 1, :, :]
                    Bl, Bh = XB[:, :, 0, :, :], XB[:, :, 1, :, :]
                    S.I("dve", "tensor_tensor", Gaa[:, 0, :, :], Fl[:, 0, :, :], Bl[:, 0, :, :], op=ALU.add,
                        reads=[("FXF", 0), ("FXB", 0)], writes=[("FGo", 0, k2)])
                    S.I("dve", "tensor_tensor", Gaa[:, 1, :, :], Fl[:, 1, :, :], Bl[:, 1, :, :], op=ALU.subtract,
                        reads=[("FXF", 0), ("FXB", 0)], writes=[("FGo", 0, k2)])
                    S.I("pool", "tensor_tensor", t65[0], Fl, sg, op=ALU.mult,
                        reads=[("FXF", 0)], writes=[("FqtmpA", 0)])
                    S.I("pool", "tensor_tensor", t65[0], t65[0], Fh, op=ALU.add,
                        reads=[("FXF", 0), ("FqtmpA", 0)], writes=[("FqtmpA", 0)])
                    S.I("pool", "tensor_scalar", Gba[:, :, :, :], t65[0], C.flagcol[0:64, 0:1], None,
                        op0=ALU.mult, reads=[("FqtmpA", 0)], writes=[("FGo", 2, k2)])
                    S.I("dve", "tensor_tensor", t65[1], Bl, sg, op=ALU.mult,
                        reads=[("FXB", 0)], writes=[("FqtmpB", 0)])
                    S.I("dve", "tensor_tensor", t65[1], t65[1], Bh, op=ALU.add,
                        reads=[("FXB", 0), ("FqtmpB", 0)], writes=[("FqtmpB", 0)])
                    S.I("dve", "tensor_scalar", Gab[:, 0, :, :], t65[1][:, 0, :, :], C.flagcol[0:64, 0:1], None,
                        op0=ALU.mult, reads=[("FqtmpB", 0)], writes=[("FGo", 1, k2)])
                    S.I("dve", "tensor_scalar", Gab[:, 1, :, :], t65[1][:, 1, :, :], nflag[0:64, 0:1], None,
                        op0=ALU.mult, reads=[("FqtmpB", 0)], writes=[("FGo", 1, k2)])
                    for w_ in range(3):
                        S.dma("sp", Gd[o, cc, w_, :, :, g * 8:(g + 1) * 8, :], Go[w_][k2][:, :, :, :],
                              reads=[("FGo", w_, k2)])
        S.barrier()
        S.flush()


def phase_hyena(C, uh, Gd, yb):
    nc, S = C.nc, C.S
    W = C.W
    with ExitStack() as es:
        skb = sb(es, nc, "Hskb", [128, 2, 512], F32)
        for o in range(2):
            S.dma("sp", skb[:, o, :], W["hy_skip"][o:o + 1, :].partition_broadcast(128), writes=[("Hskb", 0)])
        z = sb(es, nc, "Hz", [128, 64, 128], F32)
        xg = sb(es, nc, "Hxg", [128, 64, 128], F32)
        Fq = alloc_fft_tiles(C, es, "Hq")
        X = sb(es, nc, "HX", [64, 2, 2, 8, 65], F32)
        Y = sb(es, nc, "HY", [64, 2, 2, 8, 65], F32)
        Gt = [[sb(es, nc, "HG%d_%d" % (w_, i), [64, 2, 8, 65], F32) for i in range(2)] for w_ in range(3)]
        PQ = [sb(es, nc, "HPQ%d" % i, [64, 2, 8, 65], F32) for i in range(4)]
        C2 = sb(es, nc, "HC2", [65, 16, 128], F32)
        Dt = sb(es, nc, "HDt", [65, 2, 16, 64], F32)
        gt = [sb(es, nc, "Hgt%d" % i, [128, 64, 8], F32) for i in range(2)]
        gi = 0
        ztoks = [("Hz", g) for g in range(16)]
        for cc in range(4):
            for q4 in range(16):
                S.dma("sp", z[:, q4 * 4:(q4 + 1) * 4, :],
                      uh[:, 1024 + cc * 128:1024 + (cc + 1) * 128].rearrange("(p n) c -> p n c", n=64)[:, q4 * 4:(q4 + 1) * 4, :],
                      writes=ztoks)
            for o in range(2):
                for q4 in range(16):
                    S.dma("sp", xg[:, q4 * 4:(q4 + 1) * 4, :],
                          uh[:, o * 512 + cc * 128:o * 512 + (cc + 1) * 128].rearrange("(p n) c -> p n c", n=64)[:, q4 * 4:(q4 + 1) * 4, :],
                          writes=[("Hxg", 0)])
                for g in range(16):
                    ch0 = g * 8
                    k2 = gi % 2
                    gi += 1
                    for w_ in range(3):
                        S.dma("sp", Gt[w_][k2][:, :, :, :], Gd[o, cc, w_, :, :, g * 8:(g + 1) * 8, :],
                              writes=[("HG", w_, k2)])
                    if g == 0:
                        fft_s1(C, Fq, z, ("Hz", 0), 0)
                    fft_rest(C, Fq, X, ("HX", 0))
                    if g + 1 < 16:
                        fft_s1(C, Fq, z, ("Hz", g + 1), ch0 + 8)
                    Gaa, Gab, Gba = Gt[0][k2], Gt[1][k2], Gt[2][k2]
                    Xa, Xb = X[:, :, 0, :, :], X[:, :, 1, :, :]

                    def bc(Gx, r):
                        return Gx[:, r:r + 1, :, :].broadcast_to([64, 2, 8, 65])
                    for half, (GA, GB) in enumerate(((Gaa, Gab), (Gba, Gaa))):
                        ta = ("HG", (0, 1, 2)[[Gaa, Gab, Gba].index(GA)], k2)
                        tb = ("HG", (0, 1, 2)[[Gaa, Gab, Gba].index(GB)], k2)
                        S.I("dve", "tensor_tensor", PQ[0][:, :, :, :], Xa, bc(GA, 0), op=ALU.mult,
                            reads=[("HX", 0), ta], writes=[("HPQ", 0)])
                        S.I("dve", "tensor_tensor", PQ[1][:, :, :, :], Xa, bc(GA, 1), op=ALU.mult,
                            reads=[("HX", 0), ta], writes=[("HPQ", 1)])
                        S.I("pool", "tensor_tensor", PQ[2][:, :, :, :], Xb, bc(GB, 0), op=ALU.mult,
                            reads=[("HX", 0), tb], writes=[("HPQ", 2)])
                        S.I("pool", "tensor_tensor", PQ[3][:, :, :, :], Xb, bc(GB, 1), op=ALU.mult,
                            reads=[("HX", 0), tb], writes=[("HPQ", 3)])
                        S.I("pool", "tensor_tensor", PQ[0][:, :, :, :], PQ[0][:, :, :, :], PQ[2][:, :, :, :], op=ALU.add,
                            reads=[("HPQ", 0), ("HPQ", 2)], writes=[("HPQ", 0)])
                        S.I("pool", "tensor_tensor", PQ[1][:, :, :, :], PQ[1][:, :, :, :], PQ[3][:, :, :, :], op=ALU.add,
                            reads=[("HPQ", 1), ("HPQ", 3)], writes=[("HPQ", 1)])
                        S.I("dve", "tensor_tensor", Y[:, 0, half, :, :], PQ[0][:, 0, :, :], PQ[1][:, 1, :, :],
                            op=ALU.subtract, reads=[("HPQ", 0), ("HPQ", 1)], writes=[("HY", half)])
                        S.I("dve", "tensor_tensor", Y[:, 1, half, :, :], PQ[0][:, 1, :, :], PQ[1][:, 0, :, :],
                            op=ALU.add, reads=[("HPQ", 0), ("HPQ", 1)], writes=[("HY", half)])
                    rhsA, rhsB = hc(C, "rhsA")[0:64, :], hc(C, "rhsB")[0:64, :]
                    for s0 in range(0, 16, 4):
                        b = C.psM[C.psM_i % len(C.psM)]
                        C.psM_i += 1
                        for s in range(4):
                            i = s0 + s
                            h, chl = i // 8, i % 8
                            S.I("pe", "matmul", C.ps[b][0:65, s * 128:(s + 1) * 128], Y[:, 0, h, chl, :], rhsA,
                                start=True, stop=False, reads=[("HY", h)], writes=[("ps", b)])
                            S.I("pe", "matmul", C.ps[b][0:65, s * 128:(s + 1) * 128], Y[:, 1, h, chl, :], rhsB,
                                start=False, stop=True, reads=[("HY", h)], writes=[("ps", b)])
                        S.I("act", "activation", out=C2[0:65, s0:s0 + 4, :],
                            in_=C.ps[b][0:65, :].rearrange("p (s c) -> p s c", s=4), func=AF.Copy,
                            reads=[("ps", b)], writes=[("HC2", 0)])
                    Tci = hc(C, "Tci", 65).unsqueeze(1).broadcast_to([65, 16, 128])
                    Tsi = hc(C, "Tsi", 65).unsqueeze(1).broadcast_to([65, 16, 128])
                    P1 = Fq.tmpA[0:65, 0:2048].rearrange("p (a c) -> p a c", a=16)
                    P2 = Fq.tmpB[0:65, 0:2048].rearrange("p (a c) -> p a c", a=16)
                    S.I("dve", "tensor_tensor", P1, C2[0:65, :, :], Tci, op=ALU.mult,
                        reads=[("HC2", 0)], writes=[("HqtmpA", 0)])
                    S.I("pool", "tensor_tensor", P2, C2[0:65, :, :], Tsi, op=ALU.mult,
                        reads=[("HC2", 0)], writes=[("HqtmpB", 0)])
                    S.I("dve", "tensor_tensor", Dt[0:65, 0, :, :], P1[:, :, 0:64], P2[:, :, 64:128], op=ALU.subtract,
                        reads=[("HqtmpA", 0), ("HqtmpB", 0)], writes=[("HDt", 0)])
                    S.I("pool", "tensor_tensor", Dt[0:65, 1, :, :], P2[:, :, 0:64], P1[:, :, 64:128], op=ALU.add,
                        reads=[("HqtmpA", 0), ("HqtmpB", 0)], writes=[("HDt", 1)])
                    b = C.psM[C.psM_i % len(C.psM)]
                    C.psM_i += 1
                    fin = [("Gc_a", 0, 0), ("Gs_a", 1, 0), ("Gc_b", 0, 1), ("Gs_b", 1, 1)]
                    for i, (gn, ri, h) in enumerate(fin):
                        S.I("pe", "matmul", C.ps[b][:, :], hc(C, gn, 65), Dt[0:65, ri, 8 * h:8 * h + 8, :],
                            start=(i == 0), stop=(i == 3), reads=[("HDt", ri)], writes=[("ps", b)])
                    yv = C.ps[b][:, :].rearrange("p (c n) -> p n c", c=8)
                    zv = z[:, :, ch0:ch0 + 8]
                    xv = xg[:, :, ch0:ch0 + 8]
                    skv = skb[:, o, cc * 128 + ch0:cc * 128 + ch0 + 8].unsqueeze(1).broadcast_to([128, 64, 8])
                    gk = gt[g % 2]
                    S.I("pool", "tensor_tensor", gk[:, :, :], zv, skv, op=ALU.mult,
                        reads=[("Hz", g), ("Hskb", 0)], writes=[("Hgt", g % 2)])
                    S.I("dve", "tensor_tensor", gk[:, :, :], yv, gk[:, :, :], op=ALU.add,
                        reads=[("ps", b), ("Hgt", g % 2)], writes=[("Hgt", g % 2)])
                    S.I("pool", "tensor_tensor", zv, gk[:, :, :], xv, op=ALU.mult,
                        reads=[("Hgt", g % 2), ("Hxg", 0)], writes=[("Hz", g)])
            for q4 in range(16):
                S.dma("sp", yb[:, cc * 128:(cc + 1) * 128].rearrange("(p n) c -> p n c", n=64)[:, q4 * 4:(q4 + 1) * 4, :],
                      z[:, q4 * 4:(q4 + 1) * 4, :], reads=ztoks)
        S.barrier()
        S.flush()


def phase_outproj(C, x_src, ya, yb, w_ap, x_dst):
    nc, S = C.nc, C.S
    tag = "O"
    with ExitStack() as es:
        wb = load_weight_bf16(C, es, w_ap, D, D, "Ow")
        xt = [sb(es, nc, "Oxt%d" % i, [128, 4, D], F32) for i in range(2)]
        yaT = [sb(es, nc, "OyaT%d" % i, [128, 4, TT], BF16) for i in range(2)]
        ybt = [sb(es, nc, "Oybt%d" % i, [128, 4, 512], F32) for i in range(2)]
        ybb = sb(es, nc, "Oybb", [128, 4, 512], BF16)
        ybT = sb(es, nc, "OybT", [128, 4, TT], BF16)
        for ti in range(NT):
            t0 = ti * TT
            sl = ti % 2
            S.dma("sp", xt[sl][:, :, :], x_src[t0:t0 + TT, :].rearrange("(s p) d -> p s d", p=128),
                  writes=[("Oxt", sl)])
            S.dma("sp", yaT[sl][:, :, :], ya[:, t0:t0 + TT].rearrange("(k p) t -> p k t", p=128),
                  writes=[("OyaT", sl)])
            S.dma("sp", ybt[sl][:, :, :], yb[t0:t0 + TT, :].rearrange("(s p) c -> p s c", p=128),
                  writes=[("Oybt", sl)])
            S.I("pool", "tensor_copy", ybb[:, :, :], ybt[sl][:, :, :], reads=[("Oybt", sl)], writes=[("Oybb", 0)])
            for kc in range(4):
                b = C.psT[C.psT_i % len(C.psT)]
                C.psT_i += 1
                for s in range(4):
                    S.I("pe", "matmul", C.ps[b][:, s * 128:(s + 1) * 128], ybb[:, s, kc * 128:(kc + 1) * 128],
                        C.ident[:, :], start=True, stop=True, reads=[("Oybb", 0)], writes=[("ps", b)])
                S.I("act", "activation", out=ybT[:, kc, :], in_=C.ps[b][:, :], func=AF.Copy,
                    reads=[("ps", b)], writes=[("OybT", kc)])
            for s in range(4):
                for nh in range(2):
                    b = C.psM[C.psM_i % len(C.psM)]
                    C.psM_i += 1
                    for kc in range(8):
                        if kc < 4:
                            lt, tok = yaT[sl][:, kc, s * 128:(s + 1) * 128], ("OyaT", sl)
                        else:
                            lt, tok = ybT[:, kc - 4, s * 128:(s + 1) * 128], ("OybT", kc - 4)
                        S.I("pe", "matmul", C.ps[b][:, :], lt, wb[:, kc, nh * 512:(nh + 1) * 512],
                            start=(kc == 0), stop=(kc == 7), reads=[tok], writes=[("ps", b)])
                    S.I("dve", "tensor_tensor", xt[sl][:, s, nh * 512:(nh + 1) * 512], C.ps[b][:, :],
                        xt[sl][:, s, nh * 512:(nh + 1) * 512], op=ALU.add,
                        reads=[("ps", b)], writes=[("Oxt", sl)])
            S.dma("pool", x_dst[t0:t0 + TT, :].rearrange("(s p) d -> p s d", p=128), xt[sl][:, :, :],
                  reads=[("Oxt", sl)])
        S.barrier()
        S.flush()


def attn_consts(is_prompt):
    d = np.arange(128) % 64
    inv = 500000.0 ** (-(np.arange(0, 16, 2, dtype=np.float64) / 16.0))
    tau = np.arange(T)
    pos = tau if is_prompt else tau % 4096
    cos = np.ones((128, T), np.float64)
    sin = np.zeros((128, T), np.float64)
    for p in range(128):
        dd = d[p]
        if dd < 16:
            ang = pos * inv[dd % 8]
            cos[p] = np.cos(ang)
            sin[p] = np.sin(ang)
    P = np.zeros((128, 128), np.float32)
    for m in range(128):
        dd = m % 64
        if dd < 8:
            P[m + 8, m] = -1.0
        elif dd < 16:
            P[m - 8, m] = 1.0
    q = np.arange(128)[:, None]
    s = np.arange(128)[None, :]
    NEG = -30000.0
    prev = np.where(s >= q, 0.0, NEG)
    cur = np.zeros((128, 128))
    nxt = np.where(s <= q, 0.0, NEG)
    band = np.concatenate([prev, cur, nxt], 1).astype(np.float32)
    full = np.full((128, 128), NEG)
    if is_prompt:
        m31, m32 = band, band
    else:
        m31 = np.concatenate([prev, cur, full], 1).astype(np.float32)
        m32 = np.concatenate([full, cur, nxt], 1).astype(np.float32)
    masks = np.stack([band, m31, m32], 0).astype(np.float32)
    return cos.astype(np.float32), sin.astype(np.float32), P, masks


def phase_qkv(C, x_src, g_row, w_ap, qd, kd, vd, cos_in, sin_in, prot_in):
    nc, S = C.nc, C.S
    tag = "Q"
    with ExitStack() as es:
        gcol = load_bcast_row(C, es, g_row, D, "Qg")
        wb = load_weight_bf16(C, es, w_ap, D, 1536, "Qw")
        prf = sb(es, nc, "Qprf", [128, 128], F32)
        prb = sb(es, nc, "Qprb", [128, 128], BF16)
        S.dma("sp", prf[:, :], prot_in, writes=[("Qprf", 0)])
        S.I("dve", "tensor_copy", prb[:, :], prf[:, :], reads=[("Qprf", 0)], writes=[("Qprb", 0)])
        N = alloc_norm_tiles(C, es, tag, TT)
        cs = [sb(es, nc, "Qcs%d" % i, [128, TT], F32) for i in range(2)]
        sn = [sb(es, nc, "Qsn%d" % i, [128, TT], F32) for i in range(2)]
        xb = [sb(es, nc, "Qxb%d" % i, [128, TT], BF16) for i in range(2)]
        t1 = [sb(es, nc, "Qt1%d" % i, [128, TT], F32) for i in range(2)]
        t2 = [sb(es, nc, "Qt2%d" % i, [128, TT], F32) for i in range(2)]
        qo = [sb(es, nc, "Qqo%d" % i, [128, TT], BF16) for i in range(2)]
        vo = [sb(es, nc, "Qvo%d" % i, [128, 256], F32) for i in range(2)]
        k = 0
        norm_load(C, N, x_src, 0)
        nxt = norm_tile(C, N, gcol, 0)
        for ti in range(NT):
            t0 = ti * TT
            if ti + 1 < NT:
                norm_load(C, N, x_src, ti + 1)
            c2 = ti % 2
            S.dma("sp", cs[c2][:, :], cos_in[:, t0:t0 + TT], writes=[("Qcs", c2)])
            S.dma("sp", sn[c2][:, :], sin_in[:, t0:t0 + TT], writes=[("Qsn", c2)])
            xs, hT, slot, tslot = nxt
            for oc in range(10):
                if oc == 6 and ti + 1 < NT:
                    nxt = norm_tile(C, N, gcol, ti + 1)
                b = C.psM[C.psM_i % len(C.psM)]
                C.psM_i += 1
                for kc in range(8):
                    S.I("pe", "matmul", C.ps[b][:, :], wb[:, kc, oc * 128:(oc + 1) * 128], hT[:, kc, :],
                        start=(kc == 0), stop=(kc == 7), reads=[("QhnT", tslot, kc)], writes=[("ps", b)])
                k2 = k % 2
                k += 1
                if C.dbg.get("qstop") == 1:
                    S.I("act", "activation", out=qo[k2][:, :], in_=C.ps[b][:, :], func=AF.Copy,
                        reads=[("ps", b)], writes=[("Qqo", k2)])
                    S.dma("sp", qd[oc * 128:(oc + 1) * 128, t0:t0 + TT], qo[k2][:, :], reads=[("Qqo", k2)])
                    continue
                S.I("dve", "tensor_copy", xb[k2][:, :], C.ps[b][:, :],
                    reads=[("ps", b)], writes=[("Qxb", k2)])
                b2 = C.psM[C.psM_i % len(C.psM)]
                C.psM_i += 1
                S.I("pe", "matmul", C.ps[b2][:, :], prb[:, :], xb[k2][:, :], start=True, stop=True,
                    reads=[("Qxb", k2), ("Qprb", 0)], writes=[("ps", b2)])
                S.I("dve", "tensor_tensor", t1[k2][:, :], C.ps[b][:, :], cs[c2][:, :], op=ALU.mult,
                    reads=[("ps", b), ("Qcs", c2)], writes=[("Qt1", k2)])
                S.I("dve", "tensor_tensor", t2[k2][:, :], C.ps[b2][:, :], sn[c2][:, :], op=ALU.mult,
                    reads=[("ps", b2), ("Qsn", c2)], writes=[("Qt2", k2)])
                S.I("pool", "tensor_tensor", qo[k2][:, :], t1[k2][:, :], t2[k2][:, :], op=ALU.add,
                    reads=[("Qt1", k2), ("Qt2", k2)], writes=[("Qqo", k2)])
                if oc < 8:
                    S.dma("pool", qd[oc * 128:(oc + 1) * 128, t0:t0 + TT], qo[k2][:, :], reads=[("Qqo", k2)])
                else:
                    S.dma("pool", kd[(oc - 8) * 128:(oc - 7) * 128, t0:t0 + TT], qo[k2][:, :], reads=[("Qqo", k2)])
            for s in range(4):
                if C.dbg.get("qstop") == 2:
                    break
                b = C.psM[C.psM_i % len(C.psM)]
                C.psM_i += 1
                for kc in range(8):
                    S.I("pe", "matmul", C.ps[b][:, 0:256], hT[:, kc, s * 128:(s + 1) * 128], wb[:, kc, 1280:1536],
                        start=(kc == 0), stop=(kc == 7), reads=[("QhnT", tslot, kc)], writes=[("ps", b)])
                S.I("act", "activation", out=vo[s % 2][:, :], in_=C.ps[b][:, 0:256], func=AF.Copy,
                    reads=[("ps", b)], writes=[("Qvo", s % 2)])
                S.dma("act", vd[t0 + s * 128:t0 + (s + 1) * 128, :], vo[s % 2][:, :], reads=[("Qvo", s % 2)])
        S.barrier()
        S.flush()


def phase_attn(C, x_src, qd, kd, vd, wo_ap, x_dst, masks_in):
    nc, S = C.nc, C.S
    W = C.W
    NB = T // 128
    with ExitStack() as es:
        wo = load_weight_bf16(C, es, wo_ap, D, D, "Two")
        sinkb = load_bcast_row(C, es, W["at_sink"], 16, "Tsink")
        mk = sb(es, nc, "Tmk", [128, 3, 384], F32)
        S.dma("sp", mk[:, :, :], masks_in.rearrange("m p s -> p m s"), writes=[("Tmk", 0)])
        qT = [sb(es, nc, "TqT%d" % i, [128, 8, 128], BF16) for i in range(2)]
        kT = [sb(es, nc, "TkT%d" % i, [128, 4, 384], BF16) for i in range(2)]
        vt = [sb(es, nc, "Tvt%d" % i, [128, 3, 256], BF16) for i in range(2)]
        vf = [sb(es, nc, "Tvf%d" % i, [128, 3, 256], F32) for i in range(2)]
        xt = [sb(es, nc, "Txt%d" % i, [128, D], F32) for i in range(2)]
        sm = [sb(es, nc, "Tsm%d" % i, [128, 384], F32) for i in range(4)]
        pb = [sb(es, nc, "Tpb%d" % i, [128, 384], BF16) for i in range(4)]
        pT = [sb(es, nc, "TpT%d" % i, [128, 3, 128], BF16) for i in range(4)]
        st = [sb(es, nc, "Tst%d" % i, [128, 8], F32) for i in range(4)]
        sbanks = [4, 5, 0, 1]
        tbanks = [2, 3]
        tbi = 0
        rdn = sb(es, nc, "Trdn", [128, 16], F32)
        ob = sb(es, nc, "Tob", [128, 16, 64], BF16)
        oT = sb(es, nc, "ToT", [128, 8, 128], BF16)
        hh = 0
        for n in range(NB):
            sl = n % 2
            kb0 = max(n - 1, 0)
            kb1 = min(n + 1, NB - 1)
            nk = kb1 - kb0 + 1
            mo = (kb0 - (n - 1)) * 128
            mi = 1 if n == NB // 2 - 1 else (2 if n == NB // 2 else 0)
            S.dma("sp", qT[sl][:, :, :], qd[:, n * 128:(n + 1) * 128].rearrange("(k p) t -> p k t", p=128),
                  writes=[("TqT", sl)])
            for kv in range(4):
                for dup in range(2):
                    S.dma("sp", kT[sl][dup * 64:(dup + 1) * 64, kv, 0:nk * 128],
                          kd[kv * 64:(kv + 1) * 64, kb0 * 128:(kb1 + 1) * 128], writes=[("TkT", sl)])
            S.dma("sp", vf[sl][:, 0:nk, :], vd[kb0 * 128:(kb1 + 1) * 128, :].rearrange("(k p) c -> p k c", p=128),
                  writes=[("Tvf", sl)])
            S.I("pool", "tensor_copy", vt[sl][:, 0:nk, :], vf[sl][:, 0:nk, :], reads=[("Tvf", sl)], writes=[("Tvt", sl)])
            S.dma("sp", xt[sl][:, :], x_src[n * 128:(n + 1) * 128, :], writes=[("Txt", sl)])
            bo = [6, 7]
            hp_ = []
            for h in range(16):
                h2 = hh % 4
                hh += 1
                bt = tbanks[tbi % 2]
                tbi += 1
                hp_.append((h2, sbanks[hh % 4], bt))

            def stA(h):
                kv = h // 4
                qc, hp = h // 2, h % 2
                h2, b, bt = hp_[h]
                smh, sth = sm[h2], st[h2]
                S.I("pe", "matmul", C.ps[b][:, 0:nk * 128], qT[sl][hp * 64:(hp + 1) * 64, qc, :],
                    kT[sl][hp * 64:(hp + 1) * 64, kv, 0:nk * 128], start=True, stop=True,
                    reads=[("TqT", sl), ("TkT", sl)], writes=[("ps", b)])
                S.I("dve", "scalar_tensor_tensor", out=smh[:, 0:nk * 128], in0=C.ps[b][:, 0:nk * 128], scalar=0.125,
                    in1=mk[:, mi, mo:mo + nk * 128], op0=ALU.mult, op1=ALU.add,
                    reads=[("ps", b), ("Tmk", 0)], writes=[("Tsm", h2)])
                S.I("dve", "reduce_max", sth[:, 0:1], smh[:, 0:nk * 128], axis=AX.X,
                    reads=[("Tsm", h2)], writes=[("Tst", h2, 0)])
                S.I("dve", "tensor_tensor", sth[:, 1:2], sth[:, 0:1], sinkb[:, h:h + 1], op=ALU.max,
                    reads=[("Tst", h2, 0), ("Tsink", 0)], writes=[("Tst", h2, 1)])
                S.I("dve", "tensor_scalar", sth[:, 2:3], sth[:, 1:2], -1.0, None, op0=ALU.mult,
                    reads=[("Tst", h2, 1)], writes=[("Tst", h2, 2)])

            def stB(h):
                h2, b, bt = hp_[h]
                smh, pbh, sth = sm[h2], pb[h2], st[h2]
                S.I("act", "activation", out=pbh[:, 0:nk * 128], in_=smh[:, 0:nk * 128], func=AF.Exp,
                    bias=sth[:, 2:3], accum_out=sth[:, 3:4],
                    reads=[("Tsm", h2), ("Tst", h2, 2)], writes=[("Tpb", h2), ("Tst", h2, 3)])
                S.I("act", "activation", out=sth[:, 4:5], in_=sinkb[:, h:h + 1], func=AF.Exp, bias=sth[:, 2:3],
                    reads=[("Tst", h2, 2), ("Tsink", 0)], writes=[("Tst", h2, 4)])
                S.I("dve", "tensor_tensor", sth[:, 5:6], sth[:, 3:4], sth[:, 4:5], op=ALU.add,
                    reads=[("Tst", h2, 3), ("Tst", h2, 4)], writes=[("Tst", h2, 5)])
                S.I("dve", "reciprocal", rdn[:, h:h + 1], sth[:, 5:6], reads=[("Tst", h2, 5)], writes=[("Trdn", h)])
                for kb in range(nk):
                    S.I("pe", "matmul", C.ps[bt][:, kb * 128:(kb + 1) * 128], pbh[:, kb * 128:(kb + 1) * 128],
                        C.ident[:, :], start=True, stop=True, reads=[("Tpb", h2)], writes=[("ps", bt)])

            def stC(h):
                kv = h // 4
                h2, b, bt = hp_[h]
                pTh = pT[h2]
                S.I("act", "activation", out=pTh[:, 0:nk, :],
                    in_=C.ps[bt][:, 0:nk * 128].rearrange("p (k c) -> p k c", k=nk), func=AF.Copy,
                    reads=[("ps", bt)], writes=[("TpT", h2)])
                bb = bo[h // 8]
                for kb in range(nk):
                    S.I("pe", "matmul", C.ps[bb][:, (h % 8) * 64:(h % 8 + 1) * 64], pTh[:, kb, :],
                        vt[sl][:, kb, kv * 64:(kv + 1) * 64], start=(kb == 0), stop=(kb == nk - 1),
                        reads=[("TpT", h2), ("Tvt", sl)], writes=[("ps", bb)])

            for i in range(18):
                if i < 16:
                    stA(i)
                if 0 <= i - 1 < 16:
                    stB(i - 1)
                if 0 <= i - 2 < 16:
                    stC(i - 2)
            for j in range(2):
                S.I("dve", "tensor_tensor", ob[:, j * 8:(j + 1) * 8, :],
                    C.ps[bo[j]][:, :].rearrange("p (h d) -> p h d", h=8),
                    rdn[:, j * 8:(j + 1) * 8].unsqueeze(2).broadcast_to([128, 8, 64]), op=ALU.mult,
                    reads=[("ps", bo[j])] + [("Trdn", hq) for hq in range(j * 8, j * 8 + 8)], writes=[("Tob", j)])
            for half in range(2):
                bt = tbanks[tbi % 2]
                tbi += 1
                for jj in range(4):
                    kc = half * 4 + jj
                    S.I("pe", "matmul", C.ps[bt][:, jj * 128:(jj + 1) * 128],
                        ob[:, 2 * kc:2 * kc + 2, :].rearrange("p h d -> p (h d)"),
                        C.ident[:, :], start=True, stop=True, reads=[("Tob", kc // 4)], writes=[("ps", bt)])
                S.I("act", "activation", out=oT[:, half * 4:(half + 1) * 4, :],
                    in_=C.ps[bt][:, :].rearrange("p (k c) -> p k c", k=4), func=AF.Copy,
                    reads=[("ps", bt)], writes=[("ToT", half)])
            for nh in range(2):
                b = sbanks[(hh + 1 + nh) % 4]
                for kc in range(8):
                    S.I("pe", "matmul", C.ps[b][:, :], oT[:, kc, :], wo[:, kc, nh * 512:(nh + 1) * 512],
                        start=(kc == 0), stop=(kc == 7), reads=[("ToT", kc // 4)], writes=[("ps", b)])
                S.I("dve", "tensor_tensor", xt[sl][:, nh * 512:(nh + 1) * 512], C.ps[b][:, :],
                    xt[sl][:, nh * 512:(nh + 1) * 512], op=ALU.add, reads=[("ps", b)], writes=[("Txt", sl)])
            S.dma("pool", x_dst[n * 128:(n + 1) * 128, :], xt[sl][:, :], reads=[("Txt", sl)])
        S.barrier()
        S.flush()


def phase_mlp(C, x_src, g_row, wup_ap, wdn_ap, x_dst, tag, final_g_row=None):
    nc, S = C.nc, C.S
    DFF = 4096
    TK = 256
    with ExitStack() as es:
        gcol = load_bcast_row(C, es, g_row, D, tag + "g")
        gfin = load_bcast_row(C, es, final_g_row, D, tag + "gf") if final_g_row is not None else None
        wup = load_weight_bf16(C, es, wup_ap, D, DFF, tag + "wu")
        wdn = load_weight_bf16(C, es, wdn_ap, DFF, D, tag + "wd")
        N = alloc_norm_tiles(C, es, tag, TK, nx=2, nT=2)
        hT = sb(es, nc, tag + "hT", [128, 32, TK], BF16)
        rl = [sb(es, nc, tag + "rl%d" % i, [128, TK], F32) for i in range(2)]
        ss2 = sb(es, nc, tag + "ss2", [128, 2], F32)
        rs2 = sb(es, nc, tag + "rs2", [128, 2], F32)
        norm_load(C, N, x_src, 0)
        nxt = norm_tile(C, N, gcol, 0)
        for ti in range(T // TK):
            if ti + 1 < T // TK:
                norm_load(C, N, x_src, ti + 1)
            xs, hTn, slot, tslot = nxt
            for fc in range(32):
                b = C.psM[C.psM_i % len(C.psM)]
                C.psM_i += 1
                for kc in range(8):
                    S.I("pe", "matmul", C.ps[b][:, 0:TK], wup[:, kc, fc * 128:(fc + 1) * 128], hTn[:, kc, :],
                        start=(kc == 0), stop=(kc == 7),
                        reads=[(tag + "hnT", tslot, kc)], writes=[("ps", b)])
                j = fc % 2
                if j == 0:
                    S.I("act", "activation", out=rl[0][:, :], in_=C.ps[b][:, 0:TK], func=AF.Relu,
                        reads=[("ps", b)], writes=[(tag + "rl", 0)])
                else:
                    S.I("dve", "tensor_scalar", rl[1][:, :], C.ps[b][:, 0:TK], 0.0, None, op0=ALU.max,
                        reads=[("ps", b)], writes=[(tag + "rl", 1)])
                S.I("pool", "tensor_tensor", hT[:, fc, :], rl[j][:, :], rl[j][:, :], op=ALU.mult,
                    reads=[(tag + "rl", j)], writes=[(tag + "hT", fc)])
            if ti + 1 < T // TK:
                nxt = norm_tile(C, N, gcol, ti + 1)
            for s in range(N.NS):
                xr = (tag + "xt", slot)
                for nh in range(2):
                    b = C.psM[C.psM_i % len(C.psM)]
                    C.psM_i += 1
                    for fc in range(32):
                        S.I("pe", "matmul", C.ps[b][:, :], hT[:, fc, s * 128:(s + 1) * 128],
                            wdn[:, fc, nh * 512:(nh + 1) * 512], start=(fc == 0), stop=(fc == 31),
                            reads=[(tag + "hT", fc)], writes=[("ps", b)])
                    S.I("dve", "tensor_tensor", xs[:, s, nh * 512:(nh + 1) * 512], C.ps[b][:, :],
                        xs[:, s, nh * 512:(nh + 1) * 512], op=ALU.add,
                        reads=[("ps", b)], writes=[xr])
                r0 = ti * TK + s * 128
                if gfin is not None:
                    S.I("pool", "memset", ss2[:, 0:1], 0.0, writes=[(tag + "ss2", 0)])
                    S.I("act", "activation", out=N.junk[:, :], in_=xs[:, s, :], func=AF.Square,
                        accum_out=ss2[:, 0:1], reads=[xr], writes=[(tag + "ss2", 0), (tag + "junk", 0)])
                    S.I("act", "activation", out=rs2[:, 0:1], in_=ss2[:, 0:1], func=AF.Ln, scale=1.0 / D,
                        bias=C.eps5[:, 0:1], reads=[(tag + "ss2", 0)], writes=[(tag + "rs2", 0)])
                    S.I("act", "activation", out=rs2[:, 0:1], in_=rs2[:, 0:1], func=AF.Exp, scale=-0.5,
                        reads=[(tag + "rs2", 0)], writes=[(tag + "rs2", 0)])
                    S.I("dve", "scalar_tensor_tensor", out=xs[:, s, :], in0=xs[:, s, :], scalar=rs2[:, 0:1],
                        in1=gfin[:, :], op0=ALU.mult, op1=ALU.mult,
                        reads=[(tag + "rs2", 0)], writes=[xr])
                S.dma("sp", x_dst[r0:r0 + 128, :], xs[:, s, :], reads=[xr])
        S.barrier()
        S.flush()


WEIGHT_SPECS = [
    ("norm_mix", (2, 1024)), ("norm_mlp", (2, 1024)), ("norm_final", (1, 1024)),
    ("ab_w_in", (1024, 2560)), ("ab_w_out", (1024, 1024)),
    ("cv_dw_w", (31, 512)), ("cv_dw_b", (1, 512)), ("cv_ln_g", (1, 512)), ("cv_ln_b", (1, 512)),
    ("hy_short_w", (3, 1536)), ("hy_short_b", (1, 1536)),
    ("hy_w1", (33, 64)), ("hy_b1", (1, 64)), ("hy_w2", (64, 64)), ("hy_b2", (1, 64)),
    ("hy_w3", (64, 64)), ("hy_b3", (1, 64)), ("hy_w4", (64, 2048)),
    ("hy_freq", (3, 64)), ("hy_decay", (1, 2048)), ("hy_skip", (2, 512)),
    ("at_w_qkv", (1024, 1536)), ("at_sink", (1, 16)), ("at_w_o", (1024, 1024)),
    ("mlp_w_up", (2, 1024, 4096)), ("mlp_w_down", (2, 4096, 1024)),
]


def build_program(dbg=None):
    nc = bass.Bass("TRN2", target_bir_lowering=False)
    C = Ctx()
    C.nc = nc
    C.dbg = dbg or {}
    W = {}
    x_in = nc.dram_tensor("x", [T, D], F32, kind="ExternalInput").ap()
    for name, shp in WEIGHT_SPECS:
        W[name] = nc.dram_tensor(name, list(shp), F32, kind="ExternalInput").ap()
    ident_in = nc.dram_tensor("ident", [128, 128], F32, kind="ExternalInput").ap()
    flag_in = nc.dram_tensor("flag", [1, 8], F32, kind="ExternalInput").ap()
    hc128_np, hc65_np = hyena_consts()
    hc128_in = nc.dram_tensor("hc128", list(hc128_np.shape), F32, kind="ExternalInput").ap()
    hc65_in = nc.dram_tensor("hc65", list(hc65_np.shape), F32, kind="ExternalInput").ap()
    zT_in = nc.dram_tensor("zT", [33, 8192], F32, kind="ExternalInput").ap()
    tneg_in = nc.dram_tensor("tneg", [128, 128], F32, kind="ExternalInput").ap()
    cos_in = nc.dram_tensor("ropecos", [128, T], F32, kind="ExternalInput").ap()
    sin_in = nc.dram_tensor("ropesin", [128, T], F32, kind="ExternalInput").ap()
    prot_in = nc.dram_tensor("prot", [128, 128], F32, kind="ExternalInput").ap()
    masks_in = nc.dram_tensor("amasks", [3, 128, 384], F32, kind="ExternalInput").ap()
    y_out = nc.dram_tensor("y", [T, D], F32, kind="ExternalOutput").ap()
    C.W = W

    def scratch(name, shape, dt=F32):
        kind = "ExternalOutput" if name in C.dbg.get("out", ()) else "Internal"
        return nc.dram_tensor(name, list(shape), dt, kind=kind).ap()

    u = scratch("u", [2560, T])
    ya = scratch("ya", [512, T], BF16)
    uh = scratch("uh", [T, 1536])
    Gd = scratch("Gd", [2, 4, 3, 64, 2, 128, 65])
    yb = scratch("yb", [T, 512])
    x1 = scratch("x1", [T, D])
    x2 = scratch("x2", [T, D])
    x3 = scratch("x3", [T, D])
    qd = scratch("qd", [1024, T], BF16)
    kd = scratch("kd", [256, T], BF16)
    vd = scratch("vd", [T, 256])
    stop = C.dbg.get("stop")

    with ExitStack() as es:
        S = Sched(nc, es)
        C.S = S
        C.ps = [es.enter_context(nc.psum_tensor("ps%d" % i, [128, 512], F32)) for i in range(8)]
        C.psT = [0, 1, 2, 3]
        C.psM = [4, 5, 6, 7]
        C.psT_i = 0
        C.psM_i = 0
        C.ident = sb(es, nc, "ident_b", [128, 128], BF16)
        C.identf = sb(es, nc, "ident_f", [128, 128], F32)
        C.eps5 = sb(es, nc, "eps5", [128, 1], F32)
        S.I("pool", "memset", C.eps5[:, :], 1e-5)
        S.dma("sp", C.identf[:, :], ident_in[:, :], writes=[("identf", 0)])
        S.I("dve", "tensor_copy", C.ident[:, :], C.identf[:, :], reads=[("identf", 0)], writes=[("ident", 0)])
        C.onesf = sb(es, nc, "onesf", [128, 128], F32)
        S.I("pool", "memset", C.onesf[:, :], 1.0)
        C.flagcol = sb(es, nc, "flagcol", [128, 8], F32)
        S.dma("sp", C.flagcol[:, :], flag_in.partition_broadcast(128))
        C.hc128 = sb(es, nc, "hc128_t", list(hc128_np.shape), F32)
        C.hc65 = sb(es, nc, "hc65_t", list(hc65_np.shape), F32)
        S.dma("sp", C.hc128[:, :], hc128_in)
        S.dma("sp", C.hc65[:, :], hc65_in)
        S.barrier()
        S.flush()

        skip = C.dbg.get("skip", "")
        if stop != "M" and "A" not in skip:
            phase_inproj(C, x_in, W["norm_mix"][0:1, :], W["ab_w_in"], 2560, u, "A")
        if stop == "A":
            return nc
        if "B" not in skip:
            phase_conformer(C, u, ya)
        if stop == "B":
            return nc
        if "S" not in skip:
            phase_shortconv(C, u, uh)
        if stop == "S":
            return nc
        if "F" not in skip:
            phase_filters(C, Gd, zT_in, tneg_in)
        if stop == "F":
            return nc
        if "H" not in skip:
            phase_hyena(C, uh, Gd, yb)
        if stop == "H":
            return nc
        if "O" not in skip:
            phase_outproj(C, x_in, ya, yb, W["ab_w_out"], x1)
        if stop == "O":
            return nc
        if "M0" not in skip:
            phase_mlp(C, x1, W["norm_mlp"][0:1, :], W["mlp_w_up"][0], W["mlp_w_down"][0], x2, "M0")
        if stop == "M0":
            return nc
        xa = x_in if "X" in skip else x2
        phase_qkv(C, xa, W["norm_mix"][1:2, :], W["at_w_qkv"], qd, kd, vd, cos_in, sin_in, prot_in)
        if stop == "Q":
            return nc
        phase_attn(C, xa, qd, kd, vd, W["at_w_o"], x3, masks_in)
        if stop == "T":
            return nc
        phase_mlp(C, x3, W["norm_mlp"][1:2, :], W["mlp_w_up"][1], W["mlp_w_down"][1], y_out, "M1",
                  final_g_row=W["norm_final"][0:1, :])
        return nc
        phase_mlp(C, x_in, W["norm_mlp"][0:1, :], W["mlp_w_up"][0], W["mlp_w_down"][0], y_out, "M0",
                  final_g_row=W["norm_final"][0:1, :])
    return nc


_CACHE = {}


def make_in_maps(inputs):
    xp = np.asarray(inputs["x_prompt"], np.float32)
    xs = np.asarray(inputs["x_sample"], np.float32)
    base = {}
    for name, shp in WEIGHT_SPECS:
        base[name] = np.ascontiguousarray(np.asarray(inputs[name], np.float32)).reshape(shp)
    base["ident"] = np.eye(128, dtype=np.float32)
    base["hc128"], base["hc65"] = hyena_consts()
    ptab = {True: hyena_pos_tables(True), False: hyena_pos_tables(False)}
    atab = {True: attn_consts(True), False: attn_consts(False)}
    maps = []
    for c in range(NCORES):
        m = dict(base)
        m["flag"] = np.full((1, 8), 1.0 if c < 4 else 0.0, np.float32)
        m["zT"], m["tneg"] = ptab[c < 4]
        m["ropecos"], m["ropesin"], m["prot"], m["amasks"] = atab[c < 4]
        if c < 4:
            m["x"] = np.ascontiguousarray(xp[c])
        else:
            m["x"] = np.ascontiguousarray(xs[2 * (c - 4):2 * (c - 4) + 2].reshape(T, D))
        maps.append(m)
    return maps


def kernel(**inputs):
    if "nc" not in _CACHE:
        _CACHE["nc"] = build_program()
    nc = _CACHE["nc"]
    maps = make_in_maps(inputs)
    res = run_bass_kernel_spmd(nc, maps, core_ids=list(range(NCORES)))
    ys = [np.asarray(r["y"], np.float32) for r in res.results]
    y_prompt = np.stack(ys[0:4], axis=0)
    y_sample = np.concatenate([y.reshape(2, 4096, D) for y in ys[4:8]], axis=0)
    return (y_prompt, y_sample)
```

```python
import math
from contextlib import ExitStack

import numpy as np
import concourse.bass as bass
import concourse.mybir as mybir
from concourse.bass_utils import run_bass_kernel_spmd

F32 = mybir.dt.float32
BF16 = mybir.dt.bfloat16
AF = mybir.ActivationFunctionType
ALU = mybir.AluOpType
AX = mybir.AxisListType

NCORES = 8
T = 8192
D = 1024
TT = 512
NT = T // TT
ENGS = ("pe", "act", "dve", "pool", "sp")


class Op:
    __slots__ = ("eng", "fn", "dma", "deps", "sig", "count", "sem", "semval")

    def __init__(self, eng, fn, dma):
        self.eng = eng
        self.fn = fn
        self.dma = dma
        self.deps = set()
        self.sig = False
        self.count = 0
        self.sem = None
        self.semval = 0


class Sched:
    NDMA_SEM = 12

    def __init__(self, nc, es):
        self.nc = nc
        self.streams = {k: [] for k in ENGS}
        self.w = {}
        self.r = {}
        self.dma_rr = {k: 0 for k in ENGS}
        self.dma_last = {}
        self.dma_cnt = {}
        self.esem = {k: es.enter_context(nc.semaphore("e_" + k)) for k in ENGS}
        self.dsem = {}
        for k in ("sp", "act", "pool"):
            for i in range(self.NDMA_SEM):
                self.dsem[(k, i)] = es.enter_context(nc.semaphore("d_%s%d" % (k, i)))
        self.ecount = {k: 0 for k in ENGS}
        self.seen = {k: {} for k in ENGS}
        self.lastc = {}

    def op(self, eng, fn, reads=(), writes=(), dma=False, deps=()):
        o = Op(eng, fn, dma)
        for d in deps:
            if d is not None:
                o.deps.add(d)
        for r in reads:
            lw = self.w.get(r)
            if lw is not None:
                o.deps.add(lw)
        for w_ in writes:
            lw = self.w.get(w_)
            if lw is not None:
                o.deps.add(lw)
            for rd in self.r.get(w_, ()):
                o.deps.add(rd)
        for r in reads:
            self.r.setdefault(r, []).append(o)
        for w_ in writes:
            self.w[w_] = o
            self.r[w_] = []
        if dma:
            i = self.dma_rr[eng]
            self.dma_rr[eng] = (i + 1) % self.NDMA_SEM
            key = (eng, i)
            prev = self.dma_last.get(key)
            if prev is not None:
                o.deps.add(prev)
            self.dma_last[key] = o
            self.dma_cnt[key] = self.dma_cnt.get(key, 0) + 1
            o.sem = key
            o.semval = 16 * self.dma_cnt[key]
        else:
            self.lastc[eng] = o
        o.deps.discard(o)
        self.streams[eng].append(o)
        return o

    def I(self, eng, name, *args, reads=(), writes=(), deps=(), **kw):
        return self.op(eng, lambda e: getattr(e, name)(*args, **kw), reads, writes, deps=deps)

    def dma(self, eng, out, in_, reads=(), writes=(), deps=(), **kw):
        return self.op(eng, lambda e: e.dma_start(out=out, in_=in_, **kw), reads, writes, dma=True, deps=deps)

    def barrier(self):
        dmas = list(self.dma_last.values())
        lastc = dict(self.lastc)
        for k in ENGS:
            o = Op(k, None, False)
            for kk, lo in lastc.items():
                if kk != k:
                    o.deps.add(lo)
            for d in dmas:
                o.deps.add(d)
            self.streams[k].append(o)
        self.w = {}
        self.r = {}

    def flush(self):
        nc = self.nc
        for k in ENGS:
            for o in self.streams[k]:
                for d in o.deps:
                    if not d.dma and (d.eng != o.eng or o.dma or o.eng != "pe"):
                        d.sig = True
        for k in ENGS:
            for o in self.streams[k]:
                if o.sig and o.count == 0:
                    self.ecount[k] += 1
                    o.count = self.ecount[k]
        streams = self.streams
        self.streams = {k: [] for k in ENGS}
        with nc.Block() as block:
            def run(k, e):
                seen = self.seen[k]
                for o in streams[k]:
                    need = {}
                    for d in o.deps:
                        if d.dma:
                            s, v = self.dsem[d.sem], d.semval
                        elif d.eng == k and not o.dma and k == "pe":
                            continue
                        else:
                            assert d.count > 0
                            s, v = self.esem[d.eng], d.count
                        if need.get(id(s), (None, 0))[1] < v:
                            need[id(s)] = (s, v)
                    for s, v in need.values():
                        if seen.get(id(s), 0) < v:
                            e.wait_ge(s, v)
                            seen[id(s)] = v
                    if o.fn is None:
                        continue
                    ins = o.fn(e)
                    if o.dma:
                        ins.then_inc(self.dsem[o.sem], 16)
                    elif o.sig:
                        ins.then_inc(self.esem[k], 1)

            @block.tensor
            def _(e):
                run("pe", e)

            @block.scalar
            def _(e):
                run("act", e)

            @block.vector
            def _(e):
                run("dve", e)

            @block.gpsimd
            def _(e):
                run("pool", e)

            @block.sync
            def _(e):
                run("sp", e)


class Ctx:
    pass


def sb(es, nc, name, shape, dt):
    return es.enter_context(nc.sbuf_tensor(name, list(shape), dt))


def load_weight_bf16(C, es, w_ap, K, N, name, eng_cast=("pool", "act")):
    nc, S = C.nc, C.S
    KC = K // 128
    wb = sb(es, nc, name, [128, KC, N], BF16)
    CW = min(N, 2048)
    with ExitStack() as es2:
        st = [sb(es2, nc, name + "_st%d" % i, [128, CW], F32) for i in range(2)]
        j = 0
        for kc in range(KC):
            for c0 in range(0, N, CW):
                cw = min(CW, N - c0)
                s = st[j % 2]
                S.dma("sp", s[:, 0:cw], w_ap[kc * 128:(kc + 1) * 128, c0:c0 + cw],
                      writes=[(name + "st", j % 2)])
                eng = eng_cast[j % len(eng_cast)]
                if eng == "act":
                    S.I("act", "activation", out=wb[:, kc, c0:c0 + cw], in_=s[:, 0:cw], func=AF.Copy,
                        reads=[(name + "st", j % 2)], writes=[(name, kc)])
                else:
                    S.I(eng, "tensor_copy", wb[:, kc, c0:c0 + cw], s[:, 0:cw],
                        reads=[(name + "st", j % 2)], writes=[(name, kc)])
                j += 1
        S.barrier()
        S.flush()
    return wb


def alloc_norm_tiles(C, es, tag, TK, nx=2, nT=2):
    nc = C.nc
    NS = TK // 128
    N = Ctx()
    N.TK, N.NS, N.tag = TK, NS, tag
    N.xt = [sb(es, nc, tag + "xt%d" % i, [128, NS, D], F32) for i in range(nx)]
    N.hn = sb(es, nc, tag + "hn", [128, NS, D], BF16)
    N.hnT = [sb(es, nc, tag + "hnT%d" % i, [128, 8, TK], BF16) for i in range(nT)]
    N.ss = [sb(es, nc, tag + "ss%d" % i, [128, NS], F32) for i in range(nx)]
    N.rstd = [sb(es, nc, tag + "rstd%d" % i, [128, NS], F32) for i in range(nx)]
    N.junk = sb(es, nc, tag + "junk", [128, D], BF16)
    return N


def norm_load(C, N, x_src, ti):
    S = C.S
    slot = ti % len(N.xt)
    S.dma("sp", N.xt[slot][:, :, :],
          x_src[ti * N.TK:(ti + 1) * N.TK, :].rearrange("(s p) d -> p s d", p=128),
          writes=[(N.tag + "xt", slot)])


def norm_tile(C, N, gcol, ti):
    S = C.S
    tag = N.tag
    slot = ti % len(N.xt)
    tslot = ti % len(N.hnT)
    xs = N.xt[slot]
    ss, rstd = N.ss[slot], N.rstd[slot]
    for s in range(N.NS):
        S.I("act", "activation", out=N.junk[:, :], in_=xs[:, s, :], func=AF.Square,
            accum_out=ss[:, s:s + 1],
            reads=[(tag + "xt", slot)], writes=[(tag + "ss", slot), (tag + "junk", 0)])
    S.I("act", "activation", out=rstd[:, :], in_=ss[:, :], func=AF.Ln, scale=1.0 / D, bias=C.eps5[:, 0:1],
        reads=[(tag + "ss", slot)], writes=[(tag + "rstd", slot)])
    S.I("act", "activation", out=rstd[:, :], in_=rstd[:, :], func=AF.Exp, scale=-0.5,
        reads=[(tag + "rstd", slot)], writes=[(tag + "rstd", slot)])
    for s in range(N.NS):
        S.I("dve", "scalar_tensor_tensor", out=N.hn[:, s, :], in0=xs[:, s, :], scalar=rstd[:, s:s + 1],
            in1=gcol[:, :], op0=ALU.mult, op1=ALU.mult,
            reads=[(tag + "xt", slot), (tag + "rstd", slot)], writes=[(tag + "hn", s)])
    hT = N.hnT[tslot]
    for kc in range(8):
        b = C.psT[C.psT_i % len(C.psT)]
        C.psT_i += 1
        for s in range(N.NS):
            S.I("pe", "matmul", C.ps[b][:, s * 128:(s + 1) * 128], N.hn[:, s, kc * 128:(kc + 1) * 128],
                C.ident[:, :], start=True, stop=True,
                reads=[(tag + "hn", s)], writes=[("ps", b)])
        if kc % 2 == 0:
            S.I("act", "activation", out=hT[:, kc, :], in_=C.ps[b][:, 0:N.TK], func=AF.Copy,
                reads=[("ps", b)], writes=[(tag + "hnT", tslot, kc)])
        else:
            S.I("dve", "tensor_copy", hT[:, kc, :], C.ps[b][:, 0:N.TK],
                reads=[("ps", b)], writes=[(tag + "hnT", tslot, kc)])
    return xs, hT, slot, tslot


def load_bcast_row(C, es, row_ap, n, name):
    t = sb(es, C.nc, name, [128, n], F32)
    C.S.dma("sp", t[:, :], row_ap.partition_broadcast(128), writes=[(name, 0)])
    return t


def phase_inproj(C, x_src, g_row, w_ap, NOUT, u_dst, tag):
    nc, S = C.nc, C.S
    with ExitStack() as es:
        gcol = load_bcast_row(C, es, g_row, D, tag + "g")
        wb = load_weight_bf16(C, es, w_ap, D, NOUT, tag + "w")
        N = alloc_norm_tiles(C, es, tag, TT)
        ost = [sb(es, nc, tag + "ost%d" % i, [128, TT], F32) for i in range(4)]
        oi = 0
        norm_load(C, N, x_src, 0)
        nxt = norm_tile(C, N, gcol, 0)
        for ti in range(NT):
            if ti + 1 < NT:
                norm_load(C, N, x_src, ti + 1)
            xs, hT, slot, tslot = nxt
            for oc in range(NOUT // 128):
                if oc == (NOUT // 128) // 2 and ti + 1 < NT:
                    nxt = norm_tile(C, N, gcol, ti + 1)
                b = C.psM[C.psM_i % len(C.psM)]
                C.psM_i += 1
                for kc in range(8):
                    S.I("pe", "matmul", C.ps[b][:, :], wb[:, kc, oc * 128:(oc + 1) * 128], hT[:, kc, :],
                        start=(kc == 0), stop=(kc == 7),
                        reads=[(tag + "hnT", tslot, kc)], writes=[("ps", b)])
                o = oi % 4
                oi += 1
                if oc % 2 == 0:
                    S.I("act", "activation", out=ost[o][:, :], in_=C.ps[b][:, :], func=AF.Copy,
                        reads=[("ps", b)], writes=[(tag + "ost", o)])
                else:
                    S.I("dve", "tensor_copy", ost[o][:, :], C.ps[b][:, :],
                        reads=[("ps", b)], writes=[(tag + "ost", o)])
                S.dma("sp", u_dst[oc * 128:(oc + 1) * 128, ti * TT:(ti + 1) * TT], ost[o][:, :],
                      reads=[(tag + "ost", o)])
        S.barrier()
        S.flush()


def colvec(C, es, ap, R, NCOL, name):
    nc, S = C.nc, C.S
    NCH = NCOL // 128
    out = sb(es, nc, name, [128, NCH, R], F32)
    with ExitStack() as es2:
        rows = sb(es2, nc, name + "_rows", [R, NCOL], F32)
        S.dma("sp", rows[:, :], ap, writes=[(name + "rows", 0)])
        for c in range(NCH):
            b = C.psT[C.psT_i % len(C.psT)]
            C.psT_i += 1
            S.I("pe", "matmul", C.ps[b][:, 0:R], rows[0:R, c * 128:(c + 1) * 128], C.identf[0:R, 0:R],
                start=True, stop=True, reads=[(name + "rows", 0)], writes=[("ps", b)])
            S.I("dve", "tensor_copy", out[:, c, :], C.ps[b][:, 0:R], reads=[("ps", b)], writes=[(name, c)])
        S.barrier()
        S.flush()
    return out


def phase_conformer(C, u, ya_dst):
    nc, S = C.nc, C.S
    W = C.W
    tag = "B"
    HW = 15
    with ExitStack() as es:
        wcol = colvec(C, es, W["cv_dw_w"], 31, 512, "Bw")
        bcol = colvec(C, es, W["cv_dw_b"], 1, 512, "Bb")
        gcol = colvec(C, es, W["cv_ln_g"], 1, 512, "Bg")
        becol = colvec(C, es, W["cv_ln_b"], 1, 512, "Bbe")
        at = [sb(es, nc, "Bat%d" % i, [128, TT + 2 * HW], F32) for i in range(4)]
        gt = [sb(es, nc, "Bgt%d" % i, [128, TT + 2 * HW], F32) for i in range(4)]
        ht = [sb(es, nc, "Bht%d" % i, [128, TT + 2 * HW], F32) for i in range(4)]
        cv = [sb(es, nc, "Bcv%d" % i, [128, TT], F32) for i in range(4)]
        sq = [sb(es, nc, "Bsq%d" % i, [128, TT], F32) for i in range(2)]
        mean = sb(es, nc, "Bmean", [128, TT], F32)
        msq = sb(es, nc, "Bmsq", [128, TT], F32)
        rstd = sb(es, nc, "Brstd", [128, TT], F32)
        t1 = [sb(es, nc, "Bt1%d" % i, [128, TT], F32) for i in range(2)]
        yo = [sb(es, nc, "Byo%d" % i, [128, TT], BF16) for i in range(2)]
        for i in range(4):
            S.I("pool", "memset", at[i][:, :], 0.0, writes=[("Bat", i)])
            S.I("pool", "memset", gt[i][:, :], 0.0, writes=[("Bgt", i)])
        for ti in range(NT):
            t0 = ti * TT
            lo = max(t0 - HW, 0)
            hi = min(t0 + TT + HW, T)
            c0 = lo - (t0 - HW)
            c1 = c0 + (hi - lo)
            for c in range(4):
                if ti == NT - 1:
                    S.I("pool", "memset", at[c][:, c1:], 0.0, writes=[("Bat", c)])
                    S.I("pool", "memset", gt[c][:, c1:], 0.0, writes=[("Bgt", c)])
                S.dma("sp", at[c][:, c0:c1], u[c * 128:(c + 1) * 128, lo:hi], writes=[("Bat", c)])
                S.dma("sp", gt[c][:, c0:c1], u[512 + c * 128:512 + (c + 1) * 128, lo:hi], writes=[("Bgt", c)])
            for c in range(4):
                S.I("act", "activation", out=gt[c][:, :], in_=gt[c][:, :], func=AF.Sigmoid,
                    reads=[("Bgt", c)], writes=[("Bgt", c)])
                eng = "dve" if c < 2 else "pool"
                S.I(eng, "tensor_tensor", ht[c][:, :], at[c][:, :], gt[c][:, :], op=ALU.mult,
                    reads=[("Bat", c), ("Bgt", c)], writes=[("Bht", c)])
                if ti == NT // 2 - 1:
                    S.I(eng, "tensor_scalar", ht[c][:, TT + HW:], ht[c][:, TT + HW:], C.flagcol[:, 0:1], None,
                        op0=ALU.mult, reads=[("Bht", c)], writes=[("Bht", c)])
                if ti == NT // 2:
                    S.I(eng, "tensor_scalar", ht[c][:, 0:HW], ht[c][:, 0:HW], C.flagcol[:, 0:1], None,
                        op0=ALU.mult, reads=[("Bht", c)], writes=[("Bht", c)])
            for j in range(31):
                for c in range(4):
                    eng = "dve"
                    if j == 0:
                        S.I(eng, "tensor_scalar", cv[c][:, :], ht[c][:, 0:TT], wcol[:, c, 0:1], bcol[:, c, 0:1],
                            op0=ALU.mult, op1=ALU.add, reads=[("Bht", c)], writes=[("Bcv", c)])
                    else:
                        S.I(eng, "scalar_tensor_tensor", out=cv[c][:, :], in0=ht[c][:, j:j + TT],
                            scalar=wcol[:, c, j:j + 1], in1=cv[c][:, :], op0=ALU.mult, op1=ALU.add,
                            reads=[("Bht", c)], writes=[("Bcv", c)])
            b1 = C.psM[C.psM_i % len(C.psM)]
            C.psM_i += 1
            b2 = C.psM[C.psM_i % len(C.psM)]
            C.psM_i += 1
            for c in range(4):
                S.I("pe", "matmul", C.ps[b1][:, :], C.onesf[:, :], cv[c][:, :], start=(c == 0), stop=(c == 3),
                    reads=[("Bcv", c)], writes=[("ps", b1)])
            for c in range(4):
                S.I("act", "activation", out=sq[c % 2][:, :], in_=cv[c][:, :], func=AF.Square,
                    reads=[("Bcv", c)], writes=[("Bsq", c % 2)])
                S.I("pe", "matmul", C.ps[b2][:, :], C.onesf[:, :], sq[c % 2][:, :], start=(c == 0), stop=(c == 3),
                    reads=[("Bsq", c % 2)], writes=[("ps", b2)])
            S.I("act", "activation", out=mean[:, :], in_=C.ps[b1][:, :], func=AF.Copy, scale=1.0 / 512,
                reads=[("ps", b1)], writes=[("Bmean", 0)])
            S.I("dve", "tensor_tensor", msq[:, :], mean[:, :], mean[:, :], op=ALU.mult,
                reads=[("Bmean", 0)], writes=[("Bmsq", 0)])
            S.I("dve", "scalar_tensor_tensor", out=rstd[:, :], in0=C.ps[b2][:, :], scalar=1.0 / 512, in1=msq[:, :],
                op0=ALU.mult, op1=ALU.subtract, reads=[("ps", b2), ("Bmsq", 0)], writes=[("Brstd", 0)])
            S.I("act", "activation", out=rstd[:, :], in_=rstd[:, :], func=AF.Ln, bias=C.eps5[:, 0:1],
                reads=[("Brstd", 0)], writes=[("Brstd", 0)])
            S.I("act", "activation", out=rstd[:, :], in_=rstd[:, :], func=AF.Exp, scale=-0.5,
                reads=[("Brstd", 0)], writes=[("Brstd", 0)])
            for c in range(4):
                k = c % 2
                S.I("dve", "tensor_tensor", t1[k][:, :], cv[c][:, :], mean[:, :], op=ALU.subtract,
                    reads=[("Bcv", c), ("Bmean", 0)], writes=[("Bt1", k)])
                S.I("pool", "tensor_tensor", t1[k][:, :], t1[k][:, :], rstd[:, :], op=ALU.mult,
                    reads=[("Bt1", k), ("Brstd", 0)], writes=[("Bt1", k)])
                S.I("act", "activation", out=yo[k][:, :], in_=t1[k][:, :], func=AF.Silu,
                    scale=gcol[:, c, 0:1], bias=becol[:, c, 0:1],
                    reads=[("Bt1", k)], writes=[("Byo", k)])
                S.dma("act", ya_dst[c * 128:(c + 1) * 128, t0:t0 + TT], yo[k][:, :], reads=[("Byo", k)])
        S.barrier()
        S.flush()


HC128_COLS = {}
HC65_COLS = {}


def hyena_consts():
    p = np.arange(128)
    n1 = (p % 64)[:, None].astype(np.float64)
    k1 = np.arange(65)[None, :].astype(np.float64)
    th = 2 * np.pi * n1 * k1 / 128.0
    F1cat = np.concatenate([np.cos(th), -np.sin(th)], 1)
    n2 = (p % 64)[:, None].astype(np.float64)
    ph = 2 * np.pi * n2 * k1 / 8192.0
    Tc2 = np.concatenate([np.cos(ph), np.cos(ph)], 1)
    Ts2 = np.concatenate([np.sin(ph), np.sin(ph)], 1)
    q64 = np.arange(64)
    psi = 2 * np.pi * (p % 64)[:, None] * q64[None, :] / 64.0
    Mc = np.cos(psi)
    Ms = np.sin(psi)
    rhsA = np.concatenate([Mc, Ms], 1)
    rhsB = np.concatenate([-Ms, Mc], 1)
    sg = ((-1.0) ** np.arange(65))[None, :].repeat(128, 0)
    sgn2 = np.concatenate([sg, sg], 1)
    parts = [("F1cat", F1cat), ("Tc2", Tc2), ("Ts2", Ts2), ("Mc", Mc), ("Ms", Ms), ("nMs", -Ms),
             ("rhsA", rhsA), ("rhsB", rhsB), ("sgn2", sgn2)]
    off = 0
    for nm, a in parts:
        HC128_COLS[nm] = (off, a.shape[1])
        off += a.shape[1]
    hc128 = np.concatenate([a for _, a in parts], 1).astype(np.float32)
    kk = np.arange(65)[:, None].astype(np.float64)
    col = np.arange(128)[None, :]
    phi = 2 * np.pi * kk * (col % 64) / 8192.0
    Tci = np.cos(phi)
    Tsi = np.sin(phi)
    wk = np.full((65, 1), 2.0)
    wk[0, 0] = 1.0
    wk[64, 0] = 1.0
    n1c = np.arange(64)[None, :].astype(np.float64)
    thi = 2 * np.pi * kk * n1c / 128.0
    gc = wk * np.cos(thi) / 8192.0
    gs = -wk * np.sin(thi) / 8192.0
    zz = np.zeros((65, 64))
    parts = [("Tci", Tci), ("Tsi", Tsi), ("Gc_a", np.concatenate([gc, zz], 1)), ("Gs_a", np.concatenate([gs, zz], 1)),
             ("Gc_b", np.concatenate([zz, gc], 1)), ("Gs_b", np.concatenate([zz, gs], 1))]
    off = 0
    for nm, a in parts:
        HC65_COLS[nm] = (off, a.shape[1])
        off += a.shape[1]
    hc65 = np.concatenate([a for _, a in parts], 1).astype(np.float32)
    return hc128, hc65


def hyena_pos_tables(is_prompt):
    L = 8192 if is_prompt else 4096
    q = np.arange(8192)
    n2 = q // 128
    hp = q % 128
    half = hp // 64
    n1 = hp % 64
    pl = n1 * 64 + n2
    pos = pl + (4096 * half if is_prompt else 0)
    t = pos.astype(np.float64) / (L - 1)
    bands = 16
    f = np.linspace(1e-4, bands - 1, bands)
    w = 2 * np.pi * pos.astype(np.float64) / L
    fw = w[None, :] * f[:, None]
    zT = np.concatenate([t[None, :], np.cos(fw), -np.sin(fw)], 0).astype(np.float32)
    tn = -t.reshape(64, 128).T.copy()
    if not is_prompt:
        tn[64:, :] = -1e4
    tnb = tn.copy()
    tnb[0, 0] = -1e4
    return zT, np.concatenate([tn, tnb], 1).astype(np.float32)


def hc(C, name, rows=128):
    off, n = (HC128_COLS if rows == 128 else HC65_COLS)[name]
    t = C.hc128 if rows == 128 else C.hc65
    return t[0:rows, off:off + n]


def phase_shortconv(C, u, uh):
    nc, S = C.nc, C.S
    W = C.W
    with ExitStack() as es:
        wcol = colvec(C, es, W["hy_short_w"], 3, 1536, "Sw")
        bcol = colvec(C, es, W["hy_short_b"], 1, 1536, "Sb")
        it = [sb(es, nc, "Sit%d" % i, [128, TT + 2], F32) for i in range(3)]
        cv = [sb(es, nc, "Scv%d" % i, [128, TT], F32) for i in range(2)]
        ot = [sb(es, nc, "Sot%d" % i, [128, 4, 128], F32) for i in range(2)]
        k = 0
        for c in range(12):
            for ti in range(NT):
                t0 = ti * TT
                lo = max(t0 - 1, 0)
                hi = min(t0 + TT + 1, T)
                c0 = lo - (t0 - 1)
                c1 = c0 + (hi - lo)
                i3 = k % 3
                i2 = k % 2
                k += 1
                if ti == 0:
                    S.I("pool", "memset", it[i3][:, 0:1], 0.0, writes=[("Sit", i3)])
                if ti == NT - 1:
                    S.I("pool", "memset", it[i3][:, TT + 1:TT + 2], 0.0, writes=[("Sit", i3)])
                S.dma("sp", it[i3][:, c0:c1], u[1024 + c * 128:1024 + (c + 1) * 128, lo:hi], writes=[("Sit", i3)])
                if ti == NT // 2 - 1:
                    S.I("pool", "tensor_scalar", it[i3][:, TT + 1:TT + 2], it[i3][:, TT + 1:TT + 2],
                        C.flagcol[:, 0:1], None, op0=ALU.mult, reads=[("Sit", i3)], writes=[("Sit", i3)])
                if ti == NT // 2:
                    S.I("pool", "tensor_scalar", it[i3][:, 0:1], it[i3][:, 0:1],
                        C.flagcol[:, 0:1], None, op0=ALU.mult, reads=[("Sit", i3)], writes=[("Sit", i3)])
                S.I("dve", "tensor_scalar", cv[i2][:, :], it[i3][:, 0:TT], wcol[:, c, 0:1], bcol[:, c, 0:1],
                    op0=ALU.mult, op1=ALU.add, reads=[("Sit", i3)], writes=[("Scv", i2)])
                for j in (1, 2):
                    S.I("dve", "scalar_tensor_tensor", out=cv[i2][:, :], in0=it[i3][:, j:j + TT],
                        scalar=wcol[:, c, j:j + 1], in1=cv[i2][:, :], op0=ALU.mult, op1=ALU.add,
                        reads=[("Sit", i3)], writes=[("Scv", i2)])
                b = C.psT[C.psT_i % len(C.psT)]
                C.psT_i += 1
                for s in range(4):
                    S.I("pe", "matmul", C.ps[b][:, s * 128:(s + 1) * 128], cv[i2][:, s * 128:(s + 1) * 128],
                        C.identf[:, :], start=True, stop=True, reads=[("Scv", i2)], writes=[("ps", b)])
                S.I("act", "activation", out=ot[i2][:, :, :], in_=C.ps[b][:, :].rearrange("p (s c) -> p s c", s=4),
                    func=AF.Copy, reads=[("ps", b)], writes=[("Sot", i2)])
                S.dma("act", uh[t0:t0 + TT, c * 128:(c + 1) * 128].rearrange("(s p) c -> p s c", p=128),
                      ot[i2][:, :, :], reads=[("Sot", i2)])
        S.barrier()
        S.flush()


def alloc_fft_tiles(C, es, pfx):
    nc = C.nc
    Fq = Ctx()
    Fq.A2 = sb(es, nc, pfx + "A2", [64, 16, 130], F32)
    Fq.B = sb(es, nc, pfx + "B", [64, 2, 16, 65], F32)
    Fq.tmpA = sb(es, nc, pfx + "tmpA", [128, 2080], F32)
    Fq.tmpB = sb(es, nc, pfx + "tmpB", [128, 2080], F32)
    Fq.pfx = pfx
    return Fq


def fft_fwd_group(C, Fq, src, src_tok, ch0, X, X_tok):
    fft_s1(C, Fq, src, src_tok, ch0)
    fft_rest(C, Fq, X, X_tok)


def fft_s1(C, Fq, src, src_tok, ch0):
    S = C.S
    pfx = Fq.pfx
    F1 = hc(C, "F1cat")
    slot = 0
    while slot < 16:
        nb = 2
        b = C.psM[C.psM_i % len(C.psM)]
        C.psM_i += 1
        for s in range(nb):
            i = slot + s
            h, chl = i // 8, i % 8
            S.I("pe", "matmul", C.ps[b][0:64, s * 256:s * 256 + 130],
                src[h * 64:(h + 1) * 64, :, ch0 + chl], F1[h * 64:(h + 1) * 64, :],
                start=True, stop=True, reads=[src_tok], writes=[("ps", b)])
        S.I("act", "activation", out=Fq.A2[:, slot:slot + nb, :],
            in_=C.ps[b][0:64, :].rearrange("p (s c) -> p s c", s=2)[:, :, 0:130], func=AF.Copy,
            reads=[("ps", b)], writes=[(pfx + "A2", 0)])
        slot += nb


def fft_rest(C, Fq, X, X_tok):
    S = C.S
    pfx = Fq.pfx
    Tc = hc(C, "Tc2")[0:64, :].unsqueeze(1).broadcast_to([64, 16, 130])
    Ts = hc(C, "Ts2")[0:64, :].unsqueeze(1).broadcast_to([64, 16, 130])
    P1 = Fq.tmpA[0:64, 0:2080].rearrange("p (a c) -> p a c", a=16)
    P2 = Fq.tmpB[0:64, 0:2080].rearrange("p (a c) -> p a c", a=16)
    S.I("dve", "tensor_tensor", P1, Fq.A2[:, :, :], Tc, op=ALU.mult,
        reads=[(pfx + "A2", 0)], writes=[(pfx + "tmpA", 0)])
    S.I("pool", "tensor_tensor", P2, Fq.A2[:, :, :], Ts, op=ALU.mult,
        reads=[(pfx + "A2", 0)], writes=[(pfx + "tmpB", 0)])
    S.I("dve", "tensor_tensor", Fq.B[:, 0, :, :], P1[:, :, 0:65], P2[:, :, 65:130], op=ALU.add,
        reads=[(pfx + "tmpA", 0), (pfx + "tmpB", 0)], writes=[(pfx + "B", 0)])
    S.I("pool", "tensor_tensor", Fq.B[:, 1, :, :], P1[:, :, 65:130], P2[:, :, 0:65], op=ALU.subtract,
        reads=[(pfx + "tmpA", 0), (pfx + "tmpB", 0)], writes=[(pfx + "B", 1)])
    if C.dbg.get("ffstop") == 2:
        return
    Mc, Ms, nMs = hc(C, "Mc")[0:64, :], hc(C, "Ms")[0:64, :], hc(C, "nMs")[0:64, :]
    for q in range(4):
        Br = Fq.B[:, 0, 4 * q:4 * q + 4, :]
        Bi = Fq.B[:, 1, 4 * q:4 * q + 4, :]
        b1 = C.psM[C.psM_i % len(C.psM)]
        C.psM_i += 1
        b2 = C.psM[C.psM_i % len(C.psM)]
        C.psM_i += 1
        S.I("pe", "matmul", C.ps[b1][0:64, 0:260], Mc, Br, start=True, stop=False,
            reads=[(pfx + "B", 0)], writes=[("ps", b1)])
        S.I("pe", "matmul", C.ps[b1][0:64, 0:260], Ms, Bi, start=False, stop=True,
            reads=[(pfx + "B", 1)], writes=[("ps", b1)])
        S.I("pe", "matmul", C.ps[b2][0:64, 0:260], Mc, Bi, start=True, stop=False,
            reads=[(pfx + "B", 1)], writes=[("ps", b2)])
        S.I("pe", "matmul", C.ps[b2][0:64, 0:260], nMs, Br, start=False, stop=True,
            reads=[(pfx + "B", 0)], writes=[("ps", b2)])
        h, c4 = q // 2, 4 * (q % 2)
        S.I("act", "activation", out=X[:, 0, h, c4:c4 + 4, :],
            in_=C.ps[b1][0:64, 0:260].rearrange("p (a c) -> p a c", a=4), func=AF.Copy,
            reads=[("ps", b1)], writes=[X_tok])
        S.I("act", "activation", out=X[:, 1, h, c4:c4 + 4, :],
            in_=C.ps[b2][0:64, 0:260].rearrange("p (a c) -> p a c", a=4), func=AF.Copy,
            reads=[("ps", b2)], writes=[X_tok])


def phase_filters(C, Gd, zT_in, tneg_in):
    nc, S = C.nc, C.S
    W = C.W
    with ExitStack() as es:
        fcol = sb(es, nc, "Ffcol", [64, 3], F32)
        bcol = sb(es, nc, "Fbcol", [64, 3], F32)
        f8 = sb(es, nc, "Ff8", [64, 3], F32)
        b8 = sb(es, nc, "Fb8", [64, 3], F32)
        f4 = sb(es, nc, "Ff4", [64, 3], F32)
        b4 = sb(es, nc, "Fb4", [64, 3], F32)
        with ExitStack() as es2:
            rows = sb(es2, nc, "Frows", [6, 64], F32)
            S.dma("sp", rows[0:3, :], W["hy_freq"], writes=[("Frows", 0)])
            for i, nm in enumerate(("hy_b1", "hy_b2", "hy_b3")):
                S.dma("sp", rows[3 + i:4 + i, :], W[nm], writes=[("Frows", 0)])
            b = C.psT[C.psT_i % len(C.psT)]
            C.psT_i += 1
            S.I("pe", "matmul", C.ps[b][0:64, 0:6], rows[0:6, 0:64], C.identf[0:6, 0:6], start=True, stop=True,
                reads=[("Frows", 0)], writes=[("ps", b)])
            S.I("dve", "tensor_copy", fcol[:, :], C.ps[b][0:64, 0:3], reads=[("ps", b)], writes=[("Ffcol", 0)])
            S.I("dve", "tensor_copy", bcol[:, :], C.ps[b][0:64, 3:6], reads=[("ps", b)], writes=[("Fbcol", 0)])
            S.I("dve", "tensor_tensor", bcol[:, :], bcol[:, :], fcol[:, :], op=ALU.mult,
                reads=[("Ffcol", 0), ("Fbcol", 0)], writes=[("Fbcol", 0)])
            S.I("dve", "tensor_scalar", f8[:, :], fcol[:, :], 0.125, None, op0=ALU.mult,
                reads=[("Ffcol", 0)], writes=[("Ff8", 0)])
            S.I("dve", "tensor_scalar", b8[:, :], bcol[:, :], 0.125, None, op0=ALU.mult,
                reads=[("Fbcol", 0)], writes=[("Fb8", 0)])
            S.I("dve", "tensor_scalar", f4[:, :], fcol[:, :], 0.25, None, op0=ALU.mult,
                reads=[("Ffcol", 0)], writes=[("Ff4", 0)])
            S.I("dve", "tensor_scalar", b4[:, :], bcol[:, :], 0.25, None, op0=ALU.mult,
                reads=[("Fbcol", 0)], writes=[("Fb4", 0)])
            S.barrier()
            S.flush()
        w1 = sb(es, nc, "Fw1", [33, 64], F32)
        w2 = sb(es, nc, "Fw2", [64, 64], F32)
        w3 = sb(es, nc, "Fw3", [64, 64], F32)
        w4 = sb(es, nc, "Fw4", [64, 2048], F32)
        S.dma("sp", w1[:, :], W["hy_w1"])
        S.dma("sp", w2[:, :], W["hy_w2"])
        S.dma("sp", w3[:, :], W["hy_w3"])
        S.dma("sp", w4[:, :], W["hy_w4"])
        absd = sb(es, nc, "Fabsd", [128, 2048], F32)
        S.dma("sp", absd[:, :], W["hy_decay"].partition_broadcast(128), writes=[("Fabsd", 0)])
        S.I("act", "activation", out=absd[:, :], in_=absd[:, :], func=AF.Abs,
            reads=[("Fabsd", 0)], writes=[("Fabsd", 0)])
        tneg = sb(es, nc, "Ftneg", [128, 128], F32)
        S.dma("sp", tneg[:, :], tneg_in)
        negpi = sb(es, nc, "Fnegpi", [128, 1], F32)
        S.I("pool", "memset", negpi[:, :], -math.pi)
        eps6 = sb(es, nc, "Feps6", [128, 1], F32)
        S.I("pool", "memset", eps6[:, :], 1e-6)
        h3T = sb(es, nc, "Fh3T", [64, 8192], F32)
        S.barrier()
        S.flush()
        if C.dbg.get("fstop") == 1:
            return
        with ExitStack() as es2:
            zt = [sb(es2, nc, "Fzt%d" % i, [33, 512], F32) for i in range(2)]
            ha = [sb(es2, nc, "Fha%d" % i, [64, 512], F32) for i in range(2)]
            hb = [sb(es2, nc, "Fhb%d" % i, [64, 512], F32) for i in range(2)]
            hc_ = [sb(es2, nc, "Fhc%d" % i, [64, 512], F32) for i in range(2)]
            for blk in range(16):
                i2 = blk % 2
                S.dma("sp", zt[i2][:, :], zT_in[:, blk * 512:(blk + 1) * 512], writes=[("Fzt", i2)])
                cur = zt[i2][0:33, :]
                cur_tok = ("Fzt", i2)
                for l, wl in enumerate((w1, w2, w3)):
                    K = 33 if l == 0 else 64
                    b = C.psM[C.psM_i % len(C.psM)]
                    C.psM_i += 1
                    S.I("pe", "matmul", C.ps[b][0:64, :], wl[0:K, :], cur, start=True, stop=True,
                        reads=[cur_tok], writes=[("ps", b)])
                    sa, sb_ = ha[i2], hc_[i2]
                    S.I("act", "activation", out=sa[:, :], in_=C.ps[b][0:64, :], func=AF.Sin,
                        scale=f8[:, l:l + 1], bias=b8[:, l:l + 1], reads=[("ps", b)], writes=[("Fha", i2)])
                    S.I("act", "activation", out=sb_[:, :], in_=C.ps[b][0:64, :], func=AF.Sin,
                        scale=f4[:, l:l + 1], bias=b4[:, l:l + 1], reads=[("ps", b)], writes=[("Fhc", i2)])
                    S.I("dve", "tensor_tensor", sa[:, :], sa[:, :], sa[:, :], op=ALU.mult,
                        reads=[("Fha", i2)], writes=[("Fha", i2)])
                    S.I("dve", "tensor_scalar", sa[:, :], sa[:, :], -2.0, 1.0, op0=ALU.mult, op1=ALU.add,
                        reads=[("Fha", i2)], writes=[("Fha", i2)])
                    S.I("dve", "scalar_tensor_tensor", out=sa[:, :], in0=sb_[:, :], scalar=2.0, in1=sa[:, :],
                        op0=ALU.mult, op1=ALU.mult, reads=[("Fha", i2), ("Fhc", i2)], writes=[("Fha", i2)])
                    S.I("dve", "tensor_tensor", sb_[:, :], sb_[:, :], sb_[:, :], op=ALU.mult,
                        reads=[("Fhc", i2)], writes=[("Fhc", i2)])
                    S.I("dve", "tensor_scalar", sb_[:, :], sb_[:, :], -2.0, 1.0, op0=ALU.mult, op1=ALU.add,
                        reads=[("Fhc", i2)], writes=[("Fhc", i2)])
                    dst = h3T[:, blk * 512:(blk + 1) * 512] if l == 2 else hb[i2][:, :]
                    dtok = ("Fh3T", blk) if l == 2 else ("Fhb", i2)
                    S.I("dve", "scalar_tensor_tensor", out=dst, in0=sa[:, :], scalar=2.0, in1=sb_[:, :],
                        op0=ALU.mult, op1=ALU.mult, reads=[("Fha", i2), ("Fhc", i2)], writes=[dtok])
                    cur = hb[i2][:, :]
                    cur_tok = ("Fhb", i2)
            S.barrier()
            S.flush()
        if C.dbg.get("fstop") == 2:
            return
        fw = sb(es, nc, "Ffw", [128, 64, 128], F32)
        bw = sb(es, nc, "Fbw", [128, 64, 128], F32)
        win = [sb(es, nc, "Fwin%d" % i, [128, 4, 128], F32) for i in range(2)]
        sq = [sb(es, nc, "Fsq%d" % i, [128, 4, 128], F32) for i in range(2)]
        scl = sb(es, nc, "Fscl", [128, 128], F32)
        Fq = alloc_fft_tiles(C, es, "Fq")
        XF = sb(es, nc, "FXF", [64, 2, 2, 8, 65], F32)
        XB = sb(es, nc, "FXB", [64, 2, 2, 8, 65], F32)
        Go = [[sb(es, nc, "FGo%d_%d" % (w_, 0), [64, 2, 8, 65], F32)] * 2 for w_ in range(3)]
        t65 = [Fq.tmpA[0:64, 0:1040].rearrange("p (r a c) -> p r a c", r=2, a=8),
               Fq.tmpB[0:64, 0:1040].rearrange("p (r a c) -> p r a c", r=2, a=8)]
        nflag = sb(es, nc, "Fnflag", [128, 1], F32)
        S.I("dve", "tensor_scalar", nflag[:, :], C.flagcol[:, 0:1], -1.0, None, op0=ALU.mult)
        wi = 0
        gi = 0
        for o in range(2):
            for cc in range(4):
                bss = C.psT[0]
                nmm = 0
                for d, dst in ((0, fw), (1, bw)):
                    colbase = (o * 2 + d) * 512 + cc * 128
                    for nb in range(16):
                        b = C.psM[C.psM_i % len(C.psM)]
                        C.psM_i += 1
                        wv = win[wi % 2]
                        wt = ("Fwin", wi % 2)
                        sv = sq[wi % 2]
                        st_ = ("Fsq", wi % 2)
                        wi += 1
                        for s in range(4):
                            n2 = nb * 4 + s
                            S.I("pe", "matmul", C.ps[b][:, s * 128:(s + 1) * 128], h3T[0:64, n2 * 128:(n2 + 1) * 128],
                                w4[0:64, colbase:colbase + 128], start=True, stop=True,
                                reads=[("Fh3T", n2 // 4)], writes=[("ps", b)])
                            S.I("act", "activation", out=wv[:, s, :], in_=absd[:, colbase:colbase + 128], func=AF.Exp,
                                scale=tneg[:, d * 64 + n2:d * 64 + n2 + 1], reads=[("Fabsd", 0)], writes=[wt])
                        S.I("dve", "tensor_tensor", dst[:, nb * 4:(nb + 1) * 4, :],
                            C.ps[b][:, :].rearrange("p (s c) -> p s c", s=4), wv[:, :, :], op=ALU.mult,
                            reads=[("ps", b), wt], writes=[("Ffilt", d, nb)])
                        S.I("pool", "tensor_tensor", sv[:, :, :], dst[:, nb * 4:(nb + 1) * 4, :],
                            dst[:, nb * 4:(nb + 1) * 4, :], op=ALU.mult, reads=[("Ffilt", d, nb)], writes=[st_])
                        for s in range(4):
                            S.I("pe", "matmul", C.ps[bss][:, 0:128], C.onesf[:, :], sv[:, s, :],
                                start=(nmm == 0), stop=(nmm == 127), reads=[st_], writes=[("ps", bss)])
                            nmm += 1
                S.I("act", "activation", out=scl[:, :], in_=C.ps[bss][:, 0:128], func=AF.Ln, bias=eps6[:, 0:1],
                    reads=[("ps", bss)], writes=[("Fscl", 0)])
                S.I("act", "activation", out=scl[:, :], in_=scl[:, :], func=AF.Exp, scale=-0.5,
                    reads=[("Fscl", 0)], writes=[("Fscl", 0)])
                sclb = scl[:, :].unsqueeze(1).broadcast_to([128, 64, 128])
                allf = [("Ffilt", 0, nb) for nb in range(16)]
                allb = [("Ffilt", 1, nb) for nb in range(16)]
                S.I("dve", "tensor_tensor", fw[:, :, :], fw[:, :, :], sclb, op=ALU.mult,
                    reads=[("Fscl", 0)] + allf, writes=allf + [("Ffw", 0)])
                S.I("pool", "tensor_tensor", bw[:, :, :], bw[:, :, :], sclb, op=ALU.mult,
                    reads=[("Fscl", 0)] + allb, writes=allb + [("Fbw", 0)])
                if C.dbg.get("fstop") == 3:
                    S.barrier()
                    S.flush()
                    return
                sg = hc(C, "sgn2")[0:64, :].rearrange("p (r c) -> p r c", r=2).unsqueeze(2).broadcast_to([64, 2, 8, 65])
                for g in range(16):
                    ch0 = g * 8
                    if g == 0:
                        fft_s1(C, Fq, fw, ("Ffw", 0), ch0)
                    fft_rest(C, Fq, XF, ("FXF", 0))
                    fft_s1(C, Fq, bw, ("Fbw", 0), ch0)
                    fft_rest(C, Fq, XB, ("FXB", 0))
                    if g + 1 < 16:
                        fft_s1(C, Fq, fw, ("Ffw", 0), ch0 + 8)
                    if C.dbg.get("fstop") == 4:
                        S.barrier()
                        S.flush()
                        return
                    k2 = 0
                    Gaa, Gab, Gba = Go[0][k2], Go[1][k2], Go[2][k2]
                    Fl, Fh = XF[:, :, 0, :, :], XF[:, :, 1, :, :]
                    Bl, Bh = XB[:, :, 0, :, :], XB[:, :, 1, :, :]
                    S.I("dve", "tensor_tensor", Gaa[:, 0, :, :], Fl[:, 0, :, :], Bl[:, 0, :, :], op=ALU.add,
                        reads=[("FXF", 0), ("FXB", 0)], writes=[("FGo", 0, k2)])
                    S.I("dve", "tensor_tensor", Gaa[:, 1, :, :], Fl[:, 1, :, :], Bl[:, 1, :, :], op=ALU.subtract,
                        reads=[("FXF", 0), ("FXB", 0)], writes=[("FGo", 0, k2)])
                    S.I("pool", "tensor_tensor", t65[0], Fl, sg, op=ALU.mult,
                        reads=[("FXF", 0)], writes=[("FqtmpA", 0)])
                    S.I("pool", "tensor_tensor", t65[0], t65[0], Fh, op=ALU.add,
                        reads=[("FXF", 0), ("FqtmpA", 0)], writes=[("FqtmpA", 0)])
                    S.I("pool", "tensor_scalar", Gba[:, :, :, :], t65[0], C.flagcol[0:64, 0:1], None,
                        op0=ALU.mult, reads=[("FqtmpA", 0)], writes=[("FGo", 2, k2)])
                    S.I("dve", "tensor_tensor", t65[1], Bl, sg, op=ALU.mult,
                        reads=[("FXB", 0)], writes=[("FqtmpB", 0)])
                    S.I("dve", "tensor_tensor", t65[1], t65[1], Bh, op=ALU.add,
                        reads=[("FXB", 0), ("FqtmpB", 0)], writes=[("FqtmpB", 0)])
                    S.I("dve", "tensor_scalar", Gab[:, 0, :, :], t65[1][:, 0, :, :], C.flagcol[0:64, 0:1], None,
                        op0=ALU.mult, reads=[("FqtmpB", 0)], writes=[("FGo", 1, k2)])
                    S.I("dve", "tensor_scalar", Gab[:, 1, :, :], t65[1][:, 1, :, :], nflag[0:64, 0:1], None,
                        op0=ALU.mult, reads=[("FqtmpB", 0)], writes=[("FGo", 1, k2)])
                    for w_ in range(3):
                        S.dma("sp", Gd[o, cc, w_, :, :, g * 8:(g + 1) * 8, :], Go[w_][k2][:, :, :, :],
                              reads=[("FGo", w_, k2)])
        S.barrier()
        S.flush()


def phase_hyena(C, uh, Gd, yb):
    nc, S = C.nc, C.S
    W = C.W
    with ExitStack() as es:
        skb = sb(es, nc, "Hskb", [128, 2, 512], F32)
        for o in range(2):
            S.dma("sp", skb[:, o, :], W["hy_skip"][o:o + 1, :].partition_broadcast(128), writes=[("Hskb", 0)])
        z = sb(es, nc, "Hz", [128, 64, 128], F32)
        xg = sb(es, nc, "Hxg", [128, 64, 128], F32)
        Fq = alloc_fft_tiles(C, es, "Hq")
        X = sb(es, nc, "HX", [64, 2, 2, 8, 65], F32)
        Y = sb(es, nc, "HY", [64, 2, 2, 8, 65], F32)
        Gt = [[sb(es, nc, "HG%d_%d" % (w_, i), [64, 2, 8, 65], F32) for i in range(2)] for w_ in range(3)]
        PQ = [sb(es, nc, "HPQ%d" % i, [64, 2, 8, 65], F32) for i in range(4)]
        C2 = sb(es, nc, "HC2", [65, 16, 128], F32)
        Dt = sb(es, nc, "HDt", [65, 2, 16, 64], F32)
        gt = [sb(es, nc, "Hgt%d" % i, [128, 64, 8], F32) for i in range(2)]
        gi = 0
        ztoks = [("Hz", g) for g in range(16)]
        for cc in range(4):
            for q4 in range(16):
                S.dma("sp", z[:, q4 * 4:(q4 + 1) * 4, :],
                      uh[:, 1024 + cc * 128:1024 + (cc + 1) * 128].rearrange("(p n) c -> p n c", n=64)[:, q4 * 4:(q4 + 1) * 4, :],
                      writes=ztoks)
            for o in range(2):
                for q4 in range(16):
                    S.dma("sp", xg[:, q4 * 4:(q4 + 1) * 4, :],
                          uh[:, o * 512 + cc * 128:o * 512 + (cc + 1) * 128].rearrange("(p n) c -> p n c", n=64)[:, q4 * 4:(q4 + 1) * 4, :],
                          writes=[("Hxg", 0)])
                for g in range(16):
                    ch0 = g * 8
                    k2 = gi % 2
                    gi += 1
                    for w_ in range(3):
                        S.dma("sp", Gt[w_][k2][:, :, :, :], Gd[o, cc, w_, :, :, g * 8:(g + 1) * 8, :],
                              writes=[("HG", w_, k2)])
                    if g == 0:
                        fft_s1(C, Fq, z, ("Hz", 0), 0)
                    fft_rest(C, Fq, X, ("HX", 0))
                    if g + 1 < 16:
                        fft_s1(C, Fq, z, ("Hz", g + 1), ch0 + 8)
                    Gaa, Gab, Gba = Gt[0][k2], Gt[1][k2], Gt[2][k2]
                    Xa, Xb = X[:, :, 0, :, :], X[:, :, 1, :, :]

                    def bc(Gx, r):
                        return Gx[:, r:r + 1, :, :].broadcast_to([64, 2, 8, 65])
                    for half, (GA, GB) in enumerate(((Gaa, Gab), (Gba, Gaa))):
                        ta = ("HG", (0, 1, 2)[[Gaa, Gab, Gba].index(GA)], k2)
                        tb = ("HG", (0, 1, 2)[[Gaa, Gab, Gba].index(GB)], k2)
                        S.I("dve", "tensor_tensor", PQ[0][:, :, :, :], Xa, bc(GA, 0), op=ALU.mult,
                            reads=[("HX", 0), ta], writes=[("HPQ", 0)])
                        S.I("dve", "tensor_tensor", PQ[1][:, :, :, :], Xa, bc(GA, 1), op=ALU.mult,
                            reads=[("HX", 0), ta], writes=[("HPQ", 1)])
                        S.I("pool", "tensor_tensor", PQ[2][:, :, :, :], Xb, bc(GB, 0), op=ALU.mult,
                            reads=[("HX", 0), tb], writes=[("HPQ", 2)])
                        S.I("pool", "tensor_tensor", PQ[3][:, :, :, :], Xb, bc(GB, 1), op=ALU.mult,
                            reads=[("HX", 0), tb], writes=[("HPQ", 3)])
                        S.I("pool", "tensor_tensor", PQ[0][:, :, :, :], PQ[0][:, :, :, :], PQ[2][:, :, :, :], op=ALU.add,
                            reads=[("HPQ", 0), ("HPQ", 2)], writes=[("HPQ", 0)])
                        S.I("pool", "tensor_tensor", PQ[1][:, :, :, :], PQ[1][:, :, :, :], PQ[3][:, :, :, :], op=ALU.add,
                            reads=[("HPQ", 1), ("HPQ", 3)], writes=[("HPQ", 1)])
                        S.I("dve", "tensor_tensor", Y[:, 0, half, :, :], PQ[0][:, 0, :, :], PQ[1][:, 1, :, :],
                            op=ALU.subtract, reads=[("HPQ", 0), ("HPQ", 1)], writes=[("HY", half)])
                        S.I("dve", "tensor_tensor", Y[:, 1, half, :, :], PQ[0][:, 1, :, :], PQ[1][:, 0, :, :],
                            op=ALU.add, reads=[("HPQ", 0), ("HPQ", 1)], writes=[("HY", half)])
                    rhsA, rhsB = hc(C, "rhsA")[0:64, :], hc(C, "rhsB")[0:64, :]
                    for s0 in range(0, 16, 4):
                        b = C.psM[C.psM_i % len(C.psM)]
                        C.psM_i += 1
                        for s in range(4):
                            i = s0 + s
                            h, chl = i // 8, i % 8
                            S.I("pe", "matmul", C.ps[b][0:65, s * 128:(s + 1) * 128], Y[:, 0, h, chl, :], rhsA,
                                start=True, stop=False, reads=[("HY", h)], writes=[("ps", b)])
                            S.I("pe", "matmul", C.ps[b][0:65, s * 128:(s + 1) * 128], Y[:, 1, h, chl, :], rhsB,
                                start=False, stop=True, reads=[("HY", h)], writes=[("ps", b)])
                        S.I("act", "activation", out=C2[0:65, s0:s0 + 4, :],
                            in_=C.ps[b][0:65, :].rearrange("p (s c) -> p s c", s=4), func=AF.Copy,
                            reads=[("ps", b)], writes=[("HC2", 0)])
                    Tci = hc(C, "Tci", 65).unsqueeze(1).broadcast_to([65, 16, 128])
                    Tsi = hc(C, "Tsi", 65).unsqueeze(1).broadcast_to([65, 16, 128])
                    P1 = Fq.tmpA[0:65, 0:2048].rearrange("p (a c) -> p a c", a=16)
                    P2 = Fq.tmpB[0:65, 0:2048].rearrange("p (a c) -> p a c", a=16)
                    S.I("dve", "tensor_tensor", P1, C2[0:65, :, :], Tci, op=ALU.mult,
                        reads=[("HC2", 0)], writes=[("HqtmpA", 0)])
                    S.I("pool", "tensor_tensor", P2, C2[0:65, :, :], Tsi, op=ALU.mult,
                        reads=[("HC2", 0)], writes=[("HqtmpB", 0)])
                    S.I("dve", "tensor_tensor", Dt[0:65, 0, :, :], P1[:, :, 0:64], P2[:, :, 64:128], op=ALU.subtract,
                        reads=[("HqtmpA", 0), ("HqtmpB", 0)], writes=[("HDt", 0)])
                    S.I("pool", "tensor_tensor", Dt[0:65, 1, :, :], P2[:, :, 0:64], P1[:, :, 64:128], op=ALU.add,
                        reads=[("HqtmpA", 0), ("HqtmpB", 0)], writes=[("HDt", 1)])
                    b = C.psM[C.psM_i % len(C.psM)]
                    C.psM_i += 1
                    fin = [("Gc_a", 0, 0), ("Gs_a", 1, 0), ("Gc_b", 0, 1), ("Gs_b", 1, 1)]
                    for i, (gn, ri, h) in enumerate(fin):
                        S.I("pe", "matmul", C.ps[b][:, :], hc(C, gn, 65), Dt[0:65, ri, 8 * h:8 * h + 8, :],
                            start=(i == 0), stop=(i == 3), reads=[("HDt", ri)], writes=[("ps", b)])
                    yv = C.ps[b][:, :].rearrange("p (c n) -> p n c", c=8)
                    zv = z[:, :, ch0:ch0 + 8]
                    xv = xg[:, :, ch0:ch0 + 8]
                    skv = skb[:, o, cc * 128 + ch0:cc * 128 + ch0 + 8].unsqueeze(1).broadcast_to([128, 64, 8])
                    gk = gt[g % 2]
                    S.I("pool", "tensor_tensor", gk[:, :, :], zv, skv, op=ALU.mult,
                        reads=[("Hz", g), ("Hskb", 0)], writes=[("Hgt", g % 2)])
                    S.I("dve", "tensor_tensor", gk[:, :, :], yv, gk[:, :, :], op=ALU.add,
                        reads=[("ps", b), ("Hgt", g % 2)], writes=[("Hgt", g % 2)])
                    S.I("pool", "tensor_tensor", zv, gk[:, :, :], xv, op=ALU.mult,
                        reads=[("Hgt", g % 2), ("Hxg", 0)], writes=[("Hz", g)])
            for q4 in range(16):
                S.dma("sp", yb[:, cc * 128:(cc + 1) * 128].rearrange("(p n) c -> p n c", n=64)[:, q4 * 4:(q4 + 1) * 4, :],
                      z[:, q4 * 4:(q4 + 1) * 4, :], reads=ztoks)
        S.barrier()
        S.flush()


def phase_outproj(C, x_src, ya, yb, w_ap, x_dst):
    nc, S = C.nc, C.S
    tag = "O"
    with ExitStack() as es:
        wb = load_weight_bf16(C, es, w_ap, D, D, "Ow")
        xt = [sb(es, nc, "Oxt%d" % i, [128, 4, D], F32) for i in range(2)]
        yaT = [sb(es, nc, "OyaT%d" % i, [128, 4, TT], BF16) for i in range(2)]
        ybt = [sb(es, nc, "Oybt%d" % i, [128, 4, 512], F32) for i in range(2)]
        ybb = sb(es, nc, "Oybb", [128, 4, 512], BF16)
        ybT = sb(es, nc, "OybT", [128, 4, TT], BF16)
        for ti in range(NT):
            t0 = ti * TT
            sl = ti % 2
            S.dma("sp", xt[sl][:, :, :], x_src[t0:t0 + TT, :].rearrange("(s p) d -> p s d", p=128),
                  writes=[("Oxt", sl)])
            S.dma("sp", yaT[sl][:, :, :], ya[:, t0:t0 + TT].rearrange("(k p) t -> p k t", p=128),
                  writes=[("OyaT", sl)])
            S.dma("sp", ybt[sl][:, :, :], yb[t0:t0 + TT, :].rearrange("(s p) c -> p s c", p=128),
                  writes=[("Oybt", sl)])
            S.I("pool", "tensor_copy", ybb[:, :, :], ybt[sl][:, :, :], reads=[("Oybt", sl)], writes=[("Oybb", 0)])
            for kc in range(4):
                b = C.psT[C.psT_i % len(C.psT)]
                C.psT_i += 1
                for s in range(4):
                    S.I("pe", "matmul", C.ps[b][:, s * 128:(s + 1) * 128], ybb[:, s, kc * 128:(kc + 1) * 128],
                        C.ident[:, :], start=True, stop=True, reads=[("Oybb", 0)], writes=[("ps", b)])
                S.I("act", "activation", out=ybT[:, kc, :], in_=C.ps[b][:, :], func=AF.Copy,
                    reads=[("ps", b)], writes=[("OybT", kc)])
            for s in range(4):
                for nh in range(2):
                    b = C.psM[C.psM_i % len(C.psM)]
                    C.psM_i += 1
                    for kc in range(8):
                        if kc < 4:
                            lt, tok = yaT[sl][:, kc, s * 128:(s + 1) * 128], ("OyaT", sl)
                        else:
                            lt, tok = ybT[:, kc - 4, s * 128:(s + 1) * 128], ("OybT", kc - 4)
                        S.I("pe", "matmul", C.ps[b][:, :], lt, wb[:, kc, nh * 512:(nh + 1) * 512],
                            start=(kc == 0), stop=(kc == 7), reads=[tok], writes=[("ps", b)])
                    S.I("dve", "tensor_tensor", xt[sl][:, s, nh * 512:(nh + 1) * 512], C.ps[b][:, :],
                        xt[sl][:, s, nh * 512:(nh + 1) * 512], op=ALU.add,
                        reads=[("ps", b)], writes=[("Oxt", sl)])
            S.dma("pool", x_dst[t0:t0 + TT, :].rearrange("(s p) d -> p s d", p=128), xt[sl][:, :, :],
                  reads=[("Oxt", sl)])
        S.barrier()
        S.flush()


def attn_consts(is_prompt):
    d = np.arange(128) % 64
    inv = 500000.0 ** (-(np.arange(0, 16, 2, dtype=np.float64) / 16.0))
    tau = np.arange(T)
    pos = tau if is_prompt else tau % 4096
    cos = np.ones((128, T), np.float64)
    sin = np.zeros((128, T), np.float64)
    for p in range(128):
        dd = d[p]
        if dd < 16:
            ang = pos * inv[dd % 8]
            cos[p] = np.cos(ang)
            sin[p] = np.sin(ang)
    P = np.zeros((128, 128), np.float32)
    for m in range(128):
        dd = m % 64
        if dd < 8:
            P[m + 8, m] = -1.0
        elif dd < 16:
            P[m - 8, m] = 1.0
    q = np.arange(128)[:, None]
    s = np.arange(128)[None, :]
    NEG = -30000.0
    prev = np.where(s >= q, 0.0, NEG)
    cur = np.zeros((128, 128))
    nxt = np.where(s <= q, 0.0, NEG)
    band = np.concatenate([prev, cur, nxt], 1).astype(np.float32)
    full = np.full((128, 128), NEG)
    if is_prompt:
        m31, m32 = band, band
    else:
        m31 = np.concatenate([prev, cur, full], 1).astype(np.float32)
        m32 = np.concatenate([full, cur, nxt], 1).astype(np.float32)
    masks = np.stack([band, m31, m32], 0).astype(np.float32)
    return cos.astype(np.float32), sin.astype(np.float32), P, masks


def phase_qkv(C, x_src, g_row, w_ap, qd, kd, vd, cos_in, sin_in, prot_in):
    nc, S = C.nc, C.S
    tag = "Q"
    with ExitStack() as es:
        gcol = load_bcast_row(C, es, g_row, D, "Qg")
        wb = load_weight_bf16(C, es, w_ap, D, 1536, "Qw")
        prf = sb(es, nc, "Qprf", [128, 128], F32)
        prb = sb(es, nc, "Qprb", [128, 128], BF16)
        S.dma("sp", prf[:, :], prot_in, writes=[("Qprf", 0)])
        S.I("dve", "tensor_copy", prb[:, :], prf[:, :], reads=[("Qprf", 0)], writes=[("Qprb", 0)])
        N = alloc_norm_tiles(C, es, tag, TT)
        cs = [sb(es, nc, "Qcs%d" % i, [128, TT], F32) for i in range(2)]
        sn = [sb(es, nc, "Qsn%d" % i, [128, TT], F32) for i in range(2)]
        xb = [sb(es, nc, "Qxb%d" % i, [128, TT], BF16) for i in range(2)]
        t1 = [sb(es, nc, "Qt1%d" % i, [128, TT], F32) for i in range(2)]
        t2 = [sb(es, nc, "Qt2%d" % i, [128, TT], F32) for i in range(2)]
        qo = [sb(es, nc, "Qqo%d" % i, [128, TT], BF16) for i in range(2)]
        vo = [sb(es, nc, "Qvo%d" % i, [128, 256], F32) for i in range(2)]
        k = 0
        norm_load(C, N, x_src, 0)
        nxt = norm_tile(C, N, gcol, 0)
        for ti in range(NT):
            t0 = ti * TT
            if ti + 1 < NT:
                norm_load(C, N, x_src, ti + 1)
            c2 = ti % 2
            S.dma("sp", cs[c2][:, :], cos_in[:, t0:t0 + TT], writes=[("Qcs", c2)])
            S.dma("sp", sn[c2][:, :], sin_in[:, t0:t0 + TT], writes=[("Qsn", c2)])
            xs, hT, slot, tslot = nxt
            for oc in range(10):
                if oc == 6 and ti + 1 < NT:
                    nxt = norm_tile(C, N, gcol, ti + 1)
                b = C.psM[C.psM_i % len(C.psM)]
                C.psM_i += 1
                for kc in range(8):
                    S.I("pe", "matmul", C.ps[b][:, :], wb[:, kc, oc * 128:(oc + 1) * 128], hT[:, kc, :],
                        start=(kc == 0), stop=(kc == 7), reads=[("QhnT", tslot, kc)], writes=[("ps", b)])
                k2 = k % 2
                k += 1
                if C.dbg.get("qstop") == 1:
                    S.I("act", "activation", out=qo[k2][:, :], in_=C.ps[b][:, :], func=AF.Copy,
                        reads=[("ps", b)], writes=[("Qqo", k2)])
                    S.dma("sp", qd[oc * 128:(oc + 1) * 128, t0:t0 + TT], qo[k2][:, :], reads=[("Qqo", k2)])
                    continue
                S.I("dve", "tensor_copy", xb[k2][:, :], C.ps[b][:, :],
                    reads=[("ps", b)], writes=[("Qxb", k2)])
                b2 = C.psM[C.psM_i % len(C.psM)]
                C.psM_i += 1
                S.I("pe", "matmul", C.ps[b2][:, :], prb[:, :], xb[k2][:, :], start=True, stop=True,
                    reads=[("Qxb", k2), ("Qprb", 0)], writes=[("ps", b2)])
                S.I("dve", "tensor_tensor", t1[k2][:, :], C.ps[b][:, :], cs[c2][:, :], op=ALU.mult,
                    reads=[("ps", b), ("Qcs", c2)], writes=[("Qt1", k2)])
                S.I("dve", "tensor_tensor", t2[k2][:, :], C.ps[b2][:, :], sn[c2][:, :], op=ALU.mult,
                    reads=[("ps", b2), ("Qsn", c2)], writes=[("Qt2", k2)])
                S.I("pool", "tensor_tensor", qo[k2][:, :], t1[k2][:, :], t2[k2][:, :], op=ALU.add,
                    reads=[("Qt1", k2), ("Qt2", k2)], writes=[("Qqo", k2)])
                if oc < 8:
                    S.dma("pool", qd[oc * 128:(oc + 1) * 128, t0:t0 + TT], qo[k2][:, :], reads=[("Qqo", k2)])
                else:
                    S.dma("pool", kd[(oc - 8) * 128:(oc - 7) * 128, t0:t0 + TT], qo[k2][:, :], reads=[("Qqo", k2)])
            for s in range(4):
                if C.dbg.get("qstop") == 2:
                    break
                b = C.psM[C.psM_i % len(C.psM)]
                C.psM_i += 1
                for kc in range(8):
                    S.I("pe", "matmul", C.ps[b][:, 0:256], hT[:, kc, s * 128:(s + 1) * 128], wb[:, kc, 1280:1536],
                        start=(kc == 0), stop=(kc == 7), reads=[("QhnT", tslot, kc)], writes=[("ps", b)])
                S.I("act", "activation", out=vo[s % 2][:, :], in_=C.ps[b][:, 0:256], func=AF.Copy,
                    reads=[("ps", b)], writes=[("Qvo", s % 2)])
                S.dma("act", vd[t0 + s * 128:t0 + (s + 1) * 128, :], vo[s % 2][:, :], reads=[("Qvo", s % 2)])
        S.barrier()
        S.flush()


def phase_attn(C, x_src, qd, kd, vd, wo_ap, x_dst, masks_in):
    nc, S = C.nc, C.S
    W = C.W
    NB = T // 128
    with ExitStack() as es:
        wo = load_weight_bf16(C, es, wo_ap, D, D, "Two")
        sinkb = load_bcast_row(C, es, W["at_sink"], 16, "Tsink")
        mk = sb(es, nc, "Tmk", [128, 3, 384], F32)
        S.dma("sp", mk[:, :, :], masks_in.rearrange("m p s -> p m s"), writes=[("Tmk", 0)])
        qT = [sb(es, nc, "TqT%d" % i, [128, 8, 128], BF16) for i in range(2)]
        kT = [sb(es, nc, "TkT%d" % i, [128, 4, 384], BF16) for i in range(2)]
        vt = [sb(es, nc, "Tvt%d" % i, [128, 3, 256], BF16) for i in range(2)]
        vf = [sb(es, nc, "Tvf%d" % i, [128, 3, 256], F32) for i in range(2)]
        xt = [sb(es, nc, "Txt%d" % i, [128, D], F32) for i in range(2)]
        sm = [sb(es, nc, "Tsm%d" % i, [128, 384], F32) for i in range(4)]
        pb = [sb(es, nc, "Tpb%d" % i, [128, 384], BF16) for i in range(4)]
        pT = [sb(es, nc, "TpT%d" % i, [128, 3, 128], BF16) for i in range(4)]
        st = [sb(es, nc, "Tst%d" % i, [128, 8], F32) for i in range(4)]
        sbanks = [4, 5, 0, 1]
        tbanks = [2, 3]
        tbi = 0
        rdn = sb(es, nc, "Trdn", [128, 16], F32)
        ob = sb(es, nc, "Tob", [128, 16, 64], BF16)
        oT = sb(es, nc, "ToT", [128, 8, 128], BF16)
        hh = 0
        for n in range(NB):
            sl = n % 2
            kb0 = max(n - 1, 0)
            kb1 = min(n + 1, NB - 1)
            nk = kb1 - kb0 + 1
            mo = (kb0 - (n - 1)) * 128
            mi = 1 if n == NB // 2 - 1 else (2 if n == NB // 2 else 0)
            S.dma("sp", qT[sl][:, :, :], qd[:, n * 128:(n + 1) * 128].rearrange("(k p) t -> p k t", p=128),
                  writes=[("TqT", sl)])
            for kv in range(4):
                for dup in range(2):
                    S.dma("sp", kT[sl][dup * 64:(dup + 1) * 64, kv, 0:nk * 128],
                          kd[kv * 64:(kv + 1) * 64, kb0 * 128:(kb1 + 1) * 128], writes=[("TkT", sl)])
            S.dma("sp", vf[sl][:, 0:nk, :], vd[kb0 * 128:(kb1 + 1) * 128, :].rearrange("(k p) c -> p k c", p=128),
                  writes=[("Tvf", sl)])
            S.I("pool", "tensor_copy", vt[sl][:, 0:nk, :], vf[sl][:, 0:nk, :], reads=[("Tvf", sl)], writes=[("Tvt", sl)])
            S.dma("sp", xt[sl][:, :], x_src[n * 128:(n + 1) * 128, :], writes=[("Txt", sl)])
            bo = [6, 7]
            hp_ = []
            for h in range(16):
                h2 = hh % 4
                hh += 1
                bt = tbanks[tbi % 2]
                tbi += 1
                hp_.append((h2, sbanks[hh % 4], bt))

            def stA(h):
                kv = h // 4
                qc, hp = h // 2, h % 2
                h2, b, bt = hp_[h]
                smh, sth = sm[h2], st[h2]
                S.I("pe", "matmul", C.ps[b][:, 0:nk * 128], qT[sl][hp * 64:(hp + 1) * 64, qc, :],
                    kT[sl][hp * 64:(hp + 1) * 64, kv, 0:nk * 128], start=True, stop=True,
                    reads=[("TqT", sl), ("TkT", sl)], writes=[("ps", b)])
                S.I("dve", "scalar_tensor_tensor", out=smh[:, 0:nk * 128], in0=C.ps[b][:, 0:nk * 128], scalar=0.125,
                    in1=mk[:, mi, mo:mo + nk * 128], op0=ALU.mult, op1=ALU.add,
                    reads=[("ps", b), ("Tmk", 0)], writes=[("Tsm", h2)])
                S.I("dve", "reduce_max", sth[:, 0:1], smh[:, 0:nk * 128], axis=AX.X,
                    reads=[("Tsm", h2)], writes=[("Tst", h2, 0)])
                S.I("dve", "tensor_tensor", sth[:, 1:2], sth[:, 0:1], sinkb[:, h:h + 1], op=ALU.max,
                    reads=[("Tst", h2, 0), ("Tsink", 0)], writes=[("Tst", h2, 1)])
                S.I("dve", "tensor_scalar", sth[:, 2:3], sth[:, 1:2], -1.0, None, op0=ALU.mult,
                    reads=[("Tst", h2, 1)], writes=[("Tst", h2, 2)])

            def stB(h):
                h2, b, bt = hp_[h]
                smh, pbh, sth = sm[h2], pb[h2], st[h2]
                S.I("act", "activation", out=pbh[:, 0:nk * 128], in_=smh[:, 0:nk * 128], func=AF.Exp,
                    bias=sth[:, 2:3], accum_out=sth[:, 3:4],
                    reads=[("Tsm", h2), ("Tst", h2, 2)], writes=[("Tpb", h2), ("Tst", h2, 3)])
                S.I("act", "activation", out=sth[:, 4:5], in_=sinkb[:, h:h + 1], func=AF.Exp, bias=sth[:, 2:3],
                    reads=[("Tst", h2, 2), ("Tsink", 0)], writes=[("Tst", h2, 4)])
                S.I("dve", "tensor_tensor", sth[:, 5:6], sth[:, 3:4], sth[:, 4:5], op=ALU.add,
                    reads=[("Tst", h2, 3), ("Tst", h2, 4)], writes=[("Tst", h2, 5)])
                S.I("dve", "reciprocal", rdn[:, h:h + 1], sth[:, 5:6], reads=[("Tst", h2, 5)], writes=[("Trdn", h)])
                for kb in range(nk):
                    S.I("pe", "matmul", C.ps[bt][:, kb * 128:(kb + 1) * 128], pbh[:, kb * 128:(kb + 1) * 128],
                        C.ident[:, :], start=True, stop=True, reads=[("Tpb", h2)], writes=[("ps", bt)])

            def stC(h):
                kv = h // 4
                h2, b, bt = hp_[h]
                pTh = pT[h2]
                S.I("act", "activation", out=pTh[:, 0:nk, :],
                    in_=C.ps[bt][:, 0:nk * 128].rearrange("p (k c) -> p k c", k=nk), func=AF.Copy,
                    reads=[("ps", bt)], writes=[("TpT", h2)])
                bb = bo[h // 8]
                for kb in range(nk):
                    S.I("pe", "matmul", C.ps[bb][:, (h % 8) * 64:(h % 8 + 1) * 64], pTh[:, kb, :],
                        vt[sl][:, kb, kv * 64:(kv + 1) * 64], start=(kb == 0), stop=(kb == nk - 1),
                        reads=[("TpT", h2), ("Tvt", sl)], writes=[("ps", bb)])

            for i in range(18):
                if i < 16:
                    stA(i)
                if 0 <= i - 1 < 16:
                    stB(i - 1)
                if 0 <= i - 2 < 16:
                    stC(i - 2)
            for j in range(2):
                S.I("dve", "tensor_tensor", ob[:, j * 8:(j + 1) * 8, :],
                    C.ps[bo[j]][:, :].rearrange("p (h d) -> p h d", h=8),
                    rdn[:, j * 8:(j + 1) * 8].unsqueeze(2).broadcast_to([128, 8, 64]), op=ALU.mult,
                    reads=[("ps", bo[j])] + [("Trdn", hq) for hq in range(j * 8, j * 8 + 8)], writes=[("Tob", j)])
            for half in range(2):
                bt = tbanks[tbi % 2]
                tbi += 1
                for jj in range(4):
                    kc = half * 4 + jj
                    S.I("pe", "matmul", C.ps[bt][:, jj * 128:(jj + 1) * 128],
                        ob[:, 2 * kc:2 * kc + 2, :].rearrange("p h d -> p (h d)"),
                        C.ident[:, :], start=True, stop=True, reads=[("Tob", kc // 4)], writes=[("ps", bt)])
                S.I("act", "activation", out=oT[:, half * 4:(half + 1) * 4, :],
                    in_=C.ps[bt][:, :].rearrange("p (k c) -> p k c", k=4), func=AF.Copy,
                    reads=[("ps", bt)], writes=[("ToT", half)])
            for nh in range(2):
                b = sbanks[(hh + 1 + nh) % 4]
                for kc in range(8):
                    S.I("pe", "matmul", C.ps[b][:, :], oT[:, kc, :], wo[:, kc, nh * 512:(nh + 1) * 512],
                        start=(kc == 0), stop=(kc == 7), reads=[("ToT", kc // 4)], writes=[("ps", b)])
                S.I("dve", "tensor_tensor", xt[sl][:, nh * 512:(nh + 1) * 512], C.ps[b][:, :],
                    xt[sl][:, nh * 512:(nh + 1) * 512], op=ALU.add, reads=[("ps", b)], writes=[("Txt", sl)])
            S.dma("pool", x_dst[n * 128:(n + 1) * 128, :], xt[sl][:, :], reads=[("Txt", sl)])
        S.barrier()
        S.flush()


def phase_mlp(C, x_src, g_row, wup_ap, wdn_ap, x_dst, tag, final_g_row=None):
    nc, S = C.nc, C.S
    DFF = 4096
    TK = 256
    with ExitStack() as es:
        gcol = load_bcast_row(C, es, g_row, D, tag + "g")
        gfin = load_bcast_row(C, es, final_g_row, D, tag + "gf") if final_g_row is not None else None
        wup = load_weight_bf16(C, es, wup_ap, D, DFF, tag + "wu")
        wdn = load_weight_bf16(C, es, wdn_ap, DFF, D, tag + "wd")
        N = alloc_norm_tiles(C, es, tag, TK, nx=2, nT=2)
        hT = sb(es, nc, tag + "hT", [128, 32, TK], BF16)
        rl = [sb(es, nc, tag + "rl%d" % i, [128, TK], F32) for i in range(2)]
        ss2 = sb(es, nc, tag + "ss2", [128, 2], F32)
        rs2 = sb(es, nc, tag + "rs2", [128, 2], F32)
        norm_load(C, N, x_src, 0)
        nxt = norm_tile(C, N, gcol, 0)
        for ti in range(T // TK):
            if ti + 1 < T // TK:
                norm_load(C, N, x_src, ti + 1)
            xs, hTn, slot, tslot = nxt
            for fc in range(32):
                b = C.psM[C.psM_i % len(C.psM)]
                C.psM_i += 1
                for kc in range(8):
                    S.I("pe", "matmul", C.ps[b][:, 0:TK], wup[:, kc, fc * 128:(fc + 1) * 128], hTn[:, kc, :],
                        start=(kc == 0), stop=(kc == 7),
                        reads=[(tag + "hnT", tslot, kc)], writes=[("ps", b)])
                j = fc % 2
                if j == 0:
                    S.I("act", "activation", out=rl[0][:, :], in_=C.ps[b][:, 0:TK], func=AF.Relu,
                        reads=[("ps", b)], writes=[(tag + "rl", 0)])
                else:
                    S.I("dve", "tensor_scalar", rl[1][:, :], C.ps[b][:, 0:TK], 0.0, None, op0=ALU.max,
                        reads=[("ps", b)], writes=[(tag + "rl", 1)])
                S.I("pool", "tensor_tensor", hT[:, fc, :], rl[j][:, :], rl[j][:, :], op=ALU.mult,
                    reads=[(tag + "rl", j)], writes=[(tag + "hT", fc)])
            if ti + 1 < T // TK:
                nxt = norm_tile(C, N, gcol, ti + 1)
            for s in range(N.NS):
                xr = (tag + "xt", slot)
                for nh in range(2):
                    b = C.psM[C.psM_i % len(C.psM)]
                    C.psM_i += 1
                    for fc in range(32):
                        S.I("pe", "matmul", C.ps[b][:, :], hT[:, fc, s * 128:(s + 1) * 128],
                            wdn[:, fc, nh * 512:(nh + 1) * 512], start=(fc == 0), stop=(fc == 31),
                            reads=[(tag + "hT", fc)], writes=[("ps", b)])
                    S.I("dve", "tensor_tensor", xs[:, s, nh * 512:(nh + 1) * 512], C.ps[b][:, :],
                        xs[:, s, nh * 512:(nh + 1) * 512], op=ALU.add,
                        reads=[("ps", b)], writes=[xr])
                r0 = ti * TK + s * 128
                if gfin is not None:
                    S.I("act", "activation", out=N.junk[:, :], in_=xs[:, s, :], func=AF.Square,
                        accum_out=ss2[:, 0:1], reads=[xr], writes=[(tag + "ss2", 0), (tag + "junk", 0)])
                    S.I("act", "activation", out=rs2[:, 0:1], in_=ss2[:, 0:1], func=AF.Ln, scale=1.0 / D,
                        bias=C.eps5[:, 0:1], reads=[(tag + "ss2", 0)], writes=[(tag + "rs2", 0)])
                    S.I("act", "activation", out=rs2[:, 0:1], in_=rs2[:, 0:1], func=AF.Exp, scale=-0.5,
                        reads=[(tag + "rs2", 0)], writes=[(tag + "rs2", 0)])
                    S.I("dve", "scalar_tensor_tensor", out=xs[:, s, :], in0=xs[:, s, :], scalar=rs2[:, 0:1],
                        in1=gfin[:, :], op0=ALU.mult, op1=ALU.mult,
                        reads=[(tag + "rs2", 0)], writes=[xr])
                S.dma("sp", x_dst[r0:r0 + 128, :], xs[:, s, :], reads=[xr])
        S.barrier()
        S.flush()


WEIGHT_SPECS = [
    ("norm_mix", (2, 1024)), ("norm_mlp", (2, 1024)), ("norm_final", (1, 1024)),
    ("ab_w_in", (1024, 2560)), ("ab_w_out", (1024, 1024)),
    ("cv_dw_w", (31, 512)), ("cv_dw_b", (1, 512)), ("cv_ln_g", (1, 512)), ("cv_ln_b", (1, 512)),
    ("hy_short_w", (3, 1536)), ("hy_short_b", (1, 1536)),
    ("hy_w1", (33, 64)), ("hy_b1", (1, 64)), ("hy_w2", (64, 64)), ("hy_b2", (1, 64)),
    ("hy_w3", (64, 64)), ("hy_b3", (1, 64)), ("hy_w4", (64, 2048)),
    ("hy_freq", (3, 64)), ("hy_decay", (1, 2048)), ("hy_skip", (2, 512)),
    ("at_w_qkv", (1024, 1536)), ("at_sink", (1, 16)), ("at_w_o", (1024, 1024)),
    ("mlp_w_up", (2, 1024, 4096)), ("mlp_w_down", (2, 4096, 1024)),
]


def build_program(dbg=None):
    nc = bass.Bass("TRN2", target_bir_lowering=False)
    C = Ctx()
    C.nc = nc
    C.dbg = dbg or {}
    W = {}
    x_in = nc.dram_tensor("x", [T, D], F32, kind="ExternalInput").ap()
    for name, shp in WEIGHT_SPECS:
        W[name] = nc.dram_tensor(name, list(shp), F32, kind="ExternalInput").ap()
    ident_in = nc.dram_tensor("ident", [128, 128], F32, kind="ExternalInput").ap()
    flag_in = nc.dram_tensor("flag", [1, 8], F32, kind="ExternalInput").ap()
    hc128_np, hc65_np = hyena_consts()
    hc128_in = nc.dram_tensor("hc128", list(hc128_np.shape), F32, kind="ExternalInput").ap()
    hc65_in = nc.dram_tensor("hc65", list(hc65_np.shape), F32, kind="ExternalInput").ap()
    zT_in = nc.dram_tensor("zT", [33, 8192], F32, kind="ExternalInput").ap()
    tneg_in = nc.dram_tensor("tneg", [128, 128], F32, kind="ExternalInput").ap()
    cos_in = nc.dram_tensor("ropecos", [128, T], F32, kind="ExternalInput").ap()
    sin_in = nc.dram_tensor("ropesin", [128, T], F32, kind="ExternalInput").ap()
    prot_in = nc.dram_tensor("prot", [128, 128], F32, kind="ExternalInput").ap()
    masks_in = nc.dram_tensor("amasks", [3, 128, 384], F32, kind="ExternalInput").ap()
    y_out = nc.dram_tensor("y", [T, D], F32, kind="ExternalOutput").ap()
    C.W = W

    def scratch(name, shape, dt=F32):
        kind = "ExternalOutput" if name in C.dbg.get("out", ()) else "Internal"
        return nc.dram_tensor(name, list(shape), dt, kind=kind).ap()

    u = scratch("u", [2560, T])
    ya = scratch("ya", [512, T], BF16)
    uh = scratch("uh", [T, 1536])
    Gd = scratch("Gd", [2, 4, 3, 64, 2, 128, 65])
    yb = scratch("yb", [T, 512])
    x1 = scratch("x1", [T, D])
    x2 = scratch("x2", [T, D])
    x3 = scratch("x3", [T, D])
    qd = scratch("qd", [1024, T], BF16)
    kd = scratch("kd", [256, T], BF16)
    vd = scratch("vd", [T, 256])
    stop = C.dbg.get("stop")

    with ExitStack() as es:
        S = Sched(nc, es)
        C.S = S
        C.ps = [es.enter_context(nc.psum_tensor("ps%d" % i, [128, 512], F32)) for i in range(8)]
        C.psT = [0, 1, 2, 3]
        C.psM = [4, 5, 6, 7]
        C.psT_i = 0
        C.psM_i = 0
        C.ident = sb(es, nc, "ident_b", [128, 128], BF16)
        C.identf = sb(es, nc, "ident_f", [128, 128], F32)
        C.eps5 = sb(es, nc, "eps5", [128, 1], F32)
        S.I("pool", "memset", C.eps5[:, :], 1e-5)
        S.dma("sp", C.identf[:, :], ident_in[:, :], writes=[("identf", 0)])
        S.I("dve", "tensor_copy", C.ident[:, :], C.identf[:, :], reads=[("identf", 0)], writes=[("ident", 0)])
        C.onesf = sb(es, nc, "onesf", [128, 128], F32)
        S.I("pool", "memset", C.onesf[:, :], 1.0)
        C.flagcol = sb(es, nc, "flagcol", [128, 8], F32)
        S.dma("sp", C.flagcol[:, :], flag_in.partition_broadcast(128))
        C.hc128 = sb(es, nc, "hc128_t", list(hc128_np.shape), F32)
        C.hc65 = sb(es, nc, "hc65_t", list(hc65_np.shape), F32)
        S.dma("sp", C.hc128[:, :], hc128_in)
        S.dma("sp", C.hc65[:, :], hc65_in)
        S.barrier()
        S.flush()

        skip = C.dbg.get("skip", "")
        if stop != "M" and "A" not in skip:
            phase_inproj(C, x_in, W["norm_mix"][0:1, :], W["ab_w_in"], 2560, u, "A")
        if stop == "A":
            return nc
        if "B" not in skip:
            phase_conformer(C, u, ya)
        if stop == "B":
            return nc
        if "S" not in skip:
            phase_shortconv(C, u, uh)
        if stop == "S":
            return nc
        if "F" not in skip:
            phase_filters(C, Gd, zT_in, tneg_in)
        if stop == "F":
            return nc
        if "H" not in skip:
            phase_hyena(C, uh, Gd, yb)
        if stop == "H":
            return nc
        if "O" not in skip:
            phase_outproj(C, x_in, ya, yb, W["ab_w_out"], x1)
        if stop == "O":
            return nc
        if "M0" not in skip:
            phase_mlp(C, x1, W["norm_mlp"][0:1, :], W["mlp_w_up"][0], W["mlp_w_down"][0], x2, "M0")
        if stop == "M0":
            return nc
        xa = x_in if "X" in skip else x2
        phase_qkv(C, xa, W["norm_mix"][1:2, :], W["at_w_qkv"], qd, kd, vd, cos_in, sin_in, prot_in)
        if stop == "Q":
            return nc
        phase_attn(C, xa, qd, kd, vd, W["at_w_o"], x3, masks_in)
        if stop == "T":
            return nc
        phase_mlp(C, x3, W["norm_mlp"][1:2, :], W["mlp_w_up"][1], W["mlp_w_down"][1], y_out, "M1",
                  final_g_row=W["norm_final"][0:1, :])
        return nc
        phase_mlp(C, x_in, W["norm_mlp"][0:1, :], W["mlp_w_up"][0], W["mlp_w_down"][0], y_out, "M0",
                  final_g_row=W["norm_final"][0:1, :])
    return nc


_CACHE = {}


def make_in_maps(inputs):
    xp = np.asarray(inputs["x_prompt"], np.float32)
    xs = np.asarray(inputs["x_sample"], np.float32)
    base = {}
    for name, shp in WEIGHT_SPECS:
        base[name] = np.ascontiguousarray(np.asarray(inputs[name], np.float32)).reshape(shp)
    base["ident"] = np.eye(128, dtype=np.float32)
    base["hc128"], base["hc65"] = hyena_consts()
    ptab = {True: hyena_pos_tables(True), False: hyena_pos_tables(False)}
    atab = {True: attn_consts(True), False: attn_consts(False)}
    maps = []
    for c in range(NCORES):
        m = dict(base)
        m["flag"] = np.full((1, 8), 1.0 if c < 4 else 0.0, np.float32)
        m["zT"], m["tneg"] = ptab[c < 4]
        m["ropecos"], m["ropesin"], m["prot"], m["amasks"] = atab[c < 4]
        if c < 4:
            m["x"] = np.ascontiguousarray(xp[c])
        else:
            m["x"] = np.ascontiguousarray(xs[2 * (c - 4):2 * (c - 4) + 2].reshape(T, D))
        maps.append(m)
    return maps


def kernel(**inputs):
    if "nc" not in _CACHE:
        _CACHE["nc"] = build_program()
    nc = _CACHE["nc"]
    maps = make_in_maps(inputs)
    res = run_bass_kernel_spmd(nc, maps, core_ids=list(range(NCORES)))
    ys = [np.asarray(r["y"], np.float32) for r in res.results]
    y_prompt = np.stack(ys[0:4], axis=0)
    y_sample = np.concatenate([y.reshape(2, 4096, D) for y in ys[4:8]], axis=0)
    return (y_prompt, y_sample)
```
